# Optimizing a Trainium2 kernel written in Bass

```python
import math
import jax, jax.numpy as jnp
from jax import lax
import numpy as np

D_MODEL = 1024
BATCH = 8
SEQ = 4096
DEPTH = 4

HEAD_DIM = 64
D_MIX = 2 * D_MODEL
A_HEADS = D_MIX // 4 // HEAD_DIM
A_WIDTH = A_HEADS * HEAD_DIM
A_PATTERNS = ((128, 1), (512, 4), (2048, 16))
B_WIDTH = D_MIX // 2
B_HEADS = B_WIDTH // HEAD_DIM
B_GROUPS = 2
B_STATE = 128
B_CONV = 4
B_CHUNK = 128
C_HEADS = D_MIX // 4 // HEAD_DIM
C_KV_HEADS = 2
C_REP = C_HEADS // C_KV_HEADS
C_WIDTH = C_HEADS * HEAD_DIM
C_KV_WIDTH = C_KV_HEADS * HEAD_DIM
C_WINDOW = 128
BAND_BLOCK = 128
NORM_EPS = 1e-6

A_COLS = 4 * A_WIDTH
B_CONV_CH = B_WIDTH + 2 * B_GROUPS * B_STATE
B_COLS = B_WIDTH + B_CONV_CH + B_HEADS
C_COLS = 2 * C_WIDTH + 2 * C_KV_WIDTH
IN_COLS = A_COLS + B_COLS + C_COLS

kernel_name = "hymba_dilated_ssd_sinkswa_trunk"


def rms_norm(x, w):
    xf = x.astype(jnp.float32)
    y = xf * lax.rsqrt(jnp.mean(xf * xf, axis=-1, keepdims=True) + NORM_EPS)
    return (y * w.astype(jnp.float32)).astype(x.dtype)


def banded_window_attention(q, k, v, window, sinks=None):
    n, L, hkv, r, dh = q.shape
    blk = BAND_BLOCK
    nb = -(-L // blk)
    pad = nb * blk - L
    if pad:
        q = jnp.pad(q, ((0, 0), (0, pad), (0, 0), (0, 0), (0, 0)))
        k = jnp.pad(k, ((0, 0), (0, pad), (0, 0), (0, 0)))
        v = jnp.pad(v, ((0, 0), (0, pad), (0, 0), (0, 0)))
    qb = q.reshape(n, nb, blk, hkv, r, dh)
    kb = k.reshape(n, nb, blk, hkv, dh)
    vb = v.reshape(n, nb, blk, hkv, dh)
    zero = jnp.zeros_like(kb[:, :1])
    kk = jnp.concatenate([jnp.concatenate([zero, kb[:, :-1]], axis=1), kb], axis=2)
    vv = jnp.concatenate([jnp.concatenate([zero, vb[:, :-1]], axis=1), vb], axis=2)
    s = jnp.einsum('nbqhrd,nbkhd->nbhrqk', qb, kk,
                   preferred_element_type=jnp.float32) * (dh ** -0.5)
    qi = jnp.arange(blk)[:, None]
    kj = jnp.arange(2 * blk)[None, :]
    dist = blk + qi - kj
    band = (dist >= 0) & (dist <= window)
    has_prev = (jnp.arange(nb)[:, None, None] > 0) | (kj >= blk)[None]
    valid = band[None] & has_prev
    s = jnp.where(valid[None, :, None, None], s, -jnp.inf)
    m = jnp.max(s, axis=-1)
    if sinks is not None:
        sink = sinks.astype(jnp.float32)[None, None, :, :, None]
        m = jnp.maximum(m, sink)
    p = jnp.exp(s - m[..., None])
    l = jnp.sum(p, axis=-1)
    if sinks is not None:
        l = l + jnp.exp(sink - m)
    o = jnp.einsum('nbhrqk,nbkhd->nbqhrd', p, vv.astype(jnp.float32))
    m = jnp.moveaxis(m, -1, 2)
    l = jnp.moveaxis(l, -1, 2)
    o = o / l[..., None]
    o = o.reshape(n, nb * blk, hkv, r, dh)[:, :L]
    m = m.reshape(n, nb * blk, hkv, r)[:, :L]
    l = l.reshape(n, nb * blk, hkv, r)[:, :L]
    return o, m, l


def to_strided(t, dil):
    b, s = t.shape[0], t.shape[1]
    t = t.reshape(b, s // dil, dil, *t.shape[2:])
    t = jnp.moveaxis(t, 2, 1)
    return t.reshape(b * dil, s // dil, *t.shape[3:])


def from_strided(t, b, dil):
    L = t.shape[1]
    t = t.reshape(b, dil, L, *t.shape[2:])
    t = jnp.moveaxis(t, 1, 2)
    return t.reshape(b, L * dil, *t.shape[3:])


def dilated_attention(q, k, v):
    b = q.shape[0]
    outs, ms, ls = [], [], []
    for window, dil in A_PATTERNS:
        o, m, l = banded_window_attention(to_strided(q, dil)[:, :, :, None], to_strided(k, dil),
                                          to_strided(v, dil), window // dil)
        outs.append(from_strided(o[:, :, :, 0], b, dil))
        ms.append(from_strided(m[:, :, :, 0], b, dil))
        ls.append(from_strided(l[:, :, :, 0], b, dil))
    m_all = jnp.stack(ms)
    l_all = jnp.stack(ls)
    o_all = jnp.stack(outs)
    wts = l_all * jnp.exp(m_all - jnp.max(m_all, axis=0))
    wts = wts / jnp.sum(wts, axis=0)
    return jnp.sum(o_all * wts[..., None], axis=0)


def causal_depthwise_conv(x, w, bias):
    ch = x.shape[-1]
    y = lax.conv_general_dilated(x, w[:, None, :].astype(x.dtype), window_strides=(1,),
                                 padding=((w.shape[0] - 1, 0),),
                                 dimension_numbers=('NWC', 'WIO', 'NWC'),
                                 feature_group_count=ch)
    return y + bias.astype(x.dtype)


def ssd_scan(xs, dt, a, bm, cm):
    b, s, h, p = xs.shape
    g, n = bm.shape[2], bm.shape[3]
    hg = h // g
    q = B_CHUNK
    nc = s // q
    f32 = jnp.float32
    x = xs.astype(f32).reshape(b, nc, q, g, hg, p)
    dtc = dt.reshape(b, nc, q, g, hg)
    bc = bm.astype(f32).reshape(b, nc, q, g, n)
    cc = cm.astype(f32).reshape(b, nc, q, g, n)
    xdt = x * dtc[..., None]
    a_cum = jnp.cumsum(dtc * a.reshape(g, hg), axis=2)
    seg = a_cum[:, :, :, None] - a_cum[:, :, None, :]
    causal = jnp.tril(jnp.ones((q, q), dtype=bool))
    decay = jnp.exp(jnp.where(causal[:, :, None, None], seg, -jnp.inf))
    cb = jnp.einsum('bclgn,bcsgn->bclsg', cc, bc)
    y_diag = jnp.einsum('bclsgh,bcsghp->bclghp', cb[..., None] * decay, xdt)
    decay_to_end = jnp.exp(a_cum[:, :, -1:] - a_cum)
    states = jnp.einsum('bclgn,bclghp->bcghpn', bc, xdt * decay_to_end[..., None])
    chunk_decay = jnp.exp(a_cum[:, :, -1])

    def step(h_state, inp):
        st, dec = inp
        return h_state * dec[..., None, None] + st, h_state

    h0 = jnp.zeros((b, g, hg, p, n), f32)
    _, h_in = lax.scan(step, h0, (jnp.moveaxis(states, 1, 0), jnp.moveaxis(chunk_decay, 1, 0)))
    h_in = jnp.moveaxis(h_in, 0, 1)
    y_off = jnp.einsum('bclgn,bcghpn->bclghp', cc, h_in) * jnp.exp(a_cum)[..., None]
    return (y_diag + y_off).reshape(b, s, h, p)


def mamba2_mixer(z, xbc, dt_raw, conv_w, conv_b, dt_bias, a_log, d_skip, norm_w):
    b, s, _ = xbc.shape
    xbc = jax.nn.silu(causal_depthwise_conv(xbc, conv_w, conv_b))
    xs, bm, cm = jnp.split(xbc, [B_WIDTH, B_WIDTH + B_GROUPS * B_STATE], axis=-1)
    xs = xs.reshape(b, s, B_HEADS, HEAD_DIM)
    bm = bm.reshape(b, s, B_GROUPS, B_STATE)
    cm = cm.reshape(b, s, B_GROUPS, B_STATE)
    dt = jax.nn.softplus(dt_raw.astype(jnp.float32) + dt_bias.astype(jnp.float32))
    a = -jnp.exp(a_log.astype(jnp.float32))
    y = ssd_scan(xs, dt, a, bm, cm)
    y = y + d_skip.astype(jnp.float32)[:, None] * xs.astype(jnp.float32)
    y = y.reshape(b, s, B_WIDTH) * jax.nn.silu(z.astype(jnp.float32))
    yg = y.reshape(b, s, B_GROUPS, B_WIDTH // B_GROUPS)
    yg = yg * lax.rsqrt(jnp.mean(yg * yg, axis=-1, keepdims=True) + NORM_EPS)
    y = yg.reshape(b, s, B_WIDTH) * norm_w.astype(jnp.float32)
    return y.astype(z.dtype)


def setup_inputs(seed: int = 0) -> dict:
    key = jax.random.key(seed)
    ks = jax.random.split(key, 16)
    f32 = jnp.float32
    x = jax.random.normal(ks[0], (BATCH, SEQ, D_MODEL), f32)
    c = jax.random.normal(ks[1], (BATCH, D_MODEL), f32)
    ada_w = jax.random.normal(ks[2], (DEPTH, D_MODEL, 3 * D_MODEL), f32) * (0.5 * D_MODEL ** -0.5)
    ada_b = jax.random.normal(ks[3], (DEPTH, 3 * D_MODEL), f32) * 0.02
    pre_norm_w = 1.0 + 0.02 * jax.random.normal(ks[4], (DEPTH, D_MODEL), f32)
    post_norm_w = 1.0 + 0.02 * jax.random.normal(ks[5], (DEPTH, D_MODEL), f32)
    w_in = jax.random.normal(ks[6], (DEPTH, D_MODEL, IN_COLS), f32) * D_MODEL ** -0.5
    conv_w = jax.random.normal(ks[7], (DEPTH, B_CONV, B_CONV_CH), f32) * B_CONV ** -0.5
    conv_b = 0.02 * jax.random.normal(ks[8], (DEPTH, B_CONV_CH), f32)
    dt0 = jnp.exp(jax.random.uniform(ks[9], (DEPTH, B_HEADS), f32,
                                     math.log(1e-3), math.log(1e-1)))
    dt_bias = dt0 + jnp.log(-jnp.expm1(-dt0))
    a_log = jnp.log(jax.random.uniform(ks[10], (DEPTH, B_HEADS), f32, 1.0, 16.0))
    d_skip = 1.0 + 0.02 * jax.random.normal(ks[11], (DEPTH, B_HEADS), f32)
    ssm_norm_w = 1.0 + 0.02 * jax.random.normal(ks[12], (DEPTH, B_WIDTH), f32)
    sinks = 0.5 * jax.random.normal(ks[13], (DEPTH, C_HEADS), f32)
    w_out = jax.random.normal(ks[14], (DEPTH, D_MIX, D_MODEL), f32) * D_MIX ** -0.5
    return {"x": x, "c": c, "ada_w": ada_w, "ada_b": ada_b, "pre_norm_w": pre_norm_w,
            "post_norm_w": post_norm_w, "w_in": w_in, "conv_w": conv_w, "conv_b": conv_b,
            "dt_bias": dt_bias, "a_log": a_log, "d_skip": d_skip, "ssm_norm_w": ssm_norm_w,
            "sinks": sinks, "w_out": w_out}


def reference(x, c, ada_w, ada_b, pre_norm_w, post_norm_w, w_in, conv_w, conv_b,
              dt_bias, a_log, d_skip, ssm_norm_w, sinks, w_out):
    b, s, _ = x.shape
    c_act = jax.nn.silu(c)
    for i in range(DEPTH):
        mod = c_act @ ada_w[i] + ada_b[i]
        shift, scale, gate = jnp.split(mod, 3, axis=-1)
        h = rms_norm(x, pre_norm_w[i]) * (1.0 + scale[:, None]) + shift[:, None]
        proj = h @ w_in[i]
        pa, pb, pc = jnp.split(proj, [A_COLS, A_COLS + B_COLS], axis=-1)

        qa, ka, va, za = jnp.split(pa, 4, axis=-1)
        ya = dilated_attention(qa.reshape(b, s, A_HEADS, HEAD_DIM), ka.reshape(b, s, A_HEADS, HEAD_DIM),
                               va.reshape(b, s, A_HEADS, HEAD_DIM))
        ya = ya.reshape(b, s, A_WIDTH).astype(x.dtype) * jax.nn.silu(za)

        zb, xbc, dtb = jnp.split(pb, [B_WIDTH, B_WIDTH + B_CONV_CH], axis=-1)
        yb = mamba2_mixer(zb, xbc, dtb, conv_w[i], conv_b[i], dt_bias[i], a_log[i],
                          d_skip[i], ssm_norm_w[i])

        qc, zc, kc, vc = jnp.split(pc, [C_WIDTH, 2 * C_WIDTH, 2 * C_WIDTH + C_KV_WIDTH], axis=-1)
        oc, _, _ = banded_window_attention(qc.reshape(b, s, C_KV_HEADS, C_REP, HEAD_DIM),
                                           kc.reshape(b, s, C_KV_HEADS, HEAD_DIM),
                                           vc.reshape(b, s, C_KV_HEADS, HEAD_DIM),
                                           C_WINDOW, sinks[i].reshape(C_KV_HEADS, C_REP))
        yc = oc.reshape(b, s, C_WIDTH).astype(x.dtype) * jax.nn.silu(zc)

        y = jnp.concatenate([ya, yb, yc], axis=-1) @ w_out[i]
        x = x + gate[:, None] * rms_norm(y, post_norm_w[i])
    return x
```

```python
import contextlib
import numpy as np
import concourse.bass as bass
import concourse.mybir as mybir
from concourse.bass_utils import run_bass_kernel_spmd

F32 = mybir.dt.float32
BF16 = mybir.dt.bfloat16
AF = mybir.ActivationFunctionType
ALU = mybir.AluOpType

SEQ = 4096
DM = 1024
NT = SEQ // 128
EPS = 1e-6
DEPTH = 4
IN_COLS = 5904
C_OFF = 4624
ENGS = ("pe", "act", "dve", "pool", "sp")


def sl(start, n, step=1):
    return slice(start, start + step * (n - 1) + 1, step)


class Sched:
    def __init__(self, nc, stack):
        self.nc = nc
        self.stack = stack
        self.esem = {e: stack.enter_context(nc.semaphore("s_" + e)) for e in ENGS}
        self._names = {id(v): "s_" + k for k, v in self.esem.items()}
        self.ecount = {e: 0 for e in ENGS}
        self.dsem = {}
        self.dcount = {}
        self.waited = {e: {} for e in ENGS}
        self.n_ops = 0
        self.reset_block()

    def reset_block(self):
        self.ops = []
        self.last_w = {}
        self.readers = {}

    def _dma_sem(self, key):
        if key not in self.dsem:
            self.dsem[key] = self.stack.enter_context(self.nc.semaphore("d%d" % len(self.dsem)))
            self.dcount[key] = 0
            self._names[id(self.dsem[key])] = "d_" + str(key)
        return self.dsem[key]

    def capture(self):
        self._cap = []
        return self._cap

    def end_capture(self):
        lst, self._cap = self._cap, None
        return lst

    _DUR = {"pe": 0.2, "act": 0.6, "dve": 0.75, "pool": 1.1, "sp": 2.0}

    def replay_merged(self, lists):
        if not hasattr(self, "_sim_eng"):
            self._sim_eng = {e: 0.0 for e in ENGS}
            self._sim_key = {}
        its = [list(l) for l in lists if l]
        pos = [0] * len(its)
        while True:
            best, best_t = None, None
            for i in range(len(its)):
                if pos[i] >= len(its[i]):
                    continue
                eng, fn, reads, writes, dma_key = its[i][pos[i]]
                t = self._sim_eng[eng]
                for k in reads:
                    t = max(t, self._sim_key.get(k, 0.0))
                for k in writes:
                    t = max(t, self._sim_key.get(k, 0.0))
                if best is None or t < best_t - 1e-9:
                    best, best_t = i, t
            if best is None:
                break
            a = its[best][pos[best]]
            pos[best] += 1
            eng, fn, reads, writes, dma_key = a
            fin = best_t + self._DUR[eng]
            self._sim_eng[eng] = fin
            for k in writes:
                self._sim_key[k] = fin
            self.op(*a)

    def op(self, eng, fn, reads=(), writes=(), dma_key=None):
        if getattr(self, "_cap", None) is not None:
            self._cap.append((eng, fn, tuple(reads), tuple(writes), dma_key))
            return None
        idx = len(self.ops)
        deps = set()
        for k in reads:
            w = self.last_w.get(k)
            if w is not None:
                deps.add(w)
        for k in writes:
            w = self.last_w.get(k)
            if w is not None:
                deps.add(w)
            for r in self.readers.get(k, ()):
                deps.add(r)
        deps.discard(idx)
        self.ops.append(dict(eng=eng, fn=fn, deps=deps, dma_key=dma_key, milestone=False))
        for k in reads:
            self.readers.setdefault(k, []).append(idx)
        for k in writes:
            self.last_w[k] = idx
            self.readers[k] = []
        return idx

    def emit_block(self, name=None):
        nc = self.nc
        ops = self.ops
        for o in ops:
            keep = set()
            for d in o["deps"]:
                s = ops[d]
                if s["dma_key"] is None and o["dma_key"] is None and s["eng"] == o["eng"] == "pe":
                    continue
                keep.add(d)
            o["deps"] = keep
            for d in keep:
                if ops[d]["dma_key"] is None:
                    ops[d]["milestone"] = True
        ecount = dict(self.ecount)
        dcount = dict(self.dcount)
        for o in ops:
            if o["dma_key"] is not None:
                self._dma_sem(o["dma_key"])
                dcount[o["dma_key"]] = dcount.get(o["dma_key"], 0) + 16
                o["dval"] = dcount[o["dma_key"]]
            elif o["milestone"]:
                ecount[o["eng"]] += 1
                o["mval"] = ecount[o["eng"]]
        per_eng = {e: [o for o in ops if o["eng"] == e] for e in ENGS}
        final_d = dict(dcount)
        sched = self

        def emit_engine(ename, engine):
            waited = sched.waited[ename]
            for o in per_eng[ename]:
                need = {}
                for d in o["deps"]:
                    s = ops[d]
                    if s["dma_key"] is not None:
                        sem, val = sched.dsem[s["dma_key"]], s["dval"]
                    else:
                        sem, val = sched.esem[s["eng"]], s["mval"]
                    key = sched._names[id(sem)]
                    if val > need.get(key, (None, 0))[1]:
                        need[key] = (sem, val)
                for key, (sem, val) in need.items():
                    if waited.get(key, 0) >= val:
                        continue
                    engine.wait_ge(sem, val)
                    waited[key] = val
                ins = o["fn"](engine)
                if o["dma_key"] is not None:
                    ins.then_inc(sched.dsem[o["dma_key"]], 16)
                elif o["milestone"]:
                    ins.then_inc(sched.esem[ename], 1)
            if ename == "sp":
                for k, v in final_d.items():
                    sem = sched.dsem[k]
                    key = sched._names[id(sem)]
                    if waited.get(key, 0) < v:
                        engine.wait_ge(sem, v)
                        waited[key] = v

        with nc.Block(name) as block:
            @block.tensor
            def _(e):
                emit_engine("pe", e)

            @block.scalar
            def _(e):
                emit_engine("act", e)

            @block.vector
            def _(e):
                emit_engine("dve", e)

            @block.gpsimd
            def _(e):
                emit_engine("pool", e)

            @block.sync
            def _(e):
                emit_engine("sp", e)
        self.ecount = ecount
        self.dcount = dcount
        self.n_ops += len(ops)
        self.reset_block()


class K:
    pass


def dbg(g, name, ap, shape, dtype, reads):
    if not getattr(g, "debug", False):
        return
    import os
    taps = os.environ.get("DBG_TAPS", "")
    if not any(name == t or name.startswith(t + "_") for t in taps.split(",") if t):
        return
    d = g.nc.dram_tensor("dbg_" + name, list(shape), dtype, kind="ExternalOutput").ap()
    idx = tuple(slice(None) for _ in shape)
    g.S.op("sp", lambda e: e.dma_start(out=d[idx], in_=ap), reads=reads, dma_key=("dbg", name))


def build(n_layers=DEPTH, debug=False, phases=("mod", "p1", "att", "ssd", "out"), depth_dim=DEPTH):
    nc = bass.Bass("TRN2", target_bir_lowering=False)

    def din(name, shape):
        return nc.dram_tensor(name, shape, F32, kind="ExternalInput").ap()

    g = K()
    g.nc = nc
    g.uid = [0]
    g.debug = debug

    def _uniq(name):
        g.uid[0] += 1
        return "%s_%d" % (name, g.uid[0])
    g.sbuf = lambda name, shape, dt: nc.sbuf_tensor(_uniq(name), shape, dt)
    g.psum = lambda name, shape, dt: nc.psum_tensor(_uniq(name), shape, dt)
    g.x_in = din("x", [SEQ, DM])
    g.c_col = din("c_col", [128, 8])
    DD = depth_dim
    g.ada_w = din("ada_w", [DD, DM, 3 * DM])
    g.ada_b = din("ada_b", [DD, 3 * DM])
    g.pre_w = din("pre_norm_w", [DD, DM])
    g.post_w = din("post_norm_w", [DD, DM])
    g.w_in = din("w_in", [DD, DM, IN_COLS])
    g.conv_w = din("conv_w", [DD, 4, 1536])
    g.conv_b = din("conv_b", [DD, 1536])
    g.dt_bias = din("dt_bias", [DD, 16])
    g.a_log = din("a_log", [DD, 16])
    g.d_skip = din("d_skip", [DD, 16])
    g.ssm_w = din("ssm_norm_w", [DD, DM])
    g.sinks = din("sinks", [DD, 8])
    g.w_out = din("w_out", [DD, 2 * DM, DM])
    g.consts = din("consts", [128, 7, 128])
    g.out = nc.dram_tensor("out", [SEQ, DM], F32, kind="ExternalOutput").ap()
    g.ycat = nc.dram_tensor("ycat", [2 * DM, SEQ], BF16,
                            kind="ExternalOutput" if debug else "Internal").ap()

    with contextlib.ExitStack() as st:
        S = Sched(nc, st)
        g.S = S
        sb = lambda name, shape, dt: st.enter_context(nc.sbuf_tensor(name, shape, dt))
        g.cf = sb("cf", [128, 7, 128], F32)
        g.identb = sb("identb", [128, 128], BF16)
        g.mask2 = sb("mask2", [128, 256], BF16)
        g.u1b = sb("u1b", [128, 128], BF16)
        g.cbc = sb("cbc", [128, 8, 128], F32)
        g.modb = sb("modb", [128, 3 * DM], F32)
        g.hT = sb("hT", [128, 8, SEQ], BF16)

        phase_init(g)
        for L in range(n_layers):
            src = g.x_in if L == 0 else g.out
            if "mod" in phases:
                phase_mod(g, L)
            if "p1" in phases:
                phase_p1(g, L, src)
            if "att" in phases:
                specs = []
                for hp in range(4):
                    base = hp * 128
                    specs.append(dict(q=[(base, 128)], k=[(512 + base, 128)], v=[(1024 + base, 128)],
                                      z=[(1536 + base, 128)], pats=(1, 4, 16), sink=None,
                                      rows=(base, base + 64)))
                for i in range(4):
                    specs.append(dict(q=[(C_OFF + i * 64, 64), (C_OFF + (4 + i) * 64, 64)],
                                      k=[(C_OFF + 1024, 128)], v=[(C_OFF + 1152, 128)],
                                      z=[(C_OFF + 512 + i * 64, 64), (C_OFF + 512 + (4 + i) * 64, 64)],
                                      pats=(1,), sink=(i, 4 + i), reuse_kv=(i > 0),
                                      rows=(1536 + i * 64, 1536 + (4 + i) * 64)))
                phase_att(g, L, specs)
            if "ssd" in phases:
                for grp in range(2):
                    phase_ssd(g, L, grp)
            if "out" in phases:
                phase_out(g, L, src)
        g.n_ops = S.n_ops
    return nc


def phase_init(g):
    nc, S = g.nc, g.S
    with contextlib.ExitStack() as st:
        cc = st.enter_context(g.sbuf("cc", [128, 8], F32))
        ca = st.enter_context(g.sbuf("ca", [128, 8], F32))
        S.op("sp", lambda e: e.dma_start(out=g.cf[:], in_=g.consts[:, :, :]), writes=["cf"], dma_key="cf")
        S.op("sp", lambda e: e.dma_start(out=cc[:], in_=g.c_col[:, :]), writes=["cc"], dma_key="cc")
        S.op("pool", lambda e: e.tensor_copy(out=g.identb[:], in_=g.cf[:, 0, :]), reads=["cf"], writes=["identb"])
        S.op("pool", lambda e: e.tensor_copy(out=g.mask2[:].rearrange("p (a b) -> p a b", a=2), in_=g.cf[:, 1:3, :]),
             reads=["cf"], writes=["mask2"])
        S.op("pool", lambda e: e.tensor_copy(out=g.u1b[:], in_=g.cf[:, 3, :]), reads=["cf"], writes=["u1b"])
        S.op("act", lambda e: e.activation(out=ca[:], in_=cc[:], func=AF.Silu), reads=["cc"], writes=["ca"])
        S.op("dve", lambda e: e.tensor_copy(out=g.cbc[:], in_=ca[:].unsqueeze(2).to_broadcast([128, 8, 128])),
             reads=["ca"], writes=["cbc"])
        S.emit_block("init")


def phase_mod(g, L):
    nc, S = g.nc, g.S
    with contextlib.ExitStack() as st:
        stage = [st.enter_context(g.sbuf("mstage%d" % i, [128, 8, 512], F32)) for i in range(2)]
        adab = st.enter_context(g.sbuf("adab", [128, 3 * DM], F32))
        pw = st.enter_context(g.sbuf("pw", [128, 2, DM], F32))
        ps = [st.enter_context(g.psum("mps%d" % i, [128, 512], F32)) for i in range(2)]
        S.op("sp", lambda e: e.dma_start(out=adab[:], in_=g.ada_b[L:L + 1, :].partition_broadcast(128)),
             writes=["adab"], dma_key="adab")
        S.op("sp", lambda e: e.dma_start(out=pw[:, 0, :], in_=g.pre_w[L:L + 1, :].partition_broadcast(128)),
             writes=["pw0"], dma_key="pw0")
        S.op("sp", lambda e: e.dma_start(out=pw[:, 1, :], in_=g.post_w[L:L + 1, :].partition_broadcast(128)),
             writes=["pw1"], dma_key="pw1")
        aw = g.ada_w[L].rearrange("(ch p) n -> p ch n", p=128)
        for grp in range(6):
            b = grp % 2
            cs = slice(grp * 512, (grp + 1) * 512)
            S.op("sp", lambda e, b=b, cs=cs: e.dma_start(out=stage[b][:], in_=aw[:, :, cs]),
                 writes=[("mst", b)], dma_key=("mst", b))
            for ch in range(8):
                S.op("pe", lambda e, b=b, ch=ch: e.matmul(ps[b][:], lhsT=g.cbc[:, ch, :], rhs=stage[b][:, ch, :],
                                                          start=(ch == 0), stop=(ch == 7)),
                     reads=[("mst", b), "cbc"], writes=[("mps", b)])
            S.op("dve", lambda e, b=b, cs=cs: e.tensor_tensor(out=g.modb[:, cs], in0=ps[b][:], in1=adab[:, cs], op=ALU.add),
                 reads=[("mps", b), "adab"], writes=["modb"])
        S.op("dve", lambda e: e.scalar_tensor_tensor(out=g.modb[:, DM:2 * DM], in0=g.modb[:, DM:2 * DM], scalar=1.0,
                                                     in1=pw[:, 0, :], op0=ALU.add, op1=ALU.mult),
             reads=["modb", "pw0"], writes=["modb"])
        S.op("dve", lambda e: e.tensor_tensor(out=g.modb[:, 2 * DM:3 * DM], in0=g.modb[:, 2 * DM:3 * DM], in1=pw[:, 1, :],
                                              op=ALU.mult),
             reads=["modb", "pw1"], writes=["modb"])
        S.emit_block("mod%d" % L)


def phase_p1(g, L, src):
    nc, S = g.nc, g.S
    NB = 4
    with contextlib.ExitStack() as st:
        xt = [st.enter_context(g.sbuf("p1x%d" % i, [128, DM], F32)) for i in range(NB)]
        tmp = [st.enter_context(g.sbuf("p1t%d" % i, [128, DM], F32)) for i in range(NB)]
        hb = [st.enter_context(g.sbuf("p1h%d" % i, [128, DM], BF16)) for i in range(NB)]
        junk = st.enter_context(g.sbuf("p1junk", [128, DM], BF16))
        stat = [st.enter_context(g.sbuf("p1s%d" % i, [128, 4], F32)) for i in range(NB)]
        pst = [st.enter_context(g.psum("p1ps%d" % i, [128, 8, 128], BF16)) for i in range(2)]

        def s1(t):
            b = t % NB
            rows = slice(t * 128, (t + 1) * 128)
            S.op("sp", lambda e: e.dma_start(out=xt[b][:], in_=src[rows, :]), writes=[("x", b)], dma_key=("p1x", b))
            S.op("act", lambda e: e.activation(out=junk[:], in_=xt[b][:], func=AF.Square, accum_out=stat[b][:, 0:1]),
                 reads=[("x", b)], writes=["junk", ("s0", b)])
            S.op("act", lambda e: e.activation(out=stat[b][:, 1:2], in_=stat[b][:, 0:1], func=AF.Sqrt, scale=1.0 / DM, bias=EPS),
                 reads=[("s0", b)], writes=[("s1", b)])
            S.op("dve", lambda e: e.reciprocal(out=stat[b][:, 2:3], in_=stat[b][:, 1:2]), reads=[("s1", b)], writes=[("s2", b)])
            S.op("dve", lambda e: e.scalar_tensor_tensor(out=tmp[b][:], in0=xt[b][:], scalar=stat[b][:, 2:3],
                                                         in1=g.modb[:, DM:2 * DM], op0=ALU.mult, op1=ALU.mult),
                 reads=[("x", b), ("s2", b)], writes=[("t", b)])
            S.op("pool" if t % 2 == 0 else "dve",
                 lambda e: e.tensor_tensor(out=hb[b][:], in0=tmp[b][:], in1=g.modb[:, 0:DM], op=ALU.add),
                 reads=[("t", b)], writes=[("h", b)])

        def s2(t):
            b = t % NB
            pb = t % 2
            rows = slice(t * 128, (t + 1) * 128)
            for ch in range(8):
                S.op("pe", lambda e, ch=ch: e.transpose(pst[pb][:, ch, :], hb[b][:, ch * 128:(ch + 1) * 128], g.identb[:]),
                     reads=[("h", b)], writes=[("ps", pb)])
            S.op("act", lambda e: e.activation(out=g.hT[:, :, rows], in_=pst[pb][:], func=AF.Copy),
                 reads=[("ps", pb)], writes=[("hT", t)])

        s1(0)
        s1(1)
        for t in range(NT):
            if t + 2 < NT:
                s1(t + 2)
            s2(t)
        S.emit_block("p1_%d" % L)


def _groups(d, b):
    if d == 1:
        return [b // 4]
    if d == 4:
        return [b]
    return [4 * b + i for i in range(4)]


def phase_att(g, L, specs):
    nc, S = g.nc, g.S
    with contextlib.ExitStack() as st:
        sbt = lambda name, shape, dt: st.enter_context(g.sbuf(name, shape, dt))
        wst = [sbt("awst%d" % i, [128, 8, 128], F32) for i in range(2)]
        wts = [{n: sbt("aw_%s%d" % (n, i), [128, 8, 128], BF16) for n in "qkvz"} for i in range(2)]
        QT = sbt("QT", [128, SEQ], BF16)
        KT = sbt("KT", [128, SEQ], BF16)
        VT = sbt("VTf", [128, SEQ], BF16)
        Vt = {d: sbt("Vt%d" % d, [128, 32, 2, 65], BF16) for d in (1, 4, 16)}
        NSB = 3
        Et = [sbt("E%d" % i, [128, 2, 256], BF16) for i in range(NSB)]
        Pt = [sbt("P%d" % i, [128, 2, 256], BF16) for i in range(NSB)]
        Acc = sbt("Acc", [65, 2, SEQ], F32)
        Ut = [sbt("U%d" % i, [64, 512], F32) for i in range(2)]
        Tt = [sbt("T%d" % i, [64, 512], F32) for i in range(2)]
        Yb = [sbt("Yb%d" % i, [64, 512], BF16) for i in range(2)]
        sks = [sbt("sk%d" % i, [64, 2], F32) for i in range(2)]
        Sp = [st.enter_context(g.psum("aS%d" % i, [128, 2, 512], F32)) for i in range(NSB)]
        Op = [st.enter_context(g.psum("aO%d" % i, [128, 512], F32)) for i in range(2)]
        banks = [(i, j) for i in range(NSB) for j in range(2)]
        bk = lambda ij: Sp[ij[0]][:, ij[1], :]
        bkey = lambda ij: ("Sb", ij[0], ij[1])
        wv_in = g.w_in[L].rearrange("(ch p) n -> p ch n", p=128)
        for d in (1, 4, 16):
            S.op("pool", lambda e, d=d: e.memset(Vt[d][:, :, :, 64:65], 1.0), writes=[("Vone", d)])
        pj = 0
        step = 0
        fj = 0
        def load_weights(si):
            sp = specs[si]
            wt = wts[si % 2]
            sk = sks[si % 2]
            for wi, n in enumerate("qkvz"):
                b = wi % 2
                off = 0
                for pi, (c0, cn) in enumerate(sp[n]):
                    S.op("sp", lambda e, b=b, c0=c0, cn=cn, off=off: e.dma_start(out=wst[b][:, :, off:off + cn],
                                                                               in_=wv_in[:, :, c0:c0 + cn]),
                         writes=[("wst", b, pi)], dma_key=("awst", b, pi))
                    off += cn
                ceng = ("pool", "dve", "pool", "act")[wi]
                if ceng == "act":
                    S.op("act", lambda e, b=b, n=n, wt=wt: e.activation(out=wt[n][:], in_=wst[b][:], func=AF.Copy),
                         reads=[("wst", b, 0), ("wst", b, 1)], writes=[("w", n, si % 2)])
                else:
                    S.op(ceng, lambda e, b=b, n=n, wt=wt: e.tensor_copy(out=wt[n][:], in_=wst[b][:]),
                         reads=[("wst", b, 0), ("wst", b, 1)], writes=[("w", n, si % 2)])
            if sp["sink"] is not None:
                for h in range(2):
                    hh = sp["sink"][h]
                    S.op("sp", lambda e, h=h, hh=hh, sk=sk: e.dma_start(out=sk[:, h:h + 1],
                                                                        in_=g.sinks[L:L + 1, hh:hh + 1].partition_broadcast(64)),
                         writes=[("skr", si % 2, h)], dma_key=("sk", si % 2, h))
                S.op("act", lambda e, sk=sk: e.activation(out=sk[:], in_=sk[:], func=AF.Exp),
                     reads=[("skr", si % 2, 0), ("skr", si % 2, 1)], writes=[("sk", si % 2)])

        load_weights(0)
        for si, sp in enumerate(specs):
            pats = sp["pats"]
            wt = wts[si % 2]
            sk = sks[si % 2]
            wk = lambda n, si=si: ("w", n, si % 2)
            for n in range(8):
                ts_ = slice(n * 512, (n + 1) * 512)
                for nm, eng in (("q", "act"), ("k", "dve"), ("v", "act")):
                    if nm != "q" and sp.get("reuse_kv"):
                        continue
                    ij = banks[pj % len(banks)]
                    pj += 1
                    for ch in range(8):
                        S.op("pe", lambda e, ij=ij, ch=ch, nm=nm, ts_=ts_, wt=wt: e.matmul(bk(ij), lhsT=wt[nm][:, ch, :], rhs=g.hT[:, ch, ts_],
                                                                                         start=(ch == 0), stop=(ch == 7)),
                             reads=[wk(nm)], writes=[bkey(ij)])
                    if nm == "q":
                        S.op("act", lambda e, ij=ij, ts_=ts_: e.activation(out=QT[:, ts_], in_=bk(ij), func=AF.Copy, scale=0.125),
                             reads=[bkey(ij)], writes=[("QT", n)])
                    elif nm == "k":
                        S.op("dve", lambda e, ij=ij, ts_=ts_: e.tensor_copy(out=KT[:, ts_], in_=bk(ij)),
                             reads=[bkey(ij)], writes=[("KT", n)])
                    else:
                        S.op("act", lambda e, ij=ij, ts_=ts_: e.activation(out=VT[:, ts_], in_=bk(ij), func=AF.Copy),
                             reads=[bkey(ij)], writes=[("VT", n)])
            for d in (() if sp.get("reuse_kv") else pats):
                nb = 32 // d
                for b4 in range(8):
                    ij = banks[pj % len(banks)]
                    pj += 1
                    pb16 = lambda ij: bk(ij).bitcast(BF16)
                    for j in range(4):
                        blk = b4 * 4 + j
                        r, bb = blk // nb, blk % nb
                        tok = sl(r + d * 128 * bb, 128, d)
                        S.op("pe", lambda e, ij=ij, j=j, tok=tok: e.transpose(pb16(ij)[:, j * 128:(j + 1) * 128], VT[:, tok], g.identb[:]),
                             reads=[("VT", x) for x in _groups(d, bb)], writes=[bkey(ij)])
                    vsrc = lambda ij: pb16(ij)[:, 0:512].rearrange("p (j h c) -> p j h c", j=4, h=2)
                    if b4 % 2 == 0:
                        S.op("dve", lambda e, ij=ij, b4=b4, d=d: e.tensor_copy(out=Vt[d][:, b4 * 4:(b4 + 1) * 4, :, 0:64], in_=vsrc(ij)),
                             reads=[bkey(ij)], writes=[("Vt", d, b4)])
                    else:
                        S.op("act", lambda e, ij=ij, b4=b4, d=d: e.activation(out=Vt[d][:, b4 * 4:(b4 + 1) * 4, :, 0:64], in_=vsrc(ij),
                                                                             func=AF.Copy),
                             reads=[bkey(ij)], writes=[("Vt", d, b4)])
            if si + 1 < len(specs):
                load_weights(si + 1)
            steps = []
            for pi, d in enumerate(pats):
                nb = 32 // d
                for r in range(d):
                    for bb in range(nb):
                        steps.append(dict(pi=pi, d=d, r=r, bb=bb, nb=nb, s_=step % NSB, o_=step % 2,
                                          meng="dve" if step % 2 == 0 else "pool"))
                        step += 1

            def part1(stp):
                d, r, bb, s_, meng = stp["d"], stp["r"], stp["bb"], stp["s_"], stp["meng"]
                tq = sl(r + d * 128 * bb, 128, d)
                tp = sl(r + d * 128 * (bb - 1), 128, d) if bb > 0 else None
                lo = 0 if bb > 0 else 128
                gq = _groups(d, bb)
                gk = gq + (_groups(d, bb - 1) if bb > 0 else [])
                rd = [("QT", x) for x in gq] + [("KT", x) for x in set(gk)]
                skeys = [("Sb", s_, 0), ("Sb", s_, 1)]
                for h in range(2):
                    hs = slice(64 * h, 64 * h + 64)
                    S.op("pe", lambda e, s_=s_, h=h, hs=hs, tq=tq: e.matmul(Sp[s_][:, h, 128:256], lhsT=KT[hs, tq], rhs=QT[hs, tq],
                                                                           start=True, stop=True),
                         reads=rd, writes=skeys)
                    if bb > 0:
                        S.op("pe", lambda e, s_=s_, h=h, hs=hs, tq=tq, tp=tp: e.matmul(Sp[s_][:, h, 0:128], lhsT=KT[hs, tp],
                                                                                      rhs=QT[hs, tq], start=True, stop=True),
                             reads=rd, writes=skeys)
                S.op("act", lambda e, s_=s_, lo=lo: e.activation(out=Et[s_][:, :, lo:256], in_=Sp[s_][:, :, lo:256], func=AF.Exp),
                     reads=skeys, writes=[("E", s_)])
                S.op(meng, lambda e, s_=s_, lo=lo: e.tensor_tensor(out=Pt[s_][:, :, lo:256], in0=Et[s_][:, :, lo:256],
                                                                  in1=g.mask2[:, lo:256].unsqueeze(1).to_broadcast([128, 2, 256 - lo]),
                                                                  op=ALU.mult),
                     reads=[("E", s_)], writes=[("P", s_)])

            def part2(stp):
                pi, d, r, bb, nb, s_, o_ = stp["pi"], stp["d"], stp["r"], stp["bb"], stp["nb"], stp["s_"], stp["o_"]
                blk = r * nb + bb
                tq = sl(r + d * 128 * bb, 128, d)
                gq = _groups(d, bb)
                vrd = [("Vt", d, blk // 4), ("Vone", d)] + ([("Vt", d, (blk - 1) // 4)] if bb > 0 else [])
                for h in range(2):
                    if bb > 0:
                        S.op("pe", lambda e, s_=s_, o_=o_, h=h, blk=blk, d=d: e.matmul(Op[o_][0:65, h * 128:(h + 1) * 128],
                                                                                      lhsT=Vt[d][:, blk - 1, h, :], rhs=Pt[s_][:, h, 0:128],
                                                                                      start=True, stop=False),
                             reads=[("P", s_)] + vrd, writes=[("O", o_)])
                    S.op("pe", lambda e, s_=s_, o_=o_, h=h, blk=blk, d=d, bb=bb: e.matmul(Op[o_][0:65, h * 128:(h + 1) * 128],
                                                                                         lhsT=Vt[d][:, blk, h, :], rhs=Pt[s_][:, h, 128:256],
                                                                                         start=(bb == 0), stop=True),
                         reads=[("P", s_)] + vrd, writes=[("O", o_)])
                akeys = [("acc", x, r % 4) for x in gq] if d > 1 else [("acc", gq[0], x) for x in range(4)]
                o_view = Op[o_][0:65, 0:256].rearrange("p (h q) -> p h q", h=2)
                if pi == 0:
                    S.op("dve", lambda e, tq=tq, o_view=o_view: e.tensor_copy(out=Acc[:, :, tq], in_=o_view),
                         reads=[("O", o_)], writes=akeys)
                else:
                    S.op("dve", lambda e, tq=tq, o_view=o_view: e.tensor_tensor(out=Acc[:, :, tq], in0=Acc[:, :, tq], in1=o_view, op=ALU.add),
                         reads=[("O", o_)] + akeys, writes=akeys)

            LA = NSB - 1
            for i in range(min(LA, len(steps))):
                part1(steps[i])
            for i in range(len(steps)):
                if i + LA < len(steps):
                    part1(steps[i + LA])
                part2(steps[i])
            for h in range(2):
                for n in range(8):
                    ts_ = slice(n * 512, (n + 1) * 512)
                    b = fj % 2
                    fj += 1
                    ijl = banks[pj % len(banks)]
                    pj += 1
                    ijz = banks[pj % len(banks)]
                    pj += 1
                    acc_rd = [("acc", n, x) for x in range(4)]
                    S.op("pe", lambda e, ijl=ijl, h=h, ts_=ts_: e.matmul(bk(ijl)[0:64, :], lhsT=g.cf[0:65, 5, 0:64], rhs=Acc[0:65, h, ts_],
                                                                         start=True, stop=True),
                         reads=acc_rd, writes=[bkey(ijl)])
                    for ch in range(8):
                        S.op("pe", lambda e, ijz=ijz, ch=ch, h=h, ts_=ts_, wt=wt: e.matmul(bk(ijz)[0:64, :], lhsT=wt["z"][:, ch, 64 * h:64 * h + 64],
                                                                                          rhs=g.hT[:, ch, ts_], start=(ch == 0), stop=(ch == 7)),
                             reads=[wk("z")], writes=[bkey(ijz)])
                    if sp["sink"] is not None:
                        S.op("act", lambda e, b=b, ijl=ijl, h=h, sk=sk: e.activation(out=Ut[b][:], in_=bk(ijl)[0:64, :], func=AF.Ln,
                                                                                     bias=sk[:, h:h + 1]),
                             reads=[bkey(ijl), ("sk", si % 2)], writes=[("U", b)])
                    else:
                        S.op("act", lambda e, b=b, ijl=ijl: e.activation(out=Ut[b][:], in_=bk(ijl)[0:64, :], func=AF.Ln),
                             reads=[bkey(ijl)], writes=[("U", b)])
                    S.op("act", lambda e, b=b, ijz=ijz: e.activation(out=Tt[b][:], in_=bk(ijz)[0:64, :], func=AF.Exp, scale=-1.0),
                         reads=[bkey(ijz)], writes=[("T", b)])
                    S.op("act", lambda e, b=b: e.activation(out=Tt[b][:], in_=Tt[b][:], func=AF.Ln, bias=1.0),
                         reads=[("T", b)], writes=[("T", b)])
                    S.op("pool", lambda e, b=b: e.tensor_tensor(out=Ut[b][:], in0=Ut[b][:], in1=Tt[b][:], op=ALU.add),
                         reads=[("U", b), ("T", b)], writes=[("U", b)])
                    S.op("act", lambda e, b=b: e.activation(out=Ut[b][:], in_=Ut[b][:], func=AF.Exp, scale=-1.0),
                         reads=[("U", b)], writes=[("U", b)])
                    S.op("dve", lambda e, b=b, h=h, ts_=ts_: e.tensor_tensor(out=Ut[b][:], in0=Acc[0:64, h, ts_], in1=Ut[b][:], op=ALU.mult),
                         reads=[("U", b)] + acc_rd, writes=[("U", b)])
                    S.op("dve", lambda e, b=b, ijz=ijz: e.tensor_tensor(out=Yb[b][:], in0=Ut[b][:], in1=bk(ijz)[0:64, :], op=ALU.mult),
                         reads=[("U", b), bkey(ijz)], writes=[("Y", b)])
                    row0 = sp["rows"][h]
                    S.op("sp", lambda e, b=b, row0=row0, ts_=ts_: e.dma_start(out=g.ycat[row0:row0 + 64, ts_], in_=Yb[b][:]),
                         reads=[("Y", b)], dma_key=("aY", b))
        S.emit_block()


def phase_ssd(g, L, grp):
    nc, S = g.nc, g.S
    NW = 1296
    zc0, xc0, bc0, cc0, dc0 = 2048 + grp * 512, 3072 + grp * 512, 4096 + grp * 128, 4352 + grp * 128, 4608 + grp * 8
    with contextlib.ExitStack() as st:
        sbt = lambda name, shape, dt: st.enter_context(g.sbuf(name, shape, dt))
        pst = lambda name, shape, dt: st.enter_context(g.psum(name, shape, dt))
        wB = sbt("wB", [128, 8, NW], BF16)
        wst = [sbt("bwst%d" % i, [128, 8, 128], F32) for i in range(2)]
        prm_r = sbt("prm_r", [36, 128], F32)
        prm = sbt("prm", [128, 36], F32)
        hp = sbt("hp", [128, 3, 8], F32)
        pre = [sbt("pre%d" % i, [128, 515], F32) for i in range(2)]
        halo = sbt("halo", [128, 6, 3], F32)
        cacc = [sbt("cacc%d" % i, [128, 512], F32) for i in range(2)]
        ctmp = cacc[0]
        xc = [sbt("xc%d" % i, [128, 4, 512], F32) for i in range(2)]
        BT = [sbt("BTt%d" % i, [128, 512], BF16) for i in range(2)]
        CT = [sbt("CTt%d" % i, [128, 512], BF16) for i in range(2)]
        szs = [sbt("szs%d" % i, [128, 4, 512], F32) for i in range(2)]
        dts = [sbt("dts%d" % i, [128, 3, 4, 8], F32) for i in range(2)]
        NQ = 3
        xtm = [sbt("xtm%d" % i, [128, 512], F32) for i in range(NQ)]
        Btm = [sbt("Btm%d" % i, [128, 128], BF16) for i in range(NQ)]
        xdt = [sbt("xdt%d" % i, [128, 512], BF16) for i in range(NQ)]
        xdte = [sbt("xdte%d" % i, [128, 512], BF16) for i in range(NQ)]
        tD = [sbt("tD%d" % i, [128, 512], BF16) for i in range(NQ)]
        sm = [sbt("sm%d" % i, [128, 6, 8], F32) for i in range(NQ)]
        MT = [sbt("MT%d" % i, [128, 8, 128], BF16) for i in range(NQ)]
        dAh = [sbt("dAh%d" % i, [128, 4, 8], BF16) for i in range(2)]
        dAl = [sbt("dAl%d" % i, [128, 4, 8], BF16) for i in range(2)]
        dAr = sbt("dAr", [128, 4, 8], F32)
        rhsH = sbt("rhsH", [128, 8, 128], BF16)
        rhsL = sbt("rhsL", [128, 8, 128], BF16)
        LT = sbt("LT", [128, 8, 128], BF16)
        GTm = sbt("GTm", [128, 128], F32)
        yt = sbt("yt", [128, 512], F32)
        nst = sbt("nst", [128, 4], F32)
        yn = [sbt("yn%d" % i, [128, 512], BF16) for i in range(2)]
        Ybuf = [sbt("Ybuf%d" % i, [128, 4, 512], BF16) for i in range(2)]
        H = sbt("H", [128, 512], F32)
        Hb = sbt("Hb", [128, 512], BF16)
        PA0 = pst("PA0", [128, 512], F32)
        PSEGs = [pst("PSEG%d" % i, [128, 512], F32) for i in range(2)]
        Psm = pst("Psm", [128, 512], F32)
        PBy = pst("PBy", [128, 512], F32)
        PBo = pst("PBo", [128, 512], F32)
        PS2 = pst("PS2", [128, 512], F32)
        Pbf = pst("Pbf", [128, 8, 128], BF16)
        wv_in = g.w_in[L].rearrange("(ch p) n -> p ch n", p=128)

        pieces = [(zc0 + i * 128, 128, i * 128) for i in range(4)] + [(xc0 + i * 128, 128, 512 + i * 128) for i in range(4)]
        pieces += [(bc0, 128, 1024), (cc0, 128, 1152), (dc0, 8, 1280)]
        wball = [("wB", i) for i in range(len(pieces))]
        for i, (c0, cn, o0) in enumerate(pieces):
            b = i % 2
            S.op("sp", lambda e, b=b, c0=c0, cn=cn: e.dma_start(out=wst[b][:, :, 0:cn], in_=wv_in[:, :, c0:c0 + cn]),
                 writes=[("wst", b)], dma_key=("bwst", b))
            ceng = ("act", "dve", "pool")[i % 3]
            if ceng == "act":
                S.op("act", lambda e, b=b, cn=cn, o0=o0: e.activation(out=wB[:, :, o0:o0 + cn], in_=wst[b][:, :, 0:cn], func=AF.Copy),
                     reads=[("wst", b)], writes=[("wB", i)])
            else:
                S.op(ceng, lambda e, b=b, cn=cn, o0=o0: e.tensor_copy(out=wB[:, :, o0:o0 + cn], in_=wst[b][:, :, 0:cn]),
                     reads=[("wst", b)], writes=[("wB", i)])
        S.op("pool", lambda e: e.memset(prm_r[:], 0.0), writes=["prm_r0"])
        cw = g.conv_w[L].rearrange("k (cc p) -> k cc p", p=128)
        cbv = g.conv_b[L:L + 1, :].rearrange("o (cc p) -> (o cc) p", p=128)
        swv = g.ssm_w[L:L + 1, :].rearrange("o (cc p) -> (o cc) p", p=128)
        k_ = 0
        for tap in range(4):
            S.op("sp", lambda e, tap=tap: e.dma_start(out=prm_r[tap * 6:tap * 6 + 4, :], in_=cw[tap, grp * 4:grp * 4 + 4, :]),
                 reads=["prm_r0"], writes=[("prm_r", k_)], dma_key=("prm", k_))
            k_ += 1
            for j, c_ in ((4, 8 + grp), (5, 10 + grp)):
                S.op("sp", lambda e, tap=tap, j=j, c_=c_: e.dma_start(out=prm_r[tap * 6 + j:tap * 6 + j + 1, :], in_=cw[tap, c_:c_ + 1, :]),
                     reads=["prm_r0"], writes=[("prm_r", k_)], dma_key=("prm", k_))
                k_ += 1
        S.op("sp", lambda e: e.dma_start(out=prm_r[24:28, :], in_=cbv[grp * 4:grp * 4 + 4, :]),
             reads=["prm_r0"], writes=[("prm_r", k_)], dma_key=("prm", k_))
        k_ += 1
        for j, c_ in ((28, 8 + grp), (29, 10 + grp)):
            S.op("sp", lambda e, j=j, c_=c_: e.dma_start(out=prm_r[j:j + 1, :], in_=cbv[c_:c_ + 1, :]),
                 reads=["prm_r0"], writes=[("prm_r", k_)], dma_key=("prm", k_))
            k_ += 1
        S.op("sp", lambda e: e.dma_start(out=prm_r[30:34, :], in_=swv[grp * 4:grp * 4 + 4, :]),
             reads=["prm_r0"], writes=[("prm_r", k_)], dma_key=("prm", k_))
        k_ += 1
        S.op("pe", lambda e: e.transpose(PA0[:, 0:36], prm_r[:, :], g.cf[0:36, 0, 0:36]),
             reads=[("prm_r", i) for i in range(k_)], writes=["PA0"])
        S.op("dve", lambda e: e.tensor_copy(out=prm[:], in_=PA0[:, 0:36]), reads=["PA0"], writes=["prm"])
        for i, src_ in enumerate((g.dt_bias, g.a_log, g.d_skip)):
            S.op("sp", lambda e, i=i, src_=src_: e.dma_start(out=hp[:, i, :], in_=src_[L:L + 1, grp * 8:grp * 8 + 8].partition_broadcast(128)),
                 writes=[("hp", i)], dma_key=("hp", i))
        S.op("act", lambda e: e.activation(out=hp[:, 1, :], in_=hp[:, 1, :], func=AF.Exp), reads=[("hp", 1)], writes=[("hp", 1)])
        S.op("dve", lambda e: e.tensor_scalar(out=hp[:, 1, :], in0=hp[:, 1, :], scalar1=-1.0, scalar2=None, op0=ALU.mult),
             reads=[("hp", 1)], writes=[("hp", 1)])
        S.op("pool", lambda e: e.memset(halo[:], 0.0), writes=["halo"])
        S.op("pool", lambda e: e.memset(H[:], 0.0), writes=["H"])
        S.op("pool", lambda e: e.memset(Hb[:], 0.0), writes=["Hb"])

        tri = g.cf[:, 2, :]
        u1 = g.cf[:, 3, :]
        ones = g.cf[:, 4, :]
        identf = g.cf[:, 0, :]
        v8 = lambda ap: ap.rearrange("p (h d) -> p h d", h=8)
        b8 = lambda ap: ap.unsqueeze(2).to_broadcast([128, 8, 64])

        def front(sc):
            p = sc % 2
            ts_ = slice(sc * 512, (sc + 1) * 512)
            for j in range(6):
                pb = j % 2
                wofs = 512 + j * 128 if j < 4 else (1024 if j == 4 else 1152)
                for ch in range(8):
                    S.op("pe", lambda e, ch=ch, wofs=wofs: e.matmul(PA0[:], lhsT=wB[:, ch, wofs:wofs + 128], rhs=g.hT[:, ch, ts_],
                                                                    start=(ch == 0), stop=(ch == 7)),
                         reads=wball, writes=["PA0"])
                S.op("pool", lambda e, pb=pb, j=j: e.tensor_copy(out=pre[pb][:, 0:3], in_=halo[:, j, :]),
                     reads=["halo"], writes=[("pre", pb)])
                S.op("act", lambda e, pb=pb: e.activation(out=pre[pb][:, 3:515], in_=PA0[:], func=AF.Copy),
                     reads=["PA0"], writes=[("pre", pb)])
                S.op("pool", lambda e, pb=pb, j=j: e.tensor_copy(out=halo[:, j, :], in_=pre[pb][:, 512:515]),
                     reads=[("pre", pb)], writes=["halo"])
                if False:
                    S.op("pool", lambda e, pb=pb, j=j: e.tensor_scalar(out=cacc[pb][:], in0=pre[pb][:, 0:512], scalar1=prm[:, j:j + 1], scalar2=None,
                                                                       op0=ALU.mult),
                         reads=[("pre", pb), "prm"], writes=[("cacc", pb)])
                    for tap in range(1, 4):
                        S.op("pool", lambda e, pb=pb, j=j, tap=tap: e.tensor_scalar(out=ctmp[:], in0=pre[pb][:, tap:tap + 512],
                                                                                    scalar1=prm[:, tap * 6 + j:tap * 6 + j + 1], scalar2=None,
                                                                                    op0=ALU.mult),
                             reads=[("pre", pb), "prm"], writes=["ctmp"])
                        S.op("pool", lambda e, pb=pb: e.tensor_tensor(out=cacc[pb][:], in0=cacc[pb][:], in1=ctmp[:], op=ALU.add),
                             reads=["ctmp", ("cacc", pb)], writes=[("cacc", pb)])
                else:
                    S.op("dve", lambda e, pb=pb, j=j: e.tensor_scalar(out=cacc[pb][:], in0=pre[pb][:, 0:512], scalar1=prm[:, j:j + 1], scalar2=None,
                                                                      op0=ALU.mult),
                         reads=[("pre", pb), "prm"], writes=[("cacc", pb)])
                    for tap in range(1, 4):
                        S.op("dve", lambda e, pb=pb, j=j, tap=tap: e.scalar_tensor_tensor(out=cacc[pb][:], in0=pre[pb][:, tap:tap + 512],
                                                                                         scalar=prm[:, tap * 6 + j:tap * 6 + j + 1], in1=cacc[pb][:],
                                                                                         op0=ALU.mult, op1=ALU.add),
                             reads=[("pre", pb), ("cacc", pb), "prm"], writes=[("cacc", pb)])
                dst = xc[p][:, j, :] if j < 4 else (BT[p][:] if j == 4 else CT[p][:])
                dkey = ("xc", p, j) if j < 4 else (("BT", p) if j == 4 else ("CT", p))
                S.op("act", lambda e, pb=pb, j=j, dst=dst: e.activation(out=dst, in_=cacc[pb][:], func=AF.Silu, bias=prm[:, 24 + j:25 + j]),
                     reads=[("cacc", pb), "prm"], writes=[dkey])
            for k in range(4):
                tc_ = slice((sc * 4 + k) * 128, (sc * 4 + k + 1) * 128)
                for ch in range(8):
                    S.op("pe", lambda e, ch=ch, tc_=tc_: e.matmul(PA0[:], lhsT=g.hT[:, ch, tc_], rhs=wB[:, ch, 0:512],
                                                                  start=(ch == 0), stop=(ch == 7)),
                         reads=wball, writes=["PA0"])
                S.op("act", lambda e, p=p, k=k: e.activation(out=szs[p][:, k, :], in_=PA0[:], func=AF.Silu), reads=["PA0"], writes=[("sz", p, k)])
            for k in range(4):
                tc_ = slice((sc * 4 + k) * 128, (sc * 4 + k + 1) * 128)
                for ch in range(8):
                    S.op("pe", lambda e, ch=ch, tc_=tc_, k=k: e.matmul(Psm[:, k * 8:(k + 1) * 8], lhsT=g.hT[:, ch, tc_], rhs=wB[:, ch, 1280:1288],
                                                                       start=(ch == 0), stop=(ch == 7)),
                         reads=wball, writes=["Psm"])
            S.op("dve", lambda e, p=p: e.tensor_tensor(out=dts[p][:, 0, :, :], in0=Psm[:, 0:32].rearrange("p (k h) -> p k h", k=4),
                                                       in1=hp[:, 0, :].unsqueeze(1).to_broadcast([128, 4, 8]), op=ALU.add),
                 reads=["Psm", ("hp", 0)], writes=[("dts", p, 0)])
            S.op("act", lambda e, p=p: e.activation(out=dts[p][:, 0, :, :], in_=dts[p][:, 0, :, :], func=AF.Exp),
                 reads=[("dts", p, 0)], writes=[("dts", p, 0)])
            S.op("act", lambda e, p=p: e.activation(out=dts[p][:, 1, :, :], in_=dts[p][:, 0, :, :], func=AF.Ln, bias=1.0),
                 reads=[("dts", p, 0)], writes=[("dts", p, 1)])
            S.op("dve", lambda e, p=p: e.tensor_tensor(out=dts[p][:, 2, :, :], in0=dts[p][:, 1, :, :],
                                                       in1=hp[:, 1, :].unsqueeze(1).to_broadcast([128, 4, 8]), op=ALU.mult),
                 reads=[("dts", p, 1), ("hp", 1)], writes=[("dts", p, 2)])
            S.op("dve", lambda e, p=p: e.tensor_copy(out=dAh[p][:], in_=dts[p][:, 2, :, :]), reads=[("dts", p, 2)], writes=[("dAh", p)])
            S.op("dve", lambda e, p=p: e.tensor_tensor(out=dAr[:], in0=dts[p][:, 2, :, :], in1=dAh[p][:], op=ALU.subtract),
                 reads=[("dts", p, 2), ("dAh", p)], writes=["dAr"])
            S.op("dve", lambda e, p=p: e.tensor_copy(out=dAl[p][:], in_=dAr[:]), reads=["dAr"], writes=[("dAl", p)])

        def stageA(c):
            sc, k, q = c // 4, c % 4, c % NQ
            p = sc % 2
            lc = slice(k * 128, (k + 1) * 128)
            dt_ = dts[p][:, 1, k, :]
            dA_ = dts[p][:, 2, k, :]
            trib = g.mask2[:, 128:256]
            S.op("dve", lambda e: e.tensor_tensor(out=rhsH[:], in0=trib.unsqueeze(1).to_broadcast([128, 8, 128]),
                                                  in1=dAh[p][:, k, :].unsqueeze(2).to_broadcast([128, 8, 128]), op=ALU.mult),
                 reads=[("dAh", p)], writes=["rhsH"])
            S.op("dve", lambda e: e.tensor_tensor(out=rhsL[:], in0=trib.unsqueeze(1).to_broadcast([128, 8, 128]),
                                                  in1=dAl[p][:, k, :].unsqueeze(2).to_broadcast([128, 8, 128]), op=ALU.mult),
                 reads=[("dAl", p)], writes=["rhsL"])
            for hf in range(2):
                S.op("pe", lambda e, hf=hf: e.matmul(PSEGs[hf][:], lhsT=g.u1b[:], rhs=rhsH[:, hf * 4:(hf + 1) * 4, :], start=True, stop=False),
                     reads=["rhsH"], writes=[("PSEG", hf)])
                S.op("pe", lambda e, hf=hf: e.matmul(PSEGs[hf][:], lhsT=g.u1b[:], rhs=rhsL[:, hf * 4:(hf + 1) * 4, :], start=False, stop=True),
                     reads=["rhsL"], writes=[("PSEG", hf)])
            for hf in range(2):
                S.op("act", lambda e, hf=hf: e.activation(out=LT[:, hf * 4:(hf + 1) * 4, :], in_=PSEGs[hf][:].rearrange("p (h l) -> p h l", h=4),
                                                          func=AF.Exp),
                     reads=[("PSEG", hf)], writes=[("LT", hf)])
            for j in range(4):
                S.op("pe", lambda e, j=j: e.transpose(PA0[:, j * 128:(j + 1) * 128], xc[p][:, j, lc], identf),
                     reads=[("xc", p, j)], writes=["PA0"])
            S.op("act", lambda e: e.activation(out=xtm[q][:], in_=PA0[:], func=AF.Copy), reads=["PA0"], writes=[("xtm", q)])
            S.op("pe", lambda e: e.transpose(Pbf[:, 0, :], BT[p][:, lc], g.identb[:]), reads=[("BT", p)], writes=["Pbf"])
            S.op("dve", lambda e: e.tensor_copy(out=Btm[q][:], in_=Pbf[:, 0, :]), reads=["Pbf"], writes=[("Btm", q)])
            S.op("pe", lambda e: e.matmul(Psm[:, 32:40], lhsT=tri, rhs=dA_, start=True, stop=True), reads=[("dts", p, 2)], writes=["Psm"])
            S.op("pe", lambda e: e.matmul(Psm[:, 40:48], lhsT=ones, rhs=dA_, start=True, stop=True), reads=[("dts", p, 2)], writes=["Psm"])
            S.op("dve", lambda e: e.tensor_copy(out=sm[q][:, 0, :], in_=Psm[:, 32:40]), reads=["Psm"], writes=[("sm", q, 0)])
            S.op("act", lambda e: e.activation(out=sm[q][:, 1, :], in_=sm[q][:, 0, :], func=AF.Exp), reads=[("sm", q, 0)], writes=[("sm", q, 1)])
            S.op("dve", lambda e: e.tensor_tensor(out=sm[q][:, 2, :], in0=Psm[:, 40:48], in1=sm[q][:, 0, :], op=ALU.subtract),
                 reads=["Psm", ("sm", q, 0)], writes=[("sm", q, 2)])
            S.op("act", lambda e: e.activation(out=sm[q][:, 3, :], in_=sm[q][:, 2, :], func=AF.Exp), reads=[("sm", q, 2)], writes=[("sm", q, 3)])
            S.op("act", lambda e: e.activation(out=sm[q][:, 4, :], in_=Psm[:, 40:48], func=AF.Exp), reads=["Psm"], writes=[("sm", q, 4)])
            S.op("dve", lambda e: e.tensor_tensor(out=sm[q][:, 5, :], in0=dt_, in1=sm[q][:, 3, :], op=ALU.mult),
                 reads=[("dts", p, 1), ("sm", q, 3)], writes=[("sm", q, 5)])
            S.op("pe", lambda e: e.matmul(Psm[:, 128:256], lhsT=BT[p][:, lc], rhs=CT[p][:, lc], start=True, stop=True),
                 reads=[("BT", p), ("CT", p)], writes=["Psm"])
            S.op("dve", lambda e: e.tensor_tensor(out=GTm[:], in0=Psm[:, 128:256], in1=tri, op=ALU.mult), reads=["Psm"], writes=["GTm"])
            S.op("dve", lambda e: e.tensor_tensor(out=MT[q][:], in0=LT[:], in1=GTm[:].unsqueeze(1).to_broadcast([128, 8, 128]), op=ALU.mult),
                 reads=[("LT", 0), ("LT", 1), "GTm"], writes=[("MT", q)])
            S.op("pool", lambda e: e.tensor_tensor(out=v8(xdt[q][:]), in0=v8(xtm[q][:]), in1=b8(dt_), op=ALU.mult),
                 reads=[("xtm", q), ("dts", p, 1)], writes=[("xdt", q)])
            S.op("pool", lambda e: e.tensor_tensor(out=v8(xdte[q][:]), in0=v8(xtm[q][:]), in1=b8(sm[q][:, 5, :]), op=ALU.mult),
                 reads=[("xtm", q), ("sm", q, 5)], writes=[("xdte", q)])
            S.op("pool", lambda e: e.tensor_tensor(out=v8(tD[q][:]), in0=v8(xtm[q][:]), in1=b8(hp[:, 2, :]), op=ALU.mult),
                 reads=[("xtm", q), ("hp", 2)], writes=[("tD", q)])

        def stageB1(c):
            sc, k, q = c // 4, c % 4, c % NQ
            p = sc % 2
            lc = slice(k * 128, (k + 1) * 128)
            S.op("pe", lambda e: e.matmul(PBo[:], lhsT=CT[p][:, lc], rhs=Hb[:], start=True, stop=True),
                 reads=[("CT", p), "Hb"], writes=["PBo"])
            S.op("pe", lambda e: e.matmul(PS2[:], lhsT=Btm[q][:], rhs=xdte[q][:], start=True, stop=True),
                 reads=[("Btm", q), ("xdte", q)], writes=["PS2"])
            S.op("pe", lambda e: e.matmul(PBy[:], lhsT=g.identb[:], rhs=tD[q][:], start=True, stop=False),
                 reads=[("tD", q)], writes=["PBy"])
            for h in range(8):
                S.op("pe", lambda e, h=h: e.matmul(PBy[:, h * 64:(h + 1) * 64], lhsT=MT[q][:, h, :], rhs=xdt[q][:, h * 64:(h + 1) * 64],
                                                   start=False, stop=(h == 7)),
                     reads=[("MT", q), ("xdt", q)], writes=["PBy"])
            S.op("dve", lambda e: e.tensor_tensor(out=v8(H[:]), in0=v8(H[:]), in1=b8(sm[q][:, 4, :]), op=ALU.mult),
                 reads=["H", ("sm", q, 4)], writes=["H"])
            S.op("dve", lambda e: e.tensor_tensor(out=H[:], in0=H[:], in1=PS2[:], op=ALU.add), reads=["H", "PS2"], writes=["H"])
            S.op("act", lambda e: e.activation(out=Hb[:], in_=H[:], func=AF.Copy), reads=["H"], writes=["Hb"])
            S.op("dve", lambda e: e.tensor_tensor(out=v8(yt[:]), in0=v8(PBo[:]), in1=b8(sm[q][:, 1, :]), op=ALU.mult),
                 reads=["PBo", ("sm", q, 1)], writes=["yt"])
            S.op("dve", lambda e: e.tensor_tensor(out=yt[:], in0=yt[:], in1=PBy[:], op=ALU.add), reads=["yt", "PBy"], writes=["yt"])
            S.op("pool", lambda e: e.tensor_tensor(out=yt[:], in0=yt[:], in1=szs[p][:, k, :], op=ALU.mult),
                 reads=["yt", ("sz", p, k)], writes=["yt"])
            yq = c % 2
            S.op("act", lambda e: e.activation(out=yn[yq][:], in_=yt[:], func=AF.Square, accum_out=nst[:, 0:1]),
                 reads=["yt"], writes=[("yn", yq), ("nst", 0)])
            S.op("act", lambda e: e.activation(out=nst[:, 1:2], in_=nst[:, 0:1], func=AF.Ln, scale=1.0 / 512, bias=EPS),
                 reads=[("nst", 0)], writes=[("nst", 1)])
            S.op("act", lambda e: e.activation(out=nst[:, 2:3], in_=nst[:, 1:2], func=AF.Exp, scale=-0.5),
                 reads=[("nst", 1)], writes=[("nst", 2)])
            S.op("act", lambda e: e.activation(out=yn[yq][:], in_=yt[:], func=AF.Copy, scale=nst[:, 2:3]),
                 reads=["yt", ("nst", 2)], writes=[("yn", yq)])

        def stageB2(c):
            sc, k, q = c // 4, c % 4, c % 2
            p = sc % 2
            lc = slice(k * 128, (k + 1) * 128)
            for j in range(4):
                S.op("pe", lambda e, j=j: e.transpose(Pbf[:, 4 + j, :], yn[q][:, j * 128:(j + 1) * 128], g.identb[:]),
                     reads=[("yn", q)], writes=["Pbf"])
            S.op("dve", lambda e: e.tensor_tensor(out=Ybuf[p][:, :, lc], in0=Pbf[:, 4:8, :],
                                                  in1=prm[:, 30:34].unsqueeze(2).to_broadcast([128, 4, 128]), op=ALU.mult),
                 reads=["Pbf", "prm"], writes=[("Ybuf", p)])
            if k == 3:
                r0 = 512 + grp * 512
                ts_ = slice(sc * 512, (sc + 1) * 512)
                S.op("sp", lambda e: e.dma_start(out=g.ycat[r0:r0 + 512, ts_].rearrange("(cc p) t -> p cc t", p=128), in_=Ybuf[p][:]),
                     reads=[("Ybuf", p)], dma_key=("Ybuf", p))

        NCH = SEQ // 128
        front(0)
        stageA(0)
        stageA(1)
        for c in range(NCH):
            if c + 2 < NCH and (c + 2) % 4 == 0:
                front((c + 2) // 4)
            S.capture()
            stageB1(c)
            lb1 = S.end_capture()
            S.capture()
            if c + 2 < NCH:
                stageA(c + 2)
            la = S.end_capture()
            S.capture()
            if c >= 1:
                stageB2(c - 1)
            lb2 = S.end_capture()
            S.replay_merged([lb1, la, lb2])
        stageB2(NCH - 1)
        S.emit_block()


def phase_out(g, L, src):
    nc, S = g.nc, g.S
    with contextlib.ExitStack() as st:
        sbt = lambda name, shape, dt: st.enter_context(g.sbuf(name, shape, dt))
        wo = sbt("wo", [128, 16, DM], BF16)
        wst = [sbt("owst%d" % i, [128, DM], F32) for i in range(3)]
        yc = [sbt("oyc%d" % i, [128, 16, 512], BF16) for i in range(2)]
        xt = [sbt("oxt%d" % i, [128, DM], F32) for i in range(2)]
        t1 = [sbt("ot1%d" % i, [128, DM], F32) for i in range(2)]
        junk = sbt("ojunk", [128, DM], BF16)
        stat = [sbt("ost%d" % i, [128, 4], F32) for i in range(2)]
        PY = [st.enter_context(g.psum("oPY%d" % i, [128, 2, 512], F32)) for i in range(2)]
        for cc in range(16):
            b = cc % 3
            S.op("sp", lambda e, b=b, cc=cc: e.dma_start(out=wst[b][:], in_=g.w_out[L, cc * 128:(cc + 1) * 128, :]),
                 writes=[("wst", b)], dma_key=("owst", b))
            ceng = ("act", "dve", "pool")[cc % 3]
            if ceng == "act":
                S.op("act", lambda e, b=b, cc=cc: e.activation(out=wo[:, cc, :], in_=wst[b][:], func=AF.Copy),
                     reads=[("wst", b)], writes=[("wo", cc)])
            else:
                S.op(ceng, lambda e, b=b, cc=cc: e.tensor_copy(out=wo[:, cc, :], in_=wst[b][:]), reads=[("wst", b)], writes=[("wo", cc)])
        def load_yc(gi):
            gb = gi % 2
            ts_ = slice(gi * 512, (gi + 1) * 512)
            S.op("sp", lambda e: e.dma_start(out=yc[gb][:], in_=g.ycat[:, ts_].rearrange("(cc p) t -> p cc t", p=128)),
                 writes=[("yc", gb)], dma_key=("oyc", gb))

        load_yc(0)
        for gi in range(8):
            gb = gi % 2
            if gi + 1 < 8:
                load_yc(gi + 1)
            for k in range(4):
                t = gi * 4 + k
                b = t % 2
                rows = slice(t * 128, (t + 1) * 128)
                lc = slice(k * 128, (k + 1) * 128)
                S.op("sp", lambda e, b=b, rows=rows: e.dma_start(out=xt[b][:], in_=src[rows, :]), writes=[("x", b)], dma_key=("oxt", b))
                for nh in range(2):
                    for cc in range(16):
                        S.op("pe", lambda e, b=b, nh=nh, cc=cc, gb=gb, lc=lc: e.matmul(PY[b][:, nh, :], lhsT=yc[gb][:, cc, lc],
                                                                                       rhs=wo[:, cc, nh * 512:(nh + 1) * 512],
                                                                                       start=(cc == 0), stop=(cc == 15)),
                             reads=[("yc", gb), ("wo", cc)], writes=[("PY", b)])
                S.op("act", lambda e, b=b: e.activation(out=junk[:].rearrange("p (a n) -> p a n", a=2), in_=PY[b][:], func=AF.Square,
                                                        accum_out=stat[b][:, 0:1]),
                     reads=[("PY", b)], writes=["junk", ("s0", b)])
                S.op("act", lambda e, b=b: e.activation(out=stat[b][:, 1:2], in_=stat[b][:, 0:1], func=AF.Sqrt, scale=1.0 / DM, bias=EPS),
                     reads=[("s0", b)], writes=[("s1", b)])
                S.op("dve", lambda e, b=b: e.reciprocal(out=stat[b][:, 2:3], in_=stat[b][:, 1:2]), reads=[("s1", b)], writes=[("s2", b)])
                S.op("dve", lambda e, b=b: e.scalar_tensor_tensor(out=t1[b][:].rearrange("p (a n) -> p a n", a=2), in0=PY[b][:],
                                                                  scalar=stat[b][:, 2:3],
                                                                  in1=g.modb[:, 2 * DM:3 * DM].rearrange("p (a n) -> p a n", a=2),
                                                                  op0=ALU.mult, op1=ALU.mult),
                     reads=[("PY", b), ("s2", b)], writes=[("t1", b)])
                S.op("pool", lambda e, b=b: e.tensor_tensor(out=t1[b][:], in0=t1[b][:], in1=xt[b][:], op=ALU.add),
                     reads=[("t1", b), ("x", b)], writes=[("t1", b)])
                S.op("sp", lambda e, b=b, rows=rows: e.dma_start(out=g.out[rows, :], in_=t1[b][:]), reads=[("t1", b)], dma_key=("ot1", b))
        S.emit_block()


def _consts():
    i = np.arange(128)
    c = np.zeros((128, 7, 128), np.float32)
    c[:, 0, :] = np.eye(128)
    c[:, 1, :] = (i[:, None] >= i[None, :])
    c[:, 2, :] = (i[:, None] <= i[None, :])
    c[:, 3, :] = (i[:, None] > i[None, :])
    c[:, 4, :] = 1.0
    c[64, 5, 0:64] = 1.0
    return c


_PROG = {}
FUSED = True
_WNAMES = ("ada_w", "ada_b", "pre_norm_w", "post_norm_w", "w_in", "conv_w", "conv_b", "dt_bias", "a_log",
           "d_skip", "ssm_norm_w", "sinks", "w_out")


def kernel(x, c, ada_w, ada_b, pre_norm_w, post_norm_w, w_in, conv_w, conv_b,
           dt_bias, a_log, d_skip, ssm_norm_w, sinks, w_out):
    f = lambda a: np.ascontiguousarray(np.asarray(a, dtype=np.float32))
    ws = dict(ada_w=f(ada_w), ada_b=f(ada_b), pre_norm_w=f(pre_norm_w), post_norm_w=f(post_norm_w),
              w_in=f(w_in), conv_w=f(conv_w), conv_b=f(conv_b), dt_bias=f(dt_bias), a_log=f(a_log),
              d_skip=f(d_skip), ssm_norm_w=f(ssm_norm_w), sinks=f(sinks), w_out=f(w_out))
    x = f(x)
    c = f(c)
    consts = _consts()
    ccols = [np.ascontiguousarray(c[b].reshape(8, 128).T) for b in range(8)]
    if FUSED:
        if "fused" not in _PROG:
            _PROG["fused"] = build(n_layers=DEPTH, depth_dim=DEPTH)
        in_maps = []
        for b in range(8):
            m = dict(ws)
            m["consts"] = consts
            m["x"] = x[b]
            m["c_col"] = ccols[b]
            in_maps.append(m)
        res = run_bass_kernel_spmd(_PROG["fused"], in_maps, core_ids=list(range(8)))
        return np.stack([np.asarray(r["out"]) for r in res.results], axis=0).astype(np.float32)
    if "layer" not in _PROG:
        _PROG["layer"] = build(n_layers=1, depth_dim=1)
    cur = [x[b] for b in range(8)]
    for L in range(DEPTH):
        wl = {k: np.ascontiguousarray(ws[k][L:L + 1]) for k in _WNAMES}
        in_maps = []
        for b in range(8):
            m = dict(wl)
            m["consts"] = consts
            m["x"] = cur[b]
            m["c_col"] = ccols[b]
            in_maps.append(m)
        res = run_bass_kernel_spmd(_PROG["layer"], in_maps, core_ids=list(range(8)))
        cur = [np.ascontiguousarray(np.asarray(r["out"], dtype=np.float32)) for r in res.results]
    return np.stack(cur, axis=0).astype(np.float32)
```

```python
import contextlib
import numpy as np
import concourse.bass as bass
import concourse.mybir as mybir
from concourse.bass_utils import run_bass_kernel_spmd

F32 = mybir.dt.float32
BF16 = mybir.dt.bfloat16
AF = mybir.ActivationFunctionType
ALU = mybir.AluOpType

SEQ = 4096
DM = 1024
NT = SEQ // 128
EPS = 1e-6
DEPTH = 4
IN_COLS = 5904
C_OFF = 4624
ENGS = ("pe", "act", "dve", "pool", "sp")


def sl(start, n, step=1):
    return slice(start, start + step * (n - 1) + 1, step)


class Sched:
    def __init__(self, nc, stack):
        self.nc = nc
        self.stack = stack
        self.esem = {e: stack.enter_context(nc.semaphore("s_" + e)) for e in ENGS}
        self._names = {id(v): "s_" + k for k, v in self.esem.items()}
        self.ecount = {e: 0 for e in ENGS}
        self.dsem = {}
        self.dcount = {}
        self.waited = {e: {} for e in ENGS}
        self.n_ops = 0
        self.reset_block()

    def reset_block(self):
        self.ops = []
        self.last_w = {}
        self.readers = {}

    def _dma_sem(self, key):
        if key not in self.dsem:
            self.dsem[key] = self.stack.enter_context(self.nc.semaphore("d%d" % len(self.dsem)))
            self.dcount[key] = 0
            self._names[id(self.dsem[key])] = "d_" + str(key)
        return self.dsem[key]

    def capture(self):
        self._cap = []
        return self._cap

    def end_capture(self):
        lst, self._cap = self._cap, None
        return lst

    _DUR = {"pe": 0.2, "act": 0.6, "dve": 0.75, "pool": 1.1, "sp": 2.0}

    def replay_merged(self, lists):
        if not hasattr(self, "_sim_eng"):
            self._sim_eng = {e: 0.0 for e in ENGS}
            self._sim_key = {}
        its = [list(l) for l in lists if l]
        pos = [0] * len(its)
        while True:
            best, best_t = None, None
            for i in range(len(its)):
                if pos[i] >= len(its[i]):
                    continue
                eng, fn, reads, writes, dma_key = its[i][pos[i]]
                t = self._sim_eng[eng]
                for k in reads:
                    t = max(t, self._sim_key.get(k, 0.0))
                for k in writes:
                    t = max(t, self._sim_key.get(k, 0.0))
                if best is None or t < best_t - 1e-9:
                    best, best_t = i, t
            if best is None:
                break
            a = its[best][pos[best]]
            pos[best] += 1
            eng, fn, reads, writes, dma_key = a
            fin = best_t + self._DUR[eng]
            self._sim_eng[eng] = fin
            for k in writes:
                self._sim_key[k] = fin
            self.op(*a)

    def op(self, eng, fn, reads=(), writes=(), dma_key=None):
        if getattr(self, "_cap", None) is not None:
            self._cap.append((eng, fn, tuple(reads), tuple(writes), dma_key))
            return None
        idx = len(self.ops)
        deps = set()
        for k in reads:
            w = self.last_w.get(k)
            if w is not None:
                deps.add(w)
        for k in writes:
            w = self.last_w.get(k)
            if w is not None:
                deps.add(w)
            for r in self.readers.get(k, ()):
                deps.add(r)
        deps.discard(idx)
        self.ops.append(dict(eng=eng, fn=fn, deps=deps, dma_key=dma_key, milestone=False))
        for k in reads:
            self.readers.setdefault(k, []).append(idx)
        for k in writes:
            self.last_w[k] = idx
            self.readers[k] = []
        return idx

    def emit_block(self, name=None):
        nc = self.nc
        ops = self.ops
        for o in ops:
            keep = set()
            for d in o["deps"]:
                s = ops[d]
                if s["dma_key"] is None and o["dma_key"] is None and s["eng"] == o["eng"] == "pe":
                    continue
                keep.add(d)
            o["deps"] = keep
            for d in keep:
                if ops[d]["dma_key"] is None:
                    ops[d]["milestone"] = True
        ecount = dict(self.ecount)
        dcount = dict(self.dcount)
        for o in ops:
            if o["dma_key"] is not None:
                self._dma_sem(o["dma_key"])
                dcount[o["dma_key"]] = dcount.get(o["dma_key"], 0) + 16
                o["dval"] = dcount[o["dma_key"]]
            elif o["milestone"]:
                ecount[o["eng"]] += 1
                o["mval"] = ecount[o["eng"]]
        per_eng = {e: [o for o in ops if o["eng"] == e] for e in ENGS}
        final_d = dict(dcount)
        sched = self

        def emit_engine(ename, engine):
            waited = sched.waited[ename]
            for o in per_eng[ename]:
                need = {}
                for d in o["deps"]:
                    s = ops[d]
                    if s["dma_key"] is not None:
                        sem, val = sched.dsem[s["dma_key"]], s["dval"]
                    else:
                        sem, val = sched.esem[s["eng"]], s["mval"]
                    key = sched._names[id(sem)]
                    if val > need.get(key, (None, 0))[1]:
                        need[key] = (sem, val)
                for key, (sem, val) in need.items():
                    if waited.get(key, 0) >= val:
                        continue
                    engine.wait_ge(sem, val)
                    waited[key] = val
                ins = o["fn"](engine)
                if o["dma_key"] is not None:
                    ins.then_inc(sched.dsem[o["dma_key"]], 16)
                elif o["milestone"]:
                    ins.then_inc(sched.esem[ename], 1)
            if ename == "sp":
                for k, v in final_d.items():
                    sem = sched.dsem[k]
                    key = sched._names[id(sem)]
                    if waited.get(key, 0) < v:
                        engine.wait_ge(sem, v)
                        waited[key] = v

        with nc.Block(name) as block:
            @block.tensor
            def _(e):
                emit_engine("pe", e)

            @block.scalar
            def _(e):
                emit_engine("act", e)

            @block.vector
            def _(e):
                emit_engine("dve", e)

            @block.gpsimd
            def _(e):
                emit_engine("pool", e)

            @block.sync
            def _(e):
                emit_engine("sp", e)
        self.ecount = ecount
        self.dcount = dcount
        self.n_ops += len(ops)
        self.reset_block()


class K:
    pass


def dbg(g, name, ap, shape, dtype, reads):
    if not getattr(g, "debug", False):
        return
    import os
    taps = os.environ.get("DBG_TAPS", "")
    if not any(name == t or name.startswith(t + "_") for t in taps.split(",") if t):
        return
    d = g.nc.dram_tensor("dbg_" + name, list(shape), dtype, kind="ExternalOutput").ap()
    idx = tuple(slice(None) for _ in shape)
    g.S.op("sp", lambda e: e.dma_start(out=d[idx], in_=ap), reads=reads, dma_key=("dbg", name))


def build(n_layers=DEPTH, debug=False, phases=("mod", "p1", "att", "ssd", "out"), depth_dim=DEPTH):
    nc = bass.Bass("TRN2", target_bir_lowering=False)

    def din(name, shape):
        return nc.dram_tensor(name, shape, F32, kind="ExternalInput").ap()

    g = K()
    g.nc = nc
    g.uid = [0]
    g.debug = debug

    def _uniq(name):
        g.uid[0] += 1
        return "%s_%d" % (name, g.uid[0])
    g.sbuf = lambda name, shape, dt: nc.sbuf_tensor(_uniq(name), shape, dt)
    g.psum = lambda name, shape, dt: nc.psum_tensor(_uniq(name), shape, dt)
    g.x_in = din("x", [SEQ, DM])
    g.c_col = din("c_col", [128, 8])
    DD = depth_dim
    g.ada_w = din("ada_w", [DD, DM, 3 * DM])
    g.ada_b = din("ada_b", [DD, 3 * DM])
    g.pre_w = din("pre_norm_w", [DD, DM])
    g.post_w = din("post_norm_w", [DD, DM])
    g.w_in = din("w_in", [DD, DM, IN_COLS])
    g.conv_w = din("conv_w", [DD, 4, 1536])
    g.conv_b = din("conv_b", [DD, 1536])
    g.dt_bias = din("dt_bias", [DD, 16])
    g.a_log = din("a_log", [DD, 16])
    g.d_skip = din("d_skip", [DD, 16])
    g.ssm_w = din("ssm_norm_w", [DD, DM])
    g.sinks = din("sinks", [DD, 8])
    g.w_out = din("w_out", [DD, 2 * DM, DM])
    g.consts = din("consts", [128, 7, 128])
    g.out = nc.dram_tensor("out", [SEQ, DM], F32, kind="ExternalOutput").ap()
    g.ycat = nc.dram_tensor("ycat", [2 * DM, SEQ], BF16,
                            kind="ExternalOutput" if debug else "Internal").ap()

    with contextlib.ExitStack() as st:
        S = Sched(nc, st)
        g.S = S
        sb = lambda name, shape, dt: st.enter_context(nc.sbuf_tensor(name, shape, dt))
        g.cf = sb("cf", [128, 7, 128], F32)
        g.identb = sb("identb", [128, 128], BF16)
        g.mask2 = sb("mask2", [128, 256], BF16)
        g.u1b = sb("u1b", [128, 128], BF16)
        g.cbc = sb("cbc", [128, 8, 128], F32)
        g.modb = sb("modb", [128, 3 * DM], F32)
        g.hT = sb("hT", [128, 8, SEQ], BF16)

        phase_init(g)
        for L in range(n_layers):
            src = g.x_in if L == 0 else g.out
            if "mod" in phases:
                phase_mod(g, L)
            if "p1" in phases:
                phase_p1(g, L, src)
            if "att" in phases:
                specs = []
                for hp in range(4):
                    base = hp * 128
                    specs.append(dict(q=[(base, 128)], k=[(512 + base, 128)], v=[(1024 + base, 128)],
                                      z=[(1536 + base, 128)], pats=(1, 4, 16), sink=None,
                                      rows=(base, base + 64)))
                for i in range(4):
                    specs.append(dict(q=[(C_OFF + i * 64, 64), (C_OFF + (4 + i) * 64, 64)],
                                      k=[(C_OFF + 1024, 128)], v=[(C_OFF + 1152, 128)],
                                      z=[(C_OFF + 512 + i * 64, 64), (C_OFF + 512 + (4 + i) * 64, 64)],
                                      pats=(1,), sink=(i, 4 + i), reuse_kv=(i > 0),
                                      rows=(1536 + i * 64, 1536 + (4 + i) * 64)))
                phase_att(g, L, specs)
            if "ssd" in phases:
                for grp in range(2):
                    phase_ssd(g, L, grp)
            if "out" in phases:
                phase_out(g, L, src)
        g.n_ops = S.n_ops
    return nc


def phase_init(g):
    nc, S = g.nc, g.S
    with contextlib.ExitStack() as st:
        cc = st.enter_context(g.sbuf("cc", [128, 8], F32))
        ca = st.enter_context(g.sbuf("ca", [128, 8], F32))
        S.op("sp", lambda e: e.dma_start(out=g.cf[:], in_=g.consts[:, :, :]), writes=["cf"], dma_key="cf")
        S.op("sp", lambda e: e.dma_start(out=cc[:], in_=g.c_col[:, :]), writes=["cc"], dma_key="cc")
        S.op("pool", lambda e: e.tensor_copy(out=g.identb[:], in_=g.cf[:, 0, :]), reads=["cf"], writes=["identb"])
        S.op("pool", lambda e: e.tensor_copy(out=g.mask2[:].rearrange("p (a b) -> p a b", a=2), in_=g.cf[:, 1:3, :]),
             reads=["cf"], writes=["mask2"])
        S.op("pool", lambda e: e.tensor_copy(out=g.u1b[:], in_=g.cf[:, 3, :]), reads=["cf"], writes=["u1b"])
        S.op("act", lambda e: e.activation(out=ca[:], in_=cc[:], func=AF.Silu), reads=["cc"], writes=["ca"])
        S.op("dve", lambda e: e.tensor_copy(out=g.cbc[:], in_=ca[:].unsqueeze(2).to_broadcast([128, 8, 128])),
             reads=["ca"], writes=["cbc"])
        S.emit_block("init")


def phase_mod(g, L):
    nc, S = g.nc, g.S
    with contextlib.ExitStack() as st:
        stage = [st.enter_context(g.sbuf("mstage%d" % i, [128, 8, 512], F32)) for i in range(2)]
        adab = st.enter_context(g.sbuf("adab", [128, 3 * DM], F32))
        pw = st.enter_context(g.sbuf("pw", [128, 2, DM], F32))
        ps = [st.enter_context(g.psum("mps%d" % i, [128, 512], F32)) for i in range(2)]
        S.op("sp", lambda e: e.dma_start(out=adab[:], in_=g.ada_b[L:L + 1, :].partition_broadcast(128)),
             writes=["adab"], dma_key="adab")
        S.op("sp", lambda e: e.dma_start(out=pw[:, 0, :], in_=g.pre_w[L:L + 1, :].partition_broadcast(128)),
             writes=["pw0"], dma_key="pw0")
        S.op("sp", lambda e: e.dma_start(out=pw[:, 1, :], in_=g.post_w[L:L + 1, :].partition_broadcast(128)),
             writes=["pw1"], dma_key="pw1")
        aw = g.ada_w[L].rearrange("(ch p) n -> p ch n", p=128)
        for grp in range(6):
            b = grp % 2
            cs = slice(grp * 512, (grp + 1) * 512)
            S.op("sp", lambda e, b=b, cs=cs: e.dma_start(out=stage[b][:], in_=aw[:, :, cs]),
                 writes=[("mst", b)], dma_key=("mst", b))
            for ch in range(8):
                S.op("pe", lambda e, b=b, ch=ch: e.matmul(ps[b][:], lhsT=g.cbc[:, ch, :], rhs=stage[b][:, ch, :],
                                                          start=(ch == 0), stop=(ch == 7)),
                     reads=[("mst", b), "cbc"], writes=[("mps", b)])
            S.op("dve", lambda e, b=b, cs=cs: e.tensor_tensor(out=g.modb[:, cs], in0=ps[b][:], in1=adab[:, cs], op=ALU.add),
                 reads=[("mps", b), "adab"], writes=["modb"])
        S.op("dve", lambda e: e.scalar_tensor_tensor(out=g.modb[:, DM:2 * DM], in0=g.modb[:, DM:2 * DM], scalar=1.0,
                                                     in1=pw[:, 0, :], op0=ALU.add, op1=ALU.mult),
             reads=["modb", "pw0"], writes=["modb"])
        S.op("dve", lambda e: e.tensor_tensor(out=g.modb[:, 2 * DM:3 * DM], in0=g.modb[:, 2 * DM:3 * DM], in1=pw[:, 1, :],
                                              op=ALU.mult),
             reads=["modb", "pw1"], writes=["modb"])
        S.emit_block("mod%d" % L)


def phase_p1(g, L, src):
    nc, S = g.nc, g.S
    NB = 4
    with contextlib.ExitStack() as st:
        xt = [st.enter_context(g.sbuf("p1x%d" % i, [128, DM], F32)) for i in range(NB)]
        tmp = [st.enter_context(g.sbuf("p1t%d" % i, [128, DM], F32)) for i in range(NB)]
        hb = [st.enter_context(g.sbuf("p1h%d" % i, [128, DM], BF16)) for i in range(NB)]
        junk = st.enter_context(g.sbuf("p1junk", [128, DM], BF16))
        stat = [st.enter_context(g.sbuf("p1s%d" % i, [128, 4], F32)) for i in range(NB)]
        pst = [st.enter_context(g.psum("p1ps%d" % i, [128, 8, 128], BF16)) for i in range(2)]

        def s1(t):
            b = t % NB
            rows = slice(t * 128, (t + 1) * 128)
            S.op("sp", lambda e: e.dma_start(out=xt[b][:], in_=src[rows, :]), writes=[("x", b)], dma_key=("p1x", b))
            S.op("act", lambda e: e.activation(out=junk[:], in_=xt[b][:], func=AF.Square, accum_out=stat[b][:, 0:1]),
                 reads=[("x", b)], writes=["junk", ("s0", b)])
            S.op("act", lambda e: e.activation(out=stat[b][:, 1:2], in_=stat[b][:, 0:1], func=AF.Sqrt, scale=1.0 / DM, bias=EPS),
                 reads=[("s0", b)], writes=[("s1", b)])
            S.op("dve", lambda e: e.reciprocal(out=stat[b][:, 2:3], in_=stat[b][:, 1:2]), reads=[("s1", b)], writes=[("s2", b)])
            S.op("dve", lambda e: e.scalar_tensor_tensor(out=tmp[b][:], in0=xt[b][:], scalar=stat[b][:, 2:3],
                                                         in1=g.modb[:, DM:2 * DM], op0=ALU.mult, op1=ALU.mult),
                 reads=[("x", b), ("s2", b)], writes=[("t", b)])
            S.op("pool" if t % 2 == 0 else "dve",
                 lambda e: e.tensor_tensor(out=hb[b][:], in0=tmp[b][:], in1=g.modb[:, 0:DM], op=ALU.add),
                 reads=[("t", b)], writes=[("h", b)])

        def s2(t):
            b = t % NB
            pb = t % 2
            rows = slice(t * 128, (t + 1) * 128)
            for ch in range(8):
                S.op("pe", lambda e, ch=ch: e.transpose(pst[pb][:, ch, :], hb[b][:, ch * 128:(ch + 1) * 128], g.identb[:]),
                     reads=[("h", b)], writes=[("ps", pb)])
            S.op("act", lambda e: e.activation(out=g.hT[:, :, rows], in_=pst[pb][:], func=AF.Copy),
                 reads=[("ps", pb)], writes=[("hT", t)])

        s1(0)
        s1(1)
        for t in range(NT):
            if t + 2 < NT:
                s1(t + 2)
            s2(t)
        S.emit_block("p1_%d" % L)


def _groups(d, b):
    if d == 1:
        return [b // 4]
    if d == 4:
        return [b]
    return [4 * b + i for i in range(4)]


def phase_att(g, L, specs):
    nc, S = g.nc, g.S
    with contextlib.ExitStack() as st:
        sbt = lambda name, shape, dt: st.enter_context(g.sbuf(name, shape, dt))
        wst = [sbt("awst%d" % i, [128, 8, 128], F32) for i in range(2)]
        wts = [{n: sbt("aw_%s%d" % (n, i), [128, 8, 128], BF16) for n in "qkvz"} for i in range(2)]
        QT = sbt("QT", [128, SEQ], BF16)
        KT = sbt("KT", [128, SEQ], BF16)
        VT = sbt("VTf", [128, SEQ], BF16)
        Vt = {d: sbt("Vt%d" % d, [128, 32, 2, 65], BF16) for d in (1, 4, 16)}
        NSB = 3
        Et = [sbt("E%d" % i, [128, 2, 256], BF16) for i in range(NSB)]
        Pt = [sbt("P%d" % i, [128, 2, 256], BF16) for i in range(NSB)]
        Acc = sbt("Acc", [65, 2, SEQ], F32)
        Ut = [sbt("U%d" % i, [64, 512], F32) for i in range(2)]
        Tt = [sbt("T%d" % i, [64, 512], F32) for i in range(2)]
        Yb = [sbt("Yb%d" % i, [64, 512], BF16) for i in range(2)]
        sks = [sbt("sk%d" % i, [64, 2], F32) for i in range(2)]
        Sp = [st.enter_context(g.psum("aS%d" % i, [128, 2, 512], F32)) for i in range(NSB)]
        Op = [st.enter_context(g.psum("aO%d" % i, [128, 512], F32)) for i in range(2)]
        banks = [(i, j) for i in range(NSB) for j in range(2)]
        bk = lambda ij: Sp[ij[0]][:, ij[1], :]
        bkey = lambda ij: ("Sb", ij[0], ij[1])
        wv_in = g.w_in[L].rearrange("(ch p) n -> p ch n", p=128)
        for d in (1, 4, 16):
            S.op("pool", lambda e, d=d: e.memset(Vt[d][:, :, :, 64:65], 1.0), writes=[("Vone", d)])
        pj = 0
        step = 0
        fj = 0
        def load_weights(si):
            sp = specs[si]
            wt = wts[si % 2]
            sk = sks[si % 2]
            for wi, n in enumerate("qkvz"):
                b = wi % 2
                off = 0
                for pi, (c0, cn) in enumerate(sp[n]):
                    S.op("sp", lambda e, b=b, c0=c0, cn=cn, off=off: e.dma_start(out=wst[b][:, :, off:off + cn],
                                                                               in_=wv_in[:, :, c0:c0 + cn]),
                         writes=[("wst", b, pi)], dma_key=("awst", b, pi))
                    off += cn
                ceng = ("pool", "dve", "pool", "act")[wi]
                if ceng == "act":
                    S.op("act", lambda e, b=b, n=n, wt=wt: e.activation(out=wt[n][:], in_=wst[b][:], func=AF.Copy),
                         reads=[("wst", b, 0), ("wst", b, 1)], writes=[("w", n, si % 2)])
                else:
                    S.op(ceng, lambda e, b=b, n=n, wt=wt: e.tensor_copy(out=wt[n][:], in_=wst[b][:]),
                         reads=[("wst", b, 0), ("wst", b, 1)], writes=[("w", n, si % 2)])
            if sp["sink"] is not None:
                for h in range(2):
                    hh = sp["sink"][h]
                    S.op("sp", lambda e, h=h, hh=hh, sk=sk: e.dma_start(out=sk[:, h:h + 1],
                                                                        in_=g.sinks[L:L + 1, hh:hh + 1].partition_broadcast(64)),
                         writes=[("skr", si % 2, h)], dma_key=("sk", si % 2, h))
                S.op("act", lambda e, sk=sk: e.activation(out=sk[:], in_=sk[:], func=AF.Exp),
                     reads=[("skr", si % 2, 0), ("skr", si % 2, 1)], writes=[("sk", si % 2)])

        load_weights(0)
        for si, sp in enumerate(specs):
            pats = sp["pats"]
            wt = wts[si % 2]
            sk = sks[si % 2]
            wk = lambda n, si=si: ("w", n, si % 2)
            for n in range(8):
                ts_ = slice(n * 512, (n + 1) * 512)
                for nm, eng in (("q", "act"), ("k", "dve"), ("v", "act")):
                    if nm != "q" and sp.get("reuse_kv"):
                        continue
                    ij = banks[pj % len(banks)]
                    pj += 1
                    for ch in range(8):
                        S.op("pe", lambda e, ij=ij, ch=ch, nm=nm, ts_=ts_, wt=wt: e.matmul(bk(ij), lhsT=wt[nm][:, ch, :], rhs=g.hT[:, ch, ts_],
                                                                                         start=(ch == 0), stop=(ch == 7)),
                             reads=[wk(nm)], writes=[bkey(ij)])
                    if nm == "q":
                        S.op("act", lambda e, ij=ij, ts_=ts_: e.activation(out=QT[:, ts_], in_=bk(ij), func=AF.Copy, scale=0.125),
                             reads=[bkey(ij)], writes=[("QT", n)])
                    elif nm == "k":
                        S.op("dve", lambda e, ij=ij, ts_=ts_: e.tensor_copy(out=KT[:, ts_], in_=bk(ij)),
                             reads=[bkey(ij)], writes=[("KT", n)])
                    else:
                        S.op("act", lambda e, ij=ij, ts_=ts_: e.activation(out=VT[:, ts_], in_=bk(ij), func=AF.Copy),
                             reads=[bkey(ij)], writes=[("VT", n)])
            for d in (() if sp.get("reuse_kv") else pats):
                nb = 32 // d
                for b4 in range(8):
                    ij = banks[pj % len(banks)]
                    pj += 1
                    pb16 = lambda ij: bk(ij).bitcast(BF16)
                    for j in range(4):
                        blk = b4 * 4 + j
                        r, bb = blk // nb, blk % nb
                        tok = sl(r + d * 128 * bb, 128, d)
                        S.op("pe", lambda e, ij=ij, j=j, tok=tok: e.transpose(pb16(ij)[:, j * 128:(j + 1) * 128], VT[:, tok], g.identb[:]),
                             reads=[("VT", x) for x in _groups(d, bb)], writes=[bkey(ij)])
                    vsrc = lambda ij: pb16(ij)[:, 0:512].rearrange("p (j h c) -> p j h c", j=4, h=2)
                    if b4 % 2 == 0:
                        S.op("dve", lambda e, ij=ij, b4=b4, d=d: e.tensor_copy(out=Vt[d][:, b4 * 4:(b4 + 1) * 4, :, 0:64], in_=vsrc(ij)),
                             reads=[bkey(ij)], writes=[("Vt", d, b4)])
                    else:
                        S.op("act", lambda e, ij=ij, b4=b4, d=d: e.activation(out=Vt[d][:, b4 * 4:(b4 + 1) * 4, :, 0:64], in_=vsrc(ij),
                                                                             func=AF.Copy),
                             reads=[bkey(ij)], writes=[("Vt", d, b4)])
            if si + 1 < len(specs):
                load_weights(si + 1)
            steps = []
            for pi, d in enumerate(pats):
                nb = 32 // d
                for r in range(d):
                    for bb in range(nb):
                        steps.append(dict(pi=pi, d=d, r=r, bb=bb, nb=nb, s_=step % NSB, o_=step % 2,
                                          meng="dve" if step % 2 == 0 else "pool"))
                        step += 1

            def part1(stp):
                d, r, bb, s_, meng = stp["d"], stp["r"], stp["bb"], stp["s_"], stp["meng"]
                tq = sl(r + d * 128 * bb, 128, d)
                tp = sl(r + d * 128 * (bb - 1), 128, d) if bb > 0 else None
                lo = 0 if bb > 0 else 128
                gq = _groups(d, bb)
                gk = gq + (_groups(d, bb - 1) if bb > 0 else [])
                rd = [("QT", x) for x in gq] + [("KT", x) for x in set(gk)]
                skeys = [("Sb", s_, 0), ("Sb", s_, 1)]
                for h in range(2):
                    hs = slice(64 * h, 64 * h + 64)
                    S.op("pe", lambda e, s_=s_, h=h, hs=hs, tq=tq: e.matmul(Sp[s_][:, h, 128:256], lhsT=KT[hs, tq], rhs=QT[hs, tq],
                                                                           start=True, stop=True),
                         reads=rd, writes=skeys)
                    if bb > 0:
                        S.op("pe", lambda e, s_=s_, h=h, hs=hs, tq=tq, tp=tp: e.matmul(Sp[s_][:, h, 0:128], lhsT=KT[hs, tp],
                                                                                      rhs=QT[hs, tq], start=True, stop=True),
                             reads=rd, writes=skeys)
                S.op("act", lambda e, s_=s_, lo=lo: e.activation(out=Et[s_][:, :, lo:256], in_=Sp[s_][:, :, lo:256], func=AF.Exp),
                     reads=skeys, writes=[("E", s_)])
                S.op(meng, lambda e, s_=s_, lo=lo: e.tensor_tensor(out=Pt[s_][:, :, lo:256], in0=Et[s_][:, :, lo:256],
                                                                  in1=g.mask2[:, lo:256].unsqueeze(1).to_broadcast([128, 2, 256 - lo]),
                                                                  op=ALU.mult),
                     reads=[("E", s_)], writes=[("P", s_)])

            def part2(stp):
                pi, d, r, bb, nb, s_, o_ = stp["pi"], stp["d"], stp["r"], stp["bb"], stp["nb"], stp["s_"], stp["o_"]
                blk = r * nb + bb
                tq = sl(r + d * 128 * bb, 128, d)
                gq = _groups(d, bb)
                vrd = [("Vt", d, blk // 4), ("Vone", d)] + ([("Vt", d, (blk - 1) // 4)] if bb > 0 else [])
                for h in range(2):
                    if bb > 0:
                        S.op("pe", lambda e, s_=s_, o_=o_, h=h, blk=blk, d=d: e.matmul(Op[o_][0:65, h * 128:(h + 1) * 128],
                                                                                      lhsT=Vt[d][:, blk - 1, h, :], rhs=Pt[s_][:, h, 0:128],
                                                                                      start=True, stop=False),
                             reads=[("P", s_)] + vrd, writes=[("O", o_)])
                    S.op("pe", lambda e, s_=s_, o_=o_, h=h, blk=blk, d=d, bb=bb: e.matmul(Op[o_][0:65, h * 128:(h + 1) * 128],
                                                                                         lhsT=Vt[d][:, blk, h, :], rhs=Pt[s_][:, h, 128:256],
                                                                                         start=(bb == 0), stop=True),
                         reads=[("P", s_)] + vrd, writes=[("O", o_)])
                akeys = [("acc", x, r % 4) for x in gq] if d > 1 else [("acc", gq[0], x) for x in range(4)]
                o_view = Op[o_][0:65, 0:256].rearrange("p (h q) -> p h q", h=2)
                if pi == 0:
                    S.op("dve", lambda e, tq=tq, o_view=o_view: e.tensor_copy(out=Acc[:, :, tq], in_=o_view),
                         reads=[("O", o_)], writes=akeys)
                else:
                    S.op("dve", lambda e, tq=tq, o_view=o_view: e.tensor_tensor(out=Acc[:, :, tq], in0=Acc[:, :, tq], in1=o_view, op=ALU.add),
                         reads=[("O", o_)] + akeys, writes=akeys)

            LA = NSB - 1
            for i in range(min(LA, len(steps))):
                part1(steps[i])
            for i in range(len(steps)):
                if i + LA < len(steps):
                    part1(steps[i + LA])
                part2(steps[i])
            for h in range(2):
                for n in range(8):
                    ts_ = slice(n * 512, (n + 1) * 512)
                    b = fj % 2
                    fj += 1
                    ijl = banks[pj % len(banks)]
                    pj += 1
                    ijz = banks[pj % len(banks)]
                    pj += 1
                    acc_rd = [("acc", n, x) for x in range(4)]
                    S.op("pe", lambda e, ijl=ijl, h=h, ts_=ts_: e.matmul(bk(ijl)[0:64, :], lhsT=g.cf[0:65, 5, 0:64], rhs=Acc[0:65, h, ts_],
                                                                         start=True, stop=True),
                         reads=acc_rd, writes=[bkey(ijl)])
                    for ch in range(8):
                        S.op("pe", lambda e, ijz=ijz, ch=ch, h=h, ts_=ts_, wt=wt: e.matmul(bk(ijz)[0:64, :], lhsT=wt["z"][:, ch, 64 * h:64 * h + 64],
                                                                                          rhs=g.hT[:, ch, ts_], start=(ch == 0), stop=(ch == 7)),
                             reads=[wk("z")], writes=[bkey(ijz)])
                    if sp["sink"] is not None:
                        S.op("act", lambda e, b=b, ijl=ijl, h=h, sk=sk: e.activation(out=Ut[b][:], in_=bk(ijl)[0:64, :], func=AF.Ln,
                                                                                     bias=sk[:, h:h + 1]),
                             reads=[bkey(ijl), ("sk", si % 2)], writes=[("U", b)])
                    else:
                        S.op("act", lambda e, b=b, ijl=ijl: e.activation(out=Ut[b][:], in_=bk(ijl)[0:64, :], func=AF.Ln),
                             reads=[bkey(ijl)], writes=[("U", b)])
                    S.op("act", lambda e, b=b, ijz=ijz: e.activation(out=Tt[b][:], in_=bk(ijz)[0:64, :], func=AF.Exp, scale=-1.0),
                         reads=[bkey(ijz)], writes=[("T", b)])
                    S.op("act", lambda e, b=b: e.activation(out=Tt[b][:], in_=Tt[b][:], func=AF.Ln, bias=1.0),
                         reads=[("T", b)], writes=[("T", b)])
                    S.op("pool", lambda e, b=b: e.tensor_tensor(out=Ut[b][:], in0=Ut[b][:], in1=Tt[b][:], op=ALU.add),
                         reads=[("U", b), ("T", b)], writes=[("U", b)])
                    S.op("act", lambda e, b=b: e.activation(out=Ut[b][:], in_=Ut[b][:], func=AF.Exp, scale=-1.0),
                         reads=[("U", b)], writes=[("U", b)])
                    S.op("dve", lambda e, b=b, h=h, ts_=ts_: e.tensor_tensor(out=Ut[b][:], in0=Acc[0:64, h, ts_], in1=Ut[b][:], op=ALU.mult),
                         reads=[("U", b)] + acc_rd, writes=[("U", b)])
                    S.op("dve", lambda e, b=b, ijz=ijz: e.tensor_tensor(out=Yb[b][:], in0=Ut[b][:], in1=bk(ijz)[0:64, :], op=ALU.mult),
                         reads=[("U", b), bkey(ijz)], writes=[("Y", b)])
                    row0 = sp["rows"][h]
                    S.op("sp", lambda e, b=b, row0=row0, ts_=ts_: e.dma_start(out=g.ycat[row0:row0 + 64, ts_], in_=Yb[b][:]),
                         reads=[("Y", b)], dma_key=("aY", b))
        S.emit_block()


def phase_ssd(g, L, grp):
    nc, S = g.nc, g.S
    NW = 1296
    zc0, xc0, bc0, cc0, dc0 = 2048 + grp * 512, 3072 + grp * 512, 4096 + grp * 128, 4352 + grp * 128, 4608 + grp * 8
    with contextlib.ExitStack() as st:
        sbt = lambda name, shape, dt: st.enter_context(g.sbuf(name, shape, dt))
        pst = lambda name, shape, dt: st.enter_context(g.psum(name, shape, dt))
        wB = sbt("wB", [128, 8, NW], BF16)
        wst = [sbt("bwst%d" % i, [128, 8, 128], F32) for i in range(2)]
        prm_r = sbt("prm_r", [36, 128], F32)
        prm = sbt("prm", [128, 36], F32)
        hp = sbt("hp", [128, 3, 8], F32)
        pre = [sbt("pre%d" % i, [128, 515], F32) for i in range(2)]
        halo = sbt("halo", [128, 6, 3], F32)
        cacc = [sbt("cacc%d" % i, [128, 512], F32) for i in range(2)]
        ctmp = cacc[0]
        xc = [sbt("xc%d" % i, [128, 4, 512], BF16) for i in range(2)]
        fs = [sbt("fs%d" % i, [128, 6, 4, 8], F32) for i in range(2)]
        BT = [sbt("BTt%d" % i, [128, 512], BF16) for i in range(2)]
        CT = [sbt("CTt%d" % i, [128, 512], BF16) for i in range(2)]
        szs = [sbt("szs%d" % i, [128, 4, 512], F32) for i in range(2)]
        dts = [sbt("dts%d" % i, [128, 3, 4, 8], F32) for i in range(2)]
        NQ = 3
        xtm = [sbt("xtm%d" % i, [128, 512], BF16) for i in range(NQ)]
        Btm = [sbt("Btm%d" % i, [128, 128], BF16) for i in range(NQ)]
        xdt = [sbt("xdt%d" % i, [128, 512], BF16) for i in range(NQ)]
        xdte = [sbt("xdte%d" % i, [128, 512], BF16) for i in range(NQ)]
        tD = [sbt("tD%d" % i, [128, 512], BF16) for i in range(NQ)]
        MT = [sbt("MT%d" % i, [128, 8, 128], BF16) for i in range(NQ)]
        dAh = [sbt("dAh%d" % i, [128, 4, 8], BF16) for i in range(2)]
        dAl = [sbt("dAl%d" % i, [128, 4, 8], BF16) for i in range(2)]
        dAr = sbt("dAr", [128, 4, 8], F32)
        rhsH = sbt("rhsH", [128, 8, 128], BF16)
        rhsL = sbt("rhsL", [128, 8, 128], BF16)
        LT = sbt("LT", [128, 8, 128], BF16)
        GTm = sbt("GTm", [128, 128], F32)
        yt = sbt("yt", [128, 512], F32)
        nst = sbt("nst", [128, 4], F32)
        yn = [sbt("yn%d" % i, [128, 512], BF16) for i in range(2)]
        Ybuf = [sbt("Ybuf%d" % i, [128, 4, 512], BF16) for i in range(2)]
        H = sbt("H", [128, 512], F32)
        Hb = sbt("Hb", [128, 512], BF16)
        PA0 = pst("PA0", [128, 512], F32)
        PSEGs = [pst("PSEG%d" % i, [128, 512], F32) for i in range(2)]
        Psm = pst("Psm", [128, 512], F32)
        PBy = pst("PBy", [128, 512], F32)
        PBo = pst("PBo", [128, 512], F32)
        PS2 = pst("PS2", [128, 512], F32)
        Pbf = pst("Pbf", [128, 8, 128], BF16)
        wv_in = g.w_in[L].rearrange("(ch p) n -> p ch n", p=128)

        pieces = [(zc0 + i * 128, 128, i * 128) for i in range(4)] + [(xc0 + i * 128, 128, 512 + i * 128) for i in range(4)]
        pieces += [(bc0, 128, 1024), (cc0, 128, 1152), (dc0, 8, 1280)]
        wball = [("wB", i) for i in range(len(pieces))]
        for i, (c0, cn, o0) in enumerate(pieces):
            b = i % 2
            S.op("sp", lambda e, b=b, c0=c0, cn=cn: e.dma_start(out=wst[b][:, :, 0:cn], in_=wv_in[:, :, c0:c0 + cn]),
                 writes=[("wst", b)], dma_key=("bwst", b))
            ceng = ("act", "dve", "pool")[i % 3]
            if ceng == "act":
                S.op("act", lambda e, b=b, cn=cn, o0=o0: e.activation(out=wB[:, :, o0:o0 + cn], in_=wst[b][:, :, 0:cn], func=AF.Copy),
                     reads=[("wst", b)], writes=[("wB", i)])
            else:
                S.op(ceng, lambda e, b=b, cn=cn, o0=o0: e.tensor_copy(out=wB[:, :, o0:o0 + cn], in_=wst[b][:, :, 0:cn]),
                     reads=[("wst", b)], writes=[("wB", i)])
        S.op("pool", lambda e: e.memset(prm_r[:], 0.0), writes=["prm_r0"])
        cw = g.conv_w[L].rearrange("k (cc p) -> k cc p", p=128)
        cbv = g.conv_b[L:L + 1, :].rearrange("o (cc p) -> (o cc) p", p=128)
        swv = g.ssm_w[L:L + 1, :].rearrange("o (cc p) -> (o cc) p", p=128)
        k_ = 0
        for tap in range(4):
            S.op("sp", lambda e, tap=tap: e.dma_start(out=prm_r[tap * 6:tap * 6 + 4, :], in_=cw[tap, grp * 4:grp * 4 + 4, :]),
                 reads=["prm_r0"], writes=[("prm_r", k_)], dma_key=("prm", k_))
            k_ += 1
            for j, c_ in ((4, 8 + grp), (5, 10 + grp)):
                S.op("sp", lambda e, tap=tap, j=j, c_=c_: e.dma_start(out=prm_r[tap * 6 + j:tap * 6 + j + 1, :], in_=cw[tap, c_:c_ + 1, :]),
                     reads=["prm_r0"], writes=[("prm_r", k_)], dma_key=("prm", k_))
                k_ += 1
        S.op("sp", lambda e: e.dma_start(out=prm_r[24:28, :], in_=cbv[grp * 4:grp * 4 + 4, :]),
             reads=["prm_r0"], writes=[("prm_r", k_)], dma_key=("prm", k_))
        k_ += 1
        for j, c_ in ((28, 8 + grp), (29, 10 + grp)):
            S.op("sp", lambda e, j=j, c_=c_: e.dma_start(out=prm_r[j:j + 1, :], in_=cbv[c_:c_ + 1, :]),
                 reads=["prm_r0"], writes=[("prm_r", k_)], dma_key=("prm", k_))
            k_ += 1
        S.op("sp", lambda e: e.dma_start(out=prm_r[30:34, :], in_=swv[grp * 4:grp * 4 + 4, :]),
             reads=["prm_r0"], writes=[("prm_r", k_)], dma_key=("prm", k_))
        k_ += 1
        S.op("pe", lambda e: e.transpose(PA0[:, 0:36], prm_r[:, :], g.cf[0:36, 0, 0:36]),
             reads=[("prm_r", i) for i in range(k_)], writes=["PA0"])
        S.op("dve", lambda e: e.tensor_copy(out=prm[:], in_=PA0[:, 0:36]), reads=["PA0"], writes=["prm"])
        for i, src_ in enumerate((g.dt_bias, g.a_log, g.d_skip)):
            S.op("sp", lambda e, i=i, src_=src_: e.dma_start(out=hp[:, i, :], in_=src_[L:L + 1, grp * 8:grp * 8 + 8].partition_broadcast(128)),
                 writes=[("hp", i)], dma_key=("hp", i))
        S.op("act", lambda e: e.activation(out=hp[:, 1, :], in_=hp[:, 1, :], func=AF.Exp), reads=[("hp", 1)], writes=[("hp", 1)])
        S.op("dve", lambda e: e.tensor_scalar(out=hp[:, 1, :], in0=hp[:, 1, :], scalar1=-1.0, scalar2=None, op0=ALU.mult),
             reads=[("hp", 1)], writes=[("hp", 1)])
        S.op("pool", lambda e: e.memset(halo[:], 0.0), writes=["halo"])
        S.op("pool", lambda e: e.memset(H[:], 0.0), writes=["H"])
        S.op("pool", lambda e: e.memset(Hb[:], 0.0), writes=["Hb"])

        tri = g.cf[:, 2, :]
        u1 = g.cf[:, 3, :]
        ones = g.cf[:, 4, :]
        identf = g.cf[:, 0, :]
        v8 = lambda ap: ap.rearrange("p (h d) -> p h d", h=8)
        b8 = lambda ap: ap.unsqueeze(2).to_broadcast([128, 8, 64])

        def mm_group(out_ap, lhs_fn, rhs_fn, reads, writes):
            def fn(e):
                ins = None
                for ch in range(8):
                    ins = e.matmul(out_ap, lhsT=lhs_fn(ch), rhs=rhs_fn(ch), start=(ch == 0), stop=(ch == 7))
                return ins
            S.op("pe", fn, reads=reads, writes=writes)

        def front(sc):
            p = sc % 2
            ts_ = slice(sc * 512, (sc + 1) * 512)
            for j in range(6):
                pb = j % 2
                wofs = 512 + j * 128 if j < 4 else (1024 if j == 4 else 1152)
                mm_group(PA0[:], lambda ch, wofs=wofs: wB[:, ch, wofs:wofs + 128], lambda ch: g.hT[:, ch, ts_], wball, ["PA0"])
                S.op("pool", lambda e, pb=pb, j=j: e.tensor_copy(out=pre[pb][:, 0:3], in_=halo[:, j, :]),
                     reads=["halo"], writes=[("pre", pb)])
                S.op("act", lambda e, pb=pb: e.activation(out=pre[pb][:, 3:515], in_=PA0[:], func=AF.Copy),
                     reads=["PA0"], writes=[("pre", pb)])
                S.op("pool", lambda e, pb=pb, j=j: e.tensor_copy(out=halo[:, j, :], in_=pre[pb][:, 512:515]),
                     reads=[("pre", pb)], writes=["halo"])
                if False:
                    S.op("pool", lambda e, pb=pb, j=j: e.tensor_scalar(out=cacc[pb][:], in0=pre[pb][:, 0:512], scalar1=prm[:, j:j + 1], scalar2=None,
                                                                       op0=ALU.mult),
                         reads=[("pre", pb), "prm"], writes=[("cacc", pb)])
                    for tap in range(1, 4):
                        S.op("pool", lambda e, pb=pb, j=j, tap=tap: e.tensor_scalar(out=ctmp[:], in0=pre[pb][:, tap:tap + 512],
                                                                                    scalar1=prm[:, tap * 6 + j:tap * 6 + j + 1], scalar2=None,
                                                                                    op0=ALU.mult),
                             reads=[("pre", pb), "prm"], writes=["ctmp"])
                        S.op("pool", lambda e, pb=pb: e.tensor_tensor(out=cacc[pb][:], in0=cacc[pb][:], in1=ctmp[:], op=ALU.add),
                             reads=["ctmp", ("cacc", pb)], writes=[("cacc", pb)])
                else:
                    S.op("dve", lambda e, pb=pb, j=j: e.tensor_scalar(out=cacc[pb][:], in0=pre[pb][:, 0:512], scalar1=prm[:, j:j + 1], scalar2=None,
                                                                      op0=ALU.mult),
                         reads=[("pre", pb), "prm"], writes=[("cacc", pb)])
                    for tap in range(1, 4):
                        S.op("dve", lambda e, pb=pb, j=j, tap=tap: e.scalar_tensor_tensor(out=cacc[pb][:], in0=pre[pb][:, tap:tap + 512],
                                                                                         scalar=prm[:, tap * 6 + j:tap * 6 + j + 1], in1=cacc[pb][:],
                                                                                         op0=ALU.mult, op1=ALU.add),
                             reads=[("pre", pb), ("cacc", pb), "prm"], writes=[("cacc", pb)])
                dst = xc[p][:, j, :] if j < 4 else (BT[p][:] if j == 4 else CT[p][:])
                dkey = ("xc", p, j) if j < 4 else (("BT", p) if j == 4 else ("CT", p))
                S.op("act", lambda e, pb=pb, j=j, dst=dst: e.activation(out=dst, in_=cacc[pb][:], func=AF.Silu, bias=prm[:, 24 + j:25 + j]),
                     reads=[("cacc", pb), "prm"], writes=[dkey])
            for k in range(4):
                tc_ = slice((sc * 4 + k) * 128, (sc * 4 + k + 1) * 128)
                mm_group(PA0[:], lambda ch, tc_=tc_: g.hT[:, ch, tc_], lambda ch: wB[:, ch, 0:512], wball, ["PA0"])
                S.op("act", lambda e, p=p, k=k: e.activation(out=szs[p][:, k, :], in_=PA0[:], func=AF.Silu), reads=["PA0"], writes=[("sz", p, k)])
            for k in range(4):
                tc_ = slice((sc * 4 + k) * 128, (sc * 4 + k + 1) * 128)
                mm_group(Psm[:, k * 8:(k + 1) * 8], lambda ch, tc_=tc_: g.hT[:, ch, tc_], lambda ch: wB[:, ch, 1280:1288], wball, ["Psm"])
            S.op("dve", lambda e, p=p: e.tensor_tensor(out=dts[p][:, 0, :, :], in0=Psm[:, 0:32].rearrange("p (k h) -> p k h", k=4),
                                                       in1=hp[:, 0, :].unsqueeze(1).to_broadcast([128, 4, 8]), op=ALU.add),
                 reads=["Psm", ("hp", 0)], writes=[("dts", p, 0)])
            S.op("act", lambda e, p=p: e.activation(out=dts[p][:, 0, :, :], in_=dts[p][:, 0, :, :], func=AF.Exp),
                 reads=[("dts", p, 0)], writes=[("dts", p, 0)])
            S.op("act", lambda e, p=p: e.activation(out=dts[p][:, 1, :, :], in_=dts[p][:, 0, :, :], func=AF.Ln, bias=1.0),
                 reads=[("dts", p, 0)], writes=[("dts", p, 1)])
            S.op("dve", lambda e, p=p: e.tensor_tensor(out=dts[p][:, 2, :, :], in0=dts[p][:, 1, :, :],
                                                       in1=hp[:, 1, :].unsqueeze(1).to_broadcast([128, 4, 8]), op=ALU.mult),
                 reads=[("dts", p, 1), ("hp", 1)], writes=[("dts", p, 2)])
            S.op("dve", lambda e, p=p: e.tensor_copy(out=dAh[p][:], in_=dts[p][:, 2, :, :]), reads=[("dts", p, 2)], writes=[("dAh", p)])
            S.op("dve", lambda e, p=p: e.tensor_tensor(out=dAr[:], in0=dts[p][:, 2, :, :], in1=dAh[p][:], op=ALU.subtract),
                 reads=[("dts", p, 2), ("dAh", p)], writes=["dAr"])
            S.op("dve", lambda e, p=p: e.tensor_copy(out=dAl[p][:], in_=dAr[:]), reads=["dAr"], writes=[("dAl", p)])
            for k in range(4):
                S.op("pe", lambda e, p=p, k=k: e.matmul(Psm[:, 64 + k * 8:72 + k * 8], lhsT=tri, rhs=dts[p][:, 2, k, :], start=True, stop=True),
                     reads=[("dts", p, 2)], writes=["Psm"])
                S.op("pe", lambda e, p=p, k=k: e.matmul(Psm[:, 96 + k * 8:104 + k * 8], lhsT=ones, rhs=dts[p][:, 2, k, :], start=True, stop=True),
                     reads=[("dts", p, 2)], writes=["Psm"])
            v48 = lambda ap: ap.rearrange("p (k h) -> p k h", k=4)
            S.op("dve", lambda e, p=p: e.tensor_copy(out=fs[p][:, 0, :, :], in_=v48(Psm[:, 64:96])), reads=["Psm"], writes=[("fs", p, 0)])
            S.op("act", lambda e, p=p: e.activation(out=fs[p][:, 1, :, :], in_=fs[p][:, 0, :, :], func=AF.Exp),
                 reads=[("fs", p, 0)], writes=[("fs", p, 1)])
            S.op("dve", lambda e, p=p: e.tensor_tensor(out=fs[p][:, 2, :, :], in0=v48(Psm[:, 96:128]), in1=fs[p][:, 0, :, :], op=ALU.subtract),
                 reads=["Psm", ("fs", p, 0)], writes=[("fs", p, 2)])
            S.op("act", lambda e, p=p: e.activation(out=fs[p][:, 3, :, :], in_=fs[p][:, 2, :, :], func=AF.Exp),
                 reads=[("fs", p, 2)], writes=[("fs", p, 3)])
            S.op("act", lambda e, p=p: e.activation(out=fs[p][:, 4, :, :], in_=v48(Psm[:, 96:128]), func=AF.Exp),
                 reads=["Psm"], writes=[("fs", p, 4)])
            S.op("dve", lambda e, p=p: e.tensor_tensor(out=fs[p][:, 5, :, :], in0=dts[p][:, 1, :, :], in1=fs[p][:, 3, :, :], op=ALU.mult),
                 reads=[("dts", p, 1), ("fs", p, 3)], writes=[("fs", p, 5)])

        def stageA(c):
            sc, k, q = c // 4, c % 4, c % NQ
            p = sc % 2
            lc = slice(k * 128, (k + 1) * 128)
            dt_ = dts[p][:, 1, k, :]
            dA_ = dts[p][:, 2, k, :]
            trib = g.mask2[:, 128:256]
            S.op("dve", lambda e: e.tensor_tensor(out=rhsH[:], in0=trib.unsqueeze(1).to_broadcast([128, 8, 128]),
                                                  in1=dAh[p][:, k, :].unsqueeze(2).to_broadcast([128, 8, 128]), op=ALU.mult),
                 reads=[("dAh", p)], writes=["rhsH"])
            S.op("dve", lambda e: e.tensor_tensor(out=rhsL[:], in0=trib.unsqueeze(1).to_broadcast([128, 8, 128]),
                                                  in1=dAl[p][:, k, :].unsqueeze(2).to_broadcast([128, 8, 128]), op=ALU.mult),
                 reads=[("dAl", p)], writes=["rhsL"])
            for hf in range(2):
                S.op("pe", lambda e, hf=hf: e.matmul(PSEGs[hf][:], lhsT=g.u1b[:], rhs=rhsH[:, hf * 4:(hf + 1) * 4, :], start=True, stop=False),
                     reads=["rhsH"], writes=[("PSEG", hf)])
                S.op("pe", lambda e, hf=hf: e.matmul(PSEGs[hf][:], lhsT=g.u1b[:], rhs=rhsL[:, hf * 4:(hf + 1) * 4, :], start=False, stop=True),
                     reads=["rhsL"], writes=[("PSEG", hf)])
            for hf in range(2):
                S.op("act", lambda e, hf=hf: e.activation(out=LT[:, hf * 4:(hf + 1) * 4, :], in_=PSEGs[hf][:].rearrange("p (h l) -> p h l", h=4),
                                                          func=AF.Exp),
                     reads=[("PSEG", hf)], writes=[("LT", hf)])
            PA0b = PA0[:].bitcast(BF16)
            def xtr(e):
                ins = None
                for j in range(4):
                    ins = e.transpose(PA0b[:, j * 128:(j + 1) * 128], xc[p][:, j, lc], g.identb[:])
                return ins
            S.op("pe", xtr, reads=[("xc", p, j) for j in range(4)], writes=["PA0"])
            S.op("act", lambda e: e.activation(out=xtm[q][:], in_=PA0b[:, 0:512], func=AF.Copy), reads=["PA0"], writes=[("xtm", q)])
            S.op("pe", lambda e: e.transpose(Pbf[:, 0, :], BT[p][:, lc], g.identb[:]), reads=[("BT", p)], writes=["Pbf"])
            S.op("dve", lambda e: e.tensor_copy(out=Btm[q][:], in_=Pbf[:, 0, :]), reads=["Pbf"], writes=[("Btm", q)])
            S.op("pe", lambda e: e.matmul(Psm[:, 128:256], lhsT=BT[p][:, lc], rhs=CT[p][:, lc], start=True, stop=True),
                 reads=[("BT", p), ("CT", p)], writes=["Psm"])
            S.op("dve", lambda e: e.tensor_tensor(out=GTm[:], in0=Psm[:, 128:256], in1=tri, op=ALU.mult), reads=["Psm"], writes=["GTm"])
            S.op("dve", lambda e: e.tensor_tensor(out=MT[q][:], in0=LT[:], in1=GTm[:].unsqueeze(1).to_broadcast([128, 8, 128]), op=ALU.mult),
                 reads=[("LT", 0), ("LT", 1), "GTm"], writes=[("MT", q)])
            S.op("pool", lambda e: e.tensor_tensor(out=v8(xdt[q][:]), in0=v8(xtm[q][:]), in1=b8(dt_), op=ALU.mult),
                 reads=[("xtm", q), ("dts", p, 1)], writes=[("xdt", q)])
            S.op("pool", lambda e: e.tensor_tensor(out=v8(xdte[q][:]), in0=v8(xtm[q][:]), in1=b8(fs[p][:, 5, k, :]), op=ALU.mult),
                 reads=[("xtm", q), ("fs", p, 5)], writes=[("xdte", q)])
            S.op("pool", lambda e: e.tensor_tensor(out=v8(tD[q][:]), in0=v8(xtm[q][:]), in1=b8(hp[:, 2, :]), op=ALU.mult),
                 reads=[("xtm", q), ("hp", 2)], writes=[("tD", q)])

        def stageB1(c):
            sc, k, q = c // 4, c % 4, c % NQ
            p = sc % 2
            lc = slice(k * 128, (k + 1) * 128)
            S.op("pe", lambda e: e.matmul(PBo[:], lhsT=CT[p][:, lc], rhs=Hb[:], start=True, stop=True),
                 reads=[("CT", p), "Hb"], writes=["PBo"])
            S.op("pe", lambda e: e.matmul(PS2[:], lhsT=Btm[q][:], rhs=xdte[q][:], start=True, stop=True),
                 reads=[("Btm", q), ("xdte", q)], writes=["PS2"])
            S.op("pe", lambda e: e.matmul(PBy[:], lhsT=g.identb[:], rhs=tD[q][:], start=True, stop=False),
                 reads=[("tD", q)], writes=["PBy"])
            for h in range(8):
                S.op("pe", lambda e, h=h: e.matmul(PBy[:, h * 64:(h + 1) * 64], lhsT=MT[q][:, h, :], rhs=xdt[q][:, h * 64:(h + 1) * 64],
                                                   start=False, stop=(h == 7)),
                     reads=[("MT", q), ("xdt", q)], writes=["PBy"])
            S.op("dve", lambda e: e.tensor_tensor(out=v8(H[:]), in0=v8(H[:]), in1=b8(fs[p][:, 4, k, :]), op=ALU.mult),
                 reads=["H", ("fs", p, 4)], writes=["H"])
            S.op("dve", lambda e: e.tensor_tensor(out=H[:], in0=H[:], in1=PS2[:], op=ALU.add), reads=["H", "PS2"], writes=["H"])
            S.op("act", lambda e: e.activation(out=Hb[:], in_=H[:], func=AF.Copy), reads=["H"], writes=["Hb"])
            S.op("dve", lambda e: e.tensor_tensor(out=v8(yt[:]), in0=v8(PBo[:]), in1=b8(fs[p][:, 1, k, :]), op=ALU.mult),
                 reads=["PBo", ("fs", p, 1)], writes=["yt"])
            S.op("dve", lambda e: e.tensor_tensor(out=yt[:], in0=yt[:], in1=PBy[:], op=ALU.add), reads=["yt", "PBy"], writes=["yt"])
            S.op("pool", lambda e: e.tensor_tensor(out=yt[:], in0=yt[:], in1=szs[p][:, k, :], op=ALU.mult),
                 reads=["yt", ("sz", p, k)], writes=["yt"])
            yq = c % 2
            S.op("act", lambda e: e.activation(out=yn[yq][:], in_=yt[:], func=AF.Square, accum_out=nst[:, 0:1]),
                 reads=["yt"], writes=[("yn", yq), ("nst", 0)])
            S.op("act", lambda e: e.activation(out=nst[:, 1:2], in_=nst[:, 0:1], func=AF.Ln, scale=1.0 / 512, bias=EPS),
                 reads=[("nst", 0)], writes=[("nst", 1)])
            S.op("act", lambda e: e.activation(out=nst[:, 2:3], in_=nst[:, 1:2], func=AF.Exp, scale=-0.5),
                 reads=[("nst", 1)], writes=[("nst", 2)])
            S.op("act", lambda e: e.activation(out=yn[yq][:], in_=yt[:], func=AF.Copy, scale=nst[:, 2:3]),
                 reads=["yt", ("nst", 2)], writes=[("yn", yq)])

        def stageB2(c):
            sc, k, q = c // 4, c % 4, c % 2
            p = sc % 2
            lc = slice(k * 128, (k + 1) * 128)
            for j in range(4):
                S.op("pe", lambda e, j=j: e.transpose(Pbf[:, 4 + j, :], yn[q][:, j * 128:(j + 1) * 128], g.identb[:]),
                     reads=[("yn", q)], writes=["Pbf"])
            S.op("dve", lambda e: e.tensor_tensor(out=Ybuf[p][:, :, lc], in0=Pbf[:, 4:8, :],
                                                  in1=prm[:, 30:34].unsqueeze(2).to_broadcast([128, 4, 128]), op=ALU.mult),
                 reads=["Pbf", "prm"], writes=[("Ybuf", p)])
            if k == 3:
                r0 = 512 + grp * 512
                ts_ = slice(sc * 512, (sc + 1) * 512)
                S.op("sp", lambda e: e.dma_start(out=g.ycat[r0:r0 + 512, ts_].rearrange("(cc p) t -> p cc t", p=128), in_=Ybuf[p][:]),
                     reads=[("Ybuf", p)], dma_key=("Ybuf", p))

        NCH = SEQ // 128
        front(0)
        stageA(0)
        stageA(1)
        for c in range(NCH):
            S.capture()
            if c + 3 < NCH and (c + 3) % 4 == 0:
                front((c + 3) // 4)
            lf = S.end_capture()
            S.capture()
            stageB1(c)
            lb1 = S.end_capture()
            S.capture()
            if c + 2 < NCH:
                stageA(c + 2)
            la = S.end_capture()
            S.capture()
            if c >= 1:
                stageB2(c - 1)
            lb2 = S.end_capture()
            S.replay_merged([lf])
            S.replay_merged([lb1, la, lb2])
        stageB2(NCH - 1)
        S.emit_block()


def phase_out(g, L, src):
    nc, S = g.nc, g.S
    with contextlib.ExitStack() as st:
        sbt = lambda name, shape, dt: st.enter_context(g.sbuf(name, shape, dt))
        wo = sbt("wo", [128, 16, DM], BF16)
        wst = [sbt("owst%d" % i, [128, DM], F32) for i in range(3)]
        yc = [sbt("oyc%d" % i, [128, 16, 512], BF16) for i in range(2)]
        xt = [sbt("oxt%d" % i, [128, DM], F32) for i in range(2)]
        t1 = [sbt("ot1%d" % i, [128, DM], F32) for i in range(2)]
        junk = sbt("ojunk", [128, DM], BF16)
        stat = [sbt("ost%d" % i, [128, 4], F32) for i in range(2)]
        PY = [st.enter_context(g.psum("oPY%d" % i, [128, 2, 512], F32)) for i in range(2)]
        for cc in range(16):
            b = cc % 3
            S.op("sp", lambda e, b=b, cc=cc: e.dma_start(out=wst[b][:], in_=g.w_out[L, cc * 128:(cc + 1) * 128, :]),
                 writes=[("wst", b)], dma_key=("owst", b))
            ceng = ("act", "dve", "pool")[cc % 3]
            if ceng == "act":
                S.op("act", lambda e, b=b, cc=cc: e.activation(out=wo[:, cc, :], in_=wst[b][:], func=AF.Copy),
                     reads=[("wst", b)], writes=[("wo", cc)])
            else:
                S.op(ceng, lambda e, b=b, cc=cc: e.tensor_copy(out=wo[:, cc, :], in_=wst[b][:]), reads=[("wst", b)], writes=[("wo", cc)])
        def load_yc(gi):
            gb = gi % 2
            ts_ = slice(gi * 512, (gi + 1) * 512)
            S.op("sp", lambda e: e.dma_start(out=yc[gb][:], in_=g.ycat[:, ts_].rearrange("(cc p) t -> p cc t", p=128)),
                 writes=[("yc", gb)], dma_key=("oyc", gb))

        load_yc(0)
        for gi in range(8):
            gb = gi % 2
            if gi + 1 < 8:
                load_yc(gi + 1)
            for k in range(4):
                t = gi * 4 + k
                b = t % 2
                rows = slice(t * 128, (t + 1) * 128)
                lc = slice(k * 128, (k + 1) * 128)
                S.op("sp", lambda e, b=b, rows=rows: e.dma_start(out=xt[b][:], in_=src[rows, :]), writes=[("x", b)], dma_key=("oxt", b))
                for nh in range(2):
                    for cc in range(16):
                        S.op("pe", lambda e, b=b, nh=nh, cc=cc, gb=gb, lc=lc: e.matmul(PY[b][:, nh, :], lhsT=yc[gb][:, cc, lc],
                                                                                       rhs=wo[:, cc, nh * 512:(nh + 1) * 512],
                                                                                       start=(cc == 0), stop=(cc == 15)),
                             reads=[("yc", gb), ("wo", cc)], writes=[("PY", b)])
                S.op("act", lambda e, b=b: e.activation(out=junk[:].rearrange("p (a n) -> p a n", a=2), in_=PY[b][:], func=AF.Square,
                                                        accum_out=stat[b][:, 0:1]),
                     reads=[("PY", b)], writes=["junk", ("s0", b)])
                S.op("act", lambda e, b=b: e.activation(out=stat[b][:, 1:2], in_=stat[b][:, 0:1], func=AF.Sqrt, scale=1.0 / DM, bias=EPS),
                     reads=[("s0", b)], writes=[("s1", b)])
                S.op("dve", lambda e, b=b: e.reciprocal(out=stat[b][:, 2:3], in_=stat[b][:, 1:2]), reads=[("s1", b)], writes=[("s2", b)])
                S.op("dve", lambda e, b=b: e.scalar_tensor_tensor(out=t1[b][:].rearrange("p (a n) -> p a n", a=2), in0=PY[b][:],
                                                                  scalar=stat[b][:, 2:3],
                                                                  in1=g.modb[:, 2 * DM:3 * DM].rearrange("p (a n) -> p a n", a=2),
                                                                  op0=ALU.mult, op1=ALU.mult),
                     reads=[("PY", b), ("s2", b)], writes=[("t1", b)])
                S.op("pool", lambda e, b=b: e.tensor_tensor(out=t1[b][:], in0=t1[b][:], in1=xt[b][:], op=ALU.add),
                     reads=[("t1", b), ("x", b)], writes=[("t1", b)])
                S.op("sp", lambda e, b=b, rows=rows: e.dma_start(out=g.out[rows, :], in_=t1[b][:]), reads=[("t1", b)], dma_key=("ot1", b))
        S.emit_block()


def _consts():
    i = np.arange(128)
    c = np.zeros((128, 7, 128), np.float32)
    c[:, 0, :] = np.eye(128)
    c[:, 1, :] = (i[:, None] >= i[None, :])
    c[:, 2, :] = (i[:, None] <= i[None, :])
    c[:, 3, :] = (i[:, None] > i[None, :])
    c[:, 4, :] = 1.0
    c[64, 5, 0:64] = 1.0
    return c


_PROG = {}
FUSED = True
_WNAMES = ("ada_w", "ada_b", "pre_norm_w", "post_norm_w", "w_in", "conv_w", "conv_b", "dt_bias", "a_log",
           "d_skip", "ssm_norm_w", "sinks", "w_out")


def kernel(x, c, ada_w, ada_b, pre_norm_w, post_norm_w, w_in, conv_w, conv_b,
           dt_bias, a_log, d_skip, ssm_norm_w, sinks, w_out):
    f = lambda a: np.ascontiguousarray(np.asarray(a, dtype=np.float32))
    ws = dict(ada_w=f(ada_w), ada_b=f(ada_b), pre_norm_w=f(pre_norm_w), post_norm_w=f(post_norm_w),
              w_in=f(w_in), conv_w=f(conv_w), conv_b=f(conv_b), dt_bias=f(dt_bias), a_log=f(a_log),
              d_skip=f(d_skip), ssm_norm_w=f(ssm_norm_w), sinks=f(sinks), w_out=f(w_out))
    x = f(x)
    c = f(c)
    consts = _consts()
    ccols = [np.ascontiguousarray(c[b].reshape(8, 128).T) for b in range(8)]
    if FUSED:
        if "fused" not in _PROG:
            _PROG["fused"] = build(n_layers=DEPTH, depth_dim=DEPTH)
        in_maps = []
        for b in range(8):
            m = dict(ws)
            m["consts"] = consts
            m["x"] = x[b]
            m["c_col"] = ccols[b]
            in_maps.append(m)
        res = run_bass_kernel_spmd(_PROG["fused"], in_maps, core_ids=list(range(8)))
        return np.stack([np.asarray(r["out"]) for r in res.results], axis=0).astype(np.float32)
    if "layer" not in _PROG:
        _PROG["layer"] = build(n_layers=1, depth_dim=1)
    cur = [x[b] for b in range(8)]
    for L in range(DEPTH):
        wl = {k: np.ascontiguousarray(ws[k][L:L + 1]) for k in _WNAMES}
        in_maps = []
        for b in range(8):
            m = dict(wl)
            m["consts"] = consts
            m["x"] = cur[b]
            m["c_col"] = ccols[b]
            in_maps.append(m)
        res = run_bass_kernel_spmd(_PROG["layer"], in_maps, core_ids=list(range(8)))
        cur = [np.ascontiguousarray(np.asarray(r["out"], dtype=np.float32)) for r in res.results]
    return np.stack(cur, axis=0).astype(np.float32)
```

```python
import contextlib
import numpy as np
import concourse.bass as bass
import concourse.mybir as mybir
from concourse.bass_utils import run_bass_kernel_spmd

F32 = mybir.dt.float32
BF16 = mybir.dt.bfloat16
AF = mybir.ActivationFunctionType
ALU = mybir.AluOpType

SEQ = 4096
DM = 1024
NT = SEQ // 128
EPS = 1e-6
DEPTH = 4
IN_COLS = 5904
C_OFF = 4624
ENGS = ("pe", "act", "dve", "pool", "sp")


def sl(start, n, step=1):
    return slice(start, start + step * (n - 1) + 1, step)


class Sched:
    def __init__(self, nc, stack):
        self.nc = nc
        self.stack = stack
        self.esem = {e: stack.enter_context(nc.semaphore("s_" + e)) for e in ENGS}
        self._names = {id(v): "s_" + k for k, v in self.esem.items()}
        self.ecount = {e: 0 for e in ENGS}
        self.dsem = {}
        self.dcount = {}
        self.waited = {e: {} for e in ENGS}
        self.n_ops = 0
        self.reset_block()

    def reset_block(self):
        self.ops = []
        self.last_w = {}
        self.readers = {}

    def _dma_sem(self, key):
        if key not in self.dsem:
            self.dsem[key] = self.stack.enter_context(self.nc.semaphore("d%d" % len(self.dsem)))
            self.dcount[key] = 0
            self._names[id(self.dsem[key])] = "d_" + str(key)
        return self.dsem[key]

    def capture(self):
        self._cap = []
        return self._cap

    def end_capture(self):
        lst, self._cap = self._cap, None
        return lst

    _DUR = {"pe": 0.2, "act": 0.6, "dve": 0.75, "pool": 1.1, "sp": 2.0}

    def replay_merged(self, lists):
        if not hasattr(self, "_sim_eng"):
            self._sim_eng = {e: 0.0 for e in ENGS}
            self._sim_key = {}
        its = [list(l) for l in lists if l]
        pos = [0] * len(its)
        while True:
            best, best_t = None, None
            for i in range(len(its)):
                if pos[i] >= len(its[i]):
                    continue
                eng, fn, reads, writes, dma_key = its[i][pos[i]]
                t = self._sim_eng[eng]
                for k in reads:
                    t = max(t, self._sim_key.get(k, 0.0))
                for k in writes:
                    t = max(t, self._sim_key.get(k, 0.0))
                if best is None or t < best_t - 1e-9:
                    best, best_t = i, t
            if best is None:
                break
            a = its[best][pos[best]]
            pos[best] += 1
            eng, fn, reads, writes, dma_key = a
            fin = best_t + self._DUR[eng]
            self._sim_eng[eng] = fin
            for k in writes:
                self._sim_key[k] = fin
            self.op(*a)

    def op(self, eng, fn, reads=(), writes=(), dma_key=None):
        if getattr(self, "_cap", None) is not None:
            self._cap.append((eng, fn, tuple(reads), tuple(writes), dma_key))
            return None
        idx = len(self.ops)
        deps = set()
        for k in reads:
            w = self.last_w.get(k)
            if w is not None:
                deps.add(w)
        for k in writes:
            w = self.last_w.get(k)
            if w is not None:
                deps.add(w)
            for r in self.readers.get(k, ()):
                deps.add(r)
        deps.discard(idx)
        self.ops.append(dict(eng=eng, fn=fn, deps=deps, dma_key=dma_key, milestone=False))
        for k in reads:
            self.readers.setdefault(k, []).append(idx)
        for k in writes:
            self.last_w[k] = idx
            self.readers[k] = []
        return idx

    def emit_block(self, name=None):
        nc = self.nc
        ops = self.ops
        for o in ops:
            keep = set()
            for d in o["deps"]:
                s = ops[d]
                if s["dma_key"] is None and o["dma_key"] is None and s["eng"] == o["eng"] == "pe":
                    continue
                keep.add(d)
            o["deps"] = keep
            for d in keep:
                if ops[d]["dma_key"] is None:
                    ops[d]["milestone"] = True
        ecount = dict(self.ecount)
        dcount = dict(self.dcount)
        for o in ops:
            if o["dma_key"] is not None:
                self._dma_sem(o["dma_key"])
                dcount[o["dma_key"]] = dcount.get(o["dma_key"], 0) + 16
                o["dval"] = dcount[o["dma_key"]]
            elif o["milestone"]:
                ecount[o["eng"]] += 1
                o["mval"] = ecount[o["eng"]]
        per_eng = {e: [o for o in ops if o["eng"] == e] for e in ENGS}
        final_d = dict(dcount)
        sched = self

        def emit_engine(ename, engine):
            waited = sched.waited[ename]
            for o in per_eng[ename]:
                need = {}
                for d in o["deps"]:
                    s = ops[d]
                    if s["dma_key"] is not None:
                        sem, val = sched.dsem[s["dma_key"]], s["dval"]
                    else:
                        sem, val = sched.esem[s["eng"]], s["mval"]
                    key = sched._names[id(sem)]
                    if val > need.get(key, (None, 0))[1]:
                        need[key] = (sem, val)
                for key, (sem, val) in need.items():
                    if waited.get(key, 0) >= val:
                        continue
                    engine.wait_ge(sem, val)
                    waited[key] = val
                ins = o["fn"](engine)
                if o["dma_key"] is not None:
                    ins.then_inc(sched.dsem[o["dma_key"]], 16)
                elif o["milestone"]:
                    ins.then_inc(sched.esem[ename], 1)
            if ename == "sp":
                for k, v in final_d.items():
                    sem = sched.dsem[k]
                    key = sched._names[id(sem)]
                    if waited.get(key, 0) < v:
                        engine.wait_ge(sem, v)
                        waited[key] = v

        with nc.Block(name) as block:
            @block.tensor
            def _(e):
                emit_engine("pe", e)

            @block.scalar
            def _(e):
                emit_engine("act", e)

            @block.vector
            def _(e):
                emit_engine("dve", e)

            @block.gpsimd
            def _(e):
                emit_engine("pool", e)

            @block.sync
            def _(e):
                emit_engine("sp", e)
        self.ecount = ecount
        self.dcount = dcount
        self.n_ops += len(ops)
        self.reset_block()


class K:
    pass


def dbg(g, name, ap, shape, dtype, reads):
    if not getattr(g, "debug", False):
        return
    import os
    taps = os.environ.get("DBG_TAPS", "")
    if not any(name == t or name.startswith(t + "_") for t in taps.split(",") if t):
        return
    d = g.nc.dram_tensor("dbg_" + name, list(shape), dtype, kind="ExternalOutput").ap()
    idx = tuple(slice(None) for _ in shape)
    g.S.op("sp", lambda e: e.dma_start(out=d[idx], in_=ap), reads=reads, dma_key=("dbg", name))


def build(n_layers=DEPTH, debug=False, phases=("mod", "p1", "att", "ssd", "out"), depth_dim=DEPTH):
    nc = bass.Bass("TRN2", target_bir_lowering=False)

    def din(name, shape):
        return nc.dram_tensor(name, shape, F32, kind="ExternalInput").ap()

    g = K()
    g.nc = nc
    g.uid = [0]
    g.debug = debug

    def _uniq(name):
        g.uid[0] += 1
        return "%s_%d" % (name, g.uid[0])
    g.sbuf = lambda name, shape, dt: nc.sbuf_tensor(_uniq(name), shape, dt)
    g.psum = lambda name, shape, dt: nc.psum_tensor(_uniq(name), shape, dt)
    g.x_in = din("x", [SEQ, DM])
    g.c_col = din("c_col", [128, 8])
    DD = depth_dim
    g.ada_w = din("ada_w", [DD, DM, 3 * DM])
    g.ada_b = din("ada_b", [DD, 3 * DM])
    g.pre_w = din("pre_norm_w", [DD, DM])
    g.post_w = din("post_norm_w", [DD, DM])
    g.w_in = din("w_in", [DD, DM, IN_COLS])
    g.conv_w = din("conv_w", [DD, 4, 1536])
    g.conv_b = din("conv_b", [DD, 1536])
    g.dt_bias = din("dt_bias", [DD, 16])
    g.a_log = din("a_log", [DD, 16])
    g.d_skip = din("d_skip", [DD, 16])
    g.ssm_w = din("ssm_norm_w", [DD, DM])
    g.sinks = din("sinks", [DD, 8])
    g.w_out = din("w_out", [DD, 2 * DM, DM])
    g.consts = din("consts", [128, 7, 128])
    g.out = nc.dram_tensor("out", [SEQ, DM], F32, kind="ExternalOutput").ap()
    g.ycat = nc.dram_tensor("ycat", [2 * DM, SEQ], BF16,
                            kind="ExternalOutput" if debug else "Internal").ap()

    with contextlib.ExitStack() as st:
        S = Sched(nc, st)
        g.S = S
        sb = lambda name, shape, dt: st.enter_context(nc.sbuf_tensor(name, shape, dt))
        g.cf = sb("cf", [128, 7, 128], F32)
        g.identb = sb("identb", [128, 128], BF16)
        g.mask2 = sb("mask2", [128, 256], BF16)
        g.u1b = sb("u1b", [128, 128], BF16)
        g.cbc = sb("cbc", [128, 8, 128], F32)
        g.modb = sb("modb", [128, 3 * DM], F32)
        g.hT = sb("hT", [128, 8, SEQ], BF16)

        phase_init(g)
        for L in range(n_layers):
            src = g.x_in if L == 0 else g.out
            if "mod" in phases:
                phase_mod(g, L)
            if "p1" in phases:
                phase_p1(g, L, src)
            if "att" in phases:
                specs = []
                for hp in range(4):
                    base = hp * 128
                    specs.append(dict(q=[(base, 128)], k=[(512 + base, 128)], v=[(1024 + base, 128)],
                                      z=[(1536 + base, 128)], pats=(1, 4, 16), sink=None,
                                      rows=(base, base + 64)))
                for i in range(4):
                    specs.append(dict(q=[(C_OFF + i * 64, 64), (C_OFF + (4 + i) * 64, 64)],
                                      k=[(C_OFF + 1024, 128)], v=[(C_OFF + 1152, 128)],
                                      z=[(C_OFF + 512 + i * 64, 64), (C_OFF + 512 + (4 + i) * 64, 64)],
                                      pats=(1,), sink=(i, 4 + i), reuse_kv=(i > 0),
                                      rows=(1536 + i * 64, 1536 + (4 + i) * 64)))
                phase_att(g, L, specs)
            if "ssd" in phases:
                for grp in range(2):
                    phase_ssd(g, L, grp)
            if "out" in phases:
                phase_out(g, L, src)
        g.n_ops = S.n_ops
    return nc


def phase_init(g):
    nc, S = g.nc, g.S
    with contextlib.ExitStack() as st:
        cc = st.enter_context(g.sbuf("cc", [128, 8], F32))
        ca = st.enter_context(g.sbuf("ca", [128, 8], F32))
        S.op("sp", lambda e: e.dma_start(out=g.cf[:], in_=g.consts[:, :, :]), writes=["cf"], dma_key="cf")
        S.op("sp", lambda e: e.dma_start(out=cc[:], in_=g.c_col[:, :]), writes=["cc"], dma_key="cc")
        S.op("pool", lambda e: e.tensor_copy(out=g.identb[:], in_=g.cf[:, 0, :]), reads=["cf"], writes=["identb"])
        S.op("pool", lambda e: e.tensor_copy(out=g.mask2[:].rearrange("p (a b) -> p a b", a=2), in_=g.cf[:, 1:3, :]),
             reads=["cf"], writes=["mask2"])
        S.op("pool", lambda e: e.tensor_copy(out=g.u1b[:], in_=g.cf[:, 3, :]), reads=["cf"], writes=["u1b"])
        S.op("act", lambda e: e.activation(out=ca[:], in_=cc[:], func=AF.Silu), reads=["cc"], writes=["ca"])
        S.op("dve", lambda e: e.tensor_copy(out=g.cbc[:], in_=ca[:].unsqueeze(2).to_broadcast([128, 8, 128])),
             reads=["ca"], writes=["cbc"])
        S.emit_block("init")


def phase_mod(g, L):
    nc, S = g.nc, g.S
    with contextlib.ExitStack() as st:
        stage = [st.enter_context(g.sbuf("mstage%d" % i, [128, 8, 512], F32)) for i in range(2)]
        adab = st.enter_context(g.sbuf("adab", [128, 3 * DM], F32))
        pw = st.enter_context(g.sbuf("pw", [128, 2, DM], F32))
        ps = [st.enter_context(g.psum("mps%d" % i, [128, 512], F32)) for i in range(2)]
        S.op("sp", lambda e: e.dma_start(out=adab[:], in_=g.ada_b[L:L + 1, :].partition_broadcast(128)),
             writes=["adab"], dma_key="adab")
        S.op("sp", lambda e: e.dma_start(out=pw[:, 0, :], in_=g.pre_w[L:L + 1, :].partition_broadcast(128)),
             writes=["pw0"], dma_key="pw0")
        S.op("sp", lambda e: e.dma_start(out=pw[:, 1, :], in_=g.post_w[L:L + 1, :].partition_broadcast(128)),
             writes=["pw1"], dma_key="pw1")
        aw = g.ada_w[L].rearrange("(ch p) n -> p ch n", p=128)
        for grp in range(6):
            b = grp % 2
            cs = slice(grp * 512, (grp + 1) * 512)
            S.op("sp", lambda e, b=b, cs=cs: e.dma_start(out=stage[b][:], in_=aw[:, :, cs]),
                 writes=[("mst", b)], dma_key=("mst", b))
            for ch in range(8):
                S.op("pe", lambda e, b=b, ch=ch: e.matmul(ps[b][:], lhsT=g.cbc[:, ch, :], rhs=stage[b][:, ch, :],
                                                          start=(ch == 0), stop=(ch == 7)),
                     reads=[("mst", b), "cbc"], writes=[("mps", b)])
            S.op("dve", lambda e, b=b, cs=cs: e.tensor_tensor(out=g.modb[:, cs], in0=ps[b][:], in1=adab[:, cs], op=ALU.add),
                 reads=[("mps", b), "adab"], writes=["modb"])
        S.op("dve", lambda e: e.scalar_tensor_tensor(out=g.modb[:, DM:2 * DM], in0=g.modb[:, DM:2 * DM], scalar=1.0,
                                                     in1=pw[:, 0, :], op0=ALU.add, op1=ALU.mult),
             reads=["modb", "pw0"], writes=["modb"])
        S.op("dve", lambda e: e.tensor_tensor(out=g.modb[:, 2 * DM:3 * DM], in0=g.modb[:, 2 * DM:3 * DM], in1=pw[:, 1, :],
                                              op=ALU.mult),
             reads=["modb", "pw1"], writes=["modb"])
        S.emit_block("mod%d" % L)


def phase_p1(g, L, src):
    nc, S = g.nc, g.S
    NB = 4
    with contextlib.ExitStack() as st:
        xt = [st.enter_context(g.sbuf("p1x%d" % i, [128, DM], F32)) for i in range(NB)]
        tmp = [st.enter_context(g.sbuf("p1t%d" % i, [128, DM], F32)) for i in range(NB)]
        hb = [st.enter_context(g.sbuf("p1h%d" % i, [128, DM], BF16)) for i in range(NB)]
        junk = st.enter_context(g.sbuf("p1junk", [128, DM], BF16))
        stat = [st.enter_context(g.sbuf("p1s%d" % i, [128, 4], F32)) for i in range(NB)]
        pst = [st.enter_context(g.psum("p1ps%d" % i, [128, 8, 128], BF16)) for i in range(2)]

        def s1(t):
            b = t % NB
            rows = slice(t * 128, (t + 1) * 128)
            S.op("sp", lambda e: e.dma_start(out=xt[b][:], in_=src[rows, :]), writes=[("x", b)], dma_key=("p1x", b))
            S.op("act", lambda e: e.activation(out=junk[:], in_=xt[b][:], func=AF.Square, accum_out=stat[b][:, 0:1]),
                 reads=[("x", b)], writes=["junk", ("s0", b)])
            S.op("act", lambda e: e.activation(out=stat[b][:, 1:2], in_=stat[b][:, 0:1], func=AF.Sqrt, scale=1.0 / DM, bias=EPS),
                 reads=[("s0", b)], writes=[("s1", b)])
            S.op("dve", lambda e: e.reciprocal(out=stat[b][:, 2:3], in_=stat[b][:, 1:2]), reads=[("s1", b)], writes=[("s2", b)])
            S.op("dve", lambda e: e.scalar_tensor_tensor(out=tmp[b][:], in0=xt[b][:], scalar=stat[b][:, 2:3],
                                                         in1=g.modb[:, DM:2 * DM], op0=ALU.mult, op1=ALU.mult),
                 reads=[("x", b), ("s2", b)], writes=[("t", b)])
            S.op("pool" if t % 2 == 0 else "dve",
                 lambda e: e.tensor_tensor(out=hb[b][:], in0=tmp[b][:], in1=g.modb[:, 0:DM], op=ALU.add),
                 reads=[("t", b)], writes=[("h", b)])

        def s2(t):
            b = t % NB
            pb = t % 2
            rows = slice(t * 128, (t + 1) * 128)
            for ch in range(8):
                S.op("pe", lambda e, ch=ch: e.transpose(pst[pb][:, ch, :], hb[b][:, ch * 128:(ch + 1) * 128], g.identb[:]),
                     reads=[("h", b)], writes=[("ps", pb)])
            S.op("act", lambda e: e.activation(out=g.hT[:, :, rows], in_=pst[pb][:], func=AF.Copy),
                 reads=[("ps", pb)], writes=[("hT", t)])

        s1(0)
        s1(1)
        for t in range(NT):
            if t + 2 < NT:
                s1(t + 2)
            s2(t)
        S.emit_block("p1_%d" % L)


def _groups(d, b):
    if d == 1:
        return [b // 4]
    if d == 4:
        return [b]
    return [4 * b + i for i in range(4)]


def phase_att(g, L, specs):
    nc, S = g.nc, g.S
    with contextlib.ExitStack() as st:
        sbt = lambda name, shape, dt: st.enter_context(g.sbuf(name, shape, dt))
        wst = [sbt("awst%d" % i, [128, 8, 128], F32) for i in range(2)]
        wts = [{n: sbt("aw_%s%d" % (n, i), [128, 8, 128], BF16) for n in "qkvz"} for i in range(2)]
        QT = sbt("QT", [128, SEQ], BF16)
        KT = sbt("KT", [128, SEQ], BF16)
        VT = sbt("VTf", [128, SEQ], BF16)
        Vt = {d: sbt("Vt%d" % d, [128, 32, 2, 65], BF16) for d in (1, 4, 16)}
        NSB = 3
        Et = [sbt("E%d" % i, [128, 2, 256], BF16) for i in range(NSB)]
        Pt = [sbt("P%d" % i, [128, 2, 256], BF16) for i in range(NSB)]
        Acc = sbt("Acc", [65, 2, SEQ], F32)
        Ut = [sbt("U%d" % i, [64, 512], F32) for i in range(2)]
        Tt = [sbt("T%d" % i, [64, 512], F32) for i in range(2)]
        Yb = [sbt("Yb%d" % i, [64, 512], BF16) for i in range(2)]
        sks = [sbt("sk%d" % i, [64, 2], F32) for i in range(2)]
        Sp = [st.enter_context(g.psum("aS%d" % i, [128, 2, 512], F32)) for i in range(NSB)]
        Op = [st.enter_context(g.psum("aO%d" % i, [128, 512], F32)) for i in range(2)]
        banks = [(i, j) for i in range(NSB) for j in range(2)]
        bk = lambda ij: Sp[ij[0]][:, ij[1], :]
        bkey = lambda ij: ("Sb", ij[0], ij[1])
        wv_in = g.w_in[L].rearrange("(ch p) n -> p ch n", p=128)
        for d in (1, 4, 16):
            S.op("pool", lambda e, d=d: e.memset(Vt[d][:, :, :, 64:65], 1.0), writes=[("Vone", d)])
        pj = 0
        step = 0
        fj = 0
        def load_weights(si):
            sp = specs[si]
            wt = wts[si % 2]
            sk = sks[si % 2]
            for wi, n in enumerate("qkvz"):
                b = wi % 2
                off = 0
                for pi, (c0, cn) in enumerate(sp[n]):
                    S.op("sp", lambda e, b=b, c0=c0, cn=cn, off=off: e.dma_start(out=wst[b][:, :, off:off + cn],
                                                                               in_=wv_in[:, :, c0:c0 + cn]),
                         writes=[("wst", b, pi)], dma_key=("awst", b, pi))
                    off += cn
                ceng = ("pool", "dve", "pool", "act")[wi]
                if ceng == "act":
                    S.op("act", lambda e, b=b, n=n, wt=wt: e.activation(out=wt[n][:], in_=wst[b][:], func=AF.Copy),
                         reads=[("wst", b, 0), ("wst", b, 1)], writes=[("w", n, si % 2)])
                else:
                    S.op(ceng, lambda e, b=b, n=n, wt=wt: e.tensor_copy(out=wt[n][:], in_=wst[b][:]),
                         reads=[("wst", b, 0), ("wst", b, 1)], writes=[("w", n, si % 2)])
            if sp["sink"] is not None:
                for h in range(2):
                    hh = sp["sink"][h]
                    S.op("sp", lambda e, h=h, hh=hh, sk=sk: e.dma_start(out=sk[:, h:h + 1],
                                                                        in_=g.sinks[L:L + 1, hh:hh + 1].partition_broadcast(64)),
                         writes=[("skr", si % 2, h)], dma_key=("sk", si % 2, h))
                S.op("act", lambda e, sk=sk: e.activation(out=sk[:], in_=sk[:], func=AF.Exp),
                     reads=[("skr", si % 2, 0), ("skr", si % 2, 1)], writes=[("sk", si % 2)])

        load_weights(0)
        for si, sp in enumerate(specs):
            pats = sp["pats"]
            wt = wts[si % 2]
            sk = sks[si % 2]
            wk = lambda n, si=si: ("w", n, si % 2)
            for n in range(8):
                ts_ = slice(n * 512, (n + 1) * 512)
                for nm, eng in (("q", "act"), ("k", "dve"), ("v", "act")):
                    if nm != "q" and sp.get("reuse_kv"):
                        continue
                    ij = banks[pj % len(banks)]
                    pj += 1
                    for ch in range(8):
                        S.op("pe", lambda e, ij=ij, ch=ch, nm=nm, ts_=ts_, wt=wt: e.matmul(bk(ij), lhsT=wt[nm][:, ch, :], rhs=g.hT[:, ch, ts_],
                                                                                         start=(ch == 0), stop=(ch == 7)),
                             reads=[wk(nm)], writes=[bkey(ij)])
                    if nm == "q":
                        S.op("act", lambda e, ij=ij, ts_=ts_: e.activation(out=QT[:, ts_], in_=bk(ij), func=AF.Copy, scale=0.125),
                             reads=[bkey(ij)], writes=[("QT", n)])
                    elif nm == "k":
                        S.op("dve", lambda e, ij=ij, ts_=ts_: e.tensor_copy(out=KT[:, ts_], in_=bk(ij)),
                             reads=[bkey(ij)], writes=[("KT", n)])
                    else:
                        S.op("act", lambda e, ij=ij, ts_=ts_: e.activation(out=VT[:, ts_], in_=bk(ij), func=AF.Copy),
                             reads=[bkey(ij)], writes=[("VT", n)])
            for d in (() if sp.get("reuse_kv") else pats):
                nb = 32 // d
                for b4 in range(8):
                    ij = banks[pj % len(banks)]
                    pj += 1
                    pb16 = lambda ij: bk(ij).bitcast(BF16)
                    for j in range(4):
                        blk = b4 * 4 + j
                        r, bb = blk // nb, blk % nb
                        tok = sl(r + d * 128 * bb, 128, d)
                        S.op("pe", lambda e, ij=ij, j=j, tok=tok: e.transpose(pb16(ij)[:, j * 128:(j + 1) * 128], VT[:, tok], g.identb[:]),
                             reads=[("VT", x) for x in _groups(d, bb)], writes=[bkey(ij)])
                    vsrc = lambda ij: pb16(ij)[:, 0:512].rearrange("p (j h c) -> p j h c", j=4, h=2)
                    if b4 % 2 == 0:
                        S.op("dve", lambda e, ij=ij, b4=b4, d=d: e.tensor_copy(out=Vt[d][:, b4 * 4:(b4 + 1) * 4, :, 0:64], in_=vsrc(ij)),
                             reads=[bkey(ij)], writes=[("Vt", d, b4)])
                    else:
                        S.op("act", lambda e, ij=ij, b4=b4, d=d: e.activation(out=Vt[d][:, b4 * 4:(b4 + 1) * 4, :, 0:64], in_=vsrc(ij),
                                                                             func=AF.Copy),
                             reads=[bkey(ij)], writes=[("Vt", d, b4)])
            if si + 1 < len(specs):
                load_weights(si + 1)
            steps = []
            for pi, d in enumerate(pats):
                nb = 32 // d
                for r in range(d):
                    for bb in range(nb):
                        steps.append(dict(pi=pi, d=d, r=r, bb=bb, nb=nb, s_=step % NSB, o_=step % 2,
                                          meng="dve" if step % 2 == 0 else "pool"))
                        step += 1

            def part1(stp):
                d, r, bb, s_, meng = stp["d"], stp["r"], stp["bb"], stp["s_"], stp["meng"]
                tq = sl(r + d * 128 * bb, 128, d)
                tp = sl(r + d * 128 * (bb - 1), 128, d) if bb > 0 else None
                lo = 0 if bb > 0 else 128
                gq = _groups(d, bb)
                gk = gq + (_groups(d, bb - 1) if bb > 0 else [])
                rd = [("QT", x) for x in gq] + [("KT", x) for x in set(gk)]
                skeys = [("Sb", s_, 0), ("Sb", s_, 1)]
                for h in range(2):
                    hs = slice(64 * h, 64 * h + 64)
                    S.op("pe", lambda e, s_=s_, h=h, hs=hs, tq=tq: e.matmul(Sp[s_][:, h, 128:256], lhsT=KT[hs, tq], rhs=QT[hs, tq],
                                                                           start=True, stop=True),
                         reads=rd, writes=skeys)
                    if bb > 0:
                        S.op("pe", lambda e, s_=s_, h=h, hs=hs, tq=tq, tp=tp: e.matmul(Sp[s_][:, h, 0:128], lhsT=KT[hs, tp],
                                                                                      rhs=QT[hs, tq], start=True, stop=True),
                             reads=rd, writes=skeys)
                S.op("act", lambda e, s_=s_, lo=lo: e.activation(out=Et[s_][:, :, lo:256], in_=Sp[s_][:, :, lo:256], func=AF.Exp),
                     reads=skeys, writes=[("E", s_)])
                S.op(meng, lambda e, s_=s_, lo=lo: e.tensor_tensor(out=Pt[s_][:, :, lo:256], in0=Et[s_][:, :, lo:256],
                                                                  in1=g.mask2[:, lo:256].unsqueeze(1).to_broadcast([128, 2, 256 - lo]),
                                                                  op=ALU.mult),
                     reads=[("E", s_)], writes=[("P", s_)])

            def part2(stp):
                pi, d, r, bb, nb, s_, o_ = stp["pi"], stp["d"], stp["r"], stp["bb"], stp["nb"], stp["s_"], stp["o_"]
                blk = r * nb + bb
                tq = sl(r + d * 128 * bb, 128, d)
                gq = _groups(d, bb)
                vrd = [("Vt", d, blk // 4), ("Vone", d)] + ([("Vt", d, (blk - 1) // 4)] if bb > 0 else [])
                for h in range(2):
                    if bb > 0:
                        S.op("pe", lambda e, s_=s_, o_=o_, h=h, blk=blk, d=d: e.matmul(Op[o_][0:65, h * 128:(h + 1) * 128],
                                                                                      lhsT=Vt[d][:, blk - 1, h, :], rhs=Pt[s_][:, h, 0:128],
                                                                                      start=True, stop=False),
                             reads=[("P", s_)] + vrd, writes=[("O", o_)])
                    S.op("pe", lambda e, s_=s_, o_=o_, h=h, blk=blk, d=d, bb=bb: e.matmul(Op[o_][0:65, h * 128:(h + 1) * 128],
                                                                                         lhsT=Vt[d][:, blk, h, :], rhs=Pt[s_][:, h, 128:256],
                                                                                         start=(bb == 0), stop=True),
                         reads=[("P", s_)] + vrd, writes=[("O", o_)])
                akeys = [("acc", x, r % 4) for x in gq] if d > 1 else [("acc", gq[0], x) for x in range(4)]
                o_view = Op[o_][0:65, 0:256].rearrange("p (h q) -> p h q", h=2)
                if pi == 0:
                    S.op("dve", lambda e, tq=tq, o_view=o_view: e.tensor_copy(out=Acc[:, :, tq], in_=o_view),
                         reads=[("O", o_)], writes=akeys)
                else:
                    S.op("dve", lambda e, tq=tq, o_view=o_view: e.tensor_tensor(out=Acc[:, :, tq], in0=Acc[:, :, tq], in1=o_view, op=ALU.add),
                         reads=[("O", o_)] + akeys, writes=akeys)

            LA = NSB - 1
            for i in range(min(LA, len(steps))):
                part1(steps[i])
            for i in range(len(steps)):
                if i + LA < len(steps):
                    part1(steps[i + LA])
                part2(steps[i])
            for h in range(2):
                for n in range(8):
                    ts_ = slice(n * 512, (n + 1) * 512)
                    b = fj % 2
                    fj += 1
                    ijl = banks[pj % len(banks)]
                    pj += 1
                    ijz = banks[pj % len(banks)]
                    pj += 1
                    acc_rd = [("acc", n, x) for x in range(4)]
                    S.op("pe", lambda e, ijl=ijl, h=h, ts_=ts_: e.matmul(bk(ijl)[0:64, :], lhsT=g.cf[0:65, 5, 0:64], rhs=Acc[0:65, h, ts_],
                                                                         start=True, stop=True),
                         reads=acc_rd, writes=[bkey(ijl)])
                    for ch in range(8):
                        S.op("pe", lambda e, ijz=ijz, ch=ch, h=h, ts_=ts_, wt=wt: e.matmul(bk(ijz)[0:64, :], lhsT=wt["z"][:, ch, 64 * h:64 * h + 64],
                                                                                          rhs=g.hT[:, ch, ts_], start=(ch == 0), stop=(ch == 7)),
                             reads=[wk("z")], writes=[bkey(ijz)])
                    if sp["sink"] is not None:
                        S.op("act", lambda e, b=b, ijl=ijl, h=h, sk=sk: e.activation(out=Ut[b][:], in_=bk(ijl)[0:64, :], func=AF.Ln,
                                                                                     bias=sk[:, h:h + 1]),
                             reads=[bkey(ijl), ("sk", si % 2)], writes=[("U", b)])
                    else:
                        S.op("act", lambda e, b=b, ijl=ijl: e.activation(out=Ut[b][:], in_=bk(ijl)[0:64, :], func=AF.Ln),
                             reads=[bkey(ijl)], writes=[("U", b)])
                    S.op("act", lambda e, b=b, ijz=ijz: e.activation(out=Tt[b][:], in_=bk(ijz)[0:64, :], func=AF.Exp, scale=-1.0),
                         reads=[bkey(ijz)], writes=[("T", b)])
                    S.op("act", lambda e, b=b: e.activation(out=Tt[b][:], in_=Tt[b][:], func=AF.Ln, bias=1.0),
                         reads=[("T", b)], writes=[("T", b)])
                    S.op("pool", lambda e, b=b: e.tensor_tensor(out=Ut[b][:], in0=Ut[b][:], in1=Tt[b][:], op=ALU.add),
                         reads=[("U", b), ("T", b)], writes=[("U", b)])
                    S.op("act", lambda e, b=b: e.activation(out=Ut[b][:], in_=Ut[b][:], func=AF.Exp, scale=-1.0),
                         reads=[("U", b)], writes=[("U", b)])
                    S.op("dve", lambda e, b=b, h=h, ts_=ts_: e.tensor_tensor(out=Ut[b][:], in0=Acc[0:64, h, ts_], in1=Ut[b][:], op=ALU.mult),
                         reads=[("U", b)] + acc_rd, writes=[("U", b)])
                    S.op("dve", lambda e, b=b, ijz=ijz: e.tensor_tensor(out=Yb[b][:], in0=Ut[b][:], in1=bk(ijz)[0:64, :], op=ALU.mult),
                         reads=[("U", b), bkey(ijz)], writes=[("Y", b)])
                    row0 = sp["rows"][h]
                    S.op("sp", lambda e, b=b, row0=row0, ts_=ts_: e.dma_start(out=g.ycat[row0:row0 + 64, ts_], in_=Yb[b][:]),
                         reads=[("Y", b)], dma_key=("aY", b))
        S.emit_block()


def phase_ssd(g, L, grp):
    nc, S = g.nc, g.S
    NW = 1296
    zc0, xc0, bc0, cc0, dc0 = 2048 + grp * 512, 3072 + grp * 512, 4096 + grp * 128, 4352 + grp * 128, 4608 + grp * 8
    with contextlib.ExitStack() as st:
        sbt = lambda name, shape, dt: st.enter_context(g.sbuf(name, shape, dt))
        pst = lambda name, shape, dt: st.enter_context(g.psum(name, shape, dt))
        wB = sbt("wB", [128, 8, NW], BF16)
        wst = [sbt("bwst%d" % i, [128, 8, 128], F32) for i in range(2)]
        prm_r = sbt("prm_r", [36, 128], F32)
        prm = sbt("prm", [128, 36], F32)
        hp = sbt("hp", [128, 3, 8], F32)
        pre = [sbt("pre%d" % i, [128, 515], F32) for i in range(2)]
        halo = sbt("halo", [128, 6, 3], F32)
        cacc = [sbt("cacc%d" % i, [128, 512], F32) for i in range(2)]
        ctmp = cacc[0]
        xc = [sbt("xc%d" % i, [128, 4, 512], BF16) for i in range(2)]
        fs = [sbt("fs%d" % i, [128, 6, 4, 8], F32) for i in range(2)]
        BT = [sbt("BTt%d" % i, [128, 512], BF16) for i in range(2)]
        CT = [sbt("CTt%d" % i, [128, 512], BF16) for i in range(2)]
        szs = [sbt("szs%d" % i, [128, 4, 512], F32) for i in range(2)]
        dts = [sbt("dts%d" % i, [128, 3, 4, 8], F32) for i in range(2)]
        NQ = 3
        xtm = [sbt("xtm%d" % i, [128, 512], BF16) for i in range(NQ)]
        Btm = [sbt("Btm%d" % i, [128, 128], BF16) for i in range(NQ)]
        xdt = [sbt("xdt%d" % i, [128, 512], BF16) for i in range(NQ)]
        xdte = [sbt("xdte%d" % i, [128, 512], BF16) for i in range(NQ)]
        tD = [sbt("tD%d" % i, [128, 512], BF16) for i in range(NQ)]
        MT = [sbt("MT%d" % i, [128, 8, 128], BF16) for i in range(NQ)]
        dAh = [sbt("dAh%d" % i, [128, 4, 8], BF16) for i in range(2)]
        dAl = [sbt("dAl%d" % i, [128, 4, 8], BF16) for i in range(2)]
        dAr = sbt("dAr", [128, 4, 8], F32)
        rhsH = sbt("rhsH", [128, 8, 128], BF16)
        rhsL = sbt("rhsL", [128, 8, 128], BF16)
        LT = sbt("LT", [128, 8, 128], BF16)
        GTm = sbt("GTm", [128, 128], F32)
        yt = sbt("yt", [128, 512], F32)
        nst = sbt("nst", [128, 4], F32)
        yn = [sbt("yn%d" % i, [128, 512], BF16) for i in range(2)]
        Ybuf = [sbt("Ybuf%d" % i, [128, 4, 512], BF16) for i in range(2)]
        H = sbt("H", [128, 512], F32)
        Hb = sbt("Hb", [128, 512], BF16)
        PA0 = pst("PA0", [128, 512], F32)
        PSEGs = [pst("PSEG%d" % i, [128, 512], F32) for i in range(2)]
        Psm = pst("Psm", [128, 512], F32)
        PBy = pst("PBy", [128, 512], F32)
        PBo = pst("PBo", [128, 512], F32)
        PS2 = pst("PS2", [128, 512], F32)
        Pbf = pst("Pbf", [128, 8, 128], BF16)
        wv_in = g.w_in[L].rearrange("(ch p) n -> p ch n", p=128)

        pieces = [(zc0 + i * 128, 128, i * 128) for i in range(4)] + [(xc0 + i * 128, 128, 512 + i * 128) for i in range(4)]
        pieces += [(bc0, 128, 1024), (cc0, 128, 1152), (dc0, 8, 1280)]
        wball = [("wB", i) for i in range(len(pieces))]
        for i, (c0, cn, o0) in enumerate(pieces):
            b = i % 2
            S.op("sp", lambda e, b=b, c0=c0, cn=cn: e.dma_start(out=wst[b][:, :, 0:cn], in_=wv_in[:, :, c0:c0 + cn]),
                 writes=[("wst", b)], dma_key=("bwst", b))
            ceng = ("act", "dve", "pool")[i % 3]
            if ceng == "act":
                S.op("act", lambda e, b=b, cn=cn, o0=o0: e.activation(out=wB[:, :, o0:o0 + cn], in_=wst[b][:, :, 0:cn], func=AF.Copy),
                     reads=[("wst", b)], writes=[("wB", i)])
            else:
                S.op(ceng, lambda e, b=b, cn=cn, o0=o0: e.tensor_copy(out=wB[:, :, o0:o0 + cn], in_=wst[b][:, :, 0:cn]),
                     reads=[("wst", b)], writes=[("wB", i)])
        S.op("pool", lambda e: e.memset(prm_r[:], 0.0), writes=["prm_r0"])
        cw = g.conv_w[L].rearrange("k (cc p) -> k cc p", p=128)
        cbv = g.conv_b[L:L + 1, :].rearrange("o (cc p) -> (o cc) p", p=128)
        swv = g.ssm_w[L:L + 1, :].rearrange("o (cc p) -> (o cc) p", p=128)
        k_ = 0
        for tap in range(4):
            S.op("sp", lambda e, tap=tap: e.dma_start(out=prm_r[tap * 6:tap * 6 + 4, :], in_=cw[tap, grp * 4:grp * 4 + 4, :]),
                 reads=["prm_r0"], writes=[("prm_r", k_)], dma_key=("prm", k_))
            k_ += 1
            for j, c_ in ((4, 8 + grp), (5, 10 + grp)):
                S.op("sp", lambda e, tap=tap, j=j, c_=c_: e.dma_start(out=prm_r[tap * 6 + j:tap * 6 + j + 1, :], in_=cw[tap, c_:c_ + 1, :]),
                     reads=["prm_r0"], writes=[("prm_r", k_)], dma_key=("prm", k_))
                k_ += 1
        S.op("sp", lambda e: e.dma_start(out=prm_r[24:28, :], in_=cbv[grp * 4:grp * 4 + 4, :]),
             reads=["prm_r0"], writes=[("prm_r", k_)], dma_key=("prm", k_))
        k_ += 1
        for j, c_ in ((28, 8 + grp), (29, 10 + grp)):
            S.op("sp", lambda e, j=j, c_=c_: e.dma_start(out=prm_r[j:j + 1, :], in_=cbv[c_:c_ + 1, :]),
                 reads=["prm_r0"], writes=[("prm_r", k_)], dma_key=("prm", k_))
            k_ += 1
        S.op("sp", lambda e: e.dma_start(out=prm_r[30:34, :], in_=swv[grp * 4:grp * 4 + 4, :]),
             reads=["prm_r0"], writes=[("prm_r", k_)], dma_key=("prm", k_))
        k_ += 1
        S.op("pe", lambda e: e.transpose(PA0[:, 0:36], prm_r[:, :], g.cf[0:36, 0, 0:36]),
             reads=[("prm_r", i) for i in range(k_)], writes=["PA0"])
        S.op("dve", lambda e: e.tensor_copy(out=prm[:], in_=PA0[:, 0:36]), reads=["PA0"], writes=["prm"])
        for i, src_ in enumerate((g.dt_bias, g.a_log, g.d_skip)):
            S.op("sp", lambda e, i=i, src_=src_: e.dma_start(out=hp[:, i, :], in_=src_[L:L + 1, grp * 8:grp * 8 + 8].partition_broadcast(128)),
                 writes=[("hp", i)], dma_key=("hp", i))
        S.op("act", lambda e: e.activation(out=hp[:, 1, :], in_=hp[:, 1, :], func=AF.Exp), reads=[("hp", 1)], writes=[("hp", 1)])
        S.op("dve", lambda e: e.tensor_scalar(out=hp[:, 1, :], in0=hp[:, 1, :], scalar1=-1.0, scalar2=None, op0=ALU.mult),
             reads=[("hp", 1)], writes=[("hp", 1)])
        S.op("pool", lambda e: e.memset(halo[:], 0.0), writes=["halo"])
        S.op("pool", lambda e: e.memset(H[:], 0.0), writes=["H"])
        S.op("pool", lambda e: e.memset(Hb[:], 0.0), writes=["Hb"])

        tri = g.cf[:, 2, :]
        u1 = g.cf[:, 3, :]
        ones = g.cf[:, 4, :]
        identf = g.cf[:, 0, :]
        v8 = lambda ap: ap.rearrange("p (h d) -> p h d", h=8)
        b8 = lambda ap: ap.unsqueeze(2).to_broadcast([128, 8, 64])

        def mm_group(out_ap, lhs_fn, rhs_fn, reads, writes):
            def fn(e):
                ins = None
                for ch in range(8):
                    ins = e.matmul(out_ap, lhsT=lhs_fn(ch), rhs=rhs_fn(ch), start=(ch == 0), stop=(ch == 7))
                return ins
            S.op("pe", fn, reads=reads, writes=writes)

        fbanks = [(PA0, "PA0"), (PBy, "PBy"), (PBo, "PBo"), (PS2, "PS2")]

        def front(sc):
            p = sc % 2
            ts_ = slice(sc * 512, (sc + 1) * 512)
            for j in range(6):
                pb = j % 2
                wofs = 512 + j * 128 if j < 4 else (1024 if j == 4 else 1152)
                fb, fk = fbanks[j % 4]
                mm_group(fb[:], lambda ch, wofs=wofs: wB[:, ch, wofs:wofs + 128], lambda ch: g.hT[:, ch, ts_], wball, [fk])
                S.op("pool", lambda e, pb=pb, j=j: e.tensor_copy(out=pre[pb][:, 0:3], in_=halo[:, j, :]),
                     reads=["halo"], writes=[("pre", pb)])
                S.op("act", lambda e, pb=pb, fb=fb: e.activation(out=pre[pb][:, 3:515], in_=fb[:], func=AF.Copy),
                     reads=[fk], writes=[("pre", pb)])
                S.op("pool", lambda e, pb=pb, j=j: e.tensor_copy(out=halo[:, j, :], in_=pre[pb][:, 512:515]),
                     reads=[("pre", pb)], writes=["halo"])
                if False:
                    S.op("pool", lambda e, pb=pb, j=j: e.tensor_scalar(out=cacc[pb][:], in0=pre[pb][:, 0:512], scalar1=prm[:, j:j + 1], scalar2=None,
                                                                       op0=ALU.mult),
                         reads=[("pre", pb), "prm"], writes=[("cacc", pb)])
                    for tap in range(1, 4):
                        S.op("pool", lambda e, pb=pb, j=j, tap=tap: e.tensor_scalar(out=ctmp[:], in0=pre[pb][:, tap:tap + 512],
                                                                                    scalar1=prm[:, tap * 6 + j:tap * 6 + j + 1], scalar2=None,
                                                                                    op0=ALU.mult),
                             reads=[("pre", pb), "prm"], writes=["ctmp"])
                        S.op("pool", lambda e, pb=pb: e.tensor_tensor(out=cacc[pb][:], in0=cacc[pb][:], in1=ctmp[:], op=ALU.add),
                             reads=["ctmp", ("cacc", pb)], writes=[("cacc", pb)])
                else:
                    S.op("dve", lambda e, pb=pb, j=j: e.tensor_scalar(out=cacc[pb][:], in0=pre[pb][:, 0:512], scalar1=prm[:, j:j + 1], scalar2=None,
                                                                      op0=ALU.mult),
                         reads=[("pre", pb), "prm"], writes=[("cacc", pb)])
                    for tap in range(1, 4):
                        S.op("dve", lambda e, pb=pb, j=j, tap=tap: e.scalar_tensor_tensor(out=cacc[pb][:], in0=pre[pb][:, tap:tap + 512],
                                                                                         scalar=prm[:, tap * 6 + j:tap * 6 + j + 1], in1=cacc[pb][:],
                                                                                         op0=ALU.mult, op1=ALU.add),
                             reads=[("pre", pb), ("cacc", pb), "prm"], writes=[("cacc", pb)])
                dst = xc[p][:, j, :] if j < 4 else (BT[p][:] if j == 4 else CT[p][:])
                dkey = ("xc", p, j) if j < 4 else (("BT", p) if j == 4 else ("CT", p))
                S.op("act", lambda e, pb=pb, j=j, dst=dst: e.activation(out=dst, in_=cacc[pb][:], func=AF.Silu, bias=prm[:, 24 + j:25 + j]),
                     reads=[("cacc", pb), "prm"], writes=[dkey])
            for k in range(4):
                tc_ = slice((sc * 4 + k) * 128, (sc * 4 + k + 1) * 128)
                fb, fk = fbanks[(k + 2) % 4]
                mm_group(fb[:], lambda ch, tc_=tc_: g.hT[:, ch, tc_], lambda ch: wB[:, ch, 0:512], wball, [fk])
                S.op("act", lambda e, p=p, k=k, fb=fb: e.activation(out=szs[p][:, k, :], in_=fb[:], func=AF.Silu), reads=[fk], writes=[("sz", p, k)])
            for k in range(4):
                tc_ = slice((sc * 4 + k) * 128, (sc * 4 + k + 1) * 128)
                mm_group(Psm[:, k * 8:(k + 1) * 8], lambda ch, tc_=tc_: g.hT[:, ch, tc_], lambda ch: wB[:, ch, 1280:1288], wball, ["Psm"])
            S.op("dve", lambda e, p=p: e.tensor_tensor(out=dts[p][:, 0, :, :], in0=Psm[:, 0:32].rearrange("p (k h) -> p k h", k=4),
                                                       in1=hp[:, 0, :].unsqueeze(1).to_broadcast([128, 4, 8]), op=ALU.add),
                 reads=["Psm", ("hp", 0)], writes=[("dts", p, 0)])
            S.op("act", lambda e, p=p: e.activation(out=dts[p][:, 0, :, :], in_=dts[p][:, 0, :, :], func=AF.Exp),
                 reads=[("dts", p, 0)], writes=[("dts", p, 0)])
            S.op("act", lambda e, p=p: e.activation(out=dts[p][:, 1, :, :], in_=dts[p][:, 0, :, :], func=AF.Ln, bias=1.0),
                 reads=[("dts", p, 0)], writes=[("dts", p, 1)])
            S.op("dve", lambda e, p=p: e.tensor_tensor(out=dts[p][:, 2, :, :], in0=dts[p][:, 1, :, :],
                                                       in1=hp[:, 1, :].unsqueeze(1).to_broadcast([128, 4, 8]), op=ALU.mult),
                 reads=[("dts", p, 1), ("hp", 1)], writes=[("dts", p, 2)])
            S.op("dve", lambda e, p=p: e.tensor_copy(out=dAh[p][:], in_=dts[p][:, 2, :, :]), reads=[("dts", p, 2)], writes=[("dAh", p)])
            S.op("dve", lambda e, p=p: e.tensor_tensor(out=dAr[:], in0=dts[p][:, 2, :, :], in1=dAh[p][:], op=ALU.subtract),
                 reads=[("dts", p, 2), ("dAh", p)], writes=["dAr"])
            S.op("dve", lambda e, p=p: e.tensor_copy(out=dAl[p][:], in_=dAr[:]), reads=["dAr"], writes=[("dAl", p)])
            for k in range(4):
                S.op("pe", lambda e, p=p, k=k: e.matmul(Psm[:, 64 + k * 8:72 + k * 8], lhsT=tri, rhs=dts[p][:, 2, k, :], start=True, stop=True),
                     reads=[("dts", p, 2)], writes=["Psm"])
                S.op("pe", lambda e, p=p, k=k: e.matmul(Psm[:, 96 + k * 8:104 + k * 8], lhsT=ones, rhs=dts[p][:, 2, k, :], start=True, stop=True),
                     reads=[("dts", p, 2)], writes=["Psm"])
            v48 = lambda ap: ap.rearrange("p (k h) -> p k h", k=4)
            S.op("dve", lambda e, p=p: e.tensor_copy(out=fs[p][:, 0, :, :], in_=v48(Psm[:, 64:96])), reads=["Psm"], writes=[("fs", p, 0)])
            S.op("act", lambda e, p=p: e.activation(out=fs[p][:, 1, :, :], in_=fs[p][:, 0, :, :], func=AF.Exp),
                 reads=[("fs", p, 0)], writes=[("fs", p, 1)])
            S.op("dve", lambda e, p=p: e.tensor_tensor(out=fs[p][:, 2, :, :], in0=v48(Psm[:, 96:128]), in1=fs[p][:, 0, :, :], op=ALU.subtract),
                 reads=["Psm", ("fs", p, 0)], writes=[("fs", p, 2)])
            S.op("act", lambda e, p=p: e.activation(out=fs[p][:, 3, :, :], in_=fs[p][:, 2, :, :], func=AF.Exp),
                 reads=[("fs", p, 2)], writes=[("fs", p, 3)])
            S.op("act", lambda e, p=p: e.activation(out=fs[p][:, 4, :, :], in_=v48(Psm[:, 96:128]), func=AF.Exp),
                 reads=["Psm"], writes=[("fs", p, 4)])
            S.op("dve", lambda e, p=p: e.tensor_tensor(out=fs[p][:, 5, :, :], in0=dts[p][:, 1, :, :], in1=fs[p][:, 3, :, :], op=ALU.mult),
                 reads=[("dts", p, 1), ("fs", p, 3)], writes=[("fs", p, 5)])

        def stageA(c):
            sc, k, q = c // 4, c % 4, c % NQ
            p = sc % 2
            lc = slice(k * 128, (k + 1) * 128)
            dt_ = dts[p][:, 1, k, :]
            dA_ = dts[p][:, 2, k, :]
            trib = g.mask2[:, 128:256]
            S.op("dve", lambda e: e.tensor_tensor(out=rhsH[:], in0=trib.unsqueeze(1).to_broadcast([128, 8, 128]),
                                                  in1=dAh[p][:, k, :].unsqueeze(2).to_broadcast([128, 8, 128]), op=ALU.mult),
                 reads=[("dAh", p)], writes=["rhsH"])
            S.op("dve", lambda e: e.tensor_tensor(out=rhsL[:], in0=trib.unsqueeze(1).to_broadcast([128, 8, 128]),
                                                  in1=dAl[p][:, k, :].unsqueeze(2).to_broadcast([128, 8, 128]), op=ALU.mult),
                 reads=[("dAl", p)], writes=["rhsL"])
            for hf in range(2):
                S.op("pe", lambda e, hf=hf: e.matmul(PSEGs[hf][:], lhsT=g.u1b[:], rhs=rhsH[:, hf * 4:(hf + 1) * 4, :], start=True, stop=False),
                     reads=["rhsH"], writes=[("PSEG", hf)])
                S.op("pe", lambda e, hf=hf: e.matmul(PSEGs[hf][:], lhsT=g.u1b[:], rhs=rhsL[:, hf * 4:(hf + 1) * 4, :], start=False, stop=True),
                     reads=["rhsL"], writes=[("PSEG", hf)])
            for hf in range(2):
                S.op("act", lambda e, hf=hf: e.activation(out=LT[:, hf * 4:(hf + 1) * 4, :], in_=PSEGs[hf][:].rearrange("p (h l) -> p h l", h=4),
                                                          func=AF.Exp),
                     reads=[("PSEG", hf)], writes=[("LT", hf)])
            PA0b = PA0[:].bitcast(BF16)
            def xtr(e):
                ins = None
                for j in range(4):
                    ins = e.transpose(PA0b[:, j * 128:(j + 1) * 128], xc[p][:, j, lc], g.identb[:])
                return ins
            S.op("pe", xtr, reads=[("xc", p, j) for j in range(4)], writes=["PA0"])
            S.op("act", lambda e: e.activation(out=xtm[q][:], in_=PA0b[:, 0:512], func=AF.Copy), reads=["PA0"], writes=[("xtm", q)])
            S.op("pe", lambda e: e.transpose(Pbf[:, 0, :], BT[p][:, lc], g.identb[:]), reads=[("BT", p)], writes=["Pbf"])
            S.op("dve", lambda e: e.tensor_copy(out=Btm[q][:], in_=Pbf[:, 0, :]), reads=["Pbf"], writes=[("Btm", q)])
            S.op("pe", lambda e: e.matmul(Psm[:, 128:256], lhsT=BT[p][:, lc], rhs=CT[p][:, lc], start=True, stop=True),
                 reads=[("BT", p), ("CT", p)], writes=["Psm"])
            S.op("dve", lambda e: e.tensor_tensor(out=GTm[:], in0=Psm[:, 128:256], in1=tri, op=ALU.mult), reads=["Psm"], writes=["GTm"])
            S.op("dve", lambda e: e.tensor_tensor(out=MT[q][:], in0=LT[:], in1=GTm[:].unsqueeze(1).to_broadcast([128, 8, 128]), op=ALU.mult),
                 reads=[("LT", 0), ("LT", 1), "GTm"], writes=[("MT", q)])
            S.op("pool", lambda e: e.tensor_tensor(out=v8(xdt[q][:]), in0=v8(xtm[q][:]), in1=b8(dt_), op=ALU.mult),
                 reads=[("xtm", q), ("dts", p, 1)], writes=[("xdt", q)])
            S.op("pool", lambda e: e.tensor_tensor(out=v8(xdte[q][:]), in0=v8(xtm[q][:]), in1=b8(fs[p][:, 5, k, :]), op=ALU.mult),
                 reads=[("xtm", q), ("fs", p, 5)], writes=[("xdte", q)])
            S.op("pool", lambda e: e.tensor_tensor(out=v8(tD[q][:]), in0=v8(xtm[q][:]), in1=b8(hp[:, 2, :]), op=ALU.mult),
                 reads=[("xtm", q), ("hp", 2)], writes=[("tD", q)])

        def stageB1(c):
            sc, k, q = c // 4, c % 4, c % NQ
            p = sc % 2
            lc = slice(k * 128, (k + 1) * 128)
            S.op("pe", lambda e: e.matmul(PBo[:], lhsT=CT[p][:, lc], rhs=Hb[:], start=True, stop=True),
                 reads=[("CT", p), "Hb"], writes=["PBo"])
            S.op("pe", lambda e: e.matmul(PS2[:], lhsT=Btm[q][:], rhs=xdte[q][:], start=True, stop=True),
                 reads=[("Btm", q), ("xdte", q)], writes=["PS2"])
            S.op("pe", lambda e: e.matmul(PBy[:], lhsT=g.identb[:], rhs=tD[q][:], start=True, stop=False),
                 reads=[("tD", q)], writes=["PBy"])
            for h in range(8):
                S.op("pe", lambda e, h=h: e.matmul(PBy[:, h * 64:(h + 1) * 64], lhsT=MT[q][:, h, :], rhs=xdt[q][:, h * 64:(h + 1) * 64],
                                                   start=False, stop=(h == 7)),
                     reads=[("MT", q), ("xdt", q)], writes=["PBy"])
            S.op("dve", lambda e: e.tensor_tensor(out=v8(H[:]), in0=v8(H[:]), in1=b8(fs[p][:, 4, k, :]), op=ALU.mult),
                 reads=["H", ("fs", p, 4)], writes=["H"])
            S.op("dve", lambda e: e.tensor_tensor(out=H[:], in0=H[:], in1=PS2[:], op=ALU.add), reads=["H", "PS2"], writes=["H"])
            S.op("act", lambda e: e.activation(out=Hb[:], in_=H[:], func=AF.Copy), reads=["H"], writes=["Hb"])
            S.op("dve", lambda e: e.tensor_tensor(out=v8(yt[:]), in0=v8(PBo[:]), in1=b8(fs[p][:, 1, k, :]), op=ALU.mult),
                 reads=["PBo", ("fs", p, 1)], writes=["yt"])
            S.op("dve", lambda e: e.tensor_tensor(out=yt[:], in0=yt[:], in1=PBy[:], op=ALU.add), reads=["yt", "PBy"], writes=["yt"])
            S.op("pool", lambda e: e.tensor_tensor(out=yt[:], in0=yt[:], in1=szs[p][:, k, :], op=ALU.mult),
                 reads=["yt", ("sz", p, k)], writes=["yt"])
            yq = c % 2
            S.op("act", lambda e: e.activation(out=yn[yq][:], in_=yt[:], func=AF.Square, accum_out=nst[:, 0:1]),
                 reads=["yt"], writes=[("yn", yq), ("nst", 0)])
            S.op("act", lambda e: e.activation(out=nst[:, 1:2], in_=nst[:, 0:1], func=AF.Ln, scale=1.0 / 512, bias=EPS),
                 reads=[("nst", 0)], writes=[("nst", 1)])
            S.op("act", lambda e: e.activation(out=nst[:, 2:3], in_=nst[:, 1:2], func=AF.Exp, scale=-0.5),
                 reads=[("nst", 1)], writes=[("nst", 2)])
            S.op("act", lambda e: e.activation(out=yn[yq][:], in_=yt[:], func=AF.Copy, scale=nst[:, 2:3]),
                 reads=["yt", ("nst", 2)], writes=[("yn", yq)])

        def stageB2(c):
            sc, k, q = c // 4, c % 4, c % 2
            p = sc % 2
            lc = slice(k * 128, (k + 1) * 128)
            for j in range(4):
                S.op("pe", lambda e, j=j: e.transpose(Pbf[:, 4 + j, :], yn[q][:, j * 128:(j + 1) * 128], g.identb[:]),
                     reads=[("yn", q)], writes=["Pbf"])
            S.op("dve", lambda e: e.tensor_tensor(out=Ybuf[p][:, :, lc], in0=Pbf[:, 4:8, :],
                                                  in1=prm[:, 30:34].unsqueeze(2).to_broadcast([128, 4, 128]), op=ALU.mult),
                 reads=["Pbf", "prm"], writes=[("Ybuf", p)])
            if k == 3:
                r0 = 512 + grp * 512
                ts_ = slice(sc * 512, (sc + 1) * 512)
                S.op("sp", lambda e: e.dma_start(out=g.ycat[r0:r0 + 512, ts_].rearrange("(cc p) t -> p cc t", p=128), in_=Ybuf[p][:]),
                     reads=[("Ybuf", p)], dma_key=("Ybuf", p))

        NCH = SEQ // 128
        front(0)
        stageA(0)
        stageA(1)
        for c in range(NCH):
            S.capture()
            if c + 3 < NCH and (c + 3) % 4 == 0:
                front((c + 3) // 4)
            lf = S.end_capture()
            S.capture()
            stageB1(c)
            lb1 = S.end_capture()
            S.capture()
            if c + 2 < NCH:
                stageA(c + 2)
            la = S.end_capture()
            S.capture()
            if c >= 1:
                stageB2(c - 1)
            lb2 = S.end_capture()
            S.replay_merged([lf])
            S.replay_merged([lb1, la, lb2])
        stageB2(NCH - 1)
        S.emit_block()


def phase_out(g, L, src):
    nc, S = g.nc, g.S
    with contextlib.ExitStack() as st:
        sbt = lambda name, shape, dt: st.enter_context(g.sbuf(name, shape, dt))
        wo = sbt("wo", [128, 16, DM], BF16)
        wst = [sbt("owst%d" % i, [128, DM], F32) for i in range(3)]
        yc = [sbt("oyc%d" % i, [128, 16, 512], BF16) for i in range(2)]
        xt = [sbt("oxt%d" % i, [128, DM], F32) for i in range(2)]
        t1 = [sbt("ot1%d" % i, [128, DM], F32) for i in range(2)]
        junk = sbt("ojunk", [128, DM], BF16)
        stat = [sbt("ost%d" % i, [128, 4], F32) for i in range(2)]
        PY = [st.enter_context(g.psum("oPY%d" % i, [128, 2, 512], F32)) for i in range(2)]
        for cc in range(16):
            b = cc % 3
            S.op("sp", lambda e, b=b, cc=cc: e.dma_start(out=wst[b][:], in_=g.w_out[L, cc * 128:(cc + 1) * 128, :]),
                 writes=[("wst", b)], dma_key=("owst", b))
            ceng = ("act", "dve", "pool")[cc % 3]
            if ceng == "act":
                S.op("act", lambda e, b=b, cc=cc: e.activation(out=wo[:, cc, :], in_=wst[b][:], func=AF.Copy),
                     reads=[("wst", b)], writes=[("wo", cc)])
            else:
                S.op(ceng, lambda e, b=b, cc=cc: e.tensor_copy(out=wo[:, cc, :], in_=wst[b][:]), reads=[("wst", b)], writes=[("wo", cc)])
        def load_yc(gi):
            gb = gi % 2
            ts_ = slice(gi * 512, (gi + 1) * 512)
            S.op("sp", lambda e: e.dma_start(out=yc[gb][:], in_=g.ycat[:, ts_].rearrange("(cc p) t -> p cc t", p=128)),
                 writes=[("yc", gb)], dma_key=("oyc", gb))

        load_yc(0)
        for gi in range(8):
            gb = gi % 2
            if gi + 1 < 8:
                load_yc(gi + 1)
            for k in range(4):
                t = gi * 4 + k
                b = t % 2
                rows = slice(t * 128, (t + 1) * 128)
                lc = slice(k * 128, (k + 1) * 128)
                S.op("sp", lambda e, b=b, rows=rows: e.dma_start(out=xt[b][:], in_=src[rows, :]), writes=[("x", b)], dma_key=("oxt", b))
                for nh in range(2):
                    for cc in range(16):
                        S.op("pe", lambda e, b=b, nh=nh, cc=cc, gb=gb, lc=lc: e.matmul(PY[b][:, nh, :], lhsT=yc[gb][:, cc, lc],
                                                                                       rhs=wo[:, cc, nh * 512:(nh + 1) * 512],
                                                                                       start=(cc == 0), stop=(cc == 15)),
                             reads=[("yc", gb), ("wo", cc)], writes=[("PY", b)])
                S.op("act", lambda e, b=b: e.activation(out=junk[:].rearrange("p (a n) -> p a n", a=2), in_=PY[b][:], func=AF.Square,
                                                        accum_out=stat[b][:, 0:1]),
                     reads=[("PY", b)], writes=["junk", ("s0", b)])
                S.op("act", lambda e, b=b: e.activation(out=stat[b][:, 1:2], in_=stat[b][:, 0:1], func=AF.Sqrt, scale=1.0 / DM, bias=EPS),
                     reads=[("s0", b)], writes=[("s1", b)])
                S.op("dve", lambda e, b=b: e.reciprocal(out=stat[b][:, 2:3], in_=stat[b][:, 1:2]), reads=[("s1", b)], writes=[("s2", b)])
                S.op("dve", lambda e, b=b: e.scalar_tensor_tensor(out=t1[b][:].rearrange("p (a n) -> p a n", a=2), in0=PY[b][:],
                                                                  scalar=stat[b][:, 2:3],
                                                                  in1=g.modb[:, 2 * DM:3 * DM].rearrange("p (a n) -> p a n", a=2),
                                                                  op0=ALU.mult, op1=ALU.mult),
                     reads=[("PY", b), ("s2", b)], writes=[("t1", b)])
                S.op("pool", lambda e, b=b: e.tensor_tensor(out=t1[b][:], in0=t1[b][:], in1=xt[b][:], op=ALU.add),
                     reads=[("t1", b), ("x", b)], writes=[("t1", b)])
                S.op("sp", lambda e, b=b, rows=rows: e.dma_start(out=g.out[rows, :], in_=t1[b][:]), reads=[("t1", b)], dma_key=("ot1", b))
        S.emit_block()


def _consts():
    i = np.arange(128)
    c = np.zeros((128, 7, 128), np.float32)
    c[:, 0, :] = np.eye(128)
    c[:, 1, :] = (i[:, None] >= i[None, :])
    c[:, 2, :] = (i[:, None] <= i[None, :])
    c[:, 3, :] = (i[:, None] > i[None, :])
    c[:, 4, :] = 1.0
    c[64, 5, 0:64] = 1.0
    return c


_PROG = {}
FUSED = True
_WNAMES = ("ada_w", "ada_b", "pre_norm_w", "post_norm_w", "w_in", "conv_w", "conv_b", "dt_bias", "a_log",
           "d_skip", "ssm_norm_w", "sinks", "w_out")


def kernel(x, c, ada_w, ada_b, pre_norm_w, post_norm_w, w_in, conv_w, conv_b,
           dt_bias, a_log, d_skip, ssm_norm_w, sinks, w_out):
    f = lambda a: np.ascontiguousarray(np.asarray(a, dtype=np.float32))
    ws = dict(ada_w=f(ada_w), ada_b=f(ada_b), pre_norm_w=f(pre_norm_w), post_norm_w=f(post_norm_w),
              w_in=f(w_in), conv_w=f(conv_w), conv_b=f(conv_b), dt_bias=f(dt_bias), a_log=f(a_log),
              d_skip=f(d_skip), ssm_norm_w=f(ssm_norm_w), sinks=f(sinks), w_out=f(w_out))
    x = f(x)
    c = f(c)
    consts = _consts()
    ccols = [np.ascontiguousarray(c[b].reshape(8, 128).T) for b in range(8)]
    if FUSED:
        if "fused" not in _PROG:
            _PROG["fused"] = build(n_layers=DEPTH, depth_dim=DEPTH)
        in_maps = []
        for b in range(8):
            m = dict(ws)
            m["consts"] = consts
            m["x"] = x[b]
            m["c_col"] = ccols[b]
            in_maps.append(m)
        res = run_bass_kernel_spmd(_PROG["fused"], in_maps, core_ids=list(range(8)))
        return np.stack([np.asarray(r["out"]) for r in res.results], axis=0).astype(np.float32)
    if "layer" not in _PROG:
        _PROG["layer"] = build(n_layers=1, depth_dim=1)
    cur = [x[b] for b in range(8)]
    for L in range(DEPTH):
        wl = {k: np.ascontiguousarray(ws[k][L:L + 1]) for k in _WNAMES}
        in_maps = []
        for b in range(8):
            m = dict(wl)
            m["consts"] = consts
            m["x"] = cur[b]
            m["c_col"] = ccols[b]
            in_maps.append(m)
        res = run_bass_kernel_spmd(_PROG["layer"], in_maps, core_ids=list(range(8)))
        cur = [np.ascontiguousarray(np.asarray(r["out"], dtype=np.float32)) for r in res.results]
    return np.stack(cur, axis=0).astype(np.float32)
```

```python
import contextlib
import numpy as np
import concourse.bass as bass
import concourse.mybir as mybir
from concourse.bass_utils import run_bass_kernel_spmd

F32 = mybir.dt.float32
BF16 = mybir.dt.bfloat16
AF = mybir.ActivationFunctionType
ALU = mybir.AluOpType

SEQ = 4096
DM = 1024
NT = SEQ // 128
EPS = 1e-6
DEPTH = 4
IN_COLS = 5904
C_OFF = 4624
ENGS = ("pe", "act", "dve", "pool", "sp")


def sl(start, n, step=1):
    return slice(start, start + step * (n - 1) + 1, step)


class Sched:
    def __init__(self, nc, stack):
        self.nc = nc
        self.stack = stack
        self.esem = {e: stack.enter_context(nc.semaphore("s_" + e)) for e in ENGS}
        self._names = {id(v): "s_" + k for k, v in self.esem.items()}
        self.ecount = {e: 0 for e in ENGS}
        self.dsem = {}
        self.dcount = {}
        self.waited = {e: {} for e in ENGS}
        self.n_ops = 0
        self.reset_block()

    def reset_block(self):
        self.ops = []
        self.last_w = {}
        self.readers = {}

    def _dma_sem(self, key):
        if key not in self.dsem:
            self.dsem[key] = self.stack.enter_context(self.nc.semaphore("d%d" % len(self.dsem)))
            self.dcount[key] = 0
            self._names[id(self.dsem[key])] = "d_" + str(key)
        return self.dsem[key]

    def capture(self):
        self._cap = []
        return self._cap

    def end_capture(self):
        lst, self._cap = self._cap, None
        return lst

    _DUR = {"pe": 0.2, "act": 0.6, "dve": 0.75, "pool": 1.1, "sp": 2.0}

    def replay_merged(self, lists):
        if not hasattr(self, "_sim_eng"):
            self._sim_eng = {e: 0.0 for e in ENGS}
            self._sim_key = {}
        its = [list(l) for l in lists if l]
        pos = [0] * len(its)
        while True:
            best, best_t = None, None
            for i in range(len(its)):
                if pos[i] >= len(its[i]):
                    continue
                eng, fn, reads, writes, dma_key = its[i][pos[i]]
                t = self._sim_eng[eng]
                for k in reads:
                    t = max(t, self._sim_key.get(k, 0.0))
                for k in writes:
                    t = max(t, self._sim_key.get(k, 0.0))
                if best is None or t < best_t - 1e-9:
                    best, best_t = i, t
            if best is None:
                break
            a = its[best][pos[best]]
            pos[best] += 1
            eng, fn, reads, writes, dma_key = a
            fin = best_t + self._DUR[eng]
            self._sim_eng[eng] = fin
            for k in writes:
                self._sim_key[k] = fin
            self.op(*a)

    def op(self, eng, fn, reads=(), writes=(), dma_key=None):
        if getattr(self, "_cap", None) is not None:
            self._cap.append((eng, fn, tuple(reads), tuple(writes), dma_key))
            return None
        idx = len(self.ops)
        deps = set()
        for k in reads:
            w = self.last_w.get(k)
            if w is not None:
                deps.add(w)
        for k in writes:
            w = self.last_w.get(k)
            if w is not None:
                deps.add(w)
            for r in self.readers.get(k, ()):
                deps.add(r)
        deps.discard(idx)
        self.ops.append(dict(eng=eng, fn=fn, deps=deps, dma_key=dma_key, milestone=False))
        for k in reads:
            self.readers.setdefault(k, []).append(idx)
        for k in writes:
            self.last_w[k] = idx
            self.readers[k] = []
        return idx

    def emit_block(self, name=None):
        nc = self.nc
        ops = self.ops
        for o in ops:
            keep = set()
            for d in o["deps"]:
                s = ops[d]
                if s["dma_key"] is None and o["dma_key"] is None and s["eng"] == o["eng"] == "pe":
                    continue
                keep.add(d)
            o["deps"] = keep
            for d in keep:
                if ops[d]["dma_key"] is None:
                    ops[d]["milestone"] = True
        ecount = dict(self.ecount)
        dcount = dict(self.dcount)
        for o in ops:
            if o["dma_key"] is not None:
                self._dma_sem(o["dma_key"])
                dcount[o["dma_key"]] = dcount.get(o["dma_key"], 0) + 16
                o["dval"] = dcount[o["dma_key"]]
            elif o["milestone"]:
                ecount[o["eng"]] += 1
                o["mval"] = ecount[o["eng"]]
        per_eng = {e: [o for o in ops if o["eng"] == e] for e in ENGS}
        final_d = dict(dcount)
        sched = self

        def emit_engine(ename, engine):
            waited = sched.waited[ename]
            for o in per_eng[ename]:
                need = {}
                for d in o["deps"]:
                    s = ops[d]
                    if s["dma_key"] is not None:
                        sem, val = sched.dsem[s["dma_key"]], s["dval"]
                    else:
                        sem, val = sched.esem[s["eng"]], s["mval"]
                    key = sched._names[id(sem)]
                    if val > need.get(key, (None, 0))[1]:
                        need[key] = (sem, val)
                for key, (sem, val) in need.items():
                    if waited.get(key, 0) >= val:
                        continue
                    engine.wait_ge(sem, val)
                    waited[key] = val
                ins = o["fn"](engine)
                if o["dma_key"] is not None:
                    ins.then_inc(sched.dsem[o["dma_key"]], 16)
                elif o["milestone"]:
                    ins.then_inc(sched.esem[ename], 1)
            if ename == "sp":
                for k, v in final_d.items():
                    sem = sched.dsem[k]
                    key = sched._names[id(sem)]
                    if waited.get(key, 0) < v:
                        engine.wait_ge(sem, v)
                        waited[key] = v

        with nc.Block(name) as block:
            @block.tensor
            def _(e):
                emit_engine("pe", e)

            @block.scalar
            def _(e):
                emit_engine("act", e)

            @block.vector
            def _(e):
                emit_engine("dve", e)

            @block.gpsimd
            def _(e):
                emit_engine("pool", e)

            @block.sync
            def _(e):
                emit_engine("sp", e)
        self.ecount = ecount
        self.dcount = dcount
        self.n_ops += len(ops)
        self.reset_block()


class K:
    pass


def dbg(g, name, ap, shape, dtype, reads):
    if not getattr(g, "debug", False):
        return
    import os
    taps = os.environ.get("DBG_TAPS", "")
    if not any(name == t or name.startswith(t + "_") for t in taps.split(",") if t):
        return
    d = g.nc.dram_tensor("dbg_" + name, list(shape), dtype, kind="ExternalOutput").ap()
    idx = tuple(slice(None) for _ in shape)
    g.S.op("sp", lambda e: e.dma_start(out=d[idx], in_=ap), reads=reads, dma_key=("dbg", name))


def build(n_layers=DEPTH, debug=False, phases=("mod", "p1", "att", "ssd", "out"), depth_dim=DEPTH):
    nc = bass.Bass("TRN2", target_bir_lowering=False)

    def din(name, shape):
        return nc.dram_tensor(name, shape, F32, kind="ExternalInput").ap()

    g = K()
    g.nc = nc
    g.uid = [0]
    g.debug = debug

    def _uniq(name):
        g.uid[0] += 1
        return "%s_%d" % (name, g.uid[0])
    g.sbuf = lambda name, shape, dt: nc.sbuf_tensor(_uniq(name), shape, dt)
    g.psum = lambda name, shape, dt: nc.psum_tensor(_uniq(name), shape, dt)
    g.x_in = din("x", [SEQ, DM])
    g.c_col = din("c_col", [128, 8])
    DD = depth_dim
    g.ada_w = din("ada_w", [DD, DM, 3 * DM])
    g.ada_b = din("ada_b", [DD, 3 * DM])
    g.pre_w = din("pre_norm_w", [DD, DM])
    g.post_w = din("post_norm_w", [DD, DM])
    g.w_in = din("w_in", [DD, DM, IN_COLS])
    g.conv_w = din("conv_w", [DD, 4, 1536])
    g.conv_b = din("conv_b", [DD, 1536])
    g.dt_bias = din("dt_bias", [DD, 16])
    g.a_log = din("a_log", [DD, 16])
    g.d_skip = din("d_skip", [DD, 16])
    g.ssm_w = din("ssm_norm_w", [DD, DM])
    g.sinks = din("sinks", [DD, 8])
    g.w_out = din("w_out", [DD, 2 * DM, DM])
    g.consts = din("consts", [128, 7, 128])
    g.out = nc.dram_tensor("out", [SEQ, DM], F32, kind="ExternalOutput").ap()
    g.ycat = nc.dram_tensor("ycat", [2 * DM, SEQ], BF16,
                            kind="ExternalOutput" if debug else "Internal").ap()

    with contextlib.ExitStack() as st:
        S = Sched(nc, st)
        g.S = S
        sb = lambda name, shape, dt: st.enter_context(nc.sbuf_tensor(name, shape, dt))
        g.cf = sb("cf", [128, 7, 128], F32)
        g.identb = sb("identb", [128, 128], BF16)
        g.mask2 = sb("mask2", [128, 256], BF16)
        g.u1b = sb("u1b", [128, 128], BF16)
        g.cbc = sb("cbc", [128, 8, 128], F32)
        g.modb = sb("modb", [128, 3 * DM], F32)
        g.hT = sb("hT", [128, 8, SEQ], BF16)

        phase_init(g)
        for L in range(n_layers):
            src = g.x_in if L == 0 else g.out
            if "mod" in phases:
                phase_mod(g, L)
            if "p1" in phases:
                phase_p1(g, L, src)
            if "att" in phases:
                specs = []
                for hp in range(4):
                    base = hp * 128
                    specs.append(dict(q=[(base, 128)], k=[(512 + base, 128)], v=[(1024 + base, 128)],
                                      z=[(1536 + base, 128)], pats=(1, 4, 16), sink=None,
                                      rows=(base, base + 64)))
                for i in range(4):
                    specs.append(dict(q=[(C_OFF + i * 64, 64), (C_OFF + (4 + i) * 64, 64)],
                                      k=[(C_OFF + 1024, 128)], v=[(C_OFF + 1152, 128)],
                                      z=[(C_OFF + 512 + i * 64, 64), (C_OFF + 512 + (4 + i) * 64, 64)],
                                      pats=(1,), sink=(i, 4 + i), reuse_kv=(i > 0),
                                      rows=(1536 + i * 64, 1536 + (4 + i) * 64)))
                phase_att(g, L, specs)
            if "ssd" in phases:
                for grp in range(2):
                    phase_ssd(g, L, grp)
            if "out" in phases:
                phase_out(g, L, src)
        g.n_ops = S.n_ops
    return nc


def phase_init(g):
    nc, S = g.nc, g.S
    with contextlib.ExitStack() as st:
        cc = st.enter_context(g.sbuf("cc", [128, 8], F32))
        ca = st.enter_context(g.sbuf("ca", [128, 8], F32))
        S.op("sp", lambda e: e.dma_start(out=g.cf[:], in_=g.consts[:, :, :]), writes=["cf"], dma_key="cf")
        S.op("sp", lambda e: e.dma_start(out=cc[:], in_=g.c_col[:, :]), writes=["cc"], dma_key="cc")
        S.op("pool", lambda e: e.tensor_copy(out=g.identb[:], in_=g.cf[:, 0, :]), reads=["cf"], writes=["identb"])
        S.op("pool", lambda e: e.tensor_copy(out=g.mask2[:].rearrange("p (a b) -> p a b", a=2), in_=g.cf[:, 1:3, :]),
             reads=["cf"], writes=["mask2"])
        S.op("pool", lambda e: e.tensor_copy(out=g.u1b[:], in_=g.cf[:, 3, :]), reads=["cf"], writes=["u1b"])
        S.op("act", lambda e: e.activation(out=ca[:], in_=cc[:], func=AF.Silu), reads=["cc"], writes=["ca"])
        S.op("dve", lambda e: e.tensor_copy(out=g.cbc[:], in_=ca[:].unsqueeze(2).to_broadcast([128, 8, 128])),
             reads=["ca"], writes=["cbc"])
        S.emit_block("init")


def phase_mod(g, L):
    nc, S = g.nc, g.S
    with contextlib.ExitStack() as st:
        stage = [st.enter_context(g.sbuf("mstage%d" % i, [128, 8, 512], F32)) for i in range(2)]
        adab = st.enter_context(g.sbuf("adab", [128, 3 * DM], F32))
        pw = st.enter_context(g.sbuf("pw", [128, 2, DM], F32))
        ps = [st.enter_context(g.psum("mps%d" % i, [128, 512], F32)) for i in range(2)]
        S.op("sp", lambda e: e.dma_start(out=adab[:], in_=g.ada_b[L:L + 1, :].partition_broadcast(128)),
             writes=["adab"], dma_key="adab")
        S.op("sp", lambda e: e.dma_start(out=pw[:, 0, :], in_=g.pre_w[L:L + 1, :].partition_broadcast(128)),
             writes=["pw0"], dma_key="pw0")
        S.op("sp", lambda e: e.dma_start(out=pw[:, 1, :], in_=g.post_w[L:L + 1, :].partition_broadcast(128)),
             writes=["pw1"], dma_key="pw1")
        aw = g.ada_w[L].rearrange("(ch p) n -> p ch n", p=128)
        for grp in range(6):
            b = grp % 2
            cs = slice(grp * 512, (grp + 1) * 512)
            S.op("sp", lambda e, b=b, cs=cs: e.dma_start(out=stage[b][:], in_=aw[:, :, cs]),
                 writes=[("mst", b)], dma_key=("mst", b))
            for ch in range(8):
                S.op("pe", lambda e, b=b, ch=ch: e.matmul(ps[b][:], lhsT=g.cbc[:, ch, :], rhs=stage[b][:, ch, :],
                                                          start=(ch == 0), stop=(ch == 7)),
                     reads=[("mst", b), "cbc"], writes=[("mps", b)])
            S.op("dve", lambda e, b=b, cs=cs: e.tensor_tensor(out=g.modb[:, cs], in0=ps[b][:], in1=adab[:, cs], op=ALU.add),
                 reads=[("mps", b), "adab"], writes=["modb"])
        S.op("dve", lambda e: e.scalar_tensor_tensor(out=g.modb[:, DM:2 * DM], in0=g.modb[:, DM:2 * DM], scalar=1.0,
                                                     in1=pw[:, 0, :], op0=ALU.add, op1=ALU.mult),
             reads=["modb", "pw0"], writes=["modb"])
        S.op("dve", lambda e: e.tensor_tensor(out=g.modb[:, 2 * DM:3 * DM], in0=g.modb[:, 2 * DM:3 * DM], in1=pw[:, 1, :],
                                              op=ALU.mult),
             reads=["modb", "pw1"], writes=["modb"])
        S.emit_block("mod%d" % L)


def phase_p1(g, L, src):
    nc, S = g.nc, g.S
    NB = 4
    with contextlib.ExitStack() as st:
        xt = [st.enter_context(g.sbuf("p1x%d" % i, [128, DM], F32)) for i in range(NB)]
        tmp = [st.enter_context(g.sbuf("p1t%d" % i, [128, DM], F32)) for i in range(NB)]
        hb = [st.enter_context(g.sbuf("p1h%d" % i, [128, DM], BF16)) for i in range(NB)]
        junk = st.enter_context(g.sbuf("p1junk", [128, DM], BF16))
        stat = [st.enter_context(g.sbuf("p1s%d" % i, [128, 4], F32)) for i in range(NB)]
        pst = [st.enter_context(g.psum("p1ps%d" % i, [128, 8, 128], BF16)) for i in range(2)]

        def s1(t):
            b = t % NB
            rows = slice(t * 128, (t + 1) * 128)
            S.op("sp", lambda e: e.dma_start(out=xt[b][:], in_=src[rows, :]), writes=[("x", b)], dma_key=("p1x", b))
            S.op("act", lambda e: e.activation(out=junk[:], in_=xt[b][:], func=AF.Square, accum_out=stat[b][:, 0:1]),
                 reads=[("x", b)], writes=["junk", ("s0", b)])
            S.op("act", lambda e: e.activation(out=stat[b][:, 1:2], in_=stat[b][:, 0:1], func=AF.Sqrt, scale=1.0 / DM, bias=EPS),
                 reads=[("s0", b)], writes=[("s1", b)])
            S.op("dve", lambda e: e.reciprocal(out=stat[b][:, 2:3], in_=stat[b][:, 1:2]), reads=[("s1", b)], writes=[("s2", b)])
            S.op("dve", lambda e: e.scalar_tensor_tensor(out=tmp[b][:], in0=xt[b][:], scalar=stat[b][:, 2:3],
                                                         in1=g.modb[:, DM:2 * DM], op0=ALU.mult, op1=ALU.mult),
                 reads=[("x", b), ("s2", b)], writes=[("t", b)])
            S.op("pool" if t % 2 == 0 else "dve",
                 lambda e: e.tensor_tensor(out=hb[b][:], in0=tmp[b][:], in1=g.modb[:, 0:DM], op=ALU.add),
                 reads=[("t", b)], writes=[("h", b)])

        def s2(t):
            b = t % NB
            pb = t % 2
            rows = slice(t * 128, (t + 1) * 128)
            for ch in range(8):
                S.op("pe", lambda e, ch=ch: e.transpose(pst[pb][:, ch, :], hb[b][:, ch * 128:(ch + 1) * 128], g.identb[:]),
                     reads=[("h", b)], writes=[("ps", pb)])
            S.op("act", lambda e: e.activation(out=g.hT[:, :, rows], in_=pst[pb][:], func=AF.Copy),
                 reads=[("ps", pb)], writes=[("hT", t)])

        s1(0)
        s1(1)
        for t in range(NT):
            if t + 2 < NT:
                s1(t + 2)
            s2(t)
        S.emit_block("p1_%d" % L)


def _groups(d, b):
    if d == 1:
        return [b // 4]
    if d == 4:
        return [b]
    return [4 * b + i for i in range(4)]


def phase_att(g, L, specs):
    nc, S = g.nc, g.S
    with contextlib.ExitStack() as st:
        sbt = lambda name, shape, dt: st.enter_context(g.sbuf(name, shape, dt))
        wst = [sbt("awst%d" % i, [128, 8, 128], F32) for i in range(2)]
        wts = [{n: sbt("aw_%s%d" % (n, i), [128, 8, 128], BF16) for n in "qkvz"} for i in range(2)]
        QT = sbt("QT", [128, SEQ], BF16)
        KT = sbt("KT", [128, SEQ], BF16)
        VT = sbt("VTf", [128, SEQ], BF16)
        Vt = {d: sbt("Vt%d" % d, [128, 32, 2, 65], BF16) for d in (1, 4, 16)}
        NSB = 3
        Et = [sbt("E%d" % i, [128, 2, 256], BF16) for i in range(NSB)]
        Pt = [sbt("P%d" % i, [128, 2, 256], BF16) for i in range(NSB)]
        Acc = sbt("Acc", [65, 2, SEQ], F32)
        Ut = [sbt("U%d" % i, [64, 512], F32) for i in range(2)]
        Tt = [sbt("T%d" % i, [64, 512], F32) for i in range(2)]
        Yb = [sbt("Yb%d" % i, [64, 512], BF16) for i in range(2)]
        sks = [sbt("sk%d" % i, [64, 2], F32) for i in range(2)]
        Sp = [st.enter_context(g.psum("aS%d" % i, [128, 2, 512], F32)) for i in range(NSB)]
        Op = [st.enter_context(g.psum("aO%d" % i, [128, 512], F32)) for i in range(2)]
        banks = [(i, j) for i in range(NSB) for j in range(2)]
        bk = lambda ij: Sp[ij[0]][:, ij[1], :]
        bkey = lambda ij: ("Sb", ij[0], ij[1])
        wv_in = g.w_in[L].rearrange("(ch p) n -> p ch n", p=128)
        for d in (1, 4, 16):
            S.op("pool", lambda e, d=d: e.memset(Vt[d][:, :, :, 64:65], 1.0), writes=[("Vone", d)])
        pj = 0
        step = 0
        fj = 0
        def load_weights(si):
            sp = specs[si]
            wt = wts[si % 2]
            sk = sks[si % 2]
            for wi, n in enumerate("qkvz"):
                b = wi % 2
                off = 0
                for pi, (c0, cn) in enumerate(sp[n]):
                    S.op("sp", lambda e, b=b, c0=c0, cn=cn, off=off: e.dma_start(out=wst[b][:, :, off:off + cn],
                                                                               in_=wv_in[:, :, c0:c0 + cn]),
                         writes=[("wst", b, pi)], dma_key=("awst", b, pi))
                    off += cn
                ceng = ("pool", "dve", "pool", "act")[wi]
                if ceng == "act":
                    S.op("act", lambda e, b=b, n=n, wt=wt: e.activation(out=wt[n][:], in_=wst[b][:], func=AF.Copy),
                         reads=[("wst", b, 0), ("wst", b, 1)], writes=[("w", n, si % 2)])
                else:
                    S.op(ceng, lambda e, b=b, n=n, wt=wt: e.tensor_copy(out=wt[n][:], in_=wst[b][:]),
                         reads=[("wst", b, 0), ("wst", b, 1)], writes=[("w", n, si % 2)])
            if sp["sink"] is not None:
                for h in range(2):
                    hh = sp["sink"][h]
                    S.op("sp", lambda e, h=h, hh=hh, sk=sk: e.dma_start(out=sk[:, h:h + 1],
                                                                        in_=g.sinks[L:L + 1, hh:hh + 1].partition_broadcast(64)),
                         writes=[("skr", si % 2, h)], dma_key=("sk", si % 2, h))
                S.op("act", lambda e, sk=sk: e.activation(out=sk[:], in_=sk[:], func=AF.Exp),
                     reads=[("skr", si % 2, 0), ("skr", si % 2, 1)], writes=[("sk", si % 2)])

        load_weights(0)
        for si, sp in enumerate(specs):
            pats = sp["pats"]
            wt = wts[si % 2]
            sk = sks[si % 2]
            wk = lambda n, si=si: ("w", n, si % 2)
            for n in range(8):
                ts_ = slice(n * 512, (n + 1) * 512)
                for nm, eng in (("q", "act"), ("k", "dve"), ("v", "act")):
                    if nm != "q" and sp.get("reuse_kv"):
                        continue
                    ij = banks[pj % len(banks)]
                    pj += 1
                    for ch in range(8):
                        S.op("pe", lambda e, ij=ij, ch=ch, nm=nm, ts_=ts_, wt=wt: e.matmul(bk(ij), lhsT=wt[nm][:, ch, :], rhs=g.hT[:, ch, ts_],
                                                                                         start=(ch == 0), stop=(ch == 7)),
                             reads=[wk(nm)], writes=[bkey(ij)])
                    if nm == "q":
                        S.op("act", lambda e, ij=ij, ts_=ts_: e.activation(out=QT[:, ts_], in_=bk(ij), func=AF.Copy, scale=0.125),
                             reads=[bkey(ij)], writes=[("QT", n)])
                    elif nm == "k":
                        S.op("dve", lambda e, ij=ij, ts_=ts_: e.tensor_copy(out=KT[:, ts_], in_=bk(ij)),
                             reads=[bkey(ij)], writes=[("KT", n)])
                    else:
                        S.op("act", lambda e, ij=ij, ts_=ts_: e.activation(out=VT[:, ts_], in_=bk(ij), func=AF.Copy),
                             reads=[bkey(ij)], writes=[("VT", n)])
            for d in (() if sp.get("reuse_kv") else pats):
                nb = 32 // d
                for b4 in range(8):
                    ij = banks[pj % len(banks)]
                    pj += 1
                    pb16 = lambda ij: bk(ij).bitcast(BF16)
                    for j in range(4):
                        blk = b4 * 4 + j
                        r, bb = blk // nb, blk % nb
                        tok = sl(r + d * 128 * bb, 128, d)
                        S.op("pe", lambda e, ij=ij, j=j, tok=tok: e.transpose(pb16(ij)[:, j * 128:(j + 1) * 128], VT[:, tok], g.identb[:]),
                             reads=[("VT", x) for x in _groups(d, bb)], writes=[bkey(ij)])
                    vsrc = lambda ij: pb16(ij)[:, 0:512].rearrange("p (j h c) -> p j h c", j=4, h=2)
                    if b4 % 2 == 0:
                        S.op("dve", lambda e, ij=ij, b4=b4, d=d: e.tensor_copy(out=Vt[d][:, b4 * 4:(b4 + 1) * 4, :, 0:64], in_=vsrc(ij)),
                             reads=[bkey(ij)], writes=[("Vt", d, b4)])
                    else:
                        S.op("act", lambda e, ij=ij, b4=b4, d=d: e.activation(out=Vt[d][:, b4 * 4:(b4 + 1) * 4, :, 0:64], in_=vsrc(ij),
                                                                             func=AF.Copy),
                             reads=[bkey(ij)], writes=[("Vt", d, b4)])
            if si + 1 < len(specs):
                load_weights(si + 1)
            steps = []
            for pi, d in enumerate(pats):
                nb = 32 // d
                for r in range(d):
                    for bb in range(nb):
                        steps.append(dict(pi=pi, d=d, r=r, bb=bb, nb=nb, s_=step % NSB, o_=step % 2,
                                          meng="dve" if step % 2 == 0 else "pool"))
                        step += 1

            def part1(stp):
                d, r, bb, s_, meng = stp["d"], stp["r"], stp["bb"], stp["s_"], stp["meng"]
                tq = sl(r + d * 128 * bb, 128, d)
                tp = sl(r + d * 128 * (bb - 1), 128, d) if bb > 0 else None
                lo = 0 if bb > 0 else 128
                gq = _groups(d, bb)
                gk = gq + (_groups(d, bb - 1) if bb > 0 else [])
                rd = [("QT", x) for x in gq] + [("KT", x) for x in set(gk)]
                skeys = [("Sb", s_, 0), ("Sb", s_, 1)]
                for h in range(2):
                    hs = slice(64 * h, 64 * h + 64)
                    S.op("pe", lambda e, s_=s_, h=h, hs=hs, tq=tq: e.matmul(Sp[s_][:, h, 128:256], lhsT=KT[hs, tq], rhs=QT[hs, tq],
                                                                           start=True, stop=True),
                         reads=rd, writes=skeys)
                    if bb > 0:
                        S.op("pe", lambda e, s_=s_, h=h, hs=hs, tq=tq, tp=tp: e.matmul(Sp[s_][:, h, 0:128], lhsT=KT[hs, tp],
                                                                                      rhs=QT[hs, tq], start=True, stop=True),
                             reads=rd, writes=skeys)
                S.op("act", lambda e, s_=s_, lo=lo: e.activation(out=Et[s_][:, :, lo:256], in_=Sp[s_][:, :, lo:256], func=AF.Exp),
                     reads=skeys, writes=[("E", s_)])
                S.op(meng, lambda e, s_=s_, lo=lo: e.tensor_tensor(out=Pt[s_][:, :, lo:256], in0=Et[s_][:, :, lo:256],
                                                                  in1=g.mask2[:, lo:256].unsqueeze(1).to_broadcast([128, 2, 256 - lo]),
                                                                  op=ALU.mult),
                     reads=[("E", s_)], writes=[("P", s_)])

            def part2(stp):
                pi, d, r, bb, nb, s_, o_ = stp["pi"], stp["d"], stp["r"], stp["bb"], stp["nb"], stp["s_"], stp["o_"]
                blk = r * nb + bb
                tq = sl(r + d * 128 * bb, 128, d)
                gq = _groups(d, bb)
                vrd = [("Vt", d, blk // 4), ("Vone", d)] + ([("Vt", d, (blk - 1) // 4)] if bb > 0 else [])
                for h in range(2):
                    if bb > 0:
                        S.op("pe", lambda e, s_=s_, o_=o_, h=h, blk=blk, d=d: e.matmul(Op[o_][0:65, h * 128:(h + 1) * 128],
                                                                                      lhsT=Vt[d][:, blk - 1, h, :], rhs=Pt[s_][:, h, 0:128],
                                                                                      start=True, stop=False),
                             reads=[("P", s_)] + vrd, writes=[("O", o_)])
                    S.op("pe", lambda e, s_=s_, o_=o_, h=h, blk=blk, d=d, bb=bb: e.matmul(Op[o_][0:65, h * 128:(h + 1) * 128],
                                                                                         lhsT=Vt[d][:, blk, h, :], rhs=Pt[s_][:, h, 128:256],
                                                                                         start=(bb == 0), stop=True),
                         reads=[("P", s_)] + vrd, writes=[("O", o_)])
                akeys = [("acc", x, r % 4) for x in gq] if d > 1 else [("acc", gq[0], x) for x in range(4)]
                o_view = Op[o_][0:65, 0:256].rearrange("p (h q) -> p h q", h=2)
                if pi == 0:
                    S.op("dve", lambda e, tq=tq, o_view=o_view: e.tensor_copy(out=Acc[:, :, tq], in_=o_view),
                         reads=[("O", o_)], writes=akeys)
                else:
                    S.op("dve", lambda e, tq=tq, o_view=o_view: e.tensor_tensor(out=Acc[:, :, tq], in0=Acc[:, :, tq], in1=o_view, op=ALU.add),
                         reads=[("O", o_)] + akeys, writes=akeys)

            LA = NSB - 1
            for i in range(min(LA, len(steps))):
                part1(steps[i])
            for i in range(len(steps)):
                if i + LA < len(steps):
                    part1(steps[i + LA])
                part2(steps[i])
            for h in range(2):
                for n in range(8):
                    ts_ = slice(n * 512, (n + 1) * 512)
                    b = fj % 2
                    fj += 1
                    ijl = banks[pj % len(banks)]
                    pj += 1
                    ijz = banks[pj % len(banks)]
                    pj += 1
                    acc_rd = [("acc", n, x) for x in range(4)]
                    S.op("pe", lambda e, ijl=ijl, h=h, ts_=ts_: e.matmul(bk(ijl)[0:64, :], lhsT=g.cf[0:65, 5, 0:64], rhs=Acc[0:65, h, ts_],
                                                                         start=True, stop=True),
                         reads=acc_rd, writes=[bkey(ijl)])
                    for ch in range(8):
                        S.op("pe", lambda e, ijz=ijz, ch=ch, h=h, ts_=ts_, wt=wt: e.matmul(bk(ijz)[0:64, :], lhsT=wt["z"][:, ch, 64 * h:64 * h + 64],
                                                                                          rhs=g.hT[:, ch, ts_], start=(ch == 0), stop=(ch == 7)),
                             reads=[wk("z")], writes=[bkey(ijz)])
                    if sp["sink"] is not None:
                        S.op("act", lambda e, b=b, ijl=ijl, h=h, sk=sk: e.activation(out=Ut[b][:], in_=bk(ijl)[0:64, :], func=AF.Ln,
                                                                                     bias=sk[:, h:h + 1]),
                             reads=[bkey(ijl), ("sk", si % 2)], writes=[("U", b)])
                    else:
                        S.op("act", lambda e, b=b, ijl=ijl: e.activation(out=Ut[b][:], in_=bk(ijl)[0:64, :], func=AF.Ln),
                             reads=[bkey(ijl)], writes=[("U", b)])
                    S.op("act", lambda e, b=b, ijz=ijz: e.activation(out=Tt[b][:], in_=bk(ijz)[0:64, :], func=AF.Exp, scale=-1.0),
                         reads=[bkey(ijz)], writes=[("T", b)])
                    S.op("act", lambda e, b=b: e.activation(out=Tt[b][:], in_=Tt[b][:], func=AF.Ln, bias=1.0),
                         reads=[("T", b)], writes=[("T", b)])
                    S.op("pool", lambda e, b=b: e.tensor_tensor(out=Ut[b][:], in0=Ut[b][:], in1=Tt[b][:], op=ALU.add),
                         reads=[("U", b), ("T", b)], writes=[("U", b)])
                    S.op("act", lambda e, b=b: e.activation(out=Ut[b][:], in_=Ut[b][:], func=AF.Exp, scale=-1.0),
                         reads=[("U", b)], writes=[("U", b)])
                    S.op("dve", lambda e, b=b, h=h, ts_=ts_: e.tensor_tensor(out=Ut[b][:], in0=Acc[0:64, h, ts_], in1=Ut[b][:], op=ALU.mult),
                         reads=[("U", b)] + acc_rd, writes=[("U", b)])
                    S.op("dve", lambda e, b=b, ijz=ijz: e.tensor_tensor(out=Yb[b][:], in0=Ut[b][:], in1=bk(ijz)[0:64, :], op=ALU.mult),
                         reads=[("U", b), bkey(ijz)], writes=[("Y", b)])
                    row0 = sp["rows"][h]
                    S.op("sp", lambda e, b=b, row0=row0, ts_=ts_: e.dma_start(out=g.ycat[row0:row0 + 64, ts_], in_=Yb[b][:]),
                         reads=[("Y", b)], dma_key=("aY", b))
        S.emit_block()


def phase_ssd(g, L, grp):
    nc, S = g.nc, g.S
    NW = 1296
    zc0, xc0, bc0, cc0, dc0 = 2048 + grp * 512, 3072 + grp * 512, 4096 + grp * 128, 4352 + grp * 128, 4608 + grp * 8
    with contextlib.ExitStack() as st:
        sbt = lambda name, shape, dt: st.enter_context(g.sbuf(name, shape, dt))
        pst = lambda name, shape, dt: st.enter_context(g.psum(name, shape, dt))
        wB = sbt("wB", [128, 8, NW], BF16)
        wst = [sbt("bwst%d" % i, [128, 8, 128], F32) for i in range(2)]
        prm_r = sbt("prm_r", [36, 128], F32)
        prm = sbt("prm", [128, 36], F32)
        hp = sbt("hp", [128, 3, 8], F32)
        pre = [sbt("pre%d" % i, [128, 515], F32) for i in range(2)]
        halo = sbt("halo", [128, 6, 3], F32)
        cacc = [sbt("cacc%d" % i, [128, 512], F32) for i in range(2)]
        ctmp = cacc[0]
        xc = [sbt("xc%d" % i, [128, 4, 512], BF16) for i in range(2)]
        fs = [sbt("fs%d" % i, [128, 6, 4, 8], F32) for i in range(2)]
        BT = [sbt("BTt%d" % i, [128, 512], BF16) for i in range(2)]
        CT = [sbt("CTt%d" % i, [128, 512], BF16) for i in range(2)]
        szs = [sbt("szs%d" % i, [128, 4, 512], F32) for i in range(2)]
        dts = [sbt("dts%d" % i, [128, 3, 4, 8], F32) for i in range(2)]
        NQ = 3
        xtm = [sbt("xtm%d" % i, [128, 512], BF16) for i in range(NQ)]
        Btm = [sbt("Btm%d" % i, [128, 128], BF16) for i in range(NQ)]
        xdt = [sbt("xdt%d" % i, [128, 512], BF16) for i in range(NQ)]
        xdte = [sbt("xdte%d" % i, [128, 512], BF16) for i in range(NQ)]
        tD = [sbt("tD%d" % i, [128, 512], BF16) for i in range(NQ)]
        MT = [sbt("MT%d" % i, [128, 8, 128], BF16) for i in range(NQ)]
        dAh = [sbt("dAh%d" % i, [128, 4, 8], BF16) for i in range(2)]
        dAl = [sbt("dAl%d" % i, [128, 4, 8], BF16) for i in range(2)]
        dAr = sbt("dAr", [128, 4, 8], F32)
        rhsH = sbt("rhsH", [128, 8, 128], BF16)
        rhsL = sbt("rhsL", [128, 8, 128], BF16)
        LT = sbt("LT", [128, 8, 128], BF16)
        GTm = sbt("GTm", [128, 128], F32)
        yt = sbt("yt", [128, 512], F32)
        nst = sbt("nst", [128, 4], F32)
        yn = [sbt("yn%d" % i, [128, 512], BF16) for i in range(2)]
        Ybuf = [sbt("Ybuf%d" % i, [128, 4, 512], BF16) for i in range(2)]
        H = sbt("H", [128, 512], F32)
        Hb = sbt("Hb", [128, 512], BF16)
        PA0 = pst("PA0", [128, 512], F32)
        PSEGs = [pst("PSEG%d" % i, [128, 512], F32) for i in range(2)]
        Psm = pst("Psm", [128, 512], F32)
        PBy = pst("PBy", [128, 512], F32)
        PBo = pst("PBo", [128, 512], F32)
        PS2 = pst("PS2", [128, 512], F32)
        Pbf = pst("Pbf", [128, 8, 128], BF16)
        wv_in = g.w_in[L].rearrange("(ch p) n -> p ch n", p=128)

        pieces = [(zc0 + i * 128, 128, i * 128) for i in range(4)] + [(xc0 + i * 128, 128, 512 + i * 128) for i in range(4)]
        pieces += [(bc0, 128, 1024), (cc0, 128, 1152), (dc0, 8, 1280)]
        wball = [("wB", i) for i in range(len(pieces))]
        for i, (c0, cn, o0) in enumerate(pieces):
            b = i % 2
            S.op("sp", lambda e, b=b, c0=c0, cn=cn: e.dma_start(out=wst[b][:, :, 0:cn], in_=wv_in[:, :, c0:c0 + cn]),
                 writes=[("wst", b)], dma_key=("bwst", b))
            ceng = ("act", "dve", "pool")[i % 3]
            if ceng == "act":
                S.op("act", lambda e, b=b, cn=cn, o0=o0: e.activation(out=wB[:, :, o0:o0 + cn], in_=wst[b][:, :, 0:cn], func=AF.Copy),
                     reads=[("wst", b)], writes=[("wB", i)])
            else:
                S.op(ceng, lambda e, b=b, cn=cn, o0=o0: e.tensor_copy(out=wB[:, :, o0:o0 + cn], in_=wst[b][:, :, 0:cn]),
                     reads=[("wst", b)], writes=[("wB", i)])
        S.op("pool", lambda e: e.memset(prm_r[:], 0.0), writes=["prm_r0"])
        cw = g.conv_w[L].rearrange("k (cc p) -> k cc p", p=128)
        cbv = g.conv_b[L:L + 1, :].rearrange("o (cc p) -> (o cc) p", p=128)
        swv = g.ssm_w[L:L + 1, :].rearrange("o (cc p) -> (o cc) p", p=128)
        k_ = 0
        for tap in range(4):
            S.op("sp", lambda e, tap=tap: e.dma_start(out=prm_r[tap * 6:tap * 6 + 4, :], in_=cw[tap, grp * 4:grp * 4 + 4, :]),
                 reads=["prm_r0"], writes=[("prm_r", k_)], dma_key=("prm", k_))
            k_ += 1
            for j, c_ in ((4, 8 + grp), (5, 10 + grp)):
                S.op("sp", lambda e, tap=tap, j=j, c_=c_: e.dma_start(out=prm_r[tap * 6 + j:tap * 6 + j + 1, :], in_=cw[tap, c_:c_ + 1, :]),
                     reads=["prm_r0"], writes=[("prm_r", k_)], dma_key=("prm", k_))
                k_ += 1
        S.op("sp", lambda e: e.dma_start(out=prm_r[24:28, :], in_=cbv[grp * 4:grp * 4 + 4, :]),
             reads=["prm_r0"], writes=[("prm_r", k_)], dma_key=("prm", k_))
        k_ += 1
        for j, c_ in ((28, 8 + grp), (29, 10 + grp)):
            S.op("sp", lambda e, j=j, c_=c_: e.dma_start(out=prm_r[j:j + 1, :], in_=cbv[c_:c_ + 1, :]),
                 reads=["prm_r0"], writes=[("prm_r", k_)], dma_key=("prm", k_))
            k_ += 1
        S.op("sp", lambda e: e.dma_start(out=prm_r[30:34, :], in_=swv[grp * 4:grp * 4 + 4, :]),
             reads=["prm_r0"], writes=[("prm_r", k_)], dma_key=("prm", k_))
        k_ += 1
        S.op("pe", lambda e: e.transpose(PA0[:, 0:36], prm_r[:, :], g.cf[0:36, 0, 0:36]),
             reads=[("prm_r", i) for i in range(k_)], writes=["PA0"])
        S.op("dve", lambda e: e.tensor_copy(out=prm[:], in_=PA0[:, 0:36]), reads=["PA0"], writes=["prm"])
        for i, src_ in enumerate((g.dt_bias, g.a_log, g.d_skip)):
            S.op("sp", lambda e, i=i, src_=src_: e.dma_start(out=hp[:, i, :], in_=src_[L:L + 1, grp * 8:grp * 8 + 8].partition_broadcast(128)),
                 writes=[("hp", i)], dma_key=("hp", i))
        S.op("act", lambda e: e.activation(out=hp[:, 1, :], in_=hp[:, 1, :], func=AF.Exp), reads=[("hp", 1)], writes=[("hp", 1)])
        S.op("dve", lambda e: e.tensor_scalar(out=hp[:, 1, :], in0=hp[:, 1, :], scalar1=-1.0, scalar2=None, op0=ALU.mult),
             reads=[("hp", 1)], writes=[("hp", 1)])
        S.op("pool", lambda e: e.memset(halo[:], 0.0), writes=["halo"])
        S.op("pool", lambda e: e.memset(H[:], 0.0), writes=["H"])
        S.op("pool", lambda e: e.memset(Hb[:], 0.0), writes=["Hb"])

        tri = g.cf[:, 2, :]
        u1 = g.cf[:, 3, :]
        ones = g.cf[:, 4, :]
        identf = g.cf[:, 0, :]
        v8 = lambda ap: ap.rearrange("p (h d) -> p h d", h=8)
        b8 = lambda ap: ap.unsqueeze(2).to_broadcast([128, 8, 64])

        def mm_group(out_ap, lhs_fn, rhs_fn, reads, writes):
            def fn(e):
                ins = None
                for ch in range(8):
                    ins = e.matmul(out_ap, lhsT=lhs_fn(ch), rhs=rhs_fn(ch), start=(ch == 0), stop=(ch == 7))
                return ins
            S.op("pe", fn, reads=reads, writes=writes)

        fbanks = [(PA0, "PA0"), (PBy, "PBy"), (PBo, "PBo"), (PS2, "PS2")]

        def front(sc):
            p = sc % 2
            ts_ = slice(sc * 512, (sc + 1) * 512)
            def f_proj(j):
                pb = j % 2
                wofs = 512 + j * 128 if j < 4 else (1024 if j == 4 else 1152)
                fb, fk = fbanks[j % 4]
                mm_group(fb[:], lambda ch, wofs=wofs: wB[:, ch, wofs:wofs + 128], lambda ch: g.hT[:, ch, ts_], wball, [fk])
                S.op("pool", lambda e, pb=pb, j=j: e.tensor_copy(out=pre[pb][:, 0:3], in_=halo[:, j, :]),
                     reads=["halo"], writes=[("pre", pb)])
                S.op("act", lambda e, pb=pb, fb=fb: e.activation(out=pre[pb][:, 3:515], in_=fb[:], func=AF.Copy),
                     reads=[fk], writes=[("pre", pb)])
                S.op("pool", lambda e, pb=pb, j=j: e.tensor_copy(out=halo[:, j, :], in_=pre[pb][:, 512:515]),
                     reads=[("pre", pb)], writes=["halo"])

            def f_conv(j):
                pb = j % 2
                if False:
                    S.op("pool", lambda e, pb=pb, j=j: e.tensor_scalar(out=cacc[pb][:], in0=pre[pb][:, 0:512], scalar1=prm[:, j:j + 1], scalar2=None,
                                                                       op0=ALU.mult),
                         reads=[("pre", pb), "prm"], writes=[("cacc", pb)])
                    for tap in range(1, 4):
                        S.op("pool", lambda e, pb=pb, j=j, tap=tap: e.tensor_scalar(out=ctmp[:], in0=pre[pb][:, tap:tap + 512],
                                                                                    scalar1=prm[:, tap * 6 + j:tap * 6 + j + 1], scalar2=None,
                                                                                    op0=ALU.mult),
                             reads=[("pre", pb), "prm"], writes=["ctmp"])
                        S.op("pool", lambda e, pb=pb: e.tensor_tensor(out=cacc[pb][:], in0=cacc[pb][:], in1=ctmp[:], op=ALU.add),
                             reads=["ctmp", ("cacc", pb)], writes=[("cacc", pb)])
                else:
                    S.op("dve", lambda e, pb=pb, j=j: e.tensor_scalar(out=cacc[pb][:], in0=pre[pb][:, 0:512], scalar1=prm[:, j:j + 1], scalar2=None,
                                                                      op0=ALU.mult),
                         reads=[("pre", pb), "prm"], writes=[("cacc", pb)])
                    for tap in range(1, 4):
                        S.op("dve", lambda e, pb=pb, j=j, tap=tap: e.scalar_tensor_tensor(out=cacc[pb][:], in0=pre[pb][:, tap:tap + 512],
                                                                                         scalar=prm[:, tap * 6 + j:tap * 6 + j + 1], in1=cacc[pb][:],
                                                                                         op0=ALU.mult, op1=ALU.add),
                             reads=[("pre", pb), ("cacc", pb), "prm"], writes=[("cacc", pb)])

            def f_silu(j):
                pb = j % 2
                dst = xc[p][:, j, :] if j < 4 else (BT[p][:] if j == 4 else CT[p][:])
                dkey = ("xc", p, j) if j < 4 else (("BT", p) if j == 4 else ("CT", p))
                S.op("act", lambda e, pb=pb, j=j, dst=dst: e.activation(out=dst, in_=cacc[pb][:], func=AF.Silu, bias=prm[:, 24 + j:25 + j]),
                     reads=[("cacc", pb), "prm"], writes=[dkey])

            f_proj(0)
            for j in range(6):
                if j + 1 < 6:
                    f_proj(j + 1)
                f_conv(j)
                f_silu(j)
            for k in range(4):
                tc_ = slice((sc * 4 + k) * 128, (sc * 4 + k + 1) * 128)
                fb, fk = fbanks[(k + 2) % 4]
                mm_group(fb[:], lambda ch, tc_=tc_: g.hT[:, ch, tc_], lambda ch: wB[:, ch, 0:512], wball, [fk])
                S.op("act", lambda e, p=p, k=k, fb=fb: e.activation(out=szs[p][:, k, :], in_=fb[:], func=AF.Silu), reads=[fk], writes=[("sz", p, k)])
            for k in range(4):
                tc_ = slice((sc * 4 + k) * 128, (sc * 4 + k + 1) * 128)
                mm_group(Psm[:, k * 8:(k + 1) * 8], lambda ch, tc_=tc_: g.hT[:, ch, tc_], lambda ch: wB[:, ch, 1280:1288], wball, ["Psm"])
            S.op("dve", lambda e, p=p: e.tensor_tensor(out=dts[p][:, 0, :, :], in0=Psm[:, 0:32].rearrange("p (k h) -> p k h", k=4),
                                                       in1=hp[:, 0, :].unsqueeze(1).to_broadcast([128, 4, 8]), op=ALU.add),
                 reads=["Psm", ("hp", 0)], writes=[("dts", p, 0)])
            S.op("act", lambda e, p=p: e.activation(out=dts[p][:, 0, :, :], in_=dts[p][:, 0, :, :], func=AF.Exp),
                 reads=[("dts", p, 0)], writes=[("dts", p, 0)])
            S.op("act", lambda e, p=p: e.activation(out=dts[p][:, 1, :, :], in_=dts[p][:, 0, :, :], func=AF.Ln, bias=1.0),
                 reads=[("dts", p, 0)], writes=[("dts", p, 1)])
            S.op("dve", lambda e, p=p: e.tensor_tensor(out=dts[p][:, 2, :, :], in0=dts[p][:, 1, :, :],
                                                       in1=hp[:, 1, :].unsqueeze(1).to_broadcast([128, 4, 8]), op=ALU.mult),
                 reads=[("dts", p, 1), ("hp", 1)], writes=[("dts", p, 2)])
            S.op("dve", lambda e, p=p: e.tensor_copy(out=dAh[p][:], in_=dts[p][:, 2, :, :]), reads=[("dts", p, 2)], writes=[("dAh", p)])
            S.op("dve", lambda e, p=p: e.tensor_tensor(out=dAr[:], in0=dts[p][:, 2, :, :], in1=dAh[p][:], op=ALU.subtract),
                 reads=[("dts", p, 2), ("dAh", p)], writes=["dAr"])
            S.op("dve", lambda e, p=p: e.tensor_copy(out=dAl[p][:], in_=dAr[:]), reads=["dAr"], writes=[("dAl", p)])
            for k in range(4):
                S.op("pe", lambda e, p=p, k=k: e.matmul(Psm[:, 64 + k * 8:72 + k * 8], lhsT=tri, rhs=dts[p][:, 2, k, :], start=True, stop=True),
                     reads=[("dts", p, 2)], writes=["Psm"])
                S.op("pe", lambda e, p=p, k=k: e.matmul(Psm[:, 96 + k * 8:104 + k * 8], lhsT=ones, rhs=dts[p][:, 2, k, :], start=True, stop=True),
                     reads=[("dts", p, 2)], writes=["Psm"])
            v48 = lambda ap: ap.rearrange("p (k h) -> p k h", k=4)
            S.op("dve", lambda e, p=p: e.tensor_copy(out=fs[p][:, 0, :, :], in_=v48(Psm[:, 64:96])), reads=["Psm"], writes=[("fs", p, 0)])
            S.op("act", lambda e, p=p: e.activation(out=fs[p][:, 1, :, :], in_=fs[p][:, 0, :, :], func=AF.Exp),
                 reads=[("fs", p, 0)], writes=[("fs", p, 1)])
            S.op("dve", lambda e, p=p: e.tensor_tensor(out=fs[p][:, 2, :, :], in0=v48(Psm[:, 96:128]), in1=fs[p][:, 0, :, :], op=ALU.subtract),
                 reads=["Psm", ("fs", p, 0)], writes=[("fs", p, 2)])
            S.op("act", lambda e, p=p: e.activation(out=fs[p][:, 3, :, :], in_=fs[p][:, 2, :, :], func=AF.Exp),
                 reads=[("fs", p, 2)], writes=[("fs", p, 3)])
            S.op("act", lambda e, p=p: e.activation(out=fs[p][:, 4, :, :], in_=v48(Psm[:, 96:128]), func=AF.Exp),
                 reads=["Psm"], writes=[("fs", p, 4)])
            S.op("dve", lambda e, p=p: e.tensor_tensor(out=fs[p][:, 5, :, :], in0=dts[p][:, 1, :, :], in1=fs[p][:, 3, :, :], op=ALU.mult),
                 reads=[("dts", p, 1), ("fs", p, 3)], writes=[("fs", p, 5)])

        def stageA(c):
            sc, k, q = c // 4, c % 4, c % NQ
            p = sc % 2
            lc = slice(k * 128, (k + 1) * 128)
            dt_ = dts[p][:, 1, k, :]
            dA_ = dts[p][:, 2, k, :]
            trib = g.mask2[:, 128:256]
            S.op("dve", lambda e: e.tensor_tensor(out=rhsH[:], in0=trib.unsqueeze(1).to_broadcast([128, 8, 128]),
                                                  in1=dAh[p][:, k, :].unsqueeze(2).to_broadcast([128, 8, 128]), op=ALU.mult),
                 reads=[("dAh", p)], writes=["rhsH"])
            S.op("dve", lambda e: e.tensor_tensor(out=rhsL[:], in0=trib.unsqueeze(1).to_broadcast([128, 8, 128]),
                                                  in1=dAl[p][:, k, :].unsqueeze(2).to_broadcast([128, 8, 128]), op=ALU.mult),
                 reads=[("dAl", p)], writes=["rhsL"])
            for hf in range(2):
                S.op("pe", lambda e, hf=hf: e.matmul(PSEGs[hf][:], lhsT=g.u1b[:], rhs=rhsH[:, hf * 4:(hf + 1) * 4, :], start=True, stop=False),
                     reads=["rhsH"], writes=[("PSEG", hf)])
                S.op("pe", lambda e, hf=hf: e.matmul(PSEGs[hf][:], lhsT=g.u1b[:], rhs=rhsL[:, hf * 4:(hf + 1) * 4, :], start=False, stop=True),
                     reads=["rhsL"], writes=[("PSEG", hf)])
            for hf in range(2):
                S.op("act", lambda e, hf=hf: e.activation(out=LT[:, hf * 4:(hf + 1) * 4, :], in_=PSEGs[hf][:].rearrange("p (h l) -> p h l", h=4),
                                                          func=AF.Exp),
                     reads=[("PSEG", hf)], writes=[("LT", hf)])
            PA0b = PA0[:].bitcast(BF16)
            def xtr(e):
                ins = None
                for j in range(4):
                    ins = e.transpose(PA0b[:, j * 128:(j + 1) * 128], xc[p][:, j, lc], g.identb[:])
                return ins
            S.op("pe", xtr, reads=[("xc", p, j) for j in range(4)], writes=["PA0"])
            S.op("act", lambda e: e.activation(out=xtm[q][:], in_=PA0b[:, 0:512], func=AF.Copy), reads=["PA0"], writes=[("xtm", q)])
            S.op("pe", lambda e: e.transpose(Pbf[:, 0, :], BT[p][:, lc], g.identb[:]), reads=[("BT", p)], writes=["Pbf"])
            S.op("dve", lambda e: e.tensor_copy(out=Btm[q][:], in_=Pbf[:, 0, :]), reads=["Pbf"], writes=[("Btm", q)])
            S.op("pe", lambda e: e.matmul(Psm[:, 128:256], lhsT=BT[p][:, lc], rhs=CT[p][:, lc], start=True, stop=True),
                 reads=[("BT", p), ("CT", p)], writes=["Psm"])
            S.op("dve", lambda e: e.tensor_tensor(out=GTm[:], in0=Psm[:, 128:256], in1=tri, op=ALU.mult), reads=["Psm"], writes=["GTm"])
            S.op("dve", lambda e: e.tensor_tensor(out=MT[q][:], in0=LT[:], in1=GTm[:].unsqueeze(1).to_broadcast([128, 8, 128]), op=ALU.mult),
                 reads=[("LT", 0), ("LT", 1), "GTm"], writes=[("MT", q)])
            S.op("pool", lambda e: e.tensor_tensor(out=v8(xdt[q][:]), in0=v8(xtm[q][:]), in1=b8(dt_), op=ALU.mult),
                 reads=[("xtm", q), ("dts", p, 1)], writes=[("xdt", q)])
            S.op("pool", lambda e: e.tensor_tensor(out=v8(xdte[q][:]), in0=v8(xtm[q][:]), in1=b8(fs[p][:, 5, k, :]), op=ALU.mult),
                 reads=[("xtm", q), ("fs", p, 5)], writes=[("xdte", q)])
            S.op("pool", lambda e: e.tensor_tensor(out=v8(tD[q][:]), in0=v8(xtm[q][:]), in1=b8(hp[:, 2, :]), op=ALU.mult),
                 reads=[("xtm", q), ("hp", 2)], writes=[("tD", q)])

        def stageB1(c):
            sc, k, q = c // 4, c % 4, c % NQ
            p = sc % 2
            lc = slice(k * 128, (k + 1) * 128)
            S.op("pe", lambda e: e.matmul(PBo[:], lhsT=CT[p][:, lc], rhs=Hb[:], start=True, stop=True),
                 reads=[("CT", p), "Hb"], writes=["PBo"])
            S.op("pe", lambda e: e.matmul(PS2[:], lhsT=Btm[q][:], rhs=xdte[q][:], start=True, stop=True),
                 reads=[("Btm", q), ("xdte", q)], writes=["PS2"])
            S.op("pe", lambda e: e.matmul(PBy[:], lhsT=g.identb[:], rhs=tD[q][:], start=True, stop=False),
                 reads=[("tD", q)], writes=["PBy"])
            for h in range(8):
                S.op("pe", lambda e, h=h: e.matmul(PBy[:, h * 64:(h + 1) * 64], lhsT=MT[q][:, h, :], rhs=xdt[q][:, h * 64:(h + 1) * 64],
                                                   start=False, stop=(h == 7)),
                     reads=[("MT", q), ("xdt", q)], writes=["PBy"])
            S.op("dve", lambda e: e.tensor_tensor(out=v8(H[:]), in0=v8(H[:]), in1=b8(fs[p][:, 4, k, :]), op=ALU.mult),
                 reads=["H", ("fs", p, 4)], writes=["H"])
            S.op("dve", lambda e: e.tensor_tensor(out=H[:], in0=H[:], in1=PS2[:], op=ALU.add), reads=["H", "PS2"], writes=["H"])
            S.op("act", lambda e: e.activation(out=Hb[:], in_=H[:], func=AF.Copy), reads=["H"], writes=["Hb"])
            S.op("dve", lambda e: e.tensor_tensor(out=v8(yt[:]), in0=v8(PBo[:]), in1=b8(fs[p][:, 1, k, :]), op=ALU.mult),
                 reads=["PBo", ("fs", p, 1)], writes=["yt"])
            S.op("dve", lambda e: e.tensor_tensor(out=yt[:], in0=yt[:], in1=PBy[:], op=ALU.add), reads=["yt", "PBy"], writes=["yt"])
            S.op("pool", lambda e: e.tensor_tensor(out=yt[:], in0=yt[:], in1=szs[p][:, k, :], op=ALU.mult),
                 reads=["yt", ("sz", p, k)], writes=["yt"])
            yq = c % 2
            S.op("act", lambda e: e.activation(out=yn[yq][:], in_=yt[:], func=AF.Square, accum_out=nst[:, 0:1]),
                 reads=["yt"], writes=[("yn", yq), ("nst", 0)])
            S.op("act", lambda e: e.activation(out=nst[:, 1:2], in_=nst[:, 0:1], func=AF.Ln, scale=1.0 / 512, bias=EPS),
                 reads=[("nst", 0)], writes=[("nst", 1)])
            S.op("act", lambda e: e.activation(out=nst[:, 2:3], in_=nst[:, 1:2], func=AF.Exp, scale=-0.5),
                 reads=[("nst", 1)], writes=[("nst", 2)])
            S.op("act", lambda e: e.activation(out=yn[yq][:], in_=yt[:], func=AF.Copy, scale=nst[:, 2:3]),
                 reads=["yt", ("nst", 2)], writes=[("yn", yq)])

        def stageB2(c):
            sc, k, q = c // 4, c % 4, c % 2
            p = sc % 2
            lc = slice(k * 128, (k + 1) * 128)
            for j in range(4):
                S.op("pe", lambda e, j=j: e.transpose(Pbf[:, 4 + j, :], yn[q][:, j * 128:(j + 1) * 128], g.identb[:]),
                     reads=[("yn", q)], writes=["Pbf"])
            S.op("dve", lambda e: e.tensor_tensor(out=Ybuf[p][:, :, lc], in0=Pbf[:, 4:8, :],
                                                  in1=prm[:, 30:34].unsqueeze(2).to_broadcast([128, 4, 128]), op=ALU.mult),
                 reads=["Pbf", "prm"], writes=[("Ybuf", p)])
            if k == 3:
                r0 = 512 + grp * 512
                ts_ = slice(sc * 512, (sc + 1) * 512)
                S.op("sp", lambda e: e.dma_start(out=g.ycat[r0:r0 + 512, ts_].rearrange("(cc p) t -> p cc t", p=128), in_=Ybuf[p][:]),
                     reads=[("Ybuf", p)], dma_key=("Ybuf", p))

        NCH = SEQ // 128
        front(0)
        stageA(0)
        stageA(1)
        for c in range(NCH):
            S.capture()
            if c + 3 < NCH and (c + 3) % 4 == 0:
                front((c + 3) // 4)
            lf = S.end_capture()
            S.capture()
            stageB1(c)
            lb1 = S.end_capture()
            S.capture()
            if c + 2 < NCH:
                stageA(c + 2)
            la = S.end_capture()
            S.capture()
            if c >= 1:
                stageB2(c - 1)
            lb2 = S.end_capture()
            S.replay_merged([lf])
            S.replay_merged([lb1, la, lb2])
        stageB2(NCH - 1)
        S.emit_block()


def phase_out(g, L, src):
    nc, S = g.nc, g.S
    with contextlib.ExitStack() as st:
        sbt = lambda name, shape, dt: st.enter_context(g.sbuf(name, shape, dt))
        wo = sbt("wo", [128, 16, DM], BF16)
        wst = [sbt("owst%d" % i, [128, DM], F32) for i in range(3)]
        yc = [sbt("oyc%d" % i, [128, 16, 512], BF16) for i in range(2)]
        xt = [sbt("oxt%d" % i, [128, DM], F32) for i in range(2)]
        t1 = [sbt("ot1%d" % i, [128, DM], F32) for i in range(2)]
        junk = sbt("ojunk", [128, DM], BF16)
        stat = [sbt("ost%d" % i, [128, 4], F32) for i in range(2)]
        PY = [st.enter_context(g.psum("oPY%d" % i, [128, 2, 512], F32)) for i in range(2)]
        for cc in range(16):
            b = cc % 3
            S.op("sp", lambda e, b=b, cc=cc: e.dma_start(out=wst[b][:], in_=g.w_out[L, cc * 128:(cc + 1) * 128, :]),
                 writes=[("wst", b)], dma_key=("owst", b))
            ceng = ("act", "dve", "pool")[cc % 3]
            if ceng == "act":
                S.op("act", lambda e, b=b, cc=cc: e.activation(out=wo[:, cc, :], in_=wst[b][:], func=AF.Copy),
                     reads=[("wst", b)], writes=[("wo", cc)])
            else:
                S.op(ceng, lambda e, b=b, cc=cc: e.tensor_copy(out=wo[:, cc, :], in_=wst[b][:]), reads=[("wst", b)], writes=[("wo", cc)])
        def load_yc(gi):
            gb = gi % 2
            ts_ = slice(gi * 512, (gi + 1) * 512)
            S.op("sp", lambda e: e.dma_start(out=yc[gb][:], in_=g.ycat[:, ts_].rearrange("(cc p) t -> p cc t", p=128)),
                 writes=[("yc", gb)], dma_key=("oyc", gb))

        load_yc(0)
        for gi in range(8):
            gb = gi % 2
            if gi + 1 < 8:
                load_yc(gi + 1)
            for k in range(4):
                t = gi * 4 + k
                b = t % 2
                rows = slice(t * 128, (t + 1) * 128)
                lc = slice(k * 128, (k + 1) * 128)
                S.op("sp", lambda e, b=b, rows=rows: e.dma_start(out=xt[b][:], in_=src[rows, :]), writes=[("x", b)], dma_key=("oxt", b))
                for nh in range(2):
                    for cc in range(16):
                        S.op("pe", lambda e, b=b, nh=nh, cc=cc, gb=gb, lc=lc: e.matmul(PY[b][:, nh, :], lhsT=yc[gb][:, cc, lc],
                                                                                       rhs=wo[:, cc, nh * 512:(nh + 1) * 512],
                                                                                       start=(cc == 0), stop=(cc == 15)),
                             reads=[("yc", gb), ("wo", cc)], writes=[("PY", b)])
                S.op("act", lambda e, b=b: e.activation(out=junk[:].rearrange("p (a n) -> p a n", a=2), in_=PY[b][:], func=AF.Square,
                                                        accum_out=stat[b][:, 0:1]),
                     reads=[("PY", b)], writes=["junk", ("s0", b)])
                S.op("act", lambda e, b=b: e.activation(out=stat[b][:, 1:2], in_=stat[b][:, 0:1], func=AF.Sqrt, scale=1.0 / DM, bias=EPS),
                     reads=[("s0", b)], writes=[("s1", b)])
                S.op("dve", lambda e, b=b: e.reciprocal(out=stat[b][:, 2:3], in_=stat[b][:, 1:2]), reads=[("s1", b)], writes=[("s2", b)])
                S.op("dve", lambda e, b=b: e.scalar_tensor_tensor(out=t1[b][:].rearrange("p (a n) -> p a n", a=2), in0=PY[b][:],
                                                                  scalar=stat[b][:, 2:3],
                                                                  in1=g.modb[:, 2 * DM:3 * DM].rearrange("p (a n) -> p a n", a=2),
                                                                  op0=ALU.mult, op1=ALU.mult),
                     reads=[("PY", b), ("s2", b)], writes=[("t1", b)])
                S.op("pool", lambda e, b=b: e.tensor_tensor(out=t1[b][:], in0=t1[b][:], in1=xt[b][:], op=ALU.add),
                     reads=[("t1", b), ("x", b)], writes=[("t1", b)])
                S.op("sp", lambda e, b=b, rows=rows: e.dma_start(out=g.out[rows, :], in_=t1[b][:]), reads=[("t1", b)], dma_key=("ot1", b))
        S.emit_block()


def _consts():
    i = np.arange(128)
    c = np.zeros((128, 7, 128), np.float32)
    c[:, 0, :] = np.eye(128)
    c[:, 1, :] = (i[:, None] >= i[None, :])
    c[:, 2, :] = (i[:, None] <= i[None, :])
    c[:, 3, :] = (i[:, None] > i[None, :])
    c[:, 4, :] = 1.0
    c[64, 5, 0:64] = 1.0
    return c


_PROG = {}
FUSED = True
_WNAMES = ("ada_w", "ada_b", "pre_norm_w", "post_norm_w", "w_in", "conv_w", "conv_b", "dt_bias", "a_log",
           "d_skip", "ssm_norm_w", "sinks", "w_out")


def kernel(x, c, ada_w, ada_b, pre_norm_w, post_norm_w, w_in, conv_w, conv_b,
           dt_bias, a_log, d_skip, ssm_norm_w, sinks, w_out):
    f = lambda a: np.ascontiguousarray(np.asarray(a, dtype=np.float32))
    ws = dict(ada_w=f(ada_w), ada_b=f(ada_b), pre_norm_w=f(pre_norm_w), post_norm_w=f(post_norm_w),
              w_in=f(w_in), conv_w=f(conv_w), conv_b=f(conv_b), dt_bias=f(dt_bias), a_log=f(a_log),
              d_skip=f(d_skip), ssm_norm_w=f(ssm_norm_w), sinks=f(sinks), w_out=f(w_out))
    x = f(x)
    c = f(c)
    consts = _consts()
    ccols = [np.ascontiguousarray(c[b].reshape(8, 128).T) for b in range(8)]
    if FUSED:
        if "fused" not in _PROG:
            _PROG["fused"] = build(n_layers=DEPTH, depth_dim=DEPTH)
        in_maps = []
        for b in range(8):
            m = dict(ws)
            m["consts"] = consts
            m["x"] = x[b]
            m["c_col"] = ccols[b]
            in_maps.append(m)
        res = run_bass_kernel_spmd(_PROG["fused"], in_maps, core_ids=list(range(8)))
        return np.stack([np.asarray(r["out"]) for r in res.results], axis=0).astype(np.float32)
    if "layer" not in _PROG:
        _PROG["layer"] = build(n_layers=1, depth_dim=1)
    cur = [x[b] for b in range(8)]
    for L in range(DEPTH):
        wl = {k: np.ascontiguousarray(ws[k][L:L + 1]) for k in _WNAMES}
        in_maps = []
        for b in range(8):
            m = dict(wl)
            m["consts"] = consts
            m["x"] = cur[b]
            m["c_col"] = ccols[b]
            in_maps.append(m)
        res = run_bass_kernel_spmd(_PROG["layer"], in_maps, core_ids=list(range(8)))
        cur = [np.ascontiguousarray(np.asarray(r["out"], dtype=np.float32)) for r in res.results]
    return np.stack(cur, axis=0).astype(np.float32)
```

```python
import contextlib
import numpy as np
import concourse.bass as bass
import concourse.mybir as mybir
from concourse.bass_utils import run_bass_kernel_spmd

F32 = mybir.dt.float32
BF16 = mybir.dt.bfloat16
AF = mybir.ActivationFunctionType
ALU = mybir.AluOpType

SEQ = 4096
DM = 1024
NT = SEQ // 128
EPS = 1e-6
DEPTH = 4
IN_COLS = 5904
C_OFF = 4624
ENGS = ("pe", "act", "dve", "pool", "sp")


def sl(start, n, step=1):
    return slice(start, start + step * (n - 1) + 1, step)


class Sched:
    def __init__(self, nc, stack):
        self.nc = nc
        self.stack = stack
        self.esem = {e: stack.enter_context(nc.semaphore("s_" + e)) for e in ENGS}
        self._names = {id(v): "s_" + k for k, v in self.esem.items()}
        self.ecount = {e: 0 for e in ENGS}
        self.dsem = {}
        self.dcount = {}
        self.waited = {e: {} for e in ENGS}
        self.n_ops = 0
        self.reset_block()

    def reset_block(self):
        self.ops = []
        self.last_w = {}
        self.readers = {}

    def _dma_sem(self, key):
        if key not in self.dsem:
            self.dsem[key] = self.stack.enter_context(self.nc.semaphore("d%d" % len(self.dsem)))
            self.dcount[key] = 0
            self._names[id(self.dsem[key])] = "d_" + str(key)
        return self.dsem[key]

    def capture(self):
        self._cap = []
        return self._cap

    def end_capture(self):
        lst, self._cap = self._cap, None
        return lst

    _DUR = {"pe": 0.2, "act": 0.6, "dve": 0.75, "pool": 1.1, "sp": 2.0}

    def replay_merged(self, lists):
        if not hasattr(self, "_sim_eng"):
            self._sim_eng = {e: 0.0 for e in ENGS}
            self._sim_key = {}
        its = [list(l) for l in lists if l]
        pos = [0] * len(its)
        while True:
            best, best_t = None, None
            for i in range(len(its)):
                if pos[i] >= len(its[i]):
                    continue
                eng, fn, reads, writes, dma_key = its[i][pos[i]]
                t = self._sim_eng[eng]
                for k in reads:
                    t = max(t, self._sim_key.get(k, 0.0))
                for k in writes:
                    t = max(t, self._sim_key.get(k, 0.0))
                if best is None or t < best_t - 1e-9:
                    best, best_t = i, t
            if best is None:
                break
            a = its[best][pos[best]]
            pos[best] += 1
            eng, fn, reads, writes, dma_key = a
            fin = best_t + self._DUR[eng]
            self._sim_eng[eng] = fin
            for k in writes:
                self._sim_key[k] = fin
            self.op(*a)

    def op(self, eng, fn, reads=(), writes=(), dma_key=None):
        if getattr(self, "_cap", None) is not None:
            self._cap.append((eng, fn, tuple(reads), tuple(writes), dma_key))
            return None
        idx = len(self.ops)
        deps = set()
        for k in reads:
            w = self.last_w.get(k)
            if w is not None:
                deps.add(w)
        for k in writes:
            w = self.last_w.get(k)
            if w is not None:
                deps.add(w)
            for r in self.readers.get(k, ()):
                deps.add(r)
        deps.discard(idx)
        self.ops.append(dict(eng=eng, fn=fn, deps=deps, dma_key=dma_key, milestone=False))
        for k in reads:
            self.readers.setdefault(k, []).append(idx)
        for k in writes:
            self.last_w[k] = idx
            self.readers[k] = []
        return idx

    def emit_block(self, name=None):
        nc = self.nc
        ops = self.ops
        for o in ops:
            keep = set()
            for d in o["deps"]:
                s = ops[d]
                if s["dma_key"] is None and o["dma_key"] is None and s["eng"] == o["eng"] == "pe":
                    continue
                keep.add(d)
            o["deps"] = keep
            for d in keep:
                if ops[d]["dma_key"] is None:
                    ops[d]["milestone"] = True
        ecount = dict(self.ecount)
        dcount = dict(self.dcount)
        for o in ops:
            if o["dma_key"] is not None:
                self._dma_sem(o["dma_key"])
                dcount[o["dma_key"]] = dcount.get(o["dma_key"], 0) + 16
                o["dval"] = dcount[o["dma_key"]]
            elif o["milestone"]:
                ecount[o["eng"]] += 1
                o["mval"] = ecount[o["eng"]]
        per_eng = {e: [o for o in ops if o["eng"] == e] for e in ENGS}
        final_d = dict(dcount)
        sched = self

        def emit_engine(ename, engine):
            waited = sched.waited[ename]
            for o in per_eng[ename]:
                need = {}
                for d in o["deps"]:
                    s = ops[d]
                    if s["dma_key"] is not None:
                        sem, val = sched.dsem[s["dma_key"]], s["dval"]
                    else:
                        sem, val = sched.esem[s["eng"]], s["mval"]
                    key = sched._names[id(sem)]
                    if val > need.get(key, (None, 0))[1]:
                        need[key] = (sem, val)
                for key, (sem, val) in need.items():
                    if waited.get(key, 0) >= val:
                        continue
                    engine.wait_ge(sem, val)
                    waited[key] = val
                ins = o["fn"](engine)
                if o["dma_key"] is not None:
                    ins.then_inc(sched.dsem[o["dma_key"]], 16)
                elif o["milestone"]:
                    ins.then_inc(sched.esem[ename], 1)
            if ename == "sp":
                for k, v in final_d.items():
                    sem = sched.dsem[k]
                    key = sched._names[id(sem)]
                    if waited.get(key, 0) < v:
                        engine.wait_ge(sem, v)
                        waited[key] = v

        with nc.Block(name) as block:
            @block.tensor
            def _(e):
                emit_engine("pe", e)

            @block.scalar
            def _(e):
                emit_engine("act", e)

            @block.vector
            def _(e):
                emit_engine("dve", e)

            @block.gpsimd
            def _(e):
                emit_engine("pool", e)

            @block.sync
            def _(e):
                emit_engine("sp", e)
        self.ecount = ecount
        self.dcount = dcount
        self.n_ops += len(ops)
        self.reset_block()


class K:
    pass


def dbg(g, name, ap, shape, dtype, reads):
    if not getattr(g, "debug", False):
        return
    import os
    taps = os.environ.get("DBG_TAPS", "")
    if not any(name == t or name.startswith(t + "_") for t in taps.split(",") if t):
        return
    d = g.nc.dram_tensor("dbg_" + name, list(shape), dtype, kind="ExternalOutput").ap()
    idx = tuple(slice(None) for _ in shape)
    g.S.op("sp", lambda e: e.dma_start(out=d[idx], in_=ap), reads=reads, dma_key=("dbg", name))


def build(n_layers=DEPTH, debug=False, phases=("mod", "p1", "att", "ssd", "out"), depth_dim=DEPTH):
    nc = bass.Bass("TRN2", target_bir_lowering=False)

    def din(name, shape):
        return nc.dram_tensor(name, shape, F32, kind="ExternalInput").ap()

    g = K()
    g.nc = nc
    g.uid = [0]
    g.debug = debug

    def _uniq(name):
        g.uid[0] += 1
        return "%s_%d" % (name, g.uid[0])
    g.sbuf = lambda name, shape, dt: nc.sbuf_tensor(_uniq(name), shape, dt)
    g.psum = lambda name, shape, dt: nc.psum_tensor(_uniq(name), shape, dt)
    g.x_in = din("x", [SEQ, DM])
    g.c_col = din("c_col", [128, 8])
    DD = depth_dim
    g.ada_w = din("ada_w", [DD, DM, 3 * DM])
    g.ada_b = din("ada_b", [DD, 3 * DM])
    g.pre_w = din("pre_norm_w", [DD, DM])
    g.post_w = din("post_norm_w", [DD, DM])
    g.w_in = din("w_in", [DD, DM, IN_COLS])
    g.conv_w = din("conv_w", [DD, 4, 1536])
    g.conv_b = din("conv_b", [DD, 1536])
    g.dt_bias = din("dt_bias", [DD, 16])
    g.a_log = din("a_log", [DD, 16])
    g.d_skip = din("d_skip", [DD, 16])
    g.ssm_w = din("ssm_norm_w", [DD, DM])
    g.sinks = din("sinks", [DD, 8])
    g.w_out = din("w_out", [DD, 2 * DM, DM])
    g.consts = din("consts", [128, 7, 128])
    g.out = nc.dram_tensor("out", [SEQ, DM], F32, kind="ExternalOutput").ap()
    g.ycat = nc.dram_tensor("ycat", [2 * DM, SEQ], BF16,
                            kind="ExternalOutput" if debug else "Internal").ap()

    with contextlib.ExitStack() as st:
        S = Sched(nc, st)
        g.S = S
        sb = lambda name, shape, dt: st.enter_context(nc.sbuf_tensor(name, shape, dt))
        g.cf = sb("cf", [128, 7, 128], F32)
        g.identb = sb("identb", [128, 128], BF16)
        g.mask2 = sb("mask2", [128, 256], BF16)
        g.u1b = sb("u1b", [128, 128], BF16)
        g.cbc = sb("cbc", [128, 8, 128], F32)
        g.modb = sb("modb", [128, 3 * DM], F32)
        g.hT = sb("hT", [128, 8, SEQ], BF16)

        phase_init(g)
        for L in range(n_layers):
            src = g.x_in if L == 0 else g.out
            if "mod" in phases:
                phase_mod(g, L)
            if "p1" in phases:
                phase_p1(g, L, src)
            if "att" in phases:
                specs = []
                for hp in range(4):
                    base = hp * 128
                    specs.append(dict(q=[(base, 128)], k=[(512 + base, 128)], v=[(1024 + base, 128)],
                                      z=[(1536 + base, 128)], pats=(1, 4, 16), sink=None,
                                      rows=(base, base + 64)))
                for i in range(4):
                    specs.append(dict(q=[(C_OFF + i * 64, 64), (C_OFF + (4 + i) * 64, 64)],
                                      k=[(C_OFF + 1024, 128)], v=[(C_OFF + 1152, 128)],
                                      z=[(C_OFF + 512 + i * 64, 64), (C_OFF + 512 + (4 + i) * 64, 64)],
                                      pats=(1,), sink=(i, 4 + i), reuse_kv=(i > 0),
                                      rows=(1536 + i * 64, 1536 + (4 + i) * 64)))
                phase_att(g, L, specs)
            if "ssd" in phases:
                for grp in range(2):
                    phase_ssd(g, L, grp)
            if "out" in phases:
                phase_out(g, L, src)
        g.n_ops = S.n_ops
    return nc


def phase_init(g):
    nc, S = g.nc, g.S
    with contextlib.ExitStack() as st:
        cc = st.enter_context(g.sbuf("cc", [128, 8], F32))
        ca = st.enter_context(g.sbuf("ca", [128, 8], F32))
        S.op("sp", lambda e: e.dma_start(out=g.cf[:], in_=g.consts[:, :, :]), writes=["cf"], dma_key="cf")
        S.op("sp", lambda e: e.dma_start(out=cc[:], in_=g.c_col[:, :]), writes=["cc"], dma_key="cc")
        S.op("pool", lambda e: e.tensor_copy(out=g.identb[:], in_=g.cf[:, 0, :]), reads=["cf"], writes=["identb"])
        S.op("pool", lambda e: e.tensor_copy(out=g.mask2[:].rearrange("p (a b) -> p a b", a=2), in_=g.cf[:, 1:3, :]),
             reads=["cf"], writes=["mask2"])
        S.op("pool", lambda e: e.tensor_copy(out=g.u1b[:], in_=g.cf[:, 3, :]), reads=["cf"], writes=["u1b"])
        S.op("act", lambda e: e.activation(out=ca[:], in_=cc[:], func=AF.Silu), reads=["cc"], writes=["ca"])
        S.op("dve", lambda e: e.tensor_copy(out=g.cbc[:], in_=ca[:].unsqueeze(2).to_broadcast([128, 8, 128])),
             reads=["ca"], writes=["cbc"])
        S.emit_block("init")


def phase_mod(g, L):
    nc, S = g.nc, g.S
    with contextlib.ExitStack() as st:
        stage = [st.enter_context(g.sbuf("mstage%d" % i, [128, 8, 512], F32)) for i in range(2)]
        adab = st.enter_context(g.sbuf("adab", [128, 3 * DM], F32))
        pw = st.enter_context(g.sbuf("pw", [128, 2, DM], F32))
        ps = [st.enter_context(g.psum("mps%d" % i, [128, 512], F32)) for i in range(2)]
        S.op("sp", lambda e: e.dma_start(out=adab[:], in_=g.ada_b[L:L + 1, :].partition_broadcast(128)),
             writes=["adab"], dma_key="adab")
        S.op("sp", lambda e: e.dma_start(out=pw[:, 0, :], in_=g.pre_w[L:L + 1, :].partition_broadcast(128)),
             writes=["pw0"], dma_key="pw0")
        S.op("sp", lambda e: e.dma_start(out=pw[:, 1, :], in_=g.post_w[L:L + 1, :].partition_broadcast(128)),
             writes=["pw1"], dma_key="pw1")
        aw = g.ada_w[L].rearrange("(ch p) n -> p ch n", p=128)
        for grp in range(6):
            b = grp % 2
            cs = slice(grp * 512, (grp + 1) * 512)
            S.op("sp", lambda e, b=b, cs=cs: e.dma_start(out=stage[b][:], in_=aw[:, :, cs]),
                 writes=[("mst", b)], dma_key=("mst", b))
            for ch in range(8):
                S.op("pe", lambda e, b=b, ch=ch: e.matmul(ps[b][:], lhsT=g.cbc[:, ch, :], rhs=stage[b][:, ch, :],
                                                          start=(ch == 0), stop=(ch == 7)),
                     reads=[("mst", b), "cbc"], writes=[("mps", b)])
            S.op("dve", lambda e, b=b, cs=cs: e.tensor_tensor(out=g.modb[:, cs], in0=ps[b][:], in1=adab[:, cs], op=ALU.add),
                 reads=[("mps", b), "adab"], writes=["modb"])
        S.op("dve", lambda e: e.scalar_tensor_tensor(out=g.modb[:, DM:2 * DM], in0=g.modb[:, DM:2 * DM], scalar=1.0,
                                                     in1=pw[:, 0, :], op0=ALU.add, op1=ALU.mult),
             reads=["modb", "pw0"], writes=["modb"])
        S.op("dve", lambda e: e.tensor_tensor(out=g.modb[:, 2 * DM:3 * DM], in0=g.modb[:, 2 * DM:3 * DM], in1=pw[:, 1, :],
                                              op=ALU.mult),
             reads=["modb", "pw1"], writes=["modb"])
        S.emit_block("mod%d" % L)


def phase_p1(g, L, src):
    nc, S = g.nc, g.S
    NB = 4
    with contextlib.ExitStack() as st:
        xt = [st.enter_context(g.sbuf("p1x%d" % i, [128, DM], F32)) for i in range(NB)]
        tmp = [st.enter_context(g.sbuf("p1t%d" % i, [128, DM], F32)) for i in range(NB)]
        hb = [st.enter_context(g.sbuf("p1h%d" % i, [128, DM], BF16)) for i in range(NB)]
        junk = st.enter_context(g.sbuf("p1junk", [128, DM], BF16))
        stat = [st.enter_context(g.sbuf("p1s%d" % i, [128, 4], F32)) for i in range(NB)]
        pst = [st.enter_context(g.psum("p1ps%d" % i, [128, 8, 128], BF16)) for i in range(2)]

        def s1(t):
            b = t % NB
            rows = slice(t * 128, (t + 1) * 128)
            S.op("sp", lambda e: e.dma_start(out=xt[b][:], in_=src[rows, :]), writes=[("x", b)], dma_key=("p1x", b))
            S.op("act", lambda e: e.activation(out=junk[:], in_=xt[b][:], func=AF.Square, accum_out=stat[b][:, 0:1]),
                 reads=[("x", b)], writes=["junk", ("s0", b)])
            S.op("act", lambda e: e.activation(out=stat[b][:, 1:2], in_=stat[b][:, 0:1], func=AF.Sqrt, scale=1.0 / DM, bias=EPS),
                 reads=[("s0", b)], writes=[("s1", b)])
            S.op("dve", lambda e: e.reciprocal(out=stat[b][:, 2:3], in_=stat[b][:, 1:2]), reads=[("s1", b)], writes=[("s2", b)])
            S.op("dve", lambda e: e.scalar_tensor_tensor(out=tmp[b][:], in0=xt[b][:], scalar=stat[b][:, 2:3],
                                                         in1=g.modb[:, DM:2 * DM], op0=ALU.mult, op1=ALU.mult),
                 reads=[("x", b), ("s2", b)], writes=[("t", b)])
            S.op("pool" if t % 2 == 0 else "dve",
                 lambda e: e.tensor_tensor(out=hb[b][:], in0=tmp[b][:], in1=g.modb[:, 0:DM], op=ALU.add),
                 reads=[("t", b)], writes=[("h", b)])

        def s2(t):
            b = t % NB
            pb = t % 2
            rows = slice(t * 128, (t + 1) * 128)
            for ch in range(8):
                S.op("pe", lambda e, ch=ch: e.transpose(pst[pb][:, ch, :], hb[b][:, ch * 128:(ch + 1) * 128], g.identb[:]),
                     reads=[("h", b)], writes=[("ps", pb)])
            S.op("act", lambda e: e.activation(out=g.hT[:, :, rows], in_=pst[pb][:], func=AF.Copy),
                 reads=[("ps", pb)], writes=[("hT", t)])

        s1(0)
        s1(1)
        for t in range(NT):
            if t + 2 < NT:
                s1(t + 2)
            s2(t)
        S.emit_block("p1_%d" % L)


def _groups(d, b):
    if d == 1:
        return [b // 4]
    if d == 4:
        return [b]
    return [4 * b + i for i in range(4)]


def phase_att(g, L, specs):
    nc, S = g.nc, g.S
    with contextlib.ExitStack() as st:
        sbt = lambda name, shape, dt: st.enter_context(g.sbuf(name, shape, dt))
        wst = [sbt("awst%d" % i, [128, 8, 128], F32) for i in range(2)]
        wts = [{n: sbt("aw_%s%d" % (n, i), [128, 8, 128], BF16) for n in "qkvz"} for i in range(2)]
        QT = sbt("QT", [128, SEQ], BF16)
        KT = sbt("KT", [128, SEQ], BF16)
        VT = sbt("VTf", [128, SEQ], BF16)
        Vt = {d: sbt("Vt%d" % d, [128, 32, 2, 65], BF16) for d in (1, 4, 16)}
        NSB = 3
        Et = [sbt("E%d" % i, [128, 2, 256], BF16) for i in range(NSB)]
        Pt = [sbt("P%d" % i, [128, 2, 256], BF16) for i in range(NSB)]
        Acc = sbt("Acc", [65, 2, SEQ], F32)
        Ut = [sbt("U%d" % i, [64, 512], F32) for i in range(2)]
        Tt = [sbt("T%d" % i, [64, 512], F32) for i in range(2)]
        Yb = [sbt("Yb%d" % i, [64, 512], BF16) for i in range(2)]
        sks = [sbt("sk%d" % i, [64, 2], F32) for i in range(2)]
        Sp = [st.enter_context(g.psum("aS%d" % i, [128, 2, 512], F32)) for i in range(NSB)]
        Op = [st.enter_context(g.psum("aO%d" % i, [128, 512], F32)) for i in range(2)]
        banks = [(i, j) for i in range(NSB) for j in range(2)]
        bk = lambda ij: Sp[ij[0]][:, ij[1], :]
        bkey = lambda ij: ("Sb", ij[0], ij[1])
        wv_in = g.w_in[L].rearrange("(ch p) n -> p ch n", p=128)
        for d in (1, 4, 16):
            S.op("pool", lambda e, d=d: e.memset(Vt[d][:, :, :, 64:65], 1.0), writes=[("Vone", d)])
        pj = 0
        step = 0
        fj = 0
        def load_weights(si):
            sp = specs[si]
            wt = wts[si % 2]
            sk = sks[si % 2]
            for wi, n in enumerate("qkvz"):
                b = wi % 2
                off = 0
                for pi, (c0, cn) in enumerate(sp[n]):
                    S.op("sp", lambda e, b=b, c0=c0, cn=cn, off=off: e.dma_start(out=wst[b][:, :, off:off + cn],
                                                                               in_=wv_in[:, :, c0:c0 + cn]),
                         writes=[("wst", b, pi)], dma_key=("awst", b, pi))
                    off += cn
                ceng = ("pool", "dve", "pool", "act")[wi]
                if ceng == "act":
                    S.op("act", lambda e, b=b, n=n, wt=wt: e.activation(out=wt[n][:], in_=wst[b][:], func=AF.Copy),
                         reads=[("wst", b, 0), ("wst", b, 1)], writes=[("w", n, si % 2)])
                else:
                    S.op(ceng, lambda e, b=b, n=n, wt=wt: e.tensor_copy(out=wt[n][:], in_=wst[b][:]),
                         reads=[("wst", b, 0), ("wst", b, 1)], writes=[("w", n, si % 2)])
            if sp["sink"] is not None:
                for h in range(2):
                    hh = sp["sink"][h]
                    S.op("sp", lambda e, h=h, hh=hh, sk=sk: e.dma_start(out=sk[:, h:h + 1],
                                                                        in_=g.sinks[L:L + 1, hh:hh + 1].partition_broadcast(64)),
                         writes=[("skr", si % 2, h)], dma_key=("sk", si % 2, h))
                S.op("act", lambda e, sk=sk: e.activation(out=sk[:], in_=sk[:], func=AF.Exp),
                     reads=[("skr", si % 2, 0), ("skr", si % 2, 1)], writes=[("sk", si % 2)])

        load_weights(0)
        for si, sp in enumerate(specs):
            pats = sp["pats"]
            wt = wts[si % 2]
            sk = sks[si % 2]
            wk = lambda n, si=si: ("w", n, si % 2)
            for n in range(8):
                ts_ = slice(n * 512, (n + 1) * 512)
                for nm, eng in (("q", "act"), ("k", "dve"), ("v", "act")):
                    if nm != "q" and sp.get("reuse_kv"):
                        continue
                    ij = banks[pj % len(banks)]
                    pj += 1
                    for ch in range(8):
                        S.op("pe", lambda e, ij=ij, ch=ch, nm=nm, ts_=ts_, wt=wt: e.matmul(bk(ij), lhsT=wt[nm][:, ch, :], rhs=g.hT[:, ch, ts_],
                                                                                         start=(ch == 0), stop=(ch == 7)),
                             reads=[wk(nm)], writes=[bkey(ij)])
                    if nm == "q":
                        S.op("act", lambda e, ij=ij, ts_=ts_: e.activation(out=QT[:, ts_], in_=bk(ij), func=AF.Copy, scale=0.125),
                             reads=[bkey(ij)], writes=[("QT", n)])
                    elif nm == "k":
                        S.op("dve", lambda e, ij=ij, ts_=ts_: e.tensor_copy(out=KT[:, ts_], in_=bk(ij)),
                             reads=[bkey(ij)], writes=[("KT", n)])
                    else:
                        S.op("act", lambda e, ij=ij, ts_=ts_: e.activation(out=VT[:, ts_], in_=bk(ij), func=AF.Copy),
                             reads=[bkey(ij)], writes=[("VT", n)])
            for d in (() if sp.get("reuse_kv") else pats):
                nb = 32 // d
                for b4 in range(8):
                    ij = banks[pj % len(banks)]
                    pj += 1
                    pb16 = lambda ij: bk(ij).bitcast(BF16)
                    for j in range(4):
                        blk = b4 * 4 + j
                        r, bb = blk // nb, blk % nb
                        tok = sl(r + d * 128 * bb, 128, d)
                        S.op("pe", lambda e, ij=ij, j=j, tok=tok: e.transpose(pb16(ij)[:, j * 128:(j + 1) * 128], VT[:, tok], g.identb[:]),
                             reads=[("VT", x) for x in _groups(d, bb)], writes=[bkey(ij)])
                    vsrc = lambda ij: pb16(ij)[:, 0:512].rearrange("p (j h c) -> p j h c", j=4, h=2)
                    if b4 % 2 == 0:
                        S.op("dve", lambda e, ij=ij, b4=b4, d=d: e.tensor_copy(out=Vt[d][:, b4 * 4:(b4 + 1) * 4, :, 0:64], in_=vsrc(ij)),
                             reads=[bkey(ij)], writes=[("Vt", d, b4)])
                    else:
                        S.op("act", lambda e, ij=ij, b4=b4, d=d: e.activation(out=Vt[d][:, b4 * 4:(b4 + 1) * 4, :, 0:64], in_=vsrc(ij),
                                                                             func=AF.Copy),
                             reads=[bkey(ij)], writes=[("Vt", d, b4)])
            if si + 1 < len(specs):
                load_weights(si + 1)
            steps = []
            for pi, d in enumerate(pats):
                nb = 32 // d
                for r in range(d):
                    for bb in range(nb):
                        steps.append(dict(pi=pi, d=d, r=r, bb=bb, nb=nb, s_=step % NSB, o_=step % 2,
                                          meng="dve" if step % 2 == 0 else "pool"))
                        step += 1

            def part1(stp):
                d, r, bb, s_, meng = stp["d"], stp["r"], stp["bb"], stp["s_"], stp["meng"]
                tq = sl(r + d * 128 * bb, 128, d)
                tp = sl(r + d * 128 * (bb - 1), 128, d) if bb > 0 else None
                lo = 0 if bb > 0 else 128
                gq = _groups(d, bb)
                gk = gq + (_groups(d, bb - 1) if bb > 0 else [])
                rd = [("QT", x) for x in gq] + [("KT", x) for x in set(gk)]
                skeys = [("Sb", s_, 0), ("Sb", s_, 1)]
                for h in range(2):
                    hs = slice(64 * h, 64 * h + 64)
                    S.op("pe", lambda e, s_=s_, h=h, hs=hs, tq=tq: e.matmul(Sp[s_][:, h, 128:256], lhsT=KT[hs, tq], rhs=QT[hs, tq],
                                                                           start=True, stop=True),
                         reads=rd, writes=skeys)
                    if bb > 0:
                        S.op("pe", lambda e, s_=s_, h=h, hs=hs, tq=tq, tp=tp: e.matmul(Sp[s_][:, h, 0:128], lhsT=KT[hs, tp],
                                                                                      rhs=QT[hs, tq], start=True, stop=True),
                             reads=rd, writes=skeys)
                S.op("act", lambda e, s_=s_, lo=lo: e.activation(out=Et[s_][:, :, lo:256], in_=Sp[s_][:, :, lo:256], func=AF.Exp),
                     reads=skeys, writes=[("E", s_)])
                S.op(meng, lambda e, s_=s_, lo=lo: e.tensor_tensor(out=Pt[s_][:, :, lo:256], in0=Et[s_][:, :, lo:256],
                                                                  in1=g.mask2[:, lo:256].unsqueeze(1).to_broadcast([128, 2, 256 - lo]),
                                                                  op=ALU.mult),
                     reads=[("E", s_)], writes=[("P", s_)])

            def part2(stp):
                pi, d, r, bb, nb, s_, o_ = stp["pi"], stp["d"], stp["r"], stp["bb"], stp["nb"], stp["s_"], stp["o_"]
                blk = r * nb + bb
                tq = sl(r + d * 128 * bb, 128, d)
                gq = _groups(d, bb)
                vrd = [("Vt", d, blk // 4), ("Vone", d)] + ([("Vt", d, (blk - 1) // 4)] if bb > 0 else [])
                for h in range(2):
                    if bb > 0:
                        S.op("pe", lambda e, s_=s_, o_=o_, h=h, blk=blk, d=d: e.matmul(Op[o_][0:65, h * 128:(h + 1) * 128],
                                                                                      lhsT=Vt[d][:, blk - 1, h, :], rhs=Pt[s_][:, h, 0:128],
                                                                                      start=True, stop=False),
                             reads=[("P", s_)] + vrd, writes=[("O", o_)])
                    S.op("pe", lambda e, s_=s_, o_=o_, h=h, blk=blk, d=d, bb=bb: e.matmul(Op[o_][0:65, h * 128:(h + 1) * 128],
                                                                                         lhsT=Vt[d][:, blk, h, :], rhs=Pt[s_][:, h, 128:256],
                                                                                         start=(bb == 0), stop=True),
                         reads=[("P", s_)] + vrd, writes=[("O", o_)])
                akeys = [("acc", x, r % 4) for x in gq] if d > 1 else [("acc", gq[0], x) for x in range(4)]
                o_view = Op[o_][0:65, 0:256].rearrange("p (h q) -> p h q", h=2)
                if pi == 0:
                    S.op("dve", lambda e, tq=tq, o_view=o_view: e.tensor_copy(out=Acc[:, :, tq], in_=o_view),
                         reads=[("O", o_)], writes=akeys)
                else:
                    S.op("dve", lambda e, tq=tq, o_view=o_view: e.tensor_tensor(out=Acc[:, :, tq], in0=Acc[:, :, tq], in1=o_view, op=ALU.add),
                         reads=[("O", o_)] + akeys, writes=akeys)

            LA = NSB - 1
            for i in range(min(LA, len(steps))):
                part1(steps[i])
            for i in range(len(steps)):
                if i + LA < len(steps):
                    part1(steps[i + LA])
                part2(steps[i])
            for h in range(2):
                for n in range(8):
                    ts_ = slice(n * 512, (n + 1) * 512)
                    b = fj % 2
                    fj += 1
                    ijl = banks[pj % len(banks)]
                    pj += 1
                    ijz = banks[pj % len(banks)]
                    pj += 1
                    acc_rd = [("acc", n, x) for x in range(4)]
                    S.op("pe", lambda e, ijl=ijl, h=h, ts_=ts_: e.matmul(bk(ijl)[0:64, :], lhsT=g.cf[0:65, 5, 0:64], rhs=Acc[0:65, h, ts_],
                                                                         start=True, stop=True),
                         reads=acc_rd, writes=[bkey(ijl)])
                    for ch in range(8):
                        S.op("pe", lambda e, ijz=ijz, ch=ch, h=h, ts_=ts_, wt=wt: e.matmul(bk(ijz)[0:64, :], lhsT=wt["z"][:, ch, 64 * h:64 * h + 64],
                                                                                          rhs=g.hT[:, ch, ts_], start=(ch == 0), stop=(ch == 7)),
                             reads=[wk("z")], writes=[bkey(ijz)])
                    if sp["sink"] is not None:
                        S.op("act", lambda e, b=b, ijl=ijl, h=h, sk=sk: e.activation(out=Ut[b][:], in_=bk(ijl)[0:64, :], func=AF.Ln,
                                                                                     bias=sk[:, h:h + 1]),
                             reads=[bkey(ijl), ("sk", si % 2)], writes=[("U", b)])
                    else:
                        S.op("act", lambda e, b=b, ijl=ijl: e.activation(out=Ut[b][:], in_=bk(ijl)[0:64, :], func=AF.Ln),
                             reads=[bkey(ijl)], writes=[("U", b)])
                    S.op("act", lambda e, b=b, ijz=ijz: e.activation(out=Tt[b][:], in_=bk(ijz)[0:64, :], func=AF.Exp, scale=-1.0),
                         reads=[bkey(ijz)], writes=[("T", b)])
                    S.op("act", lambda e, b=b: e.activation(out=Tt[b][:], in_=Tt[b][:], func=AF.Ln, bias=1.0),
                         reads=[("T", b)], writes=[("T", b)])
                    S.op("pool", lambda e, b=b: e.tensor_tensor(out=Ut[b][:], in0=Ut[b][:], in1=Tt[b][:], op=ALU.add),
                         reads=[("U", b), ("T", b)], writes=[("U", b)])
                    S.op("act", lambda e, b=b: e.activation(out=Ut[b][:], in_=Ut[b][:], func=AF.Exp, scale=-1.0),
                         reads=[("U", b)], writes=[("U", b)])
                    S.op("dve", lambda e, b=b, h=h, ts_=ts_: e.tensor_tensor(out=Ut[b][:], in0=Acc[0:64, h, ts_], in1=Ut[b][:], op=ALU.mult),
                         reads=[("U", b)] + acc_rd, writes=[("U", b)])
                    S.op("dve", lambda e, b=b, ijz=ijz: e.tensor_tensor(out=Yb[b][:], in0=Ut[b][:], in1=bk(ijz)[0:64, :], op=ALU.mult),
                         reads=[("U", b), bkey(ijz)], writes=[("Y", b)])
                    row0 = sp["rows"][h]
                    S.op("sp", lambda e, b=b, row0=row0, ts_=ts_: e.dma_start(out=g.ycat[row0:row0 + 64, ts_], in_=Yb[b][:]),
                         reads=[("Y", b)], dma_key=("aY", b))
        S.emit_block()


def phase_ssd(g, L, grp):
    nc, S = g.nc, g.S
    NW = 1296
    zc0, xc0, bc0, cc0, dc0 = 2048 + grp * 512, 3072 + grp * 512, 4096 + grp * 128, 4352 + grp * 128, 4608 + grp * 8
    with contextlib.ExitStack() as st:
        sbt = lambda name, shape, dt: st.enter_context(g.sbuf(name, shape, dt))
        pst = lambda name, shape, dt: st.enter_context(g.psum(name, shape, dt))
        wB = sbt("wB", [128, 8, NW], BF16)
        wst = [sbt("bwst%d" % i, [128, 8, 128], F32) for i in range(2)]
        prm_r = sbt("prm_r", [36, 128], F32)
        prm = sbt("prm", [128, 36], F32)
        hp = sbt("hp", [128, 3, 8], F32)
        pre = [sbt("pre%d" % i, [128, 515], F32) for i in range(2)]
        halo = sbt("halo", [128, 6, 3], F32)
        cacc = [sbt("cacc%d" % i, [128, 512], F32) for i in range(2)]
        ctmp = cacc[0]
        xc = [sbt("xc%d" % i, [128, 4, 512], BF16) for i in range(2)]
        fs = [sbt("fs%d" % i, [128, 6, 4, 8], F32) for i in range(2)]
        BT = [sbt("BTt%d" % i, [128, 512], BF16) for i in range(2)]
        CT = [sbt("CTt%d" % i, [128, 512], BF16) for i in range(2)]
        szs = [sbt("szs%d" % i, [128, 4, 512], F32) for i in range(2)]
        dts = [sbt("dts%d" % i, [128, 3, 4, 8], F32) for i in range(2)]
        NQ = 3
        xtm = [sbt("xtm%d" % i, [128, 512], BF16) for i in range(NQ)]
        Btm = [sbt("Btm%d" % i, [128, 128], BF16) for i in range(NQ)]
        xdt = [sbt("xdt%d" % i, [128, 512], BF16) for i in range(NQ)]
        xdte = [sbt("xdte%d" % i, [128, 512], BF16) for i in range(NQ)]
        tD = [sbt("tD%d" % i, [128, 512], BF16) for i in range(NQ)]
        MT = [sbt("MT%d" % i, [128, 8, 128], BF16) for i in range(NQ)]
        dAh = [sbt("dAh%d" % i, [128, 4, 8], BF16) for i in range(2)]
        dAl = [sbt("dAl%d" % i, [128, 4, 8], BF16) for i in range(2)]
        dAr = sbt("dAr", [128, 4, 8], F32)
        rhsH = sbt("rhsH", [128, 8, 128], BF16)
        rhsL = sbt("rhsL", [128, 8, 128], BF16)
        LT = sbt("LT", [128, 8, 128], BF16)
        GTm = sbt("GTm", [128, 128], F32)
        yt = sbt("yt", [128, 512], F32)
        nst = sbt("nst", [128, 4], F32)
        yn = [sbt("yn%d" % i, [128, 512], BF16) for i in range(2)]
        Ybuf = [sbt("Ybuf%d" % i, [128, 4, 512], BF16) for i in range(2)]
        H = sbt("H", [128, 512], F32)
        Hb = sbt("Hb", [128, 512], BF16)
        PA0 = pst("PA0", [128, 512], F32)
        PSEGs = [pst("PSEG%d" % i, [128, 512], F32) for i in range(2)]
        Psm = pst("Psm", [128, 512], F32)
        PBy = pst("PBy", [128, 512], F32)
        PBo = pst("PBo", [128, 512], F32)
        PS2 = pst("PS2", [128, 512], F32)
        Pbf = pst("Pbf", [128, 8, 128], BF16)
        wv_in = g.w_in[L].rearrange("(ch p) n -> p ch n", p=128)

        pieces = [(xc0 + i * 128, 128, 512 + i * 128) for i in range(4)] + [(bc0, 128, 1024), (cc0, 128, 1152)]
        pieces += [(zc0 + i * 128, 128, i * 128) for i in range(4)] + [(dc0, 8, 1280)]
        wball = [("wB", i) for i in range(len(pieces))]
        for i, (c0, cn, o0) in enumerate(pieces):
            b = i % 2
            S.op("sp", lambda e, b=b, c0=c0, cn=cn: e.dma_start(out=wst[b][:, :, 0:cn], in_=wv_in[:, :, c0:c0 + cn]),
                 writes=[("wst", b)], dma_key=("bwst", b))
            ceng = ("act", "dve")[i % 2]
            if ceng == "act":
                S.op("act", lambda e, b=b, cn=cn, o0=o0: e.activation(out=wB[:, :, o0:o0 + cn], in_=wst[b][:, :, 0:cn], func=AF.Copy),
                     reads=[("wst", b)], writes=[("wB", i)])
            else:
                S.op(ceng, lambda e, b=b, cn=cn, o0=o0: e.tensor_copy(out=wB[:, :, o0:o0 + cn], in_=wst[b][:, :, 0:cn]),
                     reads=[("wst", b)], writes=[("wB", i)])
        S.op("pool", lambda e: e.memset(prm_r[:], 0.0), writes=["prm_r0"])
        cw = g.conv_w[L].rearrange("k (cc p) -> k cc p", p=128)
        cbv = g.conv_b[L:L + 1, :].rearrange("o (cc p) -> (o cc) p", p=128)
        swv = g.ssm_w[L:L + 1, :].rearrange("o (cc p) -> (o cc) p", p=128)
        k_ = 0
        for tap in range(4):
            S.op("sp", lambda e, tap=tap: e.dma_start(out=prm_r[tap * 6:tap * 6 + 4, :], in_=cw[tap, grp * 4:grp * 4 + 4, :]),
                 reads=["prm_r0"], writes=[("prm_r", k_)], dma_key=("prm", k_))
            k_ += 1
            for j, c_ in ((4, 8 + grp), (5, 10 + grp)):
                S.op("sp", lambda e, tap=tap, j=j, c_=c_: e.dma_start(out=prm_r[tap * 6 + j:tap * 6 + j + 1, :], in_=cw[tap, c_:c_ + 1, :]),
                     reads=["prm_r0"], writes=[("prm_r", k_)], dma_key=("prm", k_))
                k_ += 1
        S.op("sp", lambda e: e.dma_start(out=prm_r[24:28, :], in_=cbv[grp * 4:grp * 4 + 4, :]),
             reads=["prm_r0"], writes=[("prm_r", k_)], dma_key=("prm", k_))
        k_ += 1
        for j, c_ in ((28, 8 + grp), (29, 10 + grp)):
            S.op("sp", lambda e, j=j, c_=c_: e.dma_start(out=prm_r[j:j + 1, :], in_=cbv[c_:c_ + 1, :]),
                 reads=["prm_r0"], writes=[("prm_r", k_)], dma_key=("prm", k_))
            k_ += 1
        S.op("sp", lambda e: e.dma_start(out=prm_r[30:34, :], in_=swv[grp * 4:grp * 4 + 4, :]),
             reads=["prm_r0"], writes=[("prm_r", k_)], dma_key=("prm", k_))
        k_ += 1
        S.op("pe", lambda e: e.transpose(PA0[:, 0:36], prm_r[:, :], g.cf[0:36, 0, 0:36]),
             reads=[("prm_r", i) for i in range(k_)], writes=["PA0"])
        S.op("dve", lambda e: e.tensor_copy(out=prm[:], in_=PA0[:, 0:36]), reads=["PA0"], writes=["prm"])
        for i, src_ in enumerate((g.dt_bias, g.a_log, g.d_skip)):
            S.op("sp", lambda e, i=i, src_=src_: e.dma_start(out=hp[:, i, :], in_=src_[L:L + 1, grp * 8:grp * 8 + 8].partition_broadcast(128)),
                 writes=[("hp", i)], dma_key=("hp", i))
        S.op("act", lambda e: e.activation(out=hp[:, 1, :], in_=hp[:, 1, :], func=AF.Exp), reads=[("hp", 1)], writes=[("hp", 1)])
        S.op("dve", lambda e: e.tensor_scalar(out=hp[:, 1, :], in0=hp[:, 1, :], scalar1=-1.0, scalar2=None, op0=ALU.mult),
             reads=[("hp", 1)], writes=[("hp", 1)])
        S.op("pool", lambda e: e.memset(halo[:], 0.0), writes=["halo"])
        S.op("pool", lambda e: e.memset(H[:], 0.0), writes=["H"])
        S.op("pool", lambda e: e.memset(Hb[:], 0.0), writes=["Hb"])

        tri = g.cf[:, 2, :]
        u1 = g.cf[:, 3, :]
        ones = g.cf[:, 4, :]
        identf = g.cf[:, 0, :]
        v8 = lambda ap: ap.rearrange("p (h d) -> p h d", h=8)
        b8 = lambda ap: ap.unsqueeze(2).to_broadcast([128, 8, 64])

        def mm_group(out_ap, lhs_fn, rhs_fn, reads, writes):
            def fn(e):
                ins = None
                for ch in range(8):
                    ins = e.matmul(out_ap, lhsT=lhs_fn(ch), rhs=rhs_fn(ch), start=(ch == 0), stop=(ch == 7))
                return ins
            S.op("pe", fn, reads=reads, writes=writes)

        fbanks = [(PA0, "PA0"), (PBy, "PBy"), (PBo, "PBo"), (PS2, "PS2")]

        def front(sc):
            p = sc % 2
            ts_ = slice(sc * 512, (sc + 1) * 512)
            def f_proj(j):
                pb = j % 2
                wofs = 512 + j * 128 if j < 4 else (1024 if j == 4 else 1152)
                fb, fk = fbanks[j % 4]
                mm_group(fb[:], lambda ch, wofs=wofs: wB[:, ch, wofs:wofs + 128], lambda ch: g.hT[:, ch, ts_], [("wB", j)], [fk])
                S.op("pool", lambda e, pb=pb, j=j: e.tensor_copy(out=pre[pb][:, 0:3], in_=halo[:, j, :]),
                     reads=["halo"], writes=[("pre", pb)])
                S.op("act", lambda e, pb=pb, fb=fb: e.activation(out=pre[pb][:, 3:515], in_=fb[:], func=AF.Copy),
                     reads=[fk], writes=[("pre", pb)])
                S.op("pool", lambda e, pb=pb, j=j: e.tensor_copy(out=halo[:, j, :], in_=pre[pb][:, 512:515]),
                     reads=[("pre", pb)], writes=["halo"])

            def f_conv(j):
                pb = j % 2
                if False:
                    S.op("pool", lambda e, pb=pb, j=j: e.tensor_scalar(out=cacc[pb][:], in0=pre[pb][:, 0:512], scalar1=prm[:, j:j + 1], scalar2=None,
                                                                       op0=ALU.mult),
                         reads=[("pre", pb), "prm"], writes=[("cacc", pb)])
                    for tap in range(1, 4):
                        S.op("pool", lambda e, pb=pb, j=j, tap=tap: e.tensor_scalar(out=ctmp[:], in0=pre[pb][:, tap:tap + 512],
                                                                                    scalar1=prm[:, tap * 6 + j:tap * 6 + j + 1], scalar2=None,
                                                                                    op0=ALU.mult),
                             reads=[("pre", pb), "prm"], writes=["ctmp"])
                        S.op("pool", lambda e, pb=pb: e.tensor_tensor(out=cacc[pb][:], in0=cacc[pb][:], in1=ctmp[:], op=ALU.add),
                             reads=["ctmp", ("cacc", pb)], writes=[("cacc", pb)])
                else:
                    S.op("dve", lambda e, pb=pb, j=j: e.tensor_scalar(out=cacc[pb][:], in0=pre[pb][:, 0:512], scalar1=prm[:, j:j + 1], scalar2=None,
                                                                      op0=ALU.mult),
                         reads=[("pre", pb), "prm"], writes=[("cacc", pb)])
                    for tap in range(1, 4):
                        S.op("dve", lambda e, pb=pb, j=j, tap=tap: e.scalar_tensor_tensor(out=cacc[pb][:], in0=pre[pb][:, tap:tap + 512],
                                                                                         scalar=prm[:, tap * 6 + j:tap * 6 + j + 1], in1=cacc[pb][:],
                                                                                         op0=ALU.mult, op1=ALU.add),
                             reads=[("pre", pb), ("cacc", pb), "prm"], writes=[("cacc", pb)])

            def f_silu(j):
                pb = j % 2
                dst = xc[p][:, j, :] if j < 4 else (BT[p][:] if j == 4 else CT[p][:])
                dkey = ("xc", p, j) if j < 4 else (("BT", p) if j == 4 else ("CT", p))
                S.op("act", lambda e, pb=pb, j=j, dst=dst: e.activation(out=dst, in_=cacc[pb][:], func=AF.Silu, bias=prm[:, 24 + j:25 + j]),
                     reads=[("cacc", pb), "prm"], writes=[dkey])

            f_proj(0)
            for j in range(6):
                if j + 1 < 6:
                    f_proj(j + 1)
                f_conv(j)
                f_silu(j)
            for k in range(4):
                tc_ = slice((sc * 4 + k) * 128, (sc * 4 + k + 1) * 128)
                fb, fk = fbanks[(k + 2) % 4]
                mm_group(fb[:], lambda ch, tc_=tc_: g.hT[:, ch, tc_], lambda ch: wB[:, ch, 0:512], [("wB", i) for i in range(6, 10)], [fk])
                S.op("act", lambda e, p=p, k=k, fb=fb: e.activation(out=szs[p][:, k, :], in_=fb[:], func=AF.Silu), reads=[fk], writes=[("sz", p, k)])
            for k in range(4):
                tc_ = slice((sc * 4 + k) * 128, (sc * 4 + k + 1) * 128)
                mm_group(Psm[:, k * 8:(k + 1) * 8], lambda ch, tc_=tc_: g.hT[:, ch, tc_], lambda ch: wB[:, ch, 1280:1288], [("wB", 10)], ["Psm"])
            S.op("dve", lambda e, p=p: e.tensor_tensor(out=dts[p][:, 0, :, :], in0=Psm[:, 0:32].rearrange("p (k h) -> p k h", k=4),
                                                       in1=hp[:, 0, :].unsqueeze(1).to_broadcast([128, 4, 8]), op=ALU.add),
                 reads=["Psm", ("hp", 0)], writes=[("dts", p, 0)])
            S.op("act", lambda e, p=p: e.activation(out=dts[p][:, 0, :, :], in_=dts[p][:, 0, :, :], func=AF.Exp),
                 reads=[("dts", p, 0)], writes=[("dts", p, 0)])
            S.op("act", lambda e, p=p: e.activation(out=dts[p][:, 1, :, :], in_=dts[p][:, 0, :, :], func=AF.Ln, bias=1.0),
                 reads=[("dts", p, 0)], writes=[("dts", p, 1)])
            S.op("dve", lambda e, p=p: e.tensor_tensor(out=dts[p][:, 2, :, :], in0=dts[p][:, 1, :, :],
                                                       in1=hp[:, 1, :].unsqueeze(1).to_broadcast([128, 4, 8]), op=ALU.mult),
                 reads=[("dts", p, 1), ("hp", 1)], writes=[("dts", p, 2)])
            S.op("dve", lambda e, p=p: e.tensor_copy(out=dAh[p][:], in_=dts[p][:, 2, :, :]), reads=[("dts", p, 2)], writes=[("dAh", p)])
            S.op("dve", lambda e, p=p: e.tensor_tensor(out=dAr[:], in0=dts[p][:, 2, :, :], in1=dAh[p][:], op=ALU.subtract),
                 reads=[("dts", p, 2), ("dAh", p)], writes=["dAr"])
            S.op("dve", lambda e, p=p: e.tensor_copy(out=dAl[p][:], in_=dAr[:]), reads=["dAr"], writes=[("dAl", p)])
            for k in range(4):
                S.op("pe", lambda e, p=p, k=k: e.matmul(Psm[:, 64 + k * 8:72 + k * 8], lhsT=tri, rhs=dts[p][:, 2, k, :], start=True, stop=True),
                     reads=[("dts", p, 2)], writes=["Psm"])
                S.op("pe", lambda e, p=p, k=k: e.matmul(Psm[:, 96 + k * 8:104 + k * 8], lhsT=ones, rhs=dts[p][:, 2, k, :], start=True, stop=True),
                     reads=[("dts", p, 2)], writes=["Psm"])
            v48 = lambda ap: ap.rearrange("p (k h) -> p k h", k=4)
            S.op("dve", lambda e, p=p: e.tensor_copy(out=fs[p][:, 0, :, :], in_=v48(Psm[:, 64:96])), reads=["Psm"], writes=[("fs", p, 0)])
            S.op("act", lambda e, p=p: e.activation(out=fs[p][:, 1, :, :], in_=fs[p][:, 0, :, :], func=AF.Exp),
                 reads=[("fs", p, 0)], writes=[("fs", p, 1)])
            S.op("dve", lambda e, p=p: e.tensor_tensor(out=fs[p][:, 2, :, :], in0=v48(Psm[:, 96:128]), in1=fs[p][:, 0, :, :], op=ALU.subtract),
                 reads=["Psm", ("fs", p, 0)], writes=[("fs", p, 2)])
            S.op("act", lambda e, p=p: e.activation(out=fs[p][:, 3, :, :], in_=fs[p][:, 2, :, :], func=AF.Exp),
                 reads=[("fs", p, 2)], writes=[("fs", p, 3)])
            S.op("act", lambda e, p=p: e.activation(out=fs[p][:, 4, :, :], in_=v48(Psm[:, 96:128]), func=AF.Exp),
                 reads=["Psm"], writes=[("fs", p, 4)])
            S.op("dve", lambda e, p=p: e.tensor_tensor(out=fs[p][:, 5, :, :], in0=dts[p][:, 1, :, :], in1=fs[p][:, 3, :, :], op=ALU.mult),
                 reads=[("dts", p, 1), ("fs", p, 3)], writes=[("fs", p, 5)])

        def stageA(c):
            sc, k, q = c // 4, c % 4, c % NQ
            p = sc % 2
            lc = slice(k * 128, (k + 1) * 128)
            dt_ = dts[p][:, 1, k, :]
            dA_ = dts[p][:, 2, k, :]
            trib = g.mask2[:, 128:256]
            S.op("dve", lambda e: e.tensor_tensor(out=rhsH[:], in0=trib.unsqueeze(1).to_broadcast([128, 8, 128]),
                                                  in1=dAh[p][:, k, :].unsqueeze(2).to_broadcast([128, 8, 128]), op=ALU.mult),
                 reads=[("dAh", p)], writes=["rhsH"])
            S.op("dve", lambda e: e.tensor_tensor(out=rhsL[:], in0=trib.unsqueeze(1).to_broadcast([128, 8, 128]),
                                                  in1=dAl[p][:, k, :].unsqueeze(2).to_broadcast([128, 8, 128]), op=ALU.mult),
                 reads=[("dAl", p)], writes=["rhsL"])
            for hf in range(2):
                S.op("pe", lambda e, hf=hf: e.matmul(PSEGs[hf][:], lhsT=g.u1b[:], rhs=rhsH[:, hf * 4:(hf + 1) * 4, :], start=True, stop=False),
                     reads=["rhsH"], writes=[("PSEG", hf)])
                S.op("pe", lambda e, hf=hf: e.matmul(PSEGs[hf][:], lhsT=g.u1b[:], rhs=rhsL[:, hf * 4:(hf + 1) * 4, :], start=False, stop=True),
                     reads=["rhsL"], writes=[("PSEG", hf)])
            for hf in range(2):
                S.op("act", lambda e, hf=hf: e.activation(out=LT[:, hf * 4:(hf + 1) * 4, :], in_=PSEGs[hf][:].rearrange("p (h l) -> p h l", h=4),
                                                          func=AF.Exp),
                     reads=[("PSEG", hf)], writes=[("LT", hf)])
            PA0b = PA0[:].bitcast(BF16)
            def xtr(e):
                ins = None
                for j in range(4):
                    ins = e.transpose(PA0b[:, j * 128:(j + 1) * 128], xc[p][:, j, lc], g.identb[:])
                return ins
            S.op("pe", xtr, reads=[("xc", p, j) for j in range(4)], writes=["PA0"])
            S.op("act", lambda e: e.activation(out=xtm[q][:], in_=PA0b[:, 0:512], func=AF.Copy), reads=["PA0"], writes=[("xtm", q)])
            S.op("pe", lambda e: e.transpose(Pbf[:, 0, :], BT[p][:, lc], g.identb[:]), reads=[("BT", p)], writes=["Pbf"])
            S.op("dve", lambda e: e.tensor_copy(out=Btm[q][:], in_=Pbf[:, 0, :]), reads=["Pbf"], writes=[("Btm", q)])
            S.op("pe", lambda e: e.matmul(Psm[:, 128:256], lhsT=BT[p][:, lc], rhs=CT[p][:, lc], start=True, stop=True),
                 reads=[("BT", p), ("CT", p)], writes=["Psm"])
            S.op("dve", lambda e: e.tensor_tensor(out=GTm[:], in0=Psm[:, 128:256], in1=tri, op=ALU.mult), reads=["Psm"], writes=["GTm"])
            S.op("dve", lambda e: e.tensor_tensor(out=MT[q][:], in0=LT[:], in1=GTm[:].unsqueeze(1).to_broadcast([128, 8, 128]), op=ALU.mult),
                 reads=[("LT", 0), ("LT", 1), "GTm"], writes=[("MT", q)])
            S.op("pool", lambda e: e.tensor_tensor(out=v8(xdt[q][:]), in0=v8(xtm[q][:]), in1=b8(dt_), op=ALU.mult),
                 reads=[("xtm", q), ("dts", p, 1)], writes=[("xdt", q)])
            S.op("pool", lambda e: e.tensor_tensor(out=v8(xdte[q][:]), in0=v8(xtm[q][:]), in1=b8(fs[p][:, 5, k, :]), op=ALU.mult),
                 reads=[("xtm", q), ("fs", p, 5)], writes=[("xdte", q)])
            S.op("pool", lambda e: e.tensor_tensor(out=v8(tD[q][:]), in0=v8(xtm[q][:]), in1=b8(hp[:, 2, :]), op=ALU.mult),
                 reads=[("xtm", q), ("hp", 2)], writes=[("tD", q)])

        def stageB1(c):
            sc, k, q = c // 4, c % 4, c % NQ
            p = sc % 2
            lc = slice(k * 128, (k + 1) * 128)
            S.op("pe", lambda e: e.matmul(PBo[:], lhsT=CT[p][:, lc], rhs=Hb[:], start=True, stop=True),
                 reads=[("CT", p), "Hb"], writes=["PBo"])
            S.op("pe", lambda e: e.matmul(PS2[:], lhsT=Btm[q][:], rhs=xdte[q][:], start=True, stop=True),
                 reads=[("Btm", q), ("xdte", q)], writes=["PS2"])
            S.op("pe", lambda e: e.matmul(PBy[:], lhsT=g.identb[:], rhs=tD[q][:], start=True, stop=False),
                 reads=[("tD", q)], writes=["PBy"])
            for h in range(8):
                S.op("pe", lambda e, h=h: e.matmul(PBy[:, h * 64:(h + 1) * 64], lhsT=MT[q][:, h, :], rhs=xdt[q][:, h * 64:(h + 1) * 64],
                                                   start=False, stop=(h == 7)),
                     reads=[("MT", q), ("xdt", q)], writes=["PBy"])
            S.op("dve", lambda e: e.tensor_tensor(out=v8(H[:]), in0=v8(H[:]), in1=b8(fs[p][:, 4, k, :]), op=ALU.mult),
                 reads=["H", ("fs", p, 4)], writes=["H"])
            S.op("dve", lambda e: e.tensor_tensor(out=H[:], in0=H[:], in1=PS2[:], op=ALU.add), reads=["H", "PS2"], writes=["H"])
            S.op("act", lambda e: e.activation(out=Hb[:], in_=H[:], func=AF.Copy), reads=["H"], writes=["Hb"])
            S.op("dve", lambda e: e.tensor_tensor(out=v8(yt[:]), in0=v8(PBo[:]), in1=b8(fs[p][:, 1, k, :]), op=ALU.mult),
                 reads=["PBo", ("fs", p, 1)], writes=["yt"])
            S.op("dve", lambda e: e.tensor_tensor(out=yt[:], in0=yt[:], in1=PBy[:], op=ALU.add), reads=["yt", "PBy"], writes=["yt"])
            S.op("pool", lambda e: e.tensor_tensor(out=yt[:], in0=yt[:], in1=szs[p][:, k, :], op=ALU.mult),
                 reads=["yt", ("sz", p, k)], writes=["yt"])
            yq = c % 2
            S.op("act", lambda e: e.activation(out=yn[yq][:], in_=yt[:], func=AF.Square, accum_out=nst[:, 0:1]),
                 reads=["yt"], writes=[("yn", yq), ("nst", 0)])
            S.op("act", lambda e: e.activation(out=nst[:, 1:2], in_=nst[:, 0:1], func=AF.Ln, scale=1.0 / 512, bias=EPS),
                 reads=[("nst", 0)], writes=[("nst", 1)])
            S.op("act", lambda e: e.activation(out=nst[:, 2:3], in_=nst[:, 1:2], func=AF.Exp, scale=-0.5),
                 reads=[("nst", 1)], writes=[("nst", 2)])
            S.op("act", lambda e: e.activation(out=yn[yq][:], in_=yt[:], func=AF.Copy, scale=nst[:, 2:3]),
                 reads=["yt", ("nst", 2)], writes=[("yn", yq)])

        def stageB2(c):
            sc, k, q = c // 4, c % 4, c % 2
            p = sc % 2
            lc = slice(k * 128, (k + 1) * 128)
            for j in range(4):
                S.op("pe", lambda e, j=j: e.transpose(Pbf[:, 4 + j, :], yn[q][:, j * 128:(j + 1) * 128], g.identb[:]),
                     reads=[("yn", q)], writes=["Pbf"])
            S.op("dve", lambda e: e.tensor_tensor(out=Ybuf[p][:, :, lc], in0=Pbf[:, 4:8, :],
                                                  in1=prm[:, 30:34].unsqueeze(2).to_broadcast([128, 4, 128]), op=ALU.mult),
                 reads=["Pbf", "prm"], writes=[("Ybuf", p)])
            if k == 3:
                r0 = 512 + grp * 512
                ts_ = slice(sc * 512, (sc + 1) * 512)
                S.op("sp", lambda e: e.dma_start(out=g.ycat[r0:r0 + 512, ts_].rearrange("(cc p) t -> p cc t", p=128), in_=Ybuf[p][:]),
                     reads=[("Ybuf", p)], dma_key=("Ybuf", p))

        NCH = SEQ // 128
        front(0)
        stageA(0)
        stageA(1)
        for c in range(NCH):
            S.capture()
            if c + 3 < NCH and (c + 3) % 4 == 0:
                front((c + 3) // 4)
            lf = S.end_capture()
            S.capture()
            stageB1(c)
            lb1 = S.end_capture()
            S.capture()
            if c + 2 < NCH:
                stageA(c + 2)
            la = S.end_capture()
            S.capture()
            if c >= 1:
                stageB2(c - 1)
            lb2 = S.end_capture()
            S.replay_merged([lf])
            S.replay_merged([lb1, la, lb2])
        stageB2(NCH - 1)
        S.emit_block()


def phase_out(g, L, src):
    nc, S = g.nc, g.S
    with contextlib.ExitStack() as st:
        sbt = lambda name, shape, dt: st.enter_context(g.sbuf(name, shape, dt))
        wo = sbt("wo", [128, 16, DM], BF16)
        wst = [sbt("owst%d" % i, [128, DM], F32) for i in range(3)]
        yc = [sbt("oyc%d" % i, [128, 16, 512], BF16) for i in range(2)]
        xt = [sbt("oxt%d" % i, [128, DM], F32) for i in range(2)]
        t1 = [sbt("ot1%d" % i, [128, DM], F32) for i in range(2)]
        junk = sbt("ojunk", [128, DM], BF16)
        stat = [sbt("ost%d" % i, [128, 4], F32) for i in range(2)]
        PY = [st.enter_context(g.psum("oPY%d" % i, [128, 2, 512], F32)) for i in range(2)]
        for cc in range(16):
            b = cc % 3
            S.op("sp", lambda e, b=b, cc=cc: e.dma_start(out=wst[b][:], in_=g.w_out[L, cc * 128:(cc + 1) * 128, :]),
                 writes=[("wst", b)], dma_key=("owst", b))
            ceng = ("act", "dve")[cc % 2]
            if ceng == "act":
                S.op("act", lambda e, b=b, cc=cc: e.activation(out=wo[:, cc, :], in_=wst[b][:], func=AF.Copy),
                     reads=[("wst", b)], writes=[("wo", cc)])
            else:
                S.op(ceng, lambda e, b=b, cc=cc: e.tensor_copy(out=wo[:, cc, :], in_=wst[b][:]), reads=[("wst", b)], writes=[("wo", cc)])
        def load_yc(gi):
            gb = gi % 2
            ts_ = slice(gi * 512, (gi + 1) * 512)
            S.op("sp", lambda e: e.dma_start(out=yc[gb][:], in_=g.ycat[:, ts_].rearrange("(cc p) t -> p cc t", p=128)),
                 writes=[("yc", gb)], dma_key=("oyc", gb))

        load_yc(0)
        for gi in range(8):
            gb = gi % 2
            if gi + 1 < 8:
                load_yc(gi + 1)
            for k in range(4):
                t = gi * 4 + k
                b = t % 2
                rows = slice(t * 128, (t + 1) * 128)
                lc = slice(k * 128, (k + 1) * 128)
                S.op("sp", lambda e, b=b, rows=rows: e.dma_start(out=xt[b][:], in_=src[rows, :]), writes=[("x", b)], dma_key=("oxt", b))
                for nh in range(2):
                    for cc in range(16):
                        S.op("pe", lambda e, b=b, nh=nh, cc=cc, gb=gb, lc=lc: e.matmul(PY[b][:, nh, :], lhsT=yc[gb][:, cc, lc],
                                                                                       rhs=wo[:, cc, nh * 512:(nh + 1) * 512],
                                                                                       start=(cc == 0), stop=(cc == 15)),
                             reads=[("yc", gb), ("wo", cc)], writes=[("PY", b)])
                S.op("act", lambda e, b=b: e.activation(out=junk[:].rearrange("p (a n) -> p a n", a=2), in_=PY[b][:], func=AF.Square,
                                                        accum_out=stat[b][:, 0:1]),
                     reads=[("PY", b)], writes=["junk", ("s0", b)])
                S.op("act", lambda e, b=b: e.activation(out=stat[b][:, 1:2], in_=stat[b][:, 0:1], func=AF.Sqrt, scale=1.0 / DM, bias=EPS),
                     reads=[("s0", b)], writes=[("s1", b)])
                S.op("dve", lambda e, b=b: e.reciprocal(out=stat[b][:, 2:3], in_=stat[b][:, 1:2]), reads=[("s1", b)], writes=[("s2", b)])
                S.op("dve", lambda e, b=b: e.scalar_tensor_tensor(out=t1[b][:].rearrange("p (a n) -> p a n", a=2), in0=PY[b][:],
                                                                  scalar=stat[b][:, 2:3],
                                                                  in1=g.modb[:, 2 * DM:3 * DM].rearrange("p (a n) -> p a n", a=2),
                                                                  op0=ALU.mult, op1=ALU.mult),
                     reads=[("PY", b), ("s2", b)], writes=[("t1", b)])
                S.op("pool", lambda e, b=b: e.tensor_tensor(out=t1[b][:], in0=t1[b][:], in1=xt[b][:], op=ALU.add),
                     reads=[("t1", b), ("x", b)], writes=[("t1", b)])
                S.op("sp", lambda e, b=b, rows=rows: e.dma_start(out=g.out[rows, :], in_=t1[b][:]), reads=[("t1", b)], dma_key=("ot1", b))
        S.emit_block()


def _consts():
    i = np.arange(128)
    c = np.zeros((128, 7, 128), np.float32)
    c[:, 0, :] = np.eye(128)
    c[:, 1, :] = (i[:, None] >= i[None, :])
    c[:, 2, :] = (i[:, None] <= i[None, :])
    c[:, 3, :] = (i[:, None] > i[None, :])
    c[:, 4, :] = 1.0
    c[64, 5, 0:64] = 1.0
    return c


_PROG = {}
FUSED = True
_WNAMES = ("ada_w", "ada_b", "pre_norm_w", "post_norm_w", "w_in", "conv_w", "conv_b", "dt_bias", "a_log",
           "d_skip", "ssm_norm_w", "sinks", "w_out")


def kernel(x, c, ada_w, ada_b, pre_norm_w, post_norm_w, w_in, conv_w, conv_b,
           dt_bias, a_log, d_skip, ssm_norm_w, sinks, w_out):
    f = lambda a: np.ascontiguousarray(np.asarray(a, dtype=np.float32))
    ws = dict(ada_w=f(ada_w), ada_b=f(ada_b), pre_norm_w=f(pre_norm_w), post_norm_w=f(post_norm_w),
              w_in=f(w_in), conv_w=f(conv_w), conv_b=f(conv_b), dt_bias=f(dt_bias), a_log=f(a_log),
              d_skip=f(d_skip), ssm_norm_w=f(ssm_norm_w), sinks=f(sinks), w_out=f(w_out))
    x = f(x)
    c = f(c)
    consts = _consts()
    ccols = [np.ascontiguousarray(c[b].reshape(8, 128).T) for b in range(8)]
    if FUSED:
        if "fused" not in _PROG:
            _PROG["fused"] = build(n_layers=DEPTH, depth_dim=DEPTH)
        in_maps = []
        for b in range(8):
            m = dict(ws)
            m["consts"] = consts
            m["x"] = x[b]
            m["c_col"] = ccols[b]
            in_maps.append(m)
        res = run_bass_kernel_spmd(_PROG["fused"], in_maps, core_ids=list(range(8)))
        return np.stack([np.asarray(r["out"]) for r in res.results], axis=0).astype(np.float32)
    if "layer" not in _PROG:
        _PROG["layer"] = build(n_layers=1, depth_dim=1)
    cur = [x[b] for b in range(8)]
    for L in range(DEPTH):
        wl = {k: np.ascontiguousarray(ws[k][L:L + 1]) for k in _WNAMES}
        in_maps = []
        for b in range(8):
            m = dict(wl)
            m["consts"] = consts
            m["x"] = cur[b]
            m["c_col"] = ccols[b]
            in_maps.append(m)
        res = run_bass_kernel_spmd(_PROG["layer"], in_maps, core_ids=list(range(8)))
        cur = [np.ascontiguousarray(np.asarray(r["out"], dtype=np.float32)) for r in res.results]
    return np.stack(cur, axis=0).astype(np.float32)
```

```python
import contextlib
import numpy as np
import concourse.bass as bass
import concourse.mybir as mybir
from concourse.bass_utils import run_bass_kernel_spmd

F32 = mybir.dt.float32
BF16 = mybir.dt.bfloat16
AF = mybir.ActivationFunctionType
ALU = mybir.AluOpType

SEQ = 4096
DM = 1024
NT = SEQ // 128
EPS = 1e-6
DEPTH = 4
IN_COLS = 5904
C_OFF = 4624
ENGS = ("pe", "act", "dve", "pool", "sp")


def sl(start, n, step=1):
    return slice(start, start + step * (n - 1) + 1, step)


class Sched:
    def __init__(self, nc, stack):
        self.nc = nc
        self.stack = stack
        self.esem = {e: stack.enter_context(nc.semaphore("s_" + e)) for e in ENGS}
        self._names = {id(v): "s_" + k for k, v in self.esem.items()}
        self.ecount = {e: 0 for e in ENGS}
        self.dsem = {}
        self.dcount = {}
        self.waited = {e: {} for e in ENGS}
        self.n_ops = 0
        self.reset_block()

    def reset_block(self):
        self.ops = []
        self.last_w = {}
        self.readers = {}

    def _dma_sem(self, key):
        if key not in self.dsem:
            self.dsem[key] = self.stack.enter_context(self.nc.semaphore("d%d" % len(self.dsem)))
            self.dcount[key] = 0
            self._names[id(self.dsem[key])] = "d_" + str(key)
        return self.dsem[key]

    def capture(self):
        self._cap = []
        return self._cap

    def end_capture(self):
        lst, self._cap = self._cap, None
        return lst

    _DUR = {"pe": 0.2, "act": 0.6, "dve": 0.75, "pool": 1.1, "sp": 2.0}

    def replay_merged(self, lists):
        if not hasattr(self, "_sim_eng"):
            self._sim_eng = {e: 0.0 for e in ENGS}
            self._sim_key = {}
        its = [list(l) for l in lists if l]
        pos = [0] * len(its)
        while True:
            best, best_t = None, None
            for i in range(len(its)):
                if pos[i] >= len(its[i]):
                    continue
                eng, fn, reads, writes, dma_key = its[i][pos[i]]
                t = self._sim_eng[eng]
                for k in reads:
                    t = max(t, self._sim_key.get(k, 0.0))
                for k in writes:
                    t = max(t, self._sim_key.get(k, 0.0))
                if best is None or t < best_t - 1e-9:
                    best, best_t = i, t
            if best is None:
                break
            a = its[best][pos[best]]
            pos[best] += 1
            eng, fn, reads, writes, dma_key = a
            fin = best_t + self._DUR[eng]
            self._sim_eng[eng] = fin
            for k in writes:
                self._sim_key[k] = fin
            self.op(*a)

    def op(self, eng, fn, reads=(), writes=(), dma_key=None):
        if getattr(self, "_cap", None) is not None:
            self._cap.append((eng, fn, tuple(reads), tuple(writes), dma_key))
            return None
        idx = len(self.ops)
        deps = set()
        for k in reads:
            w = self.last_w.get(k)
            if w is not None:
                deps.add(w)
        for k in writes:
            w = self.last_w.get(k)
            if w is not None:
                deps.add(w)
            for r in self.readers.get(k, ()):
                deps.add(r)
        deps.discard(idx)
        self.ops.append(dict(eng=eng, fn=fn, deps=deps, dma_key=dma_key, milestone=False))
        for k in reads:
            self.readers.setdefault(k, []).append(idx)
        for k in writes:
            self.last_w[k] = idx
            self.readers[k] = []
        return idx

    def emit_block(self, name=None):
        nc = self.nc
        ops = self.ops
        for o in ops:
            keep = set()
            for d in o["deps"]:
                s = ops[d]
                if s["dma_key"] is None and o["dma_key"] is None and s["eng"] == o["eng"] == "pe":
                    continue
                keep.add(d)
            o["deps"] = keep
            for d in keep:
                if ops[d]["dma_key"] is None:
                    ops[d]["milestone"] = True
        ecount = dict(self.ecount)
        dcount = dict(self.dcount)
        for o in ops:
            if o["dma_key"] is not None:
                self._dma_sem(o["dma_key"])
                dcount[o["dma_key"]] = dcount.get(o["dma_key"], 0) + 16
                o["dval"] = dcount[o["dma_key"]]
            elif o["milestone"]:
                ecount[o["eng"]] += 1
                o["mval"] = ecount[o["eng"]]
        per_eng = {e: [o for o in ops if o["eng"] == e] for e in ENGS}
        final_d = dict(dcount)
        sched = self

        def emit_engine(ename, engine):
            waited = sched.waited[ename]
            for o in per_eng[ename]:
                need = {}
                for d in o["deps"]:
                    s = ops[d]
                    if s["dma_key"] is not None:
                        sem, val = sched.dsem[s["dma_key"]], s["dval"]
                    else:
                        sem, val = sched.esem[s["eng"]], s["mval"]
                    key = sched._names[id(sem)]
                    if val > need.get(key, (None, 0))[1]:
                        need[key] = (sem, val)
                for key, (sem, val) in need.items():
                    if waited.get(key, 0) >= val:
                        continue
                    engine.wait_ge(sem, val)
                    waited[key] = val
                ins = o["fn"](engine)
                if o["dma_key"] is not None:
                    ins.then_inc(sched.dsem[o["dma_key"]], 16)
                elif o["milestone"]:
                    ins.then_inc(sched.esem[ename], 1)
            if ename == "sp":
                for k, v in final_d.items():
                    sem = sched.dsem[k]
                    key = sched._names[id(sem)]
                    if waited.get(key, 0) < v:
                        engine.wait_ge(sem, v)
                        waited[key] = v

        with nc.Block(name) as block:
            @block.tensor
            def _(e):
                emit_engine("pe", e)

            @block.scalar
            def _(e):
                emit_engine("act", e)

            @block.vector
            def _(e):
                emit_engine("dve", e)

            @block.gpsimd
            def _(e):
                emit_engine("pool", e)

            @block.sync
            def _(e):
                emit_engine("sp", e)
        self.ecount = ecount
        self.dcount = dcount
        self.n_ops += len(ops)
        self.reset_block()


class K:
    pass


def dbg(g, name, ap, shape, dtype, reads):
    if not getattr(g, "debug", False):
        return
    import os
    taps = os.environ.get("DBG_TAPS", "")
    if not any(name == t or name.startswith(t + "_") for t in taps.split(",") if t):
        return
    d = g.nc.dram_tensor("dbg_" + name, list(shape), dtype, kind="ExternalOutput").ap()
    idx = tuple(slice(None) for _ in shape)
    g.S.op("sp", lambda e: e.dma_start(out=d[idx], in_=ap), reads=reads, dma_key=("dbg", name))


def build(n_layers=DEPTH, debug=False, phases=("mod", "p1", "att", "ssd", "out"), depth_dim=DEPTH):
    nc = bass.Bass("TRN2", target_bir_lowering=False)

    def din(name, shape):
        return nc.dram_tensor(name, shape, F32, kind="ExternalInput").ap()

    g = K()
    g.nc = nc
    g.uid = [0]
    g.debug = debug

    def _uniq(name):
        g.uid[0] += 1
        return "%s_%d" % (name, g.uid[0])
    g.sbuf = lambda name, shape, dt: nc.sbuf_tensor(_uniq(name), shape, dt)
    g.psum = lambda name, shape, dt: nc.psum_tensor(_uniq(name), shape, dt)
    g.x_in = din("x", [SEQ, DM])
    g.c_col = din("c_col", [128, 8])
    DD = depth_dim
    g.ada_w = din("ada_w", [DD, DM, 3 * DM])
    g.ada_b = din("ada_b", [DD, 3 * DM])
    g.pre_w = din("pre_norm_w", [DD, DM])
    g.post_w = din("post_norm_w", [DD, DM])
    g.w_in = din("w_in", [DD, DM, IN_COLS])
    g.conv_w = din("conv_w", [DD, 4, 1536])
    g.conv_b = din("conv_b", [DD, 1536])
    g.dt_bias = din("dt_bias", [DD, 16])
    g.a_log = din("a_log", [DD, 16])
    g.d_skip = din("d_skip", [DD, 16])
    g.ssm_w = din("ssm_norm_w", [DD, DM])
    g.sinks = din("sinks", [DD, 8])
    g.w_out = din("w_out", [DD, 2 * DM, DM])
    g.consts = din("consts", [128, 7, 128])
    g.out = nc.dram_tensor("out", [SEQ, DM], F32, kind="ExternalOutput").ap()
    g.ycat = nc.dram_tensor("ycat", [2 * DM, SEQ], BF16,
                            kind="ExternalOutput" if debug else "Internal").ap()

    with contextlib.ExitStack() as st:
        S = Sched(nc, st)
        g.S = S
        sb = lambda name, shape, dt: st.enter_context(nc.sbuf_tensor(name, shape, dt))
        g.cf = sb("cf", [128, 7, 128], F32)
        g.identb = sb("identb", [128, 128], BF16)
        g.mask2 = sb("mask2", [128, 256], BF16)
        g.u1b = sb("u1b", [128, 128], BF16)
        g.cbc = sb("cbc", [128, 8, 128], F32)
        g.modb = sb("modb", [128, 3 * DM], F32)
        g.hT = sb("hT", [128, 8, SEQ], BF16)

        phase_init(g)
        for L in range(n_layers):
            src = g.x_in if L == 0 else g.out
            if "mod" in phases:
                phase_mod(g, L)
            if "p1" in phases:
                phase_p1(g, L, src)
            if "att" in phases:
                specs = []
                for hp in range(4):
                    base = hp * 128
                    specs.append(dict(q=[(base, 128)], k=[(512 + base, 128)], v=[(1024 + base, 128)],
                                      z=[(1536 + base, 128)], pats=(1, 4, 16), sink=None,
                                      rows=(base, base + 64)))
                for i in range(4):
                    specs.append(dict(q=[(C_OFF + i * 64, 64), (C_OFF + (4 + i) * 64, 64)],
                                      k=[(C_OFF + 1024, 128)], v=[(C_OFF + 1152, 128)],
                                      z=[(C_OFF + 512 + i * 64, 64), (C_OFF + 512 + (4 + i) * 64, 64)],
                                      pats=(1,), sink=(i, 4 + i), reuse_kv=(i > 0),
                                      rows=(1536 + i * 64, 1536 + (4 + i) * 64)))
                phase_att(g, L, specs)
            if "ssd" in phases:
                for grp in range(2):
                    phase_ssd(g, L, grp)
            if "out" in phases:
                phase_out(g, L, src)
        g.n_ops = S.n_ops
    return nc


def phase_init(g):
    nc, S = g.nc, g.S
    with contextlib.ExitStack() as st:
        cc = st.enter_context(g.sbuf("cc", [128, 8], F32))
        ca = st.enter_context(g.sbuf("ca", [128, 8], F32))
        S.op("sp", lambda e: e.dma_start(out=g.cf[:], in_=g.consts[:, :, :]), writes=["cf"], dma_key="cf")
        S.op("sp", lambda e: e.dma_start(out=cc[:], in_=g.c_col[:, :]), writes=["cc"], dma_key="cc")
        S.op("pool", lambda e: e.tensor_copy(out=g.identb[:], in_=g.cf[:, 0, :]), reads=["cf"], writes=["identb"])
        S.op("pool", lambda e: e.tensor_copy(out=g.mask2[:].rearrange("p (a b) -> p a b", a=2), in_=g.cf[:, 1:3, :]),
             reads=["cf"], writes=["mask2"])
        S.op("pool", lambda e: e.tensor_copy(out=g.u1b[:], in_=g.cf[:, 3, :]), reads=["cf"], writes=["u1b"])
        S.op("act", lambda e: e.activation(out=ca[:], in_=cc[:], func=AF.Silu), reads=["cc"], writes=["ca"])
        S.op("dve", lambda e: e.tensor_copy(out=g.cbc[:], in_=ca[:].unsqueeze(2).to_broadcast([128, 8, 128])),
             reads=["ca"], writes=["cbc"])
        S.emit_block("init")


def phase_mod(g, L):
    nc, S = g.nc, g.S
    with contextlib.ExitStack() as st:
        stage = [st.enter_context(g.sbuf("mstage%d" % i, [128, 8, 512], F32)) for i in range(2)]
        adab = st.enter_context(g.sbuf("adab", [128, 3 * DM], F32))
        pw = st.enter_context(g.sbuf("pw", [128, 2, DM], F32))
        ps = [st.enter_context(g.psum("mps%d" % i, [128, 512], F32)) for i in range(2)]
        S.op("sp", lambda e: e.dma_start(out=adab[:], in_=g.ada_b[L:L + 1, :].partition_broadcast(128)),
             writes=["adab"], dma_key="adab")
        S.op("sp", lambda e: e.dma_start(out=pw[:, 0, :], in_=g.pre_w[L:L + 1, :].partition_broadcast(128)),
             writes=["pw0"], dma_key="pw0")
        S.op("sp", lambda e: e.dma_start(out=pw[:, 1, :], in_=g.post_w[L:L + 1, :].partition_broadcast(128)),
             writes=["pw1"], dma_key="pw1")
        aw = g.ada_w[L].rearrange("(ch p) n -> p ch n", p=128)
        for grp in range(6):
            b = grp % 2
            cs = slice(grp * 512, (grp + 1) * 512)
            S.op("sp", lambda e, b=b, cs=cs: e.dma_start(out=stage[b][:], in_=aw[:, :, cs]),
                 writes=[("mst", b)], dma_key=("mst", b))
            for ch in range(8):
                S.op("pe", lambda e, b=b, ch=ch: e.matmul(ps[b][:], lhsT=g.cbc[:, ch, :], rhs=stage[b][:, ch, :],
                                                          start=(ch == 0), stop=(ch == 7)),
                     reads=[("mst", b), "cbc"], writes=[("mps", b)])
            S.op("dve", lambda e, b=b, cs=cs: e.tensor_tensor(out=g.modb[:, cs], in0=ps[b][:], in1=adab[:, cs], op=ALU.add),
                 reads=[("mps", b), "adab"], writes=["modb"])
        S.op("dve", lambda e: e.scalar_tensor_tensor(out=g.modb[:, DM:2 * DM], in0=g.modb[:, DM:2 * DM], scalar=1.0,
                                                     in1=pw[:, 0, :], op0=ALU.add, op1=ALU.mult),
             reads=["modb", "pw0"], writes=["modb"])
        S.op("dve", lambda e: e.tensor_tensor(out=g.modb[:, 2 * DM:3 * DM], in0=g.modb[:, 2 * DM:3 * DM], in1=pw[:, 1, :],
                                              op=ALU.mult),
             reads=["modb", "pw1"], writes=["modb"])
        S.emit_block("mod%d" % L)


def phase_p1(g, L, src):
    nc, S = g.nc, g.S
    NB = 4
    with contextlib.ExitStack() as st:
        xt = [st.enter_context(g.sbuf("p1x%d" % i, [128, DM], F32)) for i in range(NB)]
        tmp = [st.enter_context(g.sbuf("p1t%d" % i, [128, DM], F32)) for i in range(NB)]
        hb = [st.enter_context(g.sbuf("p1h%d" % i, [128, DM], BF16)) for i in range(NB)]
        junk = st.enter_context(g.sbuf("p1junk", [128, DM], BF16))
        stat = [st.enter_context(g.sbuf("p1s%d" % i, [128, 4], F32)) for i in range(NB)]
        pst = [st.enter_context(g.psum("p1ps%d" % i, [128, 8, 128], BF16)) for i in range(2)]

        def s1(t):
            b = t % NB
            rows = slice(t * 128, (t + 1) * 128)
            S.op("sp", lambda e: e.dma_start(out=xt[b][:], in_=src[rows, :]), writes=[("x", b)], dma_key=("p1x", b))
            S.op("act", lambda e: e.activation(out=junk[:], in_=xt[b][:], func=AF.Square, accum_out=stat[b][:, 0:1]),
                 reads=[("x", b)], writes=["junk", ("s0", b)])
            S.op("act", lambda e: e.activation(out=stat[b][:, 1:2], in_=stat[b][:, 0:1], func=AF.Sqrt, scale=1.0 / DM, bias=EPS),
                 reads=[("s0", b)], writes=[("s1", b)])
            S.op("dve", lambda e: e.reciprocal(out=stat[b][:, 2:3], in_=stat[b][:, 1:2]), reads=[("s1", b)], writes=[("s2", b)])
            S.op("dve", lambda e: e.scalar_tensor_tensor(out=tmp[b][:], in0=xt[b][:], scalar=stat[b][:, 2:3],
                                                         in1=g.modb[:, DM:2 * DM], op0=ALU.mult, op1=ALU.mult),
                 reads=[("x", b), ("s2", b)], writes=[("t", b)])
            S.op("pool" if t % 2 == 0 else "dve",
                 lambda e: e.tensor_tensor(out=hb[b][:], in0=tmp[b][:], in1=g.modb[:, 0:DM], op=ALU.add),
                 reads=[("t", b)], writes=[("h", b)])

        def s2(t):
            b = t % NB
            pb = t % 2
            rows = slice(t * 128, (t + 1) * 128)
            for ch in range(8):
                S.op("pe", lambda e, ch=ch: e.transpose(pst[pb][:, ch, :], hb[b][:, ch * 128:(ch + 1) * 128], g.identb[:]),
                     reads=[("h", b)], writes=[("ps", pb)])
            S.op("act", lambda e: e.activation(out=g.hT[:, :, rows], in_=pst[pb][:], func=AF.Copy),
                 reads=[("ps", pb)], writes=[("hT", t)])

        s1(0)
        s1(1)
        for t in range(NT):
            if t + 2 < NT:
                s1(t + 2)
            s2(t)
        S.emit_block("p1_%d" % L)


def _groups(d, b):
    if d == 1:
        return [b // 4]
    if d == 4:
        return [b]
    return [4 * b + i for i in range(4)]


def phase_att(g, L, specs):
    nc, S = g.nc, g.S
    with contextlib.ExitStack() as st:
        sbt = lambda name, shape, dt: st.enter_context(g.sbuf(name, shape, dt))
        wst = [sbt("awst%d" % i, [128, 8, 128], F32) for i in range(2)]
        wts = [{n: sbt("aw_%s%d" % (n, i), [128, 8, 128], BF16) for n in "qkvz"} for i in range(2)]
        QT = sbt("QT", [128, SEQ], BF16)
        KT = sbt("KT", [128, SEQ], BF16)
        VT = sbt("VTf", [128, SEQ], BF16)
        Vt = {d: sbt("Vt%d" % d, [128, 32, 2, 65], BF16) for d in (1, 4, 16)}
        NSB = 3
        Et = [sbt("E%d" % i, [128, 2, 256], BF16) for i in range(NSB)]
        Pt = [sbt("P%d" % i, [128, 2, 256], BF16) for i in range(NSB)]
        Acc = sbt("Acc", [65, 2, SEQ], F32)
        Ut = [sbt("U%d" % i, [64, 512], F32) for i in range(2)]
        Tt = [sbt("T%d" % i, [64, 512], F32) for i in range(2)]
        Yb = [sbt("Yb%d" % i, [64, 512], BF16) for i in range(2)]
        sks = [sbt("sk%d" % i, [64, 2], F32) for i in range(2)]
        Sp = [st.enter_context(g.psum("aS%d" % i, [128, 2, 512], F32)) for i in range(NSB)]
        Op = [st.enter_context(g.psum("aO%d" % i, [128, 512], F32)) for i in range(2)]
        banks = [(i, j) for i in range(NSB) for j in range(2)]
        bk = lambda ij: Sp[ij[0]][:, ij[1], :]
        bkey = lambda ij: ("Sb", ij[0], ij[1])
        wv_in = g.w_in[L].rearrange("(ch p) n -> p ch n", p=128)
        for d in (1, 4, 16):
            S.op("pool", lambda e, d=d: e.memset(Vt[d][:, :, :, 64:65], 1.0), writes=[("Vone", d)])
        pj = 0
        step = 0
        fj = 0
        def load_weights(si):
            sp = specs[si]
            wt = wts[si % 2]
            sk = sks[si % 2]
            for wi, n in enumerate("qkvz"):
                b = wi % 2
                off = 0
                for pi, (c0, cn) in enumerate(sp[n]):
                    S.op("sp", lambda e, b=b, c0=c0, cn=cn, off=off: e.dma_start(out=wst[b][:, :, off:off + cn],
                                                                               in_=wv_in[:, :, c0:c0 + cn]),
                         writes=[("wst", b, pi)], dma_key=("awst", b, pi))
                    off += cn
                ceng = ("pool", "dve", "pool", "act")[wi]
                if ceng == "act":
                    S.op("act", lambda e, b=b, n=n, wt=wt: e.activation(out=wt[n][:], in_=wst[b][:], func=AF.Copy),
                         reads=[("wst", b, 0), ("wst", b, 1)], writes=[("w", n, si % 2)])
                else:
                    S.op(ceng, lambda e, b=b, n=n, wt=wt: e.tensor_copy(out=wt[n][:], in_=wst[b][:]),
                         reads=[("wst", b, 0), ("wst", b, 1)], writes=[("w", n, si % 2)])
            if sp["sink"] is not None:
                for h in range(2):
                    hh = sp["sink"][h]
                    S.op("sp", lambda e, h=h, hh=hh, sk=sk: e.dma_start(out=sk[:, h:h + 1],
                                                                        in_=g.sinks[L:L + 1, hh:hh + 1].partition_broadcast(64)),
                         writes=[("skr", si % 2, h)], dma_key=("sk", si % 2, h))
                S.op("act", lambda e, sk=sk: e.activation(out=sk[:], in_=sk[:], func=AF.Exp),
                     reads=[("skr", si % 2, 0), ("skr", si % 2, 1)], writes=[("sk", si % 2)])

        load_weights(0)
        for si, sp in enumerate(specs):
            pats = sp["pats"]
            wt = wts[si % 2]
            sk = sks[si % 2]
            wk = lambda n, si=si: ("w", n, si % 2)
            for n in range(8):
                ts_ = slice(n * 512, (n + 1) * 512)
                for nm, eng in (("q", "act"), ("k", "dve"), ("v", "act")):
                    if nm != "q" and sp.get("reuse_kv"):
                        continue
                    ij = banks[pj % len(banks)]
                    pj += 1
                    for ch in range(8):
                        S.op("pe", lambda e, ij=ij, ch=ch, nm=nm, ts_=ts_, wt=wt: e.matmul(bk(ij), lhsT=wt[nm][:, ch, :], rhs=g.hT[:, ch, ts_],
                                                                                         start=(ch == 0), stop=(ch == 7)),
                             reads=[wk(nm)], writes=[bkey(ij)])
                    if nm == "q":
                        S.op("act", lambda e, ij=ij, ts_=ts_: e.activation(out=QT[:, ts_], in_=bk(ij), func=AF.Copy, scale=0.125),
                             reads=[bkey(ij)], writes=[("QT", n)])
                    elif nm == "k":
                        S.op("dve", lambda e, ij=ij, ts_=ts_: e.tensor_copy(out=KT[:, ts_], in_=bk(ij)),
                             reads=[bkey(ij)], writes=[("KT", n)])
                    else:
                        S.op("act", lambda e, ij=ij, ts_=ts_: e.activation(out=VT[:, ts_], in_=bk(ij), func=AF.Copy),
                             reads=[bkey(ij)], writes=[("VT", n)])
            for d in (() if sp.get("reuse_kv") else pats):
                nb = 32 // d
                for b4 in range(8):
                    ij = banks[pj % len(banks)]
                    pj += 1
                    pb16 = lambda ij: bk(ij).bitcast(BF16)
                    for j in range(4):
                        blk = b4 * 4 + j
                        r, bb = blk // nb, blk % nb
                        tok = sl(r + d * 128 * bb, 128, d)
                        S.op("pe", lambda e, ij=ij, j=j, tok=tok: e.transpose(pb16(ij)[:, j * 128:(j + 1) * 128], VT[:, tok], g.identb[:]),
                             reads=[("VT", x) for x in _groups(d, bb)], writes=[bkey(ij)])
                    vsrc = lambda ij: pb16(ij)[:, 0:512].rearrange("p (j h c) -> p j h c", j=4, h=2)
                    if b4 % 2 == 0:
                        S.op("dve", lambda e, ij=ij, b4=b4, d=d: e.tensor_copy(out=Vt[d][:, b4 * 4:(b4 + 1) * 4, :, 0:64], in_=vsrc(ij)),
                             reads=[bkey(ij)], writes=[("Vt", d, b4)])
                    else:
                        S.op("act", lambda e, ij=ij, b4=b4, d=d: e.activation(out=Vt[d][:, b4 * 4:(b4 + 1) * 4, :, 0:64], in_=vsrc(ij),
                                                                             func=AF.Copy),
                             reads=[bkey(ij)], writes=[("Vt", d, b4)])
            if si + 1 < len(specs):
                load_weights(si + 1)
            steps = []
            for pi, d in enumerate(pats):
                nb = 32 // d
                for r in range(d):
                    for bb in range(nb):
                        steps.append(dict(pi=pi, d=d, r=r, bb=bb, nb=nb, s_=step % NSB, o_=step % 2,
                                          meng="dve" if step % 2 == 0 else "pool"))
                        step += 1

            def part1(stp):
                d, r, bb, s_, meng = stp["d"], stp["r"], stp["bb"], stp["s_"], stp["meng"]
                tq = sl(r + d * 128 * bb, 128, d)
                tp = sl(r + d * 128 * (bb - 1), 128, d) if bb > 0 else None
                lo = 0 if bb > 0 else 128
                gq = _groups(d, bb)
                gk = gq + (_groups(d, bb - 1) if bb > 0 else [])
                rd = [("QT", x) for x in gq] + [("KT", x) for x in set(gk)]
                skeys = [("Sb", s_, 0), ("Sb", s_, 1)]
                for h in range(2):
                    hs = slice(64 * h, 64 * h + 64)
                    S.op("pe", lambda e, s_=s_, h=h, hs=hs, tq=tq: e.matmul(Sp[s_][:, h, 128:256], lhsT=KT[hs, tq], rhs=QT[hs, tq],
                                                                           start=True, stop=True),
                         reads=rd, writes=skeys)
                    if bb > 0:
                        S.op("pe", lambda e, s_=s_, h=h, hs=hs, tq=tq, tp=tp: e.matmul(Sp[s_][:, h, 0:128], lhsT=KT[hs, tp],
                                                                                      rhs=QT[hs, tq], start=True, stop=True),
                             reads=rd, writes=skeys)
                S.op("act", lambda e, s_=s_, lo=lo: e.activation(out=Et[s_][:, :, lo:256], in_=Sp[s_][:, :, lo:256], func=AF.Exp),
                     reads=skeys, writes=[("E", s_)])
                S.op(meng, lambda e, s_=s_, lo=lo: e.tensor_tensor(out=Pt[s_][:, :, lo:256], in0=Et[s_][:, :, lo:256],
                                                                  in1=g.mask2[:, lo:256].unsqueeze(1).to_broadcast([128, 2, 256 - lo]),
                                                                  op=ALU.mult),
                     reads=[("E", s_)], writes=[("P", s_)])

            def part2(stp):
                pi, d, r, bb, nb, s_, o_ = stp["pi"], stp["d"], stp["r"], stp["bb"], stp["nb"], stp["s_"], stp["o_"]
                blk = r * nb + bb
                tq = sl(r + d * 128 * bb, 128, d)
                gq = _groups(d, bb)
                vrd = [("Vt", d, blk // 4), ("Vone", d)] + ([("Vt", d, (blk - 1) // 4)] if bb > 0 else [])
                for h in range(2):
                    if bb > 0:
                        S.op("pe", lambda e, s_=s_, o_=o_, h=h, blk=blk, d=d: e.matmul(Op[o_][0:65, h * 128:(h + 1) * 128],
                                                                                      lhsT=Vt[d][:, blk - 1, h, :], rhs=Pt[s_][:, h, 0:128],
                                                                                      start=True, stop=False),
                             reads=[("P", s_)] + vrd, writes=[("O", o_)])
                    S.op("pe", lambda e, s_=s_, o_=o_, h=h, blk=blk, d=d, bb=bb: e.matmul(Op[o_][0:65, h * 128:(h + 1) * 128],
                                                                                         lhsT=Vt[d][:, blk, h, :], rhs=Pt[s_][:, h, 128:256],
                                                                                         start=(bb == 0), stop=True),
                         reads=[("P", s_)] + vrd, writes=[("O", o_)])
                akeys = [("acc", x, r % 4) for x in gq] if d > 1 else [("acc", gq[0], x) for x in range(4)]
                o_view = Op[o_][0:65, 0:256].rearrange("p (h q) -> p h q", h=2)
                if pi == 0:
                    S.op("dve", lambda e, tq=tq, o_view=o_view: e.tensor_copy(out=Acc[:, :, tq], in_=o_view),
                         reads=[("O", o_)], writes=akeys)
                else:
                    S.op("dve", lambda e, tq=tq, o_view=o_view: e.tensor_tensor(out=Acc[:, :, tq], in0=Acc[:, :, tq], in1=o_view, op=ALU.add),
                         reads=[("O", o_)] + akeys, writes=akeys)

            LA = NSB - 1
            for i in range(min(LA, len(steps))):
                part1(steps[i])
            for i in range(len(steps)):
                if i + LA < len(steps):
                    part1(steps[i + LA])
                part2(steps[i])
            for h in range(2):
                for n in range(8):
                    ts_ = slice(n * 512, (n + 1) * 512)
                    b = fj % 2
                    fj += 1
                    ijl = banks[pj % len(banks)]
                    pj += 1
                    ijz = banks[pj % len(banks)]
                    pj += 1
                    acc_rd = [("acc", n, x) for x in range(4)]
                    S.op("pe", lambda e, ijl=ijl, h=h, ts_=ts_: e.matmul(bk(ijl)[0:64, :], lhsT=g.cf[0:65, 5, 0:64], rhs=Acc[0:65, h, ts_],
                                                                         start=True, stop=True),
                         reads=acc_rd, writes=[bkey(ijl)])
                    for ch in range(8):
                        S.op("pe", lambda e, ijz=ijz, ch=ch, h=h, ts_=ts_, wt=wt: e.matmul(bk(ijz)[0:64, :], lhsT=wt["z"][:, ch, 64 * h:64 * h + 64],
                                                                                          rhs=g.hT[:, ch, ts_], start=(ch == 0), stop=(ch == 7)),
                             reads=[wk("z")], writes=[bkey(ijz)])
                    if sp["sink"] is not None:
                        S.op("act", lambda e, b=b, ijl=ijl, h=h, sk=sk: e.activation(out=Ut[b][:], in_=bk(ijl)[0:64, :], func=AF.Ln,
                                                                                     bias=sk[:, h:h + 1]),
                             reads=[bkey(ijl), ("sk", si % 2)], writes=[("U", b)])
                    else:
                        S.op("act", lambda e, b=b, ijl=ijl: e.activation(out=Ut[b][:], in_=bk(ijl)[0:64, :], func=AF.Ln),
                             reads=[bkey(ijl)], writes=[("U", b)])
                    S.op("act", lambda e, b=b, ijz=ijz: e.activation(out=Tt[b][:], in_=bk(ijz)[0:64, :], func=AF.Exp, scale=-1.0),
                         reads=[bkey(ijz)], writes=[("T", b)])
                    S.op("act", lambda e, b=b: e.activation(out=Tt[b][:], in_=Tt[b][:], func=AF.Ln, bias=1.0),
                         reads=[("T", b)], writes=[("T", b)])
                    S.op("pool", lambda e, b=b: e.tensor_tensor(out=Ut[b][:], in0=Ut[b][:], in1=Tt[b][:], op=ALU.add),
                         reads=[("U", b), ("T", b)], writes=[("U", b)])
                    S.op("act", lambda e, b=b: e.activation(out=Ut[b][:], in_=Ut[b][:], func=AF.Exp, scale=-1.0),
                         reads=[("U", b)], writes=[("U", b)])
                    S.op("dve", lambda e, b=b, h=h, ts_=ts_: e.tensor_tensor(out=Ut[b][:], in0=Acc[0:64, h, ts_], in1=Ut[b][:], op=ALU.mult),
                         reads=[("U", b)] + acc_rd, writes=[("U", b)])
                    S.op("dve", lambda e, b=b, ijz=ijz: e.tensor_tensor(out=Yb[b][:], in0=Ut[b][:], in1=bk(ijz)[0:64, :], op=ALU.mult),
                         reads=[("U", b), bkey(ijz)], writes=[("Y", b)])
                    row0 = sp["rows"][h]
                    S.op("sp", lambda e, b=b, row0=row0, ts_=ts_: e.dma_start(out=g.ycat[row0:row0 + 64, ts_], in_=Yb[b][:]),
                         reads=[("Y", b)], dma_key=("aY", b))
        S.emit_block()


def phase_ssd(g, L, grp):
    nc, S = g.nc, g.S
    NW = 1296
    zc0, xc0, bc0, cc0, dc0 = 2048 + grp * 512, 3072 + grp * 512, 4096 + grp * 128, 4352 + grp * 128, 4608 + grp * 8
    with contextlib.ExitStack() as st:
        sbt = lambda name, shape, dt: st.enter_context(g.sbuf(name, shape, dt))
        pst = lambda name, shape, dt: st.enter_context(g.psum(name, shape, dt))
        wB = sbt("wB", [128, 8, NW], BF16)
        wst = [sbt("bwst%d" % i, [128, 8, 128], F32) for i in range(2)]
        prm_r = sbt("prm_r", [36, 128], F32)
        prm = sbt("prm", [128, 36], F32)
        hp = sbt("hp", [128, 3, 8], F32)
        pre = [sbt("pre%d" % i, [128, 515], F32) for i in range(2)]
        halo = sbt("halo", [128, 6, 3], F32)
        cacc = [sbt("cacc%d" % i, [128, 512], F32) for i in range(2)]
        ctmp = cacc[0]
        xc = [sbt("xc%d" % i, [128, 4, 512], BF16) for i in range(2)]
        fs = [sbt("fs%d" % i, [128, 6, 4, 8], F32) for i in range(2)]
        BT = [sbt("BTt%d" % i, [128, 512], BF16) for i in range(2)]
        CT = [sbt("CTt%d" % i, [128, 512], BF16) for i in range(2)]
        szs = [sbt("szs%d" % i, [128, 4, 512], F32) for i in range(2)]
        dts = [sbt("dts%d" % i, [128, 3, 4, 8], F32) for i in range(2)]
        NQ = 3
        xtm = [sbt("xtm%d" % i, [128, 512], BF16) for i in range(NQ)]
        Btm = [sbt("Btm%d" % i, [128, 128], BF16) for i in range(NQ)]
        xdt = [sbt("xdt%d" % i, [128, 512], BF16) for i in range(NQ)]
        xdte = [sbt("xdte%d" % i, [128, 512], BF16) for i in range(NQ)]
        tD = [sbt("tD%d" % i, [128, 512], BF16) for i in range(NQ)]
        MT = [sbt("MT%d" % i, [128, 8, 128], BF16) for i in range(NQ)]
        dAh = [sbt("dAh%d" % i, [128, 4, 8], BF16) for i in range(2)]
        dAl = [sbt("dAl%d" % i, [128, 4, 8], BF16) for i in range(2)]
        dAr = sbt("dAr", [128, 4, 8], F32)
        rhsH = sbt("rhsH", [128, 8, 128], BF16)
        rhsL = sbt("rhsL", [128, 8, 128], BF16)
        LT = sbt("LT", [128, 8, 128], BF16)
        GTm = sbt("GTm", [128, 128], F32)
        yt = sbt("yt", [128, 512], F32)
        nst = sbt("nst", [128, 4], F32)
        yn = [sbt("yn%d" % i, [128, 512], BF16) for i in range(2)]
        Ybuf = [sbt("Ybuf%d" % i, [128, 4, 512], BF16) for i in range(2)]
        H = sbt("H", [128, 512], F32)
        Hb = sbt("Hb", [128, 512], BF16)
        PA0 = pst("PA0", [128, 512], F32)
        PSEGs = [pst("PSEG%d" % i, [128, 512], F32) for i in range(2)]
        Psm = pst("Psm", [128, 512], F32)
        PBy = pst("PBy", [128, 512], F32)
        PBo = pst("PBo", [128, 512], F32)
        PS2 = pst("PS2", [128, 512], F32)
        Pbf = pst("Pbf", [128, 8, 128], BF16)
        wv_in = g.w_in[L].rearrange("(ch p) n -> p ch n", p=128)

        pieces = [(xc0 + i * 128, 128, 512 + i * 128) for i in range(4)] + [(bc0, 128, 1024), (cc0, 128, 1152)]
        pieces += [(zc0 + i * 128, 128, i * 128) for i in range(4)] + [(dc0, 8, 1280)]
        wball = [("wB", i) for i in range(len(pieces))]
        for i, (c0, cn, o0) in enumerate(pieces):
            b = i % 2
            S.op("sp", lambda e, b=b, c0=c0, cn=cn: e.dma_start(out=wst[b][:, :, 0:cn], in_=wv_in[:, :, c0:c0 + cn]),
                 writes=[("wst", b)], dma_key=("bwst", b))
            ceng = ("act", "dve")[i % 2]
            if ceng == "act":
                S.op("act", lambda e, b=b, cn=cn, o0=o0: e.activation(out=wB[:, :, o0:o0 + cn], in_=wst[b][:, :, 0:cn], func=AF.Copy),
                     reads=[("wst", b)], writes=[("wB", i)])
            else:
                S.op(ceng, lambda e, b=b, cn=cn, o0=o0: e.tensor_copy(out=wB[:, :, o0:o0 + cn], in_=wst[b][:, :, 0:cn]),
                     reads=[("wst", b)], writes=[("wB", i)])
        S.op("pool", lambda e: e.memset(prm_r[:], 0.0), writes=["prm_r0"])
        cw = g.conv_w[L].rearrange("k (cc p) -> k cc p", p=128)
        cbv = g.conv_b[L:L + 1, :].rearrange("o (cc p) -> (o cc) p", p=128)
        swv = g.ssm_w[L:L + 1, :].rearrange("o (cc p) -> (o cc) p", p=128)
        k_ = 0
        for tap in range(4):
            S.op("sp", lambda e, tap=tap: e.dma_start(out=prm_r[tap * 6:tap * 6 + 4, :], in_=cw[tap, grp * 4:grp * 4 + 4, :]),
                 reads=["prm_r0"], writes=[("prm_r", k_)], dma_key=("prm", k_))
            k_ += 1
            for j, c_ in ((4, 8 + grp), (5, 10 + grp)):
                S.op("sp", lambda e, tap=tap, j=j, c_=c_: e.dma_start(out=prm_r[tap * 6 + j:tap * 6 + j + 1, :], in_=cw[tap, c_:c_ + 1, :]),
                     reads=["prm_r0"], writes=[("prm_r", k_)], dma_key=("prm", k_))
                k_ += 1
        S.op("sp", lambda e: e.dma_start(out=prm_r[24:28, :], in_=cbv[grp * 4:grp * 4 + 4, :]),
             reads=["prm_r0"], writes=[("prm_r", k_)], dma_key=("prm", k_))
        k_ += 1
        for j, c_ in ((28, 8 + grp), (29, 10 + grp)):
            S.op("sp", lambda e, j=j, c_=c_: e.dma_start(out=prm_r[j:j + 1, :], in_=cbv[c_:c_ + 1, :]),
                 reads=["prm_r0"], writes=[("prm_r", k_)], dma_key=("prm", k_))
            k_ += 1
        S.op("sp", lambda e: e.dma_start(out=prm_r[30:34, :], in_=swv[grp * 4:grp * 4 + 4, :]),
             reads=["prm_r0"], writes=[("prm_r", k_)], dma_key=("prm", k_))
        k_ += 1
        S.op("pe", lambda e: e.transpose(PA0[:, 0:36], prm_r[:, :], g.cf[0:36, 0, 0:36]),
             reads=[("prm_r", i) for i in range(k_)], writes=["PA0"])
        S.op("dve", lambda e: e.tensor_copy(out=prm[:], in_=PA0[:, 0:36]), reads=["PA0"], writes=["prm"])
        for i, src_ in enumerate((g.dt_bias, g.a_log, g.d_skip)):
            S.op("sp", lambda e, i=i, src_=src_: e.dma_start(out=hp[:, i, :], in_=src_[L:L + 1, grp * 8:grp * 8 + 8].partition_broadcast(128)),
                 writes=[("hp", i)], dma_key=("hp", i))
        S.op("act", lambda e: e.activation(out=hp[:, 1, :], in_=hp[:, 1, :], func=AF.Exp), reads=[("hp", 1)], writes=[("hp", 1)])
        S.op("dve", lambda e: e.tensor_scalar(out=hp[:, 1, :], in0=hp[:, 1, :], scalar1=-1.0, scalar2=None, op0=ALU.mult),
             reads=[("hp", 1)], writes=[("hp", 1)])
        S.op("pool", lambda e: e.memset(halo[:], 0.0), writes=["halo"])
        S.op("pool", lambda e: e.memset(H[:], 0.0), writes=["H"])
        S.op("pool", lambda e: e.memset(Hb[:], 0.0), writes=["Hb"])

        tri = g.cf[:, 2, :]
        u1 = g.cf[:, 3, :]
        ones = g.cf[:, 4, :]
        identf = g.cf[:, 0, :]
        v8 = lambda ap: ap.rearrange("p (h d) -> p h d", h=8)
        b8 = lambda ap: ap.unsqueeze(2).to_broadcast([128, 8, 64])

        def mm_group(out_ap, lhs_fn, rhs_fn, reads, writes):
            def fn(e):
                ins = None
                for ch in range(8):
                    ins = e.matmul(out_ap, lhsT=lhs_fn(ch), rhs=rhs_fn(ch), start=(ch == 0), stop=(ch == 7))
                return ins
            S.op("pe", fn, reads=reads, writes=writes)

        fbanks = [(PA0, "PA0"), (PBy, "PBy"), (PBo, "PBo"), (PS2, "PS2")]

        def front(sc):
            p = sc % 2
            ts_ = slice(sc * 512, (sc + 1) * 512)
            def f_proj(j):
                pb = j % 2
                wofs = 512 + j * 128 if j < 4 else (1024 if j == 4 else 1152)
                fb, fk = fbanks[j % 4]
                mm_group(fb[:], lambda ch, wofs=wofs: wB[:, ch, wofs:wofs + 128], lambda ch: g.hT[:, ch, ts_], [("wB", j)], [fk])
                S.op("pool", lambda e, pb=pb, j=j: e.tensor_copy(out=pre[pb][:, 0:3], in_=halo[:, j, :]),
                     reads=["halo"], writes=[("pre", pb)])
                S.op("act", lambda e, pb=pb, fb=fb: e.activation(out=pre[pb][:, 3:515], in_=fb[:], func=AF.Copy),
                     reads=[fk], writes=[("pre", pb)])
                S.op("pool", lambda e, pb=pb, j=j: e.tensor_copy(out=halo[:, j, :], in_=pre[pb][:, 512:515]),
                     reads=[("pre", pb)], writes=["halo"])

            def f_conv(j):
                pb = j % 2
                if False:
                    S.op("pool", lambda e, pb=pb, j=j: e.tensor_scalar(out=cacc[pb][:], in0=pre[pb][:, 0:512], scalar1=prm[:, j:j + 1], scalar2=None,
                                                                       op0=ALU.mult),
                         reads=[("pre", pb), "prm"], writes=[("cacc", pb)])
                    for tap in range(1, 4):
                        S.op("pool", lambda e, pb=pb, j=j, tap=tap: e.tensor_scalar(out=ctmp[:], in0=pre[pb][:, tap:tap + 512],
                                                                                    scalar1=prm[:, tap * 6 + j:tap * 6 + j + 1], scalar2=None,
                                                                                    op0=ALU.mult),
                             reads=[("pre", pb), "prm"], writes=["ctmp"])
                        S.op("pool", lambda e, pb=pb: e.tensor_tensor(out=cacc[pb][:], in0=cacc[pb][:], in1=ctmp[:], op=ALU.add),
                             reads=["ctmp", ("cacc", pb)], writes=[("cacc", pb)])
                else:
                    S.op("dve", lambda e, pb=pb, j=j: e.tensor_scalar(out=cacc[pb][:], in0=pre[pb][:, 0:512], scalar1=prm[:, j:j + 1], scalar2=None,
                                                                      op0=ALU.mult),
                         reads=[("pre", pb), "prm"], writes=[("cacc", pb)])
                    for tap in range(1, 4):
                        S.op("dve", lambda e, pb=pb, j=j, tap=tap: e.scalar_tensor_tensor(out=cacc[pb][:], in0=pre[pb][:, tap:tap + 512],
                                                                                         scalar=prm[:, tap * 6 + j:tap * 6 + j + 1], in1=cacc[pb][:],
                                                                                         op0=ALU.mult, op1=ALU.add),
                             reads=[("pre", pb), ("cacc", pb), "prm"], writes=[("cacc", pb)])

            def f_silu(j):
                pb = j % 2
                dst = xc[p][:, j, :] if j < 4 else (BT[p][:] if j == 4 else CT[p][:])
                dkey = ("xc", p, j) if j < 4 else (("BT", p) if j == 4 else ("CT", p))
                S.op("act", lambda e, pb=pb, j=j, dst=dst: e.activation(out=dst, in_=cacc[pb][:], func=AF.Silu, bias=prm[:, 24 + j:25 + j]),
                     reads=[("cacc", pb), "prm"], writes=[dkey])

            f_proj(0)
            for j in range(6):
                if j + 1 < 6:
                    f_proj(j + 1)
                f_conv(j)
                f_silu(j)
            for k in range(4):
                tc_ = slice((sc * 4 + k) * 128, (sc * 4 + k + 1) * 128)
                fb, fk = fbanks[(k + 2) % 4]
                mm_group(fb[:], lambda ch, tc_=tc_: g.hT[:, ch, tc_], lambda ch: wB[:, ch, 0:512], [("wB", i) for i in range(6, 10)], [fk])
                S.op("act", lambda e, p=p, k=k, fb=fb: e.activation(out=szs[p][:, k, :], in_=fb[:], func=AF.Silu), reads=[fk], writes=[("sz", p, k)])
            for k in range(4):
                tc_ = slice((sc * 4 + k) * 128, (sc * 4 + k + 1) * 128)
                mm_group(Psm[:, k * 8:(k + 1) * 8], lambda ch, tc_=tc_: g.hT[:, ch, tc_], lambda ch: wB[:, ch, 1280:1288], [("wB", 10)], ["Psm"])
            S.op("dve", lambda e, p=p: e.tensor_tensor(out=dts[p][:, 0, :, :], in0=Psm[:, 0:32].rearrange("p (k h) -> p k h", k=4),
                                                       in1=hp[:, 0, :].unsqueeze(1).to_broadcast([128, 4, 8]), op=ALU.add),
                 reads=["Psm", ("hp", 0)], writes=[("dts", p, 0)])
            S.op("act", lambda e, p=p: e.activation(out=dts[p][:, 0, :, :], in_=dts[p][:, 0, :, :], func=AF.Exp),
                 reads=[("dts", p, 0)], writes=[("dts", p, 0)])
            S.op("act", lambda e, p=p: e.activation(out=dts[p][:, 1, :, :], in_=dts[p][:, 0, :, :], func=AF.Ln, bias=1.0),
                 reads=[("dts", p, 0)], writes=[("dts", p, 1)])
            S.op("dve", lambda e, p=p: e.tensor_tensor(out=dts[p][:, 2, :, :], in0=dts[p][:, 1, :, :],
                                                       in1=hp[:, 1, :].unsqueeze(1).to_broadcast([128, 4, 8]), op=ALU.mult),
                 reads=[("dts", p, 1), ("hp", 1)], writes=[("dts", p, 2)])
            S.op("dve", lambda e, p=p: e.tensor_copy(out=dAh[p][:], in_=dts[p][:, 2, :, :]), reads=[("dts", p, 2)], writes=[("dAh", p)])
            S.op("dve", lambda e, p=p: e.tensor_tensor(out=dAr[:], in0=dts[p][:, 2, :, :], in1=dAh[p][:], op=ALU.subtract),
                 reads=[("dts", p, 2), ("dAh", p)], writes=["dAr"])
            S.op("dve", lambda e, p=p: e.tensor_copy(out=dAl[p][:], in_=dAr[:]), reads=["dAr"], writes=[("dAl", p)])
            for k in range(4):
                S.op("pe", lambda e, p=p, k=k: e.matmul(Psm[:, 64 + k * 8:72 + k * 8], lhsT=tri, rhs=dts[p][:, 2, k, :], start=True, stop=True),
                     reads=[("dts", p, 2)], writes=["Psm"])
                S.op("pe", lambda e, p=p, k=k: e.matmul(Psm[:, 96 + k * 8:104 + k * 8], lhsT=ones, rhs=dts[p][:, 2, k, :], start=True, stop=True),
                     reads=[("dts", p, 2)], writes=["Psm"])
            v48 = lambda ap: ap.rearrange("p (k h) -> p k h", k=4)
            S.op("dve", lambda e, p=p: e.tensor_copy(out=fs[p][:, 0, :, :], in_=v48(Psm[:, 64:96])), reads=["Psm"], writes=[("fs", p, 0)])
            S.op("act", lambda e, p=p: e.activation(out=fs[p][:, 1, :, :], in_=fs[p][:, 0, :, :], func=AF.Exp),
                 reads=[("fs", p, 0)], writes=[("fs", p, 1)])
            S.op("dve", lambda e, p=p: e.tensor_tensor(out=fs[p][:, 2, :, :], in0=v48(Psm[:, 96:128]), in1=fs[p][:, 0, :, :], op=ALU.subtract),
                 reads=["Psm", ("fs", p, 0)], writes=[("fs", p, 2)])
            S.op("act", lambda e, p=p: e.activation(out=fs[p][:, 3, :, :], in_=fs[p][:, 2, :, :], func=AF.Exp),
                 reads=[("fs", p, 2)], writes=[("fs", p, 3)])
            S.op("act", lambda e, p=p: e.activation(out=fs[p][:, 4, :, :], in_=v48(Psm[:, 96:128]), func=AF.Exp),
                 reads=["Psm"], writes=[("fs", p, 4)])
            S.op("dve", lambda e, p=p: e.tensor_tensor(out=fs[p][:, 5, :, :], in0=dts[p][:, 1, :, :], in1=fs[p][:, 3, :, :], op=ALU.mult),
                 reads=[("dts", p, 1), ("fs", p, 3)], writes=[("fs", p, 5)])

        def stageA(c):
            sc, k, q = c // 4, c % 4, c % NQ
            p = sc % 2
            lc = slice(k * 128, (k + 1) * 128)
            dt_ = dts[p][:, 1, k, :]
            dA_ = dts[p][:, 2, k, :]
            trib = g.mask2[:, 128:256]
            S.op("dve", lambda e: e.tensor_tensor(out=rhsH[:], in0=trib.unsqueeze(1).to_broadcast([128, 8, 128]),
                                                  in1=dAh[p][:, k, :].unsqueeze(2).to_broadcast([128, 8, 128]), op=ALU.mult),
                 reads=[("dAh", p)], writes=["rhsH"])
            S.op("dve", lambda e: e.tensor_tensor(out=rhsL[:], in0=trib.unsqueeze(1).to_broadcast([128, 8, 128]),
                                                  in1=dAl[p][:, k, :].unsqueeze(2).to_broadcast([128, 8, 128]), op=ALU.mult),
                 reads=[("dAl", p)], writes=["rhsL"])
            for hf in range(2):
                S.op("pe", lambda e, hf=hf: e.matmul(PSEGs[hf][:], lhsT=g.u1b[:], rhs=rhsH[:, hf * 4:(hf + 1) * 4, :], start=True, stop=False),
                     reads=["rhsH"], writes=[("PSEG", hf)])
                S.op("pe", lambda e, hf=hf: e.matmul(PSEGs[hf][:], lhsT=g.u1b[:], rhs=rhsL[:, hf * 4:(hf + 1) * 4, :], start=False, stop=True),
                     reads=["rhsL"], writes=[("PSEG", hf)])
            for hf in range(2):
                S.op("act", lambda e, hf=hf: e.activation(out=LT[:, hf * 4:(hf + 1) * 4, :], in_=PSEGs[hf][:].rearrange("p (h l) -> p h l", h=4),
                                                          func=AF.Exp),
                     reads=[("PSEG", hf)], writes=[("LT", hf)])
            PA0b = PA0[:].bitcast(BF16)
            def xtr(e):
                ins = None
                for j in range(4):
                    ins = e.transpose(PA0b[:, j * 128:(j + 1) * 128], xc[p][:, j, lc], g.identb[:])
                return ins
            S.op("pe", xtr, reads=[("xc", p, j) for j in range(4)], writes=["PA0"])
            S.op("act", lambda e: e.activation(out=xtm[q][:], in_=PA0b[:, 0:512], func=AF.Copy), reads=["PA0"], writes=[("xtm", q)])
            S.op("pe", lambda e: e.transpose(Pbf[:, 0, :], BT[p][:, lc], g.identb[:]), reads=[("BT", p)], writes=["Pbf"])
            S.op("dve", lambda e: e.tensor_copy(out=Btm[q][:], in_=Pbf[:, 0, :]), reads=["Pbf"], writes=[("Btm", q)])
            S.op("pe", lambda e: e.matmul(Psm[:, 128:256], lhsT=BT[p][:, lc], rhs=CT[p][:, lc], start=True, stop=True),
                 reads=[("BT", p), ("CT", p)], writes=["Psm"])
            S.op("dve", lambda e: e.tensor_tensor(out=GTm[:], in0=Psm[:, 128:256], in1=tri, op=ALU.mult), reads=["Psm"], writes=["GTm"])
            S.op("dve", lambda e: e.tensor_tensor(out=MT[q][:], in0=LT[:], in1=GTm[:].unsqueeze(1).to_broadcast([128, 8, 128]), op=ALU.mult),
                 reads=[("LT", 0), ("LT", 1), "GTm"], writes=[("MT", q)])
            S.op("pool", lambda e: e.tensor_tensor(out=v8(xdt[q][:]), in0=v8(xtm[q][:]), in1=b8(dt_), op=ALU.mult),
                 reads=[("xtm", q), ("dts", p, 1)], writes=[("xdt", q)])
            S.op("pool", lambda e: e.tensor_tensor(out=v8(xdte[q][:]), in0=v8(xtm[q][:]), in1=b8(fs[p][:, 5, k, :]), op=ALU.mult),
                 reads=[("xtm", q), ("fs", p, 5)], writes=[("xdte", q)])
            S.op("pool", lambda e: e.tensor_tensor(out=v8(tD[q][:]), in0=v8(xtm[q][:]), in1=b8(hp[:, 2, :]), op=ALU.mult),
                 reads=[("xtm", q), ("hp", 2)], writes=[("tD", q)])

        def stageB1(c):
            sc, k, q = c // 4, c % 4, c % NQ
            p = sc % 2
            lc = slice(k * 128, (k + 1) * 128)
            S.op("pe", lambda e: e.matmul(PBo[:], lhsT=CT[p][:, lc], rhs=Hb[:], start=True, stop=True),
                 reads=[("CT", p), "Hb"], writes=["PBo"])
            S.op("pe", lambda e: e.matmul(PS2[:], lhsT=Btm[q][:], rhs=xdte[q][:], start=True, stop=True),
                 reads=[("Btm", q), ("xdte", q)], writes=["PS2"])
            S.op("pe", lambda e: e.matmul(PBy[:], lhsT=g.identb[:], rhs=tD[q][:], start=True, stop=False),
                 reads=[("tD", q)], writes=["PBy"])
            for h in range(8):
                S.op("pe", lambda e, h=h: e.matmul(PBy[:, h * 64:(h + 1) * 64], lhsT=MT[q][:, h, :], rhs=xdt[q][:, h * 64:(h + 1) * 64],
                                                   start=False, stop=(h == 7)),
                     reads=[("MT", q), ("xdt", q)], writes=["PBy"])
            S.op("dve", lambda e: e.tensor_tensor(out=v8(H[:]), in0=v8(H[:]), in1=b8(fs[p][:, 4, k, :]), op=ALU.mult),
                 reads=["H", ("fs", p, 4)], writes=["H"])
            S.op("dve", lambda e: e.tensor_tensor(out=H[:], in0=H[:], in1=PS2[:], op=ALU.add), reads=["H", "PS2"], writes=["H"])
            S.op("act", lambda e: e.activation(out=Hb[:], in_=H[:], func=AF.Copy), reads=["H"], writes=["Hb"])
            S.op("dve", lambda e: e.tensor_tensor(out=v8(yt[:]), in0=v8(PBo[:]), in1=b8(fs[p][:, 1, k, :]), op=ALU.mult),
                 reads=["PBo", ("fs", p, 1)], writes=["yt"])
            S.op("dve", lambda e: e.tensor_tensor(out=yt[:], in0=yt[:], in1=PBy[:], op=ALU.add), reads=["yt", "PBy"], writes=["yt"])
            S.op("pool", lambda e: e.tensor_tensor(out=yt[:], in0=yt[:], in1=szs[p][:, k, :], op=ALU.mult),
                 reads=["yt", ("sz", p, k)], writes=["yt"])
            yq = c % 2
            S.op("act", lambda e: e.activation(out=yn[yq][:], in_=yt[:], func=AF.Square, accum_out=nst[:, 0:1]),
                 reads=["yt"], writes=[("yn", yq), ("nst", 0)])
            S.op("act", lambda e: e.activation(out=nst[:, 1:2], in_=nst[:, 0:1], func=AF.Ln, scale=1.0 / 512, bias=EPS),
                 reads=[("nst", 0)], writes=[("nst", 1)])
            S.op("act", lambda e: e.activation(out=nst[:, 2:3], in_=nst[:, 1:2], func=AF.Exp, scale=-0.5),
                 reads=[("nst", 1)], writes=[("nst", 2)])
            S.op("act", lambda e: e.activation(out=yn[yq][:], in_=yt[:], func=AF.Copy, scale=nst[:, 2:3]),
                 reads=["yt", ("nst", 2)], writes=[("yn", yq)])

        def stageB2(c):
            sc, k, q = c // 4, c % 4, c % 2
            p = sc % 2
            lc = slice(k * 128, (k + 1) * 128)
            for j in range(4):
                S.op("pe", lambda e, j=j: e.transpose(Pbf[:, 4 + j, :], yn[q][:, j * 128:(j + 1) * 128], g.identb[:]),
                     reads=[("yn", q)], writes=["Pbf"])
            S.op("dve", lambda e: e.tensor_tensor(out=Ybuf[p][:, :, lc], in0=Pbf[:, 4:8, :],
                                                  in1=prm[:, 30:34].unsqueeze(2).to_broadcast([128, 4, 128]), op=ALU.mult),
                 reads=["Pbf", "prm"], writes=[("Ybuf", p)])
            if k == 3:
                r0 = 512 + grp * 512
                ts_ = slice(sc * 512, (sc + 1) * 512)
                S.op("sp", lambda e: e.dma_start(out=g.ycat[r0:r0 + 512, ts_].rearrange("(cc p) t -> p cc t", p=128), in_=Ybuf[p][:]),
                     reads=[("Ybuf", p)], dma_key=("Ybuf", p))

        NCH = SEQ // 128
        front(0)
        stageA(0)
        stageA(1)
        for c in range(NCH):
            S.capture()
            if c + 3 < NCH and (c + 3) % 4 == 0:
                front((c + 3) // 4)
            lf = S.end_capture()
            S.capture()
            stageB1(c)
            lb1 = S.end_capture()
            S.capture()
            if c + 2 < NCH:
                stageA(c + 2)
            la = S.end_capture()
            S.capture()
            if c >= 1:
                stageB2(c - 1)
            lb2 = S.end_capture()
            S.replay_merged([lf])
            S.replay_merged([lb1, la, lb2])
        stageB2(NCH - 1)
        S.emit_block()


def phase_out(g, L, src):
    nc, S = g.nc, g.S
    with contextlib.ExitStack() as st:
        sbt = lambda name, shape, dt: st.enter_context(g.sbuf(name, shape, dt))
        wo = sbt("wo", [128, 16, DM], BF16)
        wst = [sbt("owst%d" % i, [128, DM], F32) for i in range(3)]
        yc = [sbt("oyc%d" % i, [128, 16, 512], BF16) for i in range(2)]
        xt = [sbt("oxt%d" % i, [128, DM], F32) for i in range(2)]
        t1 = [sbt("ot1%d" % i, [128, DM], F32) for i in range(2)]
        junk = sbt("ojunk", [128, DM], BF16)
        stat = [sbt("ost%d" % i, [128, 4], F32) for i in range(2)]
        PY = [st.enter_context(g.psum("oPY%d" % i, [128, 2, 512], F32)) for i in range(2)]
        for cc in range(16):
            b = cc % 3
            S.op("sp", lambda e, b=b, cc=cc: e.dma_start(out=wst[b][:], in_=g.w_out[L, cc * 128:(cc + 1) * 128, :]),
                 writes=[("wst", b)], dma_key=("owst", b))
            ceng = ("act", "dve")[cc % 2]
            if ceng == "act":
                S.op("act", lambda e, b=b, cc=cc: e.activation(out=wo[:, cc, :], in_=wst[b][:], func=AF.Copy),
                     reads=[("wst", b)], writes=[("wo", cc)])
            else:
                S.op(ceng, lambda e, b=b, cc=cc: e.tensor_copy(out=wo[:, cc, :], in_=wst[b][:]), reads=[("wst", b)], writes=[("wo", cc)])
        def load_yc(gi):
            gb = gi % 2
            ts_ = slice(gi * 512, (gi + 1) * 512)
            S.op("sp", lambda e: e.dma_start(out=yc[gb][:], in_=g.ycat[:, ts_].rearrange("(cc p) t -> p cc t", p=128)),
                 writes=[("yc", gb)], dma_key=("oyc", gb))

        load_yc(0)
        for gi in range(8):
            gb = gi % 2
            if gi + 1 < 8:
                load_yc(gi + 1)
            for k in range(4):
                t = gi * 4 + k
                b = t % 2
                rows = slice(t * 128, (t + 1) * 128)
                lc = slice(k * 128, (k + 1) * 128)
                S.op("sp", lambda e, b=b, rows=rows: e.dma_start(out=xt[b][:], in_=src[rows, :]), writes=[("x", b)], dma_key=("oxt", b))
                for nh in range(2):
                    for cc in range(16):
                        S.op("pe", lambda e, b=b, nh=nh, cc=cc, gb=gb, lc=lc: e.matmul(PY[b][:, nh, :], lhsT=yc[gb][:, cc, lc],
                                                                                       rhs=wo[:, cc, nh * 512:(nh + 1) * 512],
                                                                                       start=(cc == 0), stop=(cc == 15)),
                             reads=[("yc", gb), ("wo", cc)], writes=[("PY", b)])
                S.op("act", lambda e, b=b: e.activation(out=junk[:].rearrange("p (a n) -> p a n", a=2), in_=PY[b][:], func=AF.Square,
                                                        accum_out=stat[b][:, 0:1]),
                     reads=[("PY", b)], writes=["junk", ("s0", b)])
                S.op("act", lambda e, b=b: e.activation(out=stat[b][:, 1:2], in_=stat[b][:, 0:1], func=AF.Sqrt, scale=1.0 / DM, bias=EPS),
                     reads=[("s0", b)], writes=[("s1", b)])
                S.op("dve", lambda e, b=b: e.reciprocal(out=stat[b][:, 2:3], in_=stat[b][:, 1:2]), reads=[("s1", b)], writes=[("s2", b)])
                S.op("dve", lambda e, b=b: e.scalar_tensor_tensor(out=t1[b][:].rearrange("p (a n) -> p a n", a=2), in0=PY[b][:],
                                                                  scalar=stat[b][:, 2:3],
                                                                  in1=g.modb[:, 2 * DM:3 * DM].rearrange("p (a n) -> p a n", a=2),
                                                                  op0=ALU.mult, op1=ALU.mult),
                     reads=[("PY", b), ("s2", b)], writes=[("t1", b)])
                S.op("pool", lambda e, b=b: e.tensor_tensor(out=t1[b][:], in0=t1[b][:], in1=xt[b][:], op=ALU.add),
                     reads=[("t1", b), ("x", b)], writes=[("t1", b)])
                S.op("pool", lambda e, b=b, rows=rows: e.dma_start(out=g.out[rows, :], in_=t1[b][:]), reads=[("t1", b)], dma_key=("ot1", b))
        S.emit_block()


def _consts():
    i = np.arange(128)
    c = np.zeros((128, 7, 128), np.float32)
    c[:, 0, :] = np.eye(128)
    c[:, 1, :] = (i[:, None] >= i[None, :])
    c[:, 2, :] = (i[:, None] <= i[None, :])
    c[:, 3, :] = (i[:, None] > i[None, :])
    c[:, 4, :] = 1.0
    c[64, 5, 0:64] = 1.0
    return c


_PROG = {}
FUSED = True
_WNAMES = ("ada_w", "ada_b", "pre_norm_w", "post_norm_w", "w_in", "conv_w", "conv_b", "dt_bias", "a_log",
           "d_skip", "ssm_norm_w", "sinks", "w_out")


def kernel(x, c, ada_w, ada_b, pre_norm_w, post_norm_w, w_in, conv_w, conv_b,
           dt_bias, a_log, d_skip, ssm_norm_w, sinks, w_out):
    f = lambda a: np.ascontiguousarray(np.asarray(a, dtype=np.float32))
    ws = dict(ada_w=f(ada_w), ada_b=f(ada_b), pre_norm_w=f(pre_norm_w), post_norm_w=f(post_norm_w),
              w_in=f(w_in), conv_w=f(conv_w), conv_b=f(conv_b), dt_bias=f(dt_bias), a_log=f(a_log),
              d_skip=f(d_skip), ssm_norm_w=f(ssm_norm_w), sinks=f(sinks), w_out=f(w_out))
    x = f(x)
    c = f(c)
    consts = _consts()
    ccols = [np.ascontiguousarray(c[b].reshape(8, 128).T) for b in range(8)]
    if FUSED:
        if "fused" not in _PROG:
            _PROG["fused"] = build(n_layers=DEPTH, depth_dim=DEPTH)
        in_maps = []
        for b in range(8):
            m = dict(ws)
            m["consts"] = consts
            m["x"] = x[b]
            m["c_col"] = ccols[b]
            in_maps.append(m)
        res = run_bass_kernel_spmd(_PROG["fused"], in_maps, core_ids=list(range(8)))
        return np.stack([np.asarray(r["out"]) for r in res.results], axis=0).astype(np.float32)
    if "layer" not in _PROG:
        _PROG["layer"] = build(n_layers=1, depth_dim=1)
    cur = [x[b] for b in range(8)]
    for L in range(DEPTH):
        wl = {k: np.ascontiguousarray(ws[k][L:L + 1]) for k in _WNAMES}
        in_maps = []
        for b in range(8):
            m = dict(wl)
            m["consts"] = consts
            m["x"] = cur[b]
            m["c_col"] = ccols[b]
            in_maps.append(m)
        res = run_bass_kernel_spmd(_PROG["layer"], in_maps, core_ids=list(range(8)))
        cur = [np.ascontiguousarray(np.asarray(r["out"], dtype=np.float32)) for r in res.results]
    return np.stack(cur, axis=0).astype(np.float32)
```

```python
import contextlib
import numpy as np
import concourse.bass as bass
import concourse.mybir as mybir
from concourse.bass_utils import run_bass_kernel_spmd

F32 = mybir.dt.float32
BF16 = mybir.dt.bfloat16
AF = mybir.ActivationFunctionType
ALU = mybir.AluOpType

SEQ = 4096
DM = 1024
NT = SEQ // 128
EPS = 1e-6
DEPTH = 4
IN_COLS = 5904
C_OFF = 4624
ENGS = ("pe", "act", "dve", "pool", "sp")


def sl(start, n, step=1):
    return slice(start, start + step * (n - 1) + 1, step)


class Sched:
    def __init__(self, nc, stack):
        self.nc = nc
        self.stack = stack
        self.esem = {e: stack.enter_context(nc.semaphore("s_" + e)) for e in ENGS}
        self._names = {id(v): "s_" + k for k, v in self.esem.items()}
        self.ecount = {e: 0 for e in ENGS}
        self.dsem = {}
        self.dcount = {}
        self.waited = {e: {} for e in ENGS}
        self.n_ops = 0
        self.reset_block()

    def reset_block(self):
        self.ops = []
        self.last_w = {}
        self.readers = {}

    def _dma_sem(self, key):
        if key not in self.dsem:
            self.dsem[key] = self.stack.enter_context(self.nc.semaphore("d%d" % len(self.dsem)))
            self.dcount[key] = 0
            self._names[id(self.dsem[key])] = "d_" + str(key)
        return self.dsem[key]

    def capture(self):
        self._cap = []
        return self._cap

    def end_capture(self):
        lst, self._cap = self._cap, None
        return lst

    _DUR = {"pe": 0.2, "act": 0.6, "dve": 0.75, "pool": 1.1, "sp": 2.0}

    def replay_merged(self, lists):
        if not hasattr(self, "_sim_eng"):
            self._sim_eng = {e: 0.0 for e in ENGS}
            self._sim_key = {}
        its = [list(l) for l in lists if l]
        pos = [0] * len(its)
        while True:
            best, best_t = None, None
            for i in range(len(its)):
                if pos[i] >= len(its[i]):
                    continue
                eng, fn, reads, writes, dma_key = its[i][pos[i]]
                t = self._sim_eng[eng]
                for k in reads:
                    t = max(t, self._sim_key.get(k, 0.0))
                for k in writes:
                    t = max(t, self._sim_key.get(k, 0.0))
                if best is None or t < best_t - 1e-9:
                    best, best_t = i, t
            if best is None:
                break
            a = its[best][pos[best]]
            pos[best] += 1
            eng, fn, reads, writes, dma_key = a
            fin = best_t + self._DUR[eng]
            self._sim_eng[eng] = fin
            for k in writes:
                self._sim_key[k] = fin
            self.op(*a)

    def op(self, eng, fn, reads=(), writes=(), dma_key=None):
        if getattr(self, "_cap", None) is not None:
            self._cap.append((eng, fn, tuple(reads), tuple(writes), dma_key))
            return None
        idx = len(self.ops)
        deps = set()
        for k in reads:
            w = self.last_w.get(k)
            if w is not None:
                deps.add(w)
        for k in writes:
            w = self.last_w.get(k)
            if w is not None:
                deps.add(w)
            for r in self.readers.get(k, ()):
                deps.add(r)
        deps.discard(idx)
        self.ops.append(dict(eng=eng, fn=fn, deps=deps, dma_key=dma_key, milestone=False))
        for k in reads:
            self.readers.setdefault(k, []).append(idx)
        for k in writes:
            self.last_w[k] = idx
            self.readers[k] = []
        return idx

    def emit_block(self, name=None):
        nc = self.nc
        ops = self.ops
        for o in ops:
            keep = set()
            for d in o["deps"]:
                s = ops[d]
                if s["dma_key"] is None and o["dma_key"] is None and s["eng"] == o["eng"] == "pe":
                    continue
                keep.add(d)
            o["deps"] = keep
            for d in keep:
                if ops[d]["dma_key"] is None:
                    ops[d]["milestone"] = True
        ecount = dict(self.ecount)
        dcount = dict(self.dcount)
        for o in ops:
            if o["dma_key"] is not None:
                self._dma_sem(o["dma_key"])
                dcount[o["dma_key"]] = dcount.get(o["dma_key"], 0) + 16
                o["dval"] = dcount[o["dma_key"]]
            elif o["milestone"]:
                ecount[o["eng"]] += 1
                o["mval"] = ecount[o["eng"]]
        per_eng = {e: [o for o in ops if o["eng"] == e] for e in ENGS}
        final_d = dict(dcount)
        sched = self

        def emit_engine(ename, engine):
            waited = sched.waited[ename]
            for o in per_eng[ename]:
                need = {}
                for d in o["deps"]:
                    s = ops[d]
                    if s["dma_key"] is not None:
                        sem, val = sched.dsem[s["dma_key"]], s["dval"]
                    else:
                        sem, val = sched.esem[s["eng"]], s["mval"]
                    key = sched._names[id(sem)]
                    if val > need.get(key, (None, 0))[1]:
                        need[key] = (sem, val)
                for key, (sem, val) in need.items():
                    if waited.get(key, 0) >= val:
                        continue
                    engine.wait_ge(sem, val)
                    waited[key] = val
                ins = o["fn"](engine)
                if o["dma_key"] is not None:
                    ins.then_inc(sched.dsem[o["dma_key"]], 16)
                elif o["milestone"]:
                    ins.then_inc(sched.esem[ename], 1)
            if ename == "sp":
                for k, v in final_d.items():
                    sem = sched.dsem[k]
                    key = sched._names[id(sem)]
                    if waited.get(key, 0) < v:
                        engine.wait_ge(sem, v)
                        waited[key] = v

        with nc.Block(name) as block:
            @block.tensor
            def _(e):
                emit_engine("pe", e)

            @block.scalar
            def _(e):
                emit_engine("act", e)

            @block.vector
            def _(e):
                emit_engine("dve", e)

            @block.gpsimd
            def _(e):
                emit_engine("pool", e)

            @block.sync
            def _(e):
                emit_engine("sp", e)
        self.ecount = ecount
        self.dcount = dcount
        self.n_ops += len(ops)
        self.reset_block()


class K:
    pass


def dbg(g, name, ap, shape, dtype, reads):
    if not getattr(g, "debug", False):
        return
    import os
    taps = os.environ.get("DBG_TAPS", "")
    if not any(name == t or name.startswith(t + "_") for t in taps.split(",") if t):
        return
    d = g.nc.dram_tensor("dbg_" + name, list(shape), dtype, kind="ExternalOutput").ap()
    idx = tuple(slice(None) for _ in shape)
    g.S.op("sp", lambda e: e.dma_start(out=d[idx], in_=ap), reads=reads, dma_key=("dbg", name))


def build(n_layers=DEPTH, debug=False, phases=("mod", "p1", "att", "ssd", "out"), depth_dim=DEPTH):
    nc = bass.Bass("TRN2", target_bir_lowering=False)

    def din(name, shape):
        return nc.dram_tensor(name, shape, F32, kind="ExternalInput").ap()

    g = K()
    g.nc = nc
    g.uid = [0]
    g.debug = debug

    def _uniq(name):
        g.uid[0] += 1
        return "%s_%d" % (name, g.uid[0])
    g.sbuf = lambda name, shape, dt: nc.sbuf_tensor(_uniq(name), shape, dt)
    g.psum = lambda name, shape, dt: nc.psum_tensor(_uniq(name), shape, dt)
    g.x_in = din("x", [SEQ, DM])
    g.c_col = din("c_col", [128, 8])
    DD = depth_dim
    g.ada_w = din("ada_w", [DD, DM, 3 * DM])
    g.ada_b = din("ada_b", [DD, 3 * DM])
    g.pre_w = din("pre_norm_w", [DD, DM])
    g.post_w = din("post_norm_w", [DD, DM])
    g.w_in = din("w_in", [DD, DM, IN_COLS])
    g.conv_w = din("conv_w", [DD, 4, 1536])
    g.conv_b = din("conv_b", [DD, 1536])
    g.dt_bias = din("dt_bias", [DD, 16])
    g.a_log = din("a_log", [DD, 16])
    g.d_skip = din("d_skip", [DD, 16])
    g.ssm_w = din("ssm_norm_w", [DD, DM])
    g.sinks = din("sinks", [DD, 8])
    g.w_out = din("w_out", [DD, 2 * DM, DM])
    g.consts = din("consts", [128, 7, 128])
    g.out = nc.dram_tensor("out", [SEQ, DM], F32, kind="ExternalOutput").ap()
    g.ycat = nc.dram_tensor("ycat", [2 * DM, SEQ], BF16,
                            kind="ExternalOutput" if debug else "Internal").ap()

    with contextlib.ExitStack() as st:
        S = Sched(nc, st)
        g.S = S
        sb = lambda name, shape, dt: st.enter_context(nc.sbuf_tensor(name, shape, dt))
        g.cf = sb("cf", [128, 7, 128], F32)
        g.identb = sb("identb", [128, 128], BF16)
        g.mask2 = sb("mask2", [128, 256], BF16)
        g.u1b = sb("u1b", [128, 128], BF16)
        g.cbc = sb("cbc", [128, 8, 128], F32)
        g.modb = sb("modb", [128, 3 * DM], F32)
        g.hT = sb("hT", [128, 8, SEQ], BF16)

        phase_init(g)
        for L in range(n_layers):
            src = g.x_in if L == 0 else g.out
            if "mod" in phases:
                phase_mod(g, L)
            if "p1" in phases:
                phase_p1(g, L, src)
            if "att" in phases:
                specs = []
                for hp in range(4):
                    base = hp * 128
                    specs.append(dict(q=[(base, 128)], k=[(512 + base, 128)], v=[(1024 + base, 128)],
                                      z=[(1536 + base, 128)], pats=(1, 4, 16), sink=None,
                                      rows=(base, base + 64)))
                for i in range(4):
                    specs.append(dict(q=[(C_OFF + i * 64, 64), (C_OFF + (4 + i) * 64, 64)],
                                      k=[(C_OFF + 1024, 128)], v=[(C_OFF + 1152, 128)],
                                      z=[(C_OFF + 512 + i * 64, 64), (C_OFF + 512 + (4 + i) * 64, 64)],
                                      pats=(1,), sink=(i, 4 + i), reuse_kv=(i > 0),
                                      rows=(1536 + i * 64, 1536 + (4 + i) * 64)))
                phase_att(g, L, specs)
            if "ssd" in phases:
                for grp in range(2):
                    phase_ssd(g, L, grp)
            if "out" in phases:
                phase_out(g, L, src)
        g.n_ops = S.n_ops
    return nc


def phase_init(g):
    nc, S = g.nc, g.S
    with contextlib.ExitStack() as st:
        cc = st.enter_context(g.sbuf("cc", [128, 8], F32))
        ca = st.enter_context(g.sbuf("ca", [128, 8], F32))
        S.op("sp", lambda e: e.dma_start(out=g.cf[:], in_=g.consts[:, :, :]), writes=["cf"], dma_key="cf")
        S.op("sp", lambda e: e.dma_start(out=cc[:], in_=g.c_col[:, :]), writes=["cc"], dma_key="cc")
        S.op("pool", lambda e: e.tensor_copy(out=g.identb[:], in_=g.cf[:, 0, :]), reads=["cf"], writes=["identb"])
        S.op("pool", lambda e: e.tensor_copy(out=g.mask2[:].rearrange("p (a b) -> p a b", a=2), in_=g.cf[:, 1:3, :]),
             reads=["cf"], writes=["mask2"])
        S.op("pool", lambda e: e.tensor_copy(out=g.u1b[:], in_=g.cf[:, 3, :]), reads=["cf"], writes=["u1b"])
        S.op("act", lambda e: e.activation(out=ca[:], in_=cc[:], func=AF.Silu), reads=["cc"], writes=["ca"])
        S.op("dve", lambda e: e.tensor_copy(out=g.cbc[:], in_=ca[:].unsqueeze(2).to_broadcast([128, 8, 128])),
             reads=["ca"], writes=["cbc"])
        S.emit_block("init")


def phase_mod(g, L):
    nc, S = g.nc, g.S
    with contextlib.ExitStack() as st:
        stage = [st.enter_context(g.sbuf("mstage%d" % i, [128, 8, 512], F32)) for i in range(3)]
        adab = st.enter_context(g.sbuf("adab", [128, 3 * DM], F32))
        pw = st.enter_context(g.sbuf("pw", [128, 2, DM], F32))
        ps = [st.enter_context(g.psum("mps%d" % i, [128, 512], F32)) for i in range(2)]
        S.op("sp", lambda e: e.dma_start(out=adab[:], in_=g.ada_b[L:L + 1, :].partition_broadcast(128)),
             writes=["adab"], dma_key="adab")
        S.op("sp", lambda e: e.dma_start(out=pw[:, 0, :], in_=g.pre_w[L:L + 1, :].partition_broadcast(128)),
             writes=["pw0"], dma_key="pw0")
        S.op("sp", lambda e: e.dma_start(out=pw[:, 1, :], in_=g.post_w[L:L + 1, :].partition_broadcast(128)),
             writes=["pw1"], dma_key="pw1")
        aw = g.ada_w[L].rearrange("(ch p) n -> p ch n", p=128)
        for grp in range(6):
            b = grp % 3
            pb = grp % 2
            cs = slice(grp * 512, (grp + 1) * 512)
            S.op("sp", lambda e, b=b, cs=cs: e.dma_start(out=stage[b][:], in_=aw[:, :, cs]),
                 writes=[("mst", b)], dma_key=("mst", b))
            for ch in range(8):
                S.op("pe", lambda e, b=b, pb=pb, ch=ch: e.matmul(ps[pb][:], lhsT=g.cbc[:, ch, :], rhs=stage[b][:, ch, :],
                                                                 start=(ch == 0), stop=(ch == 7)),
                     reads=[("mst", b), "cbc"], writes=[("mps", pb)])
            S.op("dve", lambda e, pb=pb, cs=cs: e.tensor_tensor(out=g.modb[:, cs], in0=ps[pb][:], in1=adab[:, cs], op=ALU.add),
                 reads=[("mps", pb), "adab"], writes=["modb"])
        S.op("dve", lambda e: e.scalar_tensor_tensor(out=g.modb[:, DM:2 * DM], in0=g.modb[:, DM:2 * DM], scalar=1.0,
                                                     in1=pw[:, 0, :], op0=ALU.add, op1=ALU.mult),
             reads=["modb", "pw0"], writes=["modb"])
        S.op("dve", lambda e: e.tensor_tensor(out=g.modb[:, 2 * DM:3 * DM], in0=g.modb[:, 2 * DM:3 * DM], in1=pw[:, 1, :],
                                              op=ALU.mult),
             reads=["modb", "pw1"], writes=["modb"])
        S.emit_block("mod%d" % L)


def phase_p1(g, L, src):
    nc, S = g.nc, g.S
    NB = 4
    with contextlib.ExitStack() as st:
        xt = [st.enter_context(g.sbuf("p1x%d" % i, [128, DM], F32)) for i in range(NB)]
        tmp = [st.enter_context(g.sbuf("p1t%d" % i, [128, DM], F32)) for i in range(NB)]
        hb = [st.enter_context(g.sbuf("p1h%d" % i, [128, DM], BF16)) for i in range(NB)]
        junk = st.enter_context(g.sbuf("p1junk", [128, DM], BF16))
        stat = [st.enter_context(g.sbuf("p1s%d" % i, [128, 4], F32)) for i in range(NB)]
        pst = [st.enter_context(g.psum("p1ps%d" % i, [128, 8, 128], BF16)) for i in range(2)]

        def s1(t):
            b = t % NB
            rows = slice(t * 128, (t + 1) * 128)
            S.op("sp", lambda e: e.dma_start(out=xt[b][:], in_=src[rows, :]), writes=[("x", b)], dma_key=("p1x", b))
            S.op("act", lambda e: e.activation(out=junk[:], in_=xt[b][:], func=AF.Square, accum_out=stat[b][:, 0:1]),
                 reads=[("x", b)], writes=["junk", ("s0", b)])
            S.op("act", lambda e: e.activation(out=stat[b][:, 1:2], in_=stat[b][:, 0:1], func=AF.Sqrt, scale=1.0 / DM, bias=EPS),
                 reads=[("s0", b)], writes=[("s1", b)])
            S.op("dve", lambda e: e.reciprocal(out=stat[b][:, 2:3], in_=stat[b][:, 1:2]), reads=[("s1", b)], writes=[("s2", b)])
            S.op("dve", lambda e: e.scalar_tensor_tensor(out=tmp[b][:], in0=xt[b][:], scalar=stat[b][:, 2:3],
                                                         in1=g.modb[:, DM:2 * DM], op0=ALU.mult, op1=ALU.mult),
                 reads=[("x", b), ("s2", b)], writes=[("t", b)])
            S.op("pool" if t % 2 == 0 else "dve",
                 lambda e: e.tensor_tensor(out=hb[b][:], in0=tmp[b][:], in1=g.modb[:, 0:DM], op=ALU.add),
                 reads=[("t", b)], writes=[("h", b)])

        def s2(t):
            b = t % NB
            pb = t % 2
            rows = slice(t * 128, (t + 1) * 128)
            for ch in range(8):
                S.op("pe", lambda e, ch=ch: e.transpose(pst[pb][:, ch, :], hb[b][:, ch * 128:(ch + 1) * 128], g.identb[:]),
                     reads=[("h", b)], writes=[("ps", pb)])
            S.op("act", lambda e: e.activation(out=g.hT[:, :, rows], in_=pst[pb][:], func=AF.Copy),
                 reads=[("ps", pb)], writes=[("hT", t)])

        s1(0)
        s1(1)
        for t in range(NT):
            if t + 2 < NT:
                s1(t + 2)
            s2(t)
        S.emit_block("p1_%d" % L)


def _groups(d, b):
    if d == 1:
        return [b // 4]
    if d == 4:
        return [b]
    return [4 * b + i for i in range(4)]


def phase_att(g, L, specs):
    nc, S = g.nc, g.S
    with contextlib.ExitStack() as st:
        sbt = lambda name, shape, dt: st.enter_context(g.sbuf(name, shape, dt))
        wst = [sbt("awst%d" % i, [128, 8, 128], F32) for i in range(2)]
        wts = [{n: sbt("aw_%s%d" % (n, i), [128, 8, 128], BF16) for n in "qkvz"} for i in range(2)]
        QT = sbt("QT", [128, SEQ], BF16)
        KT = sbt("KT", [128, SEQ], BF16)
        VT = sbt("VTf", [128, SEQ], BF16)
        Vt = {d: sbt("Vt%d" % d, [128, 32, 2, 65], BF16) for d in (1, 4, 16)}
        NSB = 3
        Et = [sbt("E%d" % i, [128, 2, 256], BF16) for i in range(NSB)]
        Pt = [sbt("P%d" % i, [128, 2, 256], BF16) for i in range(NSB)]
        Acc = sbt("Acc", [65, 2, SEQ], F32)
        Ut = [sbt("U%d" % i, [64, 512], F32) for i in range(2)]
        Tt = [sbt("T%d" % i, [64, 512], F32) for i in range(2)]
        Yb = [sbt("Yb%d" % i, [64, 512], BF16) for i in range(2)]
        sks = [sbt("sk%d" % i, [64, 2], F32) for i in range(2)]
        Sp = [st.enter_context(g.psum("aS%d" % i, [128, 2, 512], F32)) for i in range(NSB)]
        Op = [st.enter_context(g.psum("aO%d" % i, [128, 512], F32)) for i in range(2)]
        banks = [(i, j) for i in range(NSB) for j in range(2)]
        bk = lambda ij: Sp[ij[0]][:, ij[1], :]
        bkey = lambda ij: ("Sb", ij[0], ij[1])
        wv_in = g.w_in[L].rearrange("(ch p) n -> p ch n", p=128)
        for d in (1, 4, 16):
            S.op("pool", lambda e, d=d: e.memset(Vt[d][:, :, :, 64:65], 1.0), writes=[("Vone", d)])
        pj = 0
        step = 0
        fj = 0
        def load_weights(si):
            sp = specs[si]
            wt = wts[si % 2]
            sk = sks[si % 2]
            for wi, n in enumerate("qkvz"):
                b = wi % 2
                off = 0
                for pi, (c0, cn) in enumerate(sp[n]):
                    S.op("sp", lambda e, b=b, c0=c0, cn=cn, off=off: e.dma_start(out=wst[b][:, :, off:off + cn],
                                                                               in_=wv_in[:, :, c0:c0 + cn]),
                         writes=[("wst", b, pi)], dma_key=("awst", b, pi))
                    off += cn
                ceng = ("pool", "dve", "pool", "act")[wi]
                if ceng == "act":
                    S.op("act", lambda e, b=b, n=n, wt=wt: e.activation(out=wt[n][:], in_=wst[b][:], func=AF.Copy),
                         reads=[("wst", b, 0), ("wst", b, 1)], writes=[("w", n, si % 2)])
                else:
                    S.op(ceng, lambda e, b=b, n=n, wt=wt: e.tensor_copy(out=wt[n][:], in_=wst[b][:]),
                         reads=[("wst", b, 0), ("wst", b, 1)], writes=[("w", n, si % 2)])
            if sp["sink"] is not None:
                for h in range(2):
                    hh = sp["sink"][h]
                    S.op("sp", lambda e, h=h, hh=hh, sk=sk: e.dma_start(out=sk[:, h:h + 1],
                                                                        in_=g.sinks[L:L + 1, hh:hh + 1].partition_broadcast(64)),
                         writes=[("skr", si % 2, h)], dma_key=("sk", si % 2, h))
                S.op("act", lambda e, sk=sk: e.activation(out=sk[:], in_=sk[:], func=AF.Exp),
                     reads=[("skr", si % 2, 0), ("skr", si % 2, 1)], writes=[("sk", si % 2)])

        load_weights(0)
        for si, sp in enumerate(specs):
            pats = sp["pats"]
            wt = wts[si % 2]
            sk = sks[si % 2]
            wk = lambda n, si=si: ("w", n, si % 2)
            for n in range(8):
                ts_ = slice(n * 512, (n + 1) * 512)
                for nm, eng in (("q", "act"), ("k", "dve"), ("v", "act")):
                    if nm != "q" and sp.get("reuse_kv"):
                        continue
                    ij = banks[pj % len(banks)]
                    pj += 1
                    for ch in range(8):
                        S.op("pe", lambda e, ij=ij, ch=ch, nm=nm, ts_=ts_, wt=wt: e.matmul(bk(ij), lhsT=wt[nm][:, ch, :], rhs=g.hT[:, ch, ts_],
                                                                                         start=(ch == 0), stop=(ch == 7)),
                             reads=[wk(nm)], writes=[bkey(ij)])
                    if nm == "q":
                        S.op("act", lambda e, ij=ij, ts_=ts_: e.activation(out=QT[:, ts_], in_=bk(ij), func=AF.Copy, scale=0.125),
                             reads=[bkey(ij)], writes=[("QT", n)])
                    elif nm == "k":
                        S.op("dve", lambda e, ij=ij, ts_=ts_: e.tensor_copy(out=KT[:, ts_], in_=bk(ij)),
                             reads=[bkey(ij)], writes=[("KT", n)])
                    else:
                        S.op("act", lambda e, ij=ij, ts_=ts_: e.activation(out=VT[:, ts_], in_=bk(ij), func=AF.Copy),
                             reads=[bkey(ij)], writes=[("VT", n)])
            for d in (() if sp.get("reuse_kv") else pats):
                nb = 32 // d
                for b4 in range(8):
                    ij = banks[pj % len(banks)]
                    pj += 1
                    pb16 = lambda ij: bk(ij).bitcast(BF16)
                    for j in range(4):
                        blk = b4 * 4 + j
                        r, bb = blk // nb, blk % nb
                        tok = sl(r + d * 128 * bb, 128, d)
                        S.op("pe", lambda e, ij=ij, j=j, tok=tok: e.transpose(pb16(ij)[:, j * 128:(j + 1) * 128], VT[:, tok], g.identb[:]),
                             reads=[("VT", x) for x in _groups(d, bb)], writes=[bkey(ij)])
                    vsrc = lambda ij: pb16(ij)[:, 0:512].rearrange("p (j h c) -> p j h c", j=4, h=2)
                    if b4 % 2 == 0:
                        S.op("dve", lambda e, ij=ij, b4=b4, d=d: e.tensor_copy(out=Vt[d][:, b4 * 4:(b4 + 1) * 4, :, 0:64], in_=vsrc(ij)),
                             reads=[bkey(ij)], writes=[("Vt", d, b4)])
                    else:
                        S.op("act", lambda e, ij=ij, b4=b4, d=d: e.activation(out=Vt[d][:, b4 * 4:(b4 + 1) * 4, :, 0:64], in_=vsrc(ij),
                                                                             func=AF.Copy),
                             reads=[bkey(ij)], writes=[("Vt", d, b4)])
            if si + 1 < len(specs):
                load_weights(si + 1)
            steps = []
            for pi, d in enumerate(pats):
                nb = 32 // d
                for r in range(d):
                    for bb in range(nb):
                        steps.append(dict(pi=pi, d=d, r=r, bb=bb, nb=nb, s_=step % NSB, o_=step % 2,
                                          meng="dve" if step % 2 == 0 else "pool"))
                        step += 1

            def part1(stp):
                d, r, bb, s_, meng = stp["d"], stp["r"], stp["bb"], stp["s_"], stp["meng"]
                tq = sl(r + d * 128 * bb, 128, d)
                tp = sl(r + d * 128 * (bb - 1), 128, d) if bb > 0 else None
                lo = 0 if bb > 0 else 128
                gq = _groups(d, bb)
                gk = gq + (_groups(d, bb - 1) if bb > 0 else [])
                rd = [("QT", x) for x in gq] + [("KT", x) for x in set(gk)]
                skeys = [("Sb", s_, 0), ("Sb", s_, 1)]
                for h in range(2):
                    hs = slice(64 * h, 64 * h + 64)
                    S.op("pe", lambda e, s_=s_, h=h, hs=hs, tq=tq: e.matmul(Sp[s_][:, h, 128:256], lhsT=KT[hs, tq], rhs=QT[hs, tq],
                                                                           start=True, stop=True),
                         reads=rd, writes=skeys)
                    if bb > 0:
                        S.op("pe", lambda e, s_=s_, h=h, hs=hs, tq=tq, tp=tp: e.matmul(Sp[s_][:, h, 0:128], lhsT=KT[hs, tp],
                                                                                      rhs=QT[hs, tq], start=True, stop=True),
                             reads=rd, writes=skeys)
                S.op("act", lambda e, s_=s_, lo=lo: e.activation(out=Et[s_][:, :, lo:256], in_=Sp[s_][:, :, lo:256], func=AF.Exp),
                     reads=skeys, writes=[("E", s_)])
                S.op(meng, lambda e, s_=s_, lo=lo: e.tensor_tensor(out=Pt[s_][:, :, lo:256], in0=Et[s_][:, :, lo:256],
                                                                  in1=g.mask2[:, lo:256].unsqueeze(1).to_broadcast([128, 2, 256 - lo]),
                                                                  op=ALU.mult),
                     reads=[("E", s_)], writes=[("P", s_)])

            def part2(stp):
                pi, d, r, bb, nb, s_, o_ = stp["pi"], stp["d"], stp["r"], stp["bb"], stp["nb"], stp["s_"], stp["o_"]
                blk = r * nb + bb
                tq = sl(r + d * 128 * bb, 128, d)
                gq = _groups(d, bb)
                vrd = [("Vt", d, blk // 4), ("Vone", d)] + ([("Vt", d, (blk - 1) // 4)] if bb > 0 else [])
                for h in range(2):
                    if bb > 0:
                        S.op("pe", lambda e, s_=s_, o_=o_, h=h, blk=blk, d=d: e.matmul(Op[o_][0:65, h * 128:(h + 1) * 128],
                                                                                      lhsT=Vt[d][:, blk - 1, h, :], rhs=Pt[s_][:, h, 0:128],
                                                                                      start=True, stop=False),
                             reads=[("P", s_)] + vrd, writes=[("O", o_)])
                    S.op("pe", lambda e, s_=s_, o_=o_, h=h, blk=blk, d=d, bb=bb: e.matmul(Op[o_][0:65, h * 128:(h + 1) * 128],
                                                                                         lhsT=Vt[d][:, blk, h, :], rhs=Pt[s_][:, h, 128:256],
                                                                                         start=(bb == 0), stop=True),
                         reads=[("P", s_)] + vrd, writes=[("O", o_)])
                akeys = [("acc", x, r % 4) for x in gq] if d > 1 else [("acc", gq[0], x) for x in range(4)]
                o_view = Op[o_][0:65, 0:256].rearrange("p (h q) -> p h q", h=2)
                if pi == 0:
                    S.op("dve", lambda e, tq=tq, o_view=o_view: e.tensor_copy(out=Acc[:, :, tq], in_=o_view),
                         reads=[("O", o_)], writes=akeys)
                else:
                    S.op("dve", lambda e, tq=tq, o_view=o_view: e.tensor_tensor(out=Acc[:, :, tq], in0=Acc[:, :, tq], in1=o_view, op=ALU.add),
                         reads=[("O", o_)] + akeys, writes=akeys)

            LA = NSB - 1
            for i in range(min(LA, len(steps))):
                part1(steps[i])
            for i in range(len(steps)):
                if i + LA < len(steps):
                    part1(steps[i + LA])
                part2(steps[i])
            for h in range(2):
                for n in range(8):
                    ts_ = slice(n * 512, (n + 1) * 512)
                    b = fj % 2
                    fj += 1
                    ijl = banks[pj % len(banks)]
                    pj += 1
                    ijz = banks[pj % len(banks)]
                    pj += 1
                    acc_rd = [("acc", n, x) for x in range(4)]
                    S.op("pe", lambda e, ijl=ijl, h=h, ts_=ts_: e.matmul(bk(ijl)[0:64, :], lhsT=g.cf[0:65, 5, 0:64], rhs=Acc[0:65, h, ts_],
                                                                         start=True, stop=True),
                         reads=acc_rd, writes=[bkey(ijl)])
                    for ch in range(8):
                        S.op("pe", lambda e, ijz=ijz, ch=ch, h=h, ts_=ts_, wt=wt: e.matmul(bk(ijz)[0:64, :], lhsT=wt["z"][:, ch, 64 * h:64 * h + 64],
                                                                                          rhs=g.hT[:, ch, ts_], start=(ch == 0), stop=(ch == 7)),
                             reads=[wk("z")], writes=[bkey(ijz)])
                    if sp["sink"] is not None:
                        S.op("act", lambda e, b=b, ijl=ijl, h=h, sk=sk: e.activation(out=Ut[b][:], in_=bk(ijl)[0:64, :], func=AF.Ln,
                                                                                     bias=sk[:, h:h + 1]),
                             reads=[bkey(ijl), ("sk", si % 2)], writes=[("U", b)])
                    else:
                        S.op("act", lambda e, b=b, ijl=ijl: e.activation(out=Ut[b][:], in_=bk(ijl)[0:64, :], func=AF.Ln),
                             reads=[bkey(ijl)], writes=[("U", b)])
                    S.op("act", lambda e, b=b, ijz=ijz: e.activation(out=Tt[b][:], in_=bk(ijz)[0:64, :], func=AF.Exp, scale=-1.0),
                         reads=[bkey(ijz)], writes=[("T", b)])
                    S.op("act", lambda e, b=b: e.activation(out=Tt[b][:], in_=Tt[b][:], func=AF.Ln, bias=1.0),
                         reads=[("T", b)], writes=[("T", b)])
                    S.op("pool", lambda e, b=b: e.tensor_tensor(out=Ut[b][:], in0=Ut[b][:], in1=Tt[b][:], op=ALU.add),
                         reads=[("U", b), ("T", b)], writes=[("U", b)])
                    S.op("act", lambda e, b=b: e.activation(out=Ut[b][:], in_=Ut[b][:], func=AF.Exp, scale=-1.0),
                         reads=[("U", b)], writes=[("U", b)])
                    S.op("dve", lambda e, b=b, h=h, ts_=ts_: e.tensor_tensor(out=Ut[b][:], in0=Acc[0:64, h, ts_], in1=Ut[b][:], op=ALU.mult),
                         reads=[("U", b)] + acc_rd, writes=[("U", b)])
                    S.op("dve", lambda e, b=b, ijz=ijz: e.tensor_tensor(out=Yb[b][:], in0=Ut[b][:], in1=bk(ijz)[0:64, :], op=ALU.mult),
                         reads=[("U", b), bkey(ijz)], writes=[("Y", b)])
                    row0 = sp["rows"][h]
                    S.op("sp", lambda e, b=b, row0=row0, ts_=ts_: e.dma_start(out=g.ycat[row0:row0 + 64, ts_], in_=Yb[b][:]),
                         reads=[("Y", b)], dma_key=("aY", b))
        S.emit_block()


def phase_ssd(g, L, grp):
    nc, S = g.nc, g.S
    NW = 1296
    zc0, xc0, bc0, cc0, dc0 = 2048 + grp * 512, 3072 + grp * 512, 4096 + grp * 128, 4352 + grp * 128, 4608 + grp * 8
    with contextlib.ExitStack() as st:
        sbt = lambda name, shape, dt: st.enter_context(g.sbuf(name, shape, dt))
        pst = lambda name, shape, dt: st.enter_context(g.psum(name, shape, dt))
        wB = sbt("wB", [128, 8, NW], BF16)
        wst = [sbt("bwst%d" % i, [128, 8, 128], F32) for i in range(2)]
        prm_r = sbt("prm_r", [36, 128], F32)
        prm = sbt("prm", [128, 36], F32)
        hp = sbt("hp", [128, 3, 8], F32)
        pre = [sbt("pre%d" % i, [128, 515], F32) for i in range(2)]
        halo = sbt("halo", [128, 6, 3], F32)
        cacc = [sbt("cacc%d" % i, [128, 512], F32) for i in range(2)]
        ctmp = cacc[0]
        xc = [sbt("xc%d" % i, [128, 4, 512], BF16) for i in range(2)]
        fs = [sbt("fs%d" % i, [128, 6, 4, 8], F32) for i in range(2)]
        BT = [sbt("BTt%d" % i, [128, 512], BF16) for i in range(2)]
        CT = [sbt("CTt%d" % i, [128, 512], BF16) for i in range(2)]
        szs = [sbt("szs%d" % i, [128, 4, 512], F32) for i in range(2)]
        dts = [sbt("dts%d" % i, [128, 3, 4, 8], F32) for i in range(2)]
        NQ = 3
        xtm = [sbt("xtm%d" % i, [128, 512], BF16) for i in range(NQ)]
        Btm = [sbt("Btm%d" % i, [128, 128], BF16) for i in range(NQ)]
        xdt = [sbt("xdt%d" % i, [128, 512], BF16) for i in range(NQ)]
        xdte = [sbt("xdte%d" % i, [128, 512], BF16) for i in range(NQ)]
        tD = [sbt("tD%d" % i, [128, 512], BF16) for i in range(NQ)]
        MT = [sbt("MT%d" % i, [128, 8, 128], BF16) for i in range(NQ)]
        dAh = [sbt("dAh%d" % i, [128, 4, 8], BF16) for i in range(2)]
        dAl = [sbt("dAl%d" % i, [128, 4, 8], BF16) for i in range(2)]
        dAr = sbt("dAr", [128, 4, 8], F32)
        rhsH = sbt("rhsH", [128, 8, 128], BF16)
        rhsL = sbt("rhsL", [128, 8, 128], BF16)
        LT = sbt("LT", [128, 8, 128], BF16)
        GTm = sbt("GTm", [128, 128], F32)
        yt = sbt("yt", [128, 512], F32)
        nst = sbt("nst", [128, 4], F32)
        yn = [sbt("yn%d" % i, [128, 512], BF16) for i in range(2)]
        Ybuf = [sbt("Ybuf%d" % i, [128, 4, 512], BF16) for i in range(2)]
        H = sbt("H", [128, 512], F32)
        Hb = sbt("Hb", [128, 512], BF16)
        PA0 = pst("PA0", [128, 512], F32)
        PSEGs = [pst("PSEG%d" % i, [128, 512], F32) for i in range(2)]
        Psm = pst("Psm", [128, 512], F32)
        PBy = pst("PBy", [128, 512], F32)
        PBo = pst("PBo", [128, 512], F32)
        PS2 = pst("PS2", [128, 512], F32)
        Pbf = pst("Pbf", [128, 8, 128], BF16)
        wv_in = g.w_in[L].rearrange("(ch p) n -> p ch n", p=128)

        pieces = [(xc0 + i * 128, 128, 512 + i * 128) for i in range(4)] + [(bc0, 128, 1024), (cc0, 128, 1152)]
        pieces += [(zc0 + i * 128, 128, i * 128) for i in range(4)] + [(dc0, 8, 1280)]
        wball = [("wB", i) for i in range(len(pieces))]
        for i, (c0, cn, o0) in enumerate(pieces):
            b = i % 2
            S.op("sp", lambda e, b=b, c0=c0, cn=cn: e.dma_start(out=wst[b][:, :, 0:cn], in_=wv_in[:, :, c0:c0 + cn]),
                 writes=[("wst", b)], dma_key=("bwst", b))
            ceng = ("act", "dve")[i % 2]
            if ceng == "act":
                S.op("act", lambda e, b=b, cn=cn, o0=o0: e.activation(out=wB[:, :, o0:o0 + cn], in_=wst[b][:, :, 0:cn], func=AF.Copy),
                     reads=[("wst", b)], writes=[("wB", i)])
            else:
                S.op(ceng, lambda e, b=b, cn=cn, o0=o0: e.tensor_copy(out=wB[:, :, o0:o0 + cn], in_=wst[b][:, :, 0:cn]),
                     reads=[("wst", b)], writes=[("wB", i)])
        S.op("pool", lambda e: e.memset(prm_r[:], 0.0), writes=["prm_r0"])
        cw = g.conv_w[L].rearrange("k (cc p) -> k cc p", p=128)
        cbv = g.conv_b[L:L + 1, :].rearrange("o (cc p) -> (o cc) p", p=128)
        swv = g.ssm_w[L:L + 1, :].rearrange("o (cc p) -> (o cc) p", p=128)
        k_ = 0
        for tap in range(4):
            S.op("sp", lambda e, tap=tap: e.dma_start(out=prm_r[tap * 6:tap * 6 + 4, :], in_=cw[tap, grp * 4:grp * 4 + 4, :]),
                 reads=["prm_r0"], writes=[("prm_r", k_)], dma_key=("prm", k_))
            k_ += 1
            for j, c_ in ((4, 8 + grp), (5, 10 + grp)):
                S.op("sp", lambda e, tap=tap, j=j, c_=c_: e.dma_start(out=prm_r[tap * 6 + j:tap * 6 + j + 1, :], in_=cw[tap, c_:c_ + 1, :]),
                     reads=["prm_r0"], writes=[("prm_r", k_)], dma_key=("prm", k_))
                k_ += 1
        S.op("sp", lambda e: e.dma_start(out=prm_r[24:28, :], in_=cbv[grp * 4:grp * 4 + 4, :]),
             reads=["prm_r0"], writes=[("prm_r", k_)], dma_key=("prm", k_))
        k_ += 1
        for j, c_ in ((28, 8 + grp), (29, 10 + grp)):
            S.op("sp", lambda e, j=j, c_=c_: e.dma_start(out=prm_r[j:j + 1, :], in_=cbv[c_:c_ + 1, :]),
                 reads=["prm_r0"], writes=[("prm_r", k_)], dma_key=("prm", k_))
            k_ += 1
        S.op("sp", lambda e: e.dma_start(out=prm_r[30:34, :], in_=swv[grp * 4:grp * 4 + 4, :]),
             reads=["prm_r0"], writes=[("prm_r", k_)], dma_key=("prm", k_))
        k_ += 1
        S.op("pe", lambda e: e.transpose(PA0[:, 0:36], prm_r[:, :], g.cf[0:36, 0, 0:36]),
             reads=[("prm_r", i) for i in range(k_)], writes=["PA0"])
        S.op("dve", lambda e: e.tensor_copy(out=prm[:], in_=PA0[:, 0:36]), reads=["PA0"], writes=["prm"])
        for i, src_ in enumerate((g.dt_bias, g.a_log, g.d_skip)):
            S.op("sp", lambda e, i=i, src_=src_: e.dma_start(out=hp[:, i, :], in_=src_[L:L + 1, grp * 8:grp * 8 + 8].partition_broadcast(128)),
                 writes=[("hp", i)], dma_key=("hp", i))
        S.op("act", lambda e: e.activation(out=hp[:, 1, :], in_=hp[:, 1, :], func=AF.Exp), reads=[("hp", 1)], writes=[("hp", 1)])
        S.op("dve", lambda e: e.tensor_scalar(out=hp[:, 1, :], in0=hp[:, 1, :], scalar1=-1.0, scalar2=None, op0=ALU.mult),
             reads=[("hp", 1)], writes=[("hp", 1)])
        S.op("pool", lambda e: e.memset(halo[:], 0.0), writes=["halo"])
        S.op("pool", lambda e: e.memset(H[:], 0.0), writes=["H"])
        S.op("pool", lambda e: e.memset(Hb[:], 0.0), writes=["Hb"])

        tri = g.cf[:, 2, :]
        u1 = g.cf[:, 3, :]
        ones = g.cf[:, 4, :]
        identf = g.cf[:, 0, :]
        v8 = lambda ap: ap.rearrange("p (h d) -> p h d", h=8)
        b8 = lambda ap: ap.unsqueeze(2).to_broadcast([128, 8, 64])

        def mm_group(out_ap, lhs_fn, rhs_fn, reads, writes):
            def fn(e):
                ins = None
                for ch in range(8):
                    ins = e.matmul(out_ap, lhsT=lhs_fn(ch), rhs=rhs_fn(ch), start=(ch == 0), stop=(ch == 7))
                return ins
            S.op("pe", fn, reads=reads, writes=writes)

        fbanks = [(PA0, "PA0"), (PBy, "PBy"), (PBo, "PBo"), (PS2, "PS2")]

        def front(sc):
            p = sc % 2
            ts_ = slice(sc * 512, (sc + 1) * 512)
            def f_proj(j):
                pb = j % 2
                wofs = 512 + j * 128 if j < 4 else (1024 if j == 4 else 1152)
                fb, fk = fbanks[j % 4]
                mm_group(fb[:], lambda ch, wofs=wofs: wB[:, ch, wofs:wofs + 128], lambda ch: g.hT[:, ch, ts_], [("wB", j)], [fk])
                S.op("pool", lambda e, pb=pb, j=j: e.tensor_copy(out=pre[pb][:, 0:3], in_=halo[:, j, :]),
                     reads=["halo"], writes=[("pre", pb)])
                S.op("act", lambda e, pb=pb, fb=fb: e.activation(out=pre[pb][:, 3:515], in_=fb[:], func=AF.Copy),
                     reads=[fk], writes=[("pre", pb)])
                S.op("pool", lambda e, pb=pb, j=j: e.tensor_copy(out=halo[:, j, :], in_=pre[pb][:, 512:515]),
                     reads=[("pre", pb)], writes=["halo"])

            def f_conv(j):
                pb = j % 2
                if False:
                    S.op("pool", lambda e, pb=pb, j=j: e.tensor_scalar(out=cacc[pb][:], in0=pre[pb][:, 0:512], scalar1=prm[:, j:j + 1], scalar2=None,
                                                                       op0=ALU.mult),
                         reads=[("pre", pb), "prm"], writes=[("cacc", pb)])
                    for tap in range(1, 4):
                        S.op("pool", lambda e, pb=pb, j=j, tap=tap: e.tensor_scalar(out=ctmp[:], in0=pre[pb][:, tap:tap + 512],
                                                                                    scalar1=prm[:, tap * 6 + j:tap * 6 + j + 1], scalar2=None,
                                                                                    op0=ALU.mult),
                             reads=[("pre", pb), "prm"], writes=["ctmp"])
                        S.op("pool", lambda e, pb=pb: e.tensor_tensor(out=cacc[pb][:], in0=cacc[pb][:], in1=ctmp[:], op=ALU.add),
                             reads=["ctmp", ("cacc", pb)], writes=[("cacc", pb)])
                else:
                    S.op("dve", lambda e, pb=pb, j=j: e.tensor_scalar(out=cacc[pb][:], in0=pre[pb][:, 0:512], scalar1=prm[:, j:j + 1], scalar2=None,
                                                                      op0=ALU.mult),
                         reads=[("pre", pb), "prm"], writes=[("cacc", pb)])
                    for tap in range(1, 4):
                        S.op("dve", lambda e, pb=pb, j=j, tap=tap: e.scalar_tensor_tensor(out=cacc[pb][:], in0=pre[pb][:, tap:tap + 512],
                                                                                         scalar=prm[:, tap * 6 + j:tap * 6 + j + 1], in1=cacc[pb][:],
                                                                                         op0=ALU.mult, op1=ALU.add),
                             reads=[("pre", pb), ("cacc", pb), "prm"], writes=[("cacc", pb)])

            def f_silu(j):
                pb = j % 2
                dst = xc[p][:, j, :] if j < 4 else (BT[p][:] if j == 4 else CT[p][:])
                dkey = ("xc", p, j) if j < 4 else (("BT", p) if j == 4 else ("CT", p))
                S.op("act", lambda e, pb=pb, j=j, dst=dst: e.activation(out=dst, in_=cacc[pb][:], func=AF.Silu, bias=prm[:, 24 + j:25 + j]),
                     reads=[("cacc", pb), "prm"], writes=[dkey])

            f_proj(0)
            for j in range(6):
                if j + 1 < 6:
                    f_proj(j + 1)
                f_conv(j)
                f_silu(j)
            for k in range(4):
                tc_ = slice((sc * 4 + k) * 128, (sc * 4 + k + 1) * 128)
                fb, fk = fbanks[(k + 2) % 4]
                mm_group(fb[:], lambda ch, tc_=tc_: g.hT[:, ch, tc_], lambda ch: wB[:, ch, 0:512], [("wB", i) for i in range(6, 10)], [fk])
                S.op("act", lambda e, p=p, k=k, fb=fb: e.activation(out=szs[p][:, k, :], in_=fb[:], func=AF.Silu), reads=[fk], writes=[("sz", p, k)])
            for k in range(4):
                tc_ = slice((sc * 4 + k) * 128, (sc * 4 + k + 1) * 128)
                mm_group(Psm[:, k * 8:(k + 1) * 8], lambda ch, tc_=tc_: g.hT[:, ch, tc_], lambda ch: wB[:, ch, 1280:1288], [("wB", 10)], ["Psm"])
            S.op("dve", lambda e, p=p: e.tensor_tensor(out=dts[p][:, 0, :, :], in0=Psm[:, 0:32].rearrange("p (k h) -> p k h", k=4),
                                                       in1=hp[:, 0, :].unsqueeze(1).to_broadcast([128, 4, 8]), op=ALU.add),
                 reads=["Psm", ("hp", 0)], writes=[("dts", p, 0)])
            S.op("act", lambda e, p=p: e.activation(out=dts[p][:, 0, :, :], in_=dts[p][:, 0, :, :], func=AF.Exp),
                 reads=[("dts", p, 0)], writes=[("dts", p, 0)])
            S.op("act", lambda e, p=p: e.activation(out=dts[p][:, 1, :, :], in_=dts[p][:, 0, :, :], func=AF.Ln, bias=1.0),
                 reads=[("dts", p, 0)], writes=[("dts", p, 1)])
            S.op("dve", lambda e, p=p: e.tensor_tensor(out=dts[p][:, 2, :, :], in0=dts[p][:, 1, :, :],
                                                       in1=hp[:, 1, :].unsqueeze(1).to_broadcast([128, 4, 8]), op=ALU.mult),
                 reads=[("dts", p, 1), ("hp", 1)], writes=[("dts", p, 2)])
            S.op("dve", lambda e, p=p: e.tensor_copy(out=dAh[p][:], in_=dts[p][:, 2, :, :]), reads=[("dts", p, 2)], writes=[("dAh", p)])
            S.op("dve", lambda e, p=p: e.tensor_tensor(out=dAr[:], in0=dts[p][:, 2, :, :], in1=dAh[p][:], op=ALU.subtract),
                 reads=[("dts", p, 2), ("dAh", p)], writes=["dAr"])
            S.op("dve", lambda e, p=p: e.tensor_copy(out=dAl[p][:], in_=dAr[:]), reads=["dAr"], writes=[("dAl", p)])
            for k in range(4):
                S.op("pe", lambda e, p=p, k=k: e.matmul(Psm[:, 64 + k * 8:72 + k * 8], lhsT=tri, rhs=dts[p][:, 2, k, :], start=True, stop=True),
                     reads=[("dts", p, 2)], writes=["Psm"])
                S.op("pe", lambda e, p=p, k=k: e.matmul(Psm[:, 96 + k * 8:104 + k * 8], lhsT=ones, rhs=dts[p][:, 2, k, :], start=True, stop=True),
                     reads=[("dts", p, 2)], writes=["Psm"])
            v48 = lambda ap: ap.rearrange("p (k h) -> p k h", k=4)
            S.op("dve", lambda e, p=p: e.tensor_copy(out=fs[p][:, 0, :, :], in_=v48(Psm[:, 64:96])), reads=["Psm"], writes=[("fs", p, 0)])
            S.op("act", lambda e, p=p: e.activation(out=fs[p][:, 1, :, :], in_=fs[p][:, 0, :, :], func=AF.Exp),
                 reads=[("fs", p, 0)], writes=[("fs", p, 1)])
            S.op("dve", lambda e, p=p: e.tensor_tensor(out=fs[p][:, 2, :, :], in0=v48(Psm[:, 96:128]), in1=fs[p][:, 0, :, :], op=ALU.subtract),
                 reads=["Psm", ("fs", p, 0)], writes=[("fs", p, 2)])
            S.op("act", lambda e, p=p: e.activation(out=fs[p][:, 3, :, :], in_=fs[p][:, 2, :, :], func=AF.Exp),
                 reads=[("fs", p, 2)], writes=[("fs", p, 3)])
            S.op("act", lambda e, p=p: e.activation(out=fs[p][:, 4, :, :], in_=v48(Psm[:, 96:128]), func=AF.Exp),
                 reads=["Psm"], writes=[("fs", p, 4)])
            S.op("dve", lambda e, p=p: e.tensor_tensor(out=fs[p][:, 5, :, :], in0=dts[p][:, 1, :, :], in1=fs[p][:, 3, :, :], op=ALU.mult),
                 reads=[("dts", p, 1), ("fs", p, 3)], writes=[("fs", p, 5)])

        def stageA(c):
            sc, k, q = c // 4, c % 4, c % NQ
            p = sc % 2
            lc = slice(k * 128, (k + 1) * 128)
            dt_ = dts[p][:, 1, k, :]
            dA_ = dts[p][:, 2, k, :]
            trib = g.mask2[:, 128:256]
            S.op("dve", lambda e: e.tensor_tensor(out=rhsH[:], in0=trib.unsqueeze(1).to_broadcast([128, 8, 128]),
                                                  in1=dAh[p][:, k, :].unsqueeze(2).to_broadcast([128, 8, 128]), op=ALU.mult),
                 reads=[("dAh", p)], writes=["rhsH"])
            S.op("dve", lambda e: e.tensor_tensor(out=rhsL[:], in0=trib.unsqueeze(1).to_broadcast([128, 8, 128]),
                                                  in1=dAl[p][:, k, :].unsqueeze(2).to_broadcast([128, 8, 128]), op=ALU.mult),
                 reads=[("dAl", p)], writes=["rhsL"])
            for hf in range(2):
                S.op("pe", lambda e, hf=hf: e.matmul(PSEGs[hf][:], lhsT=g.u1b[:], rhs=rhsH[:, hf * 4:(hf + 1) * 4, :], start=True, stop=False),
                     reads=["rhsH"], writes=[("PSEG", hf)])
                S.op("pe", lambda e, hf=hf: e.matmul(PSEGs[hf][:], lhsT=g.u1b[:], rhs=rhsL[:, hf * 4:(hf + 1) * 4, :], start=False, stop=True),
                     reads=["rhsL"], writes=[("PSEG", hf)])
            for hf in range(2):
                S.op("act", lambda e, hf=hf: e.activation(out=LT[:, hf * 4:(hf + 1) * 4, :], in_=PSEGs[hf][:].rearrange("p (h l) -> p h l", h=4),
                                                          func=AF.Exp),
                     reads=[("PSEG", hf)], writes=[("LT", hf)])
            PA0b = PA0[:].bitcast(BF16)
            def xtr(e):
                ins = None
                for j in range(4):
                    ins = e.transpose(PA0b[:, j * 128:(j + 1) * 128], xc[p][:, j, lc], g.identb[:])
                return ins
            S.op("pe", xtr, reads=[("xc", p, j) for j in range(4)], writes=["PA0"])
            S.op("act", lambda e: e.activation(out=xtm[q][:], in_=PA0b[:, 0:512], func=AF.Copy), reads=["PA0"], writes=[("xtm", q)])
            S.op("pe", lambda e: e.transpose(Pbf[:, 0, :], BT[p][:, lc], g.identb[:]), reads=[("BT", p)], writes=["Pbf"])
            S.op("dve", lambda e: e.tensor_copy(out=Btm[q][:], in_=Pbf[:, 0, :]), reads=["Pbf"], writes=[("Btm", q)])
            S.op("pe", lambda e: e.matmul(Psm[:, 128:256], lhsT=BT[p][:, lc], rhs=CT[p][:, lc], start=True, stop=True),
                 reads=[("BT", p), ("CT", p)], writes=["Psm"])
            S.op("dve", lambda e: e.tensor_tensor(out=GTm[:], in0=Psm[:, 128:256], in1=tri, op=ALU.mult), reads=["Psm"], writes=["GTm"])
            S.op("dve", lambda e: e.tensor_tensor(out=MT[q][:], in0=LT[:], in1=GTm[:].unsqueeze(1).to_broadcast([128, 8, 128]), op=ALU.mult),
                 reads=[("LT", 0), ("LT", 1), "GTm"], writes=[("MT", q)])
            S.op("pool", lambda e: e.tensor_tensor(out=v8(xdt[q][:]), in0=v8(xtm[q][:]), in1=b8(dt_), op=ALU.mult),
                 reads=[("xtm", q), ("dts", p, 1)], writes=[("xdt", q)])
            S.op("pool", lambda e: e.tensor_tensor(out=v8(xdte[q][:]), in0=v8(xtm[q][:]), in1=b8(fs[p][:, 5, k, :]), op=ALU.mult),
                 reads=[("xtm", q), ("fs", p, 5)], writes=[("xdte", q)])
            S.op("pool", lambda e: e.tensor_tensor(out=v8(tD[q][:]), in0=v8(xtm[q][:]), in1=b8(hp[:, 2, :]), op=ALU.mult),
                 reads=[("xtm", q), ("hp", 2)], writes=[("tD", q)])

        def stageB1(c):
            sc, k, q = c // 4, c % 4, c % NQ
            p = sc % 2
            lc = slice(k * 128, (k + 1) * 128)
            S.op("pe", lambda e: e.matmul(PBo[:], lhsT=CT[p][:, lc], rhs=Hb[:], start=True, stop=True),
                 reads=[("CT", p), "Hb"], writes=["PBo"])
            S.op("pe", lambda e: e.matmul(PS2[:], lhsT=Btm[q][:], rhs=xdte[q][:], start=True, stop=True),
                 reads=[("Btm", q), ("xdte", q)], writes=["PS2"])
            S.op("pe", lambda e: e.matmul(PBy[:], lhsT=g.identb[:], rhs=tD[q][:], start=True, stop=False),
                 reads=[("tD", q)], writes=["PBy"])
            for h in range(8):
                S.op("pe", lambda e, h=h: e.matmul(PBy[:, h * 64:(h + 1) * 64], lhsT=MT[q][:, h, :], rhs=xdt[q][:, h * 64:(h + 1) * 64],
                                                   start=False, stop=(h == 7)),
                     reads=[("MT", q), ("xdt", q)], writes=["PBy"])
            S.op("dve", lambda e: e.tensor_tensor(out=v8(H[:]), in0=v8(H[:]), in1=b8(fs[p][:, 4, k, :]), op=ALU.mult),
                 reads=["H", ("fs", p, 4)], writes=["H"])
            S.op("dve", lambda e: e.tensor_tensor(out=H[:], in0=H[:], in1=PS2[:], op=ALU.add), reads=["H", "PS2"], writes=["H"])
            S.op("act", lambda e: e.activation(out=Hb[:], in_=H[:], func=AF.Copy), reads=["H"], writes=["Hb"])
            S.op("dve", lambda e: e.tensor_tensor(out=v8(yt[:]), in0=v8(PBo[:]), in1=b8(fs[p][:, 1, k, :]), op=ALU.mult),
                 reads=["PBo", ("fs", p, 1)], writes=["yt"])
            S.op("dve", lambda e: e.tensor_tensor(out=yt[:], in0=yt[:], in1=PBy[:], op=ALU.add), reads=["yt", "PBy"], writes=["yt"])
            S.op("pool", lambda e: e.tensor_tensor(out=yt[:], in0=yt[:], in1=szs[p][:, k, :], op=ALU.mult),
                 reads=["yt", ("sz", p, k)], writes=["yt"])
            yq = c % 2
            S.op("act", lambda e: e.activation(out=yn[yq][:], in_=yt[:], func=AF.Square, accum_out=nst[:, 0:1]),
                 reads=["yt"], writes=[("yn", yq), ("nst", 0)])
            S.op("act", lambda e: e.activation(out=nst[:, 1:2], in_=nst[:, 0:1], func=AF.Ln, scale=1.0 / 512, bias=EPS),
                 reads=[("nst", 0)], writes=[("nst", 1)])
            S.op("act", lambda e: e.activation(out=nst[:, 2:3], in_=nst[:, 1:2], func=AF.Exp, scale=-0.5),
                 reads=[("nst", 1)], writes=[("nst", 2)])
            S.op("act", lambda e: e.activation(out=yn[yq][:], in_=yt[:], func=AF.Copy, scale=nst[:, 2:3]),
                 reads=["yt", ("nst", 2)], writes=[("yn", yq)])

        def stageB2(c):
            sc, k, q = c // 4, c % 4, c % 2
            p = sc % 2
            lc = slice(k * 128, (k + 1) * 128)
            for j in range(4):
                S.op("pe", lambda e, j=j: e.transpose(Pbf[:, 4 + j, :], yn[q][:, j * 128:(j + 1) * 128], g.identb[:]),
                     reads=[("yn", q)], writes=["Pbf"])
            S.op("dve", lambda e: e.tensor_tensor(out=Ybuf[p][:, :, lc], in0=Pbf[:, 4:8, :],
                                                  in1=prm[:, 30:34].unsqueeze(2).to_broadcast([128, 4, 128]), op=ALU.mult),
                 reads=["Pbf", "prm"], writes=[("Ybuf", p)])
            if k == 3:
                r0 = 512 + grp * 512
                ts_ = slice(sc * 512, (sc + 1) * 512)
                S.op("sp", lambda e: e.dma_start(out=g.ycat[r0:r0 + 512, ts_].rearrange("(cc p) t -> p cc t", p=128), in_=Ybuf[p][:]),
                     reads=[("Ybuf", p)], dma_key=("Ybuf", p))

        NCH = SEQ // 128
        front(0)
        stageA(0)
        stageA(1)
        for c in range(NCH):
            S.capture()
            if c + 3 < NCH and (c + 3) % 4 == 0:
                front((c + 3) // 4)
            lf = S.end_capture()
            S.capture()
            stageB1(c)
            lb1 = S.end_capture()
            S.capture()
            if c + 2 < NCH:
                stageA(c + 2)
            la = S.end_capture()
            S.capture()
            if c >= 1:
                stageB2(c - 1)
            lb2 = S.end_capture()
            S.replay_merged([lf])
            S.replay_merged([lb1, la, lb2])
        stageB2(NCH - 1)
        S.emit_block()


def phase_out(g, L, src):
    nc, S = g.nc, g.S
    with contextlib.ExitStack() as st:
        sbt = lambda name, shape, dt: st.enter_context(g.sbuf(name, shape, dt))
        wo = sbt("wo", [128, 16, DM], BF16)
        wst = [sbt("owst%d" % i, [128, DM], F32) for i in range(3)]
        yc = [sbt("oyc%d" % i, [128, 16, 512], BF16) for i in range(2)]
        xt = [sbt("oxt%d" % i, [128, DM], F32) for i in range(2)]
        t1 = [sbt("ot1%d" % i, [128, DM], F32) for i in range(2)]
        junk = sbt("ojunk", [128, DM], BF16)
        stat = [sbt("ost%d" % i, [128, 4], F32) for i in range(2)]
        PY = [st.enter_context(g.psum("oPY%d" % i, [128, 2, 512], F32)) for i in range(2)]
        S.op("sp", lambda e: e.dma_start(out=yc[0][:], in_=g.ycat[:, 0:512].rearrange("(cc p) t -> p cc t", p=128)),
             writes=[("yc", 0)], dma_key=("oyc", 0))
        for cc in range(16):
            b = cc % 3
            S.op("sp", lambda e, b=b, cc=cc: e.dma_start(out=wst[b][:], in_=g.w_out[L, cc * 128:(cc + 1) * 128, :]),
                 writes=[("wst", b)], dma_key=("owst", b))
            ceng = ("act", "dve")[cc % 2]
            if ceng == "act":
                S.op("act", lambda e, b=b, cc=cc: e.activation(out=wo[:, cc, :], in_=wst[b][:], func=AF.Copy),
                     reads=[("wst", b)], writes=[("wo", cc)])
            else:
                S.op(ceng, lambda e, b=b, cc=cc: e.tensor_copy(out=wo[:, cc, :], in_=wst[b][:]), reads=[("wst", b)], writes=[("wo", cc)])
        def load_yc(gi):
            gb = gi % 2
            ts_ = slice(gi * 512, (gi + 1) * 512)
            S.op("sp", lambda e: e.dma_start(out=yc[gb][:], in_=g.ycat[:, ts_].rearrange("(cc p) t -> p cc t", p=128)),
                 writes=[("yc", gb)], dma_key=("oyc", gb))

        for gi in range(8):
            gb = gi % 2
            if gi + 1 < 8:
                load_yc(gi + 1)
            for k in range(4):
                t = gi * 4 + k
                b = t % 2
                rows = slice(t * 128, (t + 1) * 128)
                lc = slice(k * 128, (k + 1) * 128)
                S.op("sp", lambda e, b=b, rows=rows: e.dma_start(out=xt[b][:], in_=src[rows, :]), writes=[("x", b)], dma_key=("oxt", b))
                for nh in range(2):
                    for cc in range(16):
                        S.op("pe", lambda e, b=b, nh=nh, cc=cc, gb=gb, lc=lc: e.matmul(PY[b][:, nh, :], lhsT=yc[gb][:, cc, lc],
                                                                                       rhs=wo[:, cc, nh * 512:(nh + 1) * 512],
                                                                                       start=(cc == 0), stop=(cc == 15)),
                             reads=[("yc", gb), ("wo", cc)], writes=[("PY", b)])
                S.op("act", lambda e, b=b: e.activation(out=junk[:].rearrange("p (a n) -> p a n", a=2), in_=PY[b][:], func=AF.Square,
                                                        accum_out=stat[b][:, 0:1]),
                     reads=[("PY", b)], writes=["junk", ("s0", b)])
                S.op("act", lambda e, b=b: e.activation(out=stat[b][:, 1:2], in_=stat[b][:, 0:1], func=AF.Sqrt, scale=1.0 / DM, bias=EPS),
                     reads=[("s0", b)], writes=[("s1", b)])
                S.op("dve", lambda e, b=b: e.reciprocal(out=stat[b][:, 2:3], in_=stat[b][:, 1:2]), reads=[("s1", b)], writes=[("s2", b)])
                S.op("dve", lambda e, b=b: e.scalar_tensor_tensor(out=t1[b][:].rearrange("p (a n) -> p a n", a=2), in0=PY[b][:],
                                                                  scalar=stat[b][:, 2:3],
                                                                  in1=g.modb[:, 2 * DM:3 * DM].rearrange("p (a n) -> p a n", a=2),
                                                                  op0=ALU.mult, op1=ALU.mult),
                     reads=[("PY", b), ("s2", b)], writes=[("t1", b)])
                S.op("pool", lambda e, b=b: e.tensor_tensor(out=t1[b][:], in0=t1[b][:], in1=xt[b][:], op=ALU.add),
                     reads=[("t1", b), ("x", b)], writes=[("t1", b)])
                S.op("pool", lambda e, b=b, rows=rows: e.dma_start(out=g.out[rows, :], in_=t1[b][:]), reads=[("t1", b)], dma_key=("ot1", b))
        S.emit_block()


def _consts():
    i = np.arange(128)
    c = np.zeros((128, 7, 128), np.float32)
    c[:, 0, :] = np.eye(128)
    c[:, 1, :] = (i[:, None] >= i[None, :])
    c[:, 2, :] = (i[:, None] <= i[None, :])
    c[:, 3, :] = (i[:, None] > i[None, :])
    c[:, 4, :] = 1.0
    c[64, 5, 0:64] = 1.0
    return c


_PROG = {}
FUSED = True
_WNAMES = ("ada_w", "ada_b", "pre_norm_w", "post_norm_w", "w_in", "conv_w", "conv_b", "dt_bias", "a_log",
           "d_skip", "ssm_norm_w", "sinks", "w_out")


def kernel(x, c, ada_w, ada_b, pre_norm_w, post_norm_w, w_in, conv_w, conv_b,
           dt_bias, a_log, d_skip, ssm_norm_w, sinks, w_out):
    f = lambda a: np.ascontiguousarray(np.asarray(a, dtype=np.float32))
    ws = dict(ada_w=f(ada_w), ada_b=f(ada_b), pre_norm_w=f(pre_norm_w), post_norm_w=f(post_norm_w),
              w_in=f(w_in), conv_w=f(conv_w), conv_b=f(conv_b), dt_bias=f(dt_bias), a_log=f(a_log),
              d_skip=f(d_skip), ssm_norm_w=f(ssm_norm_w), sinks=f(sinks), w_out=f(w_out))
    x = f(x)
    c = f(c)
    consts = _consts()
    ccols = [np.ascontiguousarray(c[b].reshape(8, 128).T) for b in range(8)]
    if FUSED:
        if "fused" not in _PROG:
            _PROG["fused"] = build(n_layers=DEPTH, depth_dim=DEPTH)
        in_maps = []
        for b in range(8):
            m = dict(ws)
            m["consts"] = consts
            m["x"] = x[b]
            m["c_col"] = ccols[b]
            in_maps.append(m)
        res = run_bass_kernel_spmd(_PROG["fused"], in_maps, core_ids=list(range(8)))
        return np.stack([np.asarray(r["out"]) for r in res.results], axis=0).astype(np.float32)
    if "layer" not in _PROG:
        _PROG["layer"] = build(n_layers=1, depth_dim=1)
    cur = [x[b] for b in range(8)]
    for L in range(DEPTH):
        wl = {k: np.ascontiguousarray(ws[k][L:L + 1]) for k in _WNAMES}
        in_maps = []
        for b in range(8):
            m = dict(wl)
            m["consts"] = consts
            m["x"] = cur[b]
            m["c_col"] = ccols[b]
            in_maps.append(m)
        res = run_bass_kernel_spmd(_PROG["layer"], in_maps, core_ids=list(range(8)))
        cur = [np.ascontiguousarray(np.asarray(r["out"], dtype=np.float32)) for r in res.results]
    return np.stack(cur, axis=0).astype(np.float32)
```
